# Optimizing a Trainium2 kernel written in Bass

```python
import math
import jax, jax.numpy as jnp
from jax import lax
import numpy as np

D_MODEL = 1024
BATCH = 32
SEQ = 2048
DEPTH = 4
DEC_BATCH = 8
DEC_SEQ = 2048
PAST_LEN = 128

N_MIXERS = 4
HEAD_DIM = 64
GRID_W = 64
Q_BLOCK = 128
ROPE_THETA = 10000.0
NORM_EPS = 1e-6
DA_HEADS = D_MODEL // (2 * HEAD_DIM)
DA_SUBLN_EPS = 1e-5
GQ_HEADS = D_MODEL // HEAD_DIM
GQ_KV_HEADS = 4
GQ_GROUP = GQ_HEADS // GQ_KV_HEADS
NA_HEADS = D_MODEL // HEAD_DIM
NA_WIN_ROWS = 8
NA_WIN_COLS = 16
RW_HEADS = D_MODEL // HEAD_DIM
RW_DECAY_LORA = 64
RW_AAA_LORA = 64
RW_GATE_LORA = 128
RW_GN_EPS = 64e-5
FFN_HIDDEN = 2816
CONV_WIDTH = 3

kernel_name = "hybrid_bidir_encoder_diff_gqa_natten_rwkv7"

F32 = jnp.float32


def _rms_norm(x, g, eps=NORM_EPS):
    xf = x.astype(F32)
    y = xf * lax.rsqrt(jnp.mean(xf * xf, axis=-1, keepdims=True) + eps)
    return (y * g.astype(F32)).astype(x.dtype)


def _rope(x, pos):
    dim = x.shape[-1]
    half = dim // 2
    inv = ROPE_THETA ** (-jnp.arange(half, dtype=F32) / half)
    ang = pos[:, None] * inv[None, :]
    cos = jnp.cos(ang)[None, :, None, :]
    sin = jnp.sin(ang)[None, :, None, :]
    xf = x.astype(F32)
    x1, x2 = xf[..., :half], xf[..., half:]
    return jnp.concatenate([x1 * cos - x2 * sin, x2 * cos + x1 * sin], axis=-1).astype(x.dtype)


def _axial_rope(x):
    S = x.shape[1]
    t = jnp.arange(S)
    row = (t // GRID_W).astype(F32)
    col = (t % GRID_W).astype(F32)
    h = x.shape[-1] // 2
    return jnp.concatenate([_rope(x[..., :h], row), _rope(x[..., h:], col)], axis=-1)


def _lambda_init(layer_idx):
    return 0.8 - 0.6 * math.exp(-0.3 * layer_idx)


def _diff_attention(x, w_qkv, q_norm, k_norm, lq1, lk1, lq2, lk2, subln, w_o, lambda_init):
    B, S, C = x.shape
    qkv = x @ w_qkv
    q, k, v = jnp.split(qkv, 3, axis=-1)
    pos = jnp.arange(S, dtype=F32)
    q = _rope(_rms_norm(q.reshape(B, S, 2 * DA_HEADS, HEAD_DIM), q_norm), pos)
    k = _rope(_rms_norm(k.reshape(B, S, 2 * DA_HEADS, HEAD_DIM), k_norm), pos)
    q = q.reshape(B, S, DA_HEADS, 2, HEAD_DIM)
    k = k.reshape(B, S, DA_HEADS, 2, HEAD_DIM)
    v = v.reshape(B, S, DA_HEADS, 2 * HEAD_DIM)
    lam = (jnp.exp(jnp.sum(lq1 * lk1).astype(F32)) - jnp.exp(jnp.sum(lq2 * lk2).astype(F32)) + lambda_init)
    scale = HEAD_DIM ** -0.5
    nb = S // Q_BLOCK
    q_blocks = q.reshape(B, nb, Q_BLOCK, DA_HEADS, 2, HEAD_DIM).transpose(1, 0, 2, 3, 4, 5)

    def block(q_blk):
        s = jnp.einsum('bqhcd,bkhcd->bhcqk', q_blk, k).astype(F32) * scale
        p = jax.nn.softmax(s, axis=-1)
        p = p[:, :, 0] - lam * p[:, :, 1]
        return jnp.einsum('bhqk,bkhe->bqhe', p.astype(v.dtype), v)

    o = lax.map(block, q_blocks)
    o = o.transpose(1, 0, 2, 3, 4).reshape(B, S, DA_HEADS, 2 * HEAD_DIM)
    o = _rms_norm(o, subln, eps=DA_SUBLN_EPS) * (1.0 - lambda_init)
    return o.reshape(B, S, C) @ w_o


def _gqa_axial(x, w_qkv, q_norm, k_norm, w_o):
    B, S, C = x.shape
    kvw = GQ_KV_HEADS * HEAD_DIM
    qkv = x @ w_qkv
    q = qkv[..., :C].reshape(B, S, GQ_HEADS, HEAD_DIM)
    k = qkv[..., C:C + kvw].reshape(B, S, GQ_KV_HEADS, HEAD_DIM)
    v = qkv[..., C + kvw:].reshape(B, S, GQ_KV_HEADS, HEAD_DIM)
    q = _axial_rope(_rms_norm(q, q_norm))
    k = _axial_rope(_rms_norm(k, k_norm))
    scale = HEAD_DIM ** -0.5
    nb = S // Q_BLOCK
    q_blocks = q.reshape(B, nb, Q_BLOCK, GQ_KV_HEADS, GQ_GROUP, HEAD_DIM).transpose(1, 0, 2, 3, 4, 5)

    def block(q_blk):
        s = jnp.einsum('bqkgd,bskd->bkgqs', q_blk, k).astype(F32) * scale
        p = jax.nn.softmax(s, axis=-1)
        return jnp.einsum('bkgqs,bskd->bqkgd', p.astype(v.dtype), v)

    o = lax.map(block, q_blocks)
    o = o.transpose(1, 0, 2, 3, 4, 5).reshape(B, S, C)
    return o @ w_o


def _neighborhood_attention(x, w_qkv, q_norm, k_norm, rel_bias, w_o):
    B, S, C = x.shape
    rows = S // GRID_W
    wr = min(NA_WIN_ROWS, rows)
    wc = NA_WIN_COLS
    qkv = x @ w_qkv
    q, k, v = jnp.split(qkv, 3, axis=-1)
    q = _rms_norm(q.reshape(B, S, NA_HEADS, HEAD_DIM), q_norm) * (HEAD_DIM ** -0.5)
    k = _rms_norm(k.reshape(B, S, NA_HEADS, HEAD_DIM), k_norm)
    kr = k.reshape(B, rows, GRID_W, NA_HEADS, HEAD_DIM)
    vr = v.reshape(B, rows, GRID_W, NA_HEADS, HEAD_DIM)
    q_rows = q.reshape(B, rows, GRID_W, NA_HEADS, HEAD_DIM).transpose(1, 0, 2, 3, 4)
    r_ids = jnp.arange(rows)
    row_start = jnp.clip(r_ids - wr // 2, 0, rows - wr)
    cols = jnp.arange(GRID_W)
    col_start = jnp.clip(cols - wc // 2, 0, GRID_W - wc)
    col_mask = (cols[None, :] >= col_start[:, None]) & (cols[None, :] < col_start[:, None] + wc)
    dc = jnp.clip(cols[None, :] - cols[:, None] + (wc - 1), 0, 2 * wc - 2)

    def row_block(args):
        q_row, rs, r = args
        kb = lax.dynamic_slice_in_dim(kr, rs, wr, axis=1)
        vb = lax.dynamic_slice_in_dim(vr, rs, wr, axis=1)
        s = jnp.einsum('bqhd,bjkhd->bhqjk', q_row, kb).astype(F32)
        dr = rs + jnp.arange(wr) - r + (NA_WIN_ROWS - 1)
        bias = rel_bias[:, dr[None, :, None], dc[:, None, :]]
        s = s + bias[None].astype(F32)
        s = jnp.where(col_mask[None, None, :, None, :], s, -1e30)
        p = jax.nn.softmax(s.reshape(B, NA_HEADS, GRID_W, wr * GRID_W), axis=-1)
        p = p.reshape(B, NA_HEADS, GRID_W, wr, GRID_W).astype(vb.dtype)
        return jnp.einsum('bhqjk,bjkhd->bqhd', p, vb)

    o = lax.map(row_block, (q_rows, row_start, r_ids))
    o = o.transpose(1, 0, 2, 3, 4).reshape(B, S, C)
    return o @ w_o


def _wkv7_scan(r, w, k, v, a, b, reverse):
    B, S, H, N = r.shape
    xs = tuple(jnp.moveaxis(t, 1, 0) for t in (r, w, k, v, a, b))

    def step(state, inp):
        r_t, w_t, k_t, v_t, a_t, b_t = inp
        sa = jnp.einsum('bhij,bhj->bhi', state, a_t)
        state = (state * w_t[:, :, None, :] + sa[..., None] * b_t[:, :, None, :]
                 + v_t[..., None] * k_t[:, :, None, :])
        return state, jnp.einsum('bhij,bhj->bhi', state, r_t)

    s0 = jnp.zeros((B, H, N, N), F32)
    _, ys = lax.scan(step, s0, xs, reverse=reverse)
    return jnp.moveaxis(ys, 0, 1)


def _rwkv7_time_mix(x, mu, w_r, w_k, w_v, w_o, g1, g2, k_k, k_a, r_k, ln_g, ln_b, w0, w1, w2, a0, a1, a2):
    B, S, C = x.shape
    H, N = RW_HEADS, HEAD_DIM
    xp = jnp.pad(x, ((0, 0), (1, 1), (0, 0)))
    xx = 0.5 * (xp[:, :-2] + xp[:, 2:]) - x
    xr, xw, xk, xv, xa, xg = [x + xx * mu[j] for j in range(6)]
    r = (xr @ w_r).reshape(B, S, H, N).astype(F32)
    k = (xk @ w_k).astype(F32)
    v = (xv @ w_v).reshape(B, S, H, N).astype(F32)
    g = jax.nn.sigmoid(xg @ g1) @ g2
    kk = (k * k_k.astype(F32)).reshape(B, S, H, N)
    kk = kk / jnp.maximum(jnp.sqrt(jnp.sum(kk * kk, axis=-1, keepdims=True)), 1e-12)
    ys = []
    ks = []
    for d in range(2):
        w_log = -jax.nn.softplus(-(w0[d] + jnp.tanh(xw @ w1[d]) @ w2[d]).astype(F32)) - 0.5
        decay = jnp.exp(-jnp.exp(w_log)).reshape(B, S, H, N)
        a = jax.nn.sigmoid((a0[d] + (xa @ a1[d]) @ a2[d]).astype(F32))
        kd = (k * (1.0 + (a - 1.0) * k_a.astype(F32))).reshape(B, S, H, N)
        ah = a.reshape(B, S, H, N)
        ys.append(_wkv7_scan(r, decay, kd, v, -kk, kk * ah, reverse=(d == 1)))
        ks.append(kd)
    y = ys[0] + ys[1]
    mean = jnp.mean(y, axis=-1, keepdims=True)
    var = jnp.mean(jnp.square(y - mean), axis=-1, keepdims=True)
    yn = ((y - mean) * lax.rsqrt(var + RW_GN_EPS)).reshape(B, S, C)
    yn = yn * ln_g.astype(F32) + ln_b.astype(F32)
    k_bonus = 0.5 * (ks[0] + ks[1])
    bonus = jnp.sum(r * k_bonus * r_k.astype(F32), axis=-1, keepdims=True) * v
    out = (yn + bonus.reshape(B, S, C)) * g.astype(F32)
    return out.astype(x.dtype) @ w_o


def _conv_ffn(x, w_in, conv_w, conv_b, w_out):
    u = x @ w_in
    up = jnp.pad(u, ((0, 0), (1, 1), (0, 0)))
    u = up[:, :-2] * conv_w[0] + up[:, 1:-1] * conv_w[1] + up[:, 2:] * conv_w[2] + conv_b
    gate, val = jnp.split(u, 2, axis=-1)
    return (jax.nn.silu(gate) * val) @ w_out


def setup_inputs(seed: int = 0) -> dict:
    key = jax.random.key(seed)
    ks = iter(jax.random.split(key, 128))
    C, d = D_MODEL, HEAD_DIM

    def nrm(shape, scale):
        return jax.random.normal(next(ks), shape, F32) * scale

    def gain(shape):
        return 1.0 + nrm(shape, 0.02)

    p = {}
    p['x_prompt'] = nrm((BATCH, SEQ, C), 1.0)
    p['x_sample'] = nrm((DEC_BATCH, DEC_SEQ, C), 1.0)
    for i in range(DEPTH):
        p[f'n{i}_attn'] = gain((C,))
        p[f'n{i}_ffn'] = gain((C,))
    p['a_w_qkv'] = nrm((C, 3 * C), C ** -0.5)
    p['a_q_norm'] = gain((d,))
    p['a_k_norm'] = gain((d,))
    p['a_lq1'] = nrm((d,), 0.1)
    p['a_lk1'] = nrm((d,), 0.1)
    p['a_lq2'] = nrm((d,), 0.1)
    p['a_lk2'] = nrm((d,), 0.1)
    p['a_subln'] = gain((2 * d,))
    p['a_w_o'] = nrm((C, C), C ** -0.5)
    p['b_w_qkv'] = nrm((C, C + 2 * GQ_KV_HEADS * d), C ** -0.5)
    p['b_q_norm'] = gain((d,))
    p['b_k_norm'] = gain((d,))
    p['b_w_o'] = nrm((C, C), C ** -0.5)
    p['c_w_qkv'] = nrm((C, 3 * C), C ** -0.5)
    p['c_q_norm'] = gain((d,))
    p['c_k_norm'] = gain((d,))
    p['c_rel_bias'] = nrm((NA_HEADS, 2 * NA_WIN_ROWS - 1, 2 * NA_WIN_COLS - 1), 0.1)
    p['c_w_o'] = nrm((C, C), C ** -0.5)
    p['d_mu'] = jax.random.uniform(next(ks), (6, C), F32)
    p['d_w_r'] = nrm((C, C), C ** -0.5)
    p['d_w_k'] = nrm((C, C), C ** -0.5)
    p['d_w_v'] = nrm((C, C), C ** -0.5)
    p['d_w_o'] = nrm((C, C), C ** -0.5)
    p['d_g1'] = nrm((C, RW_GATE_LORA), C ** -0.5)
    p['d_g2'] = nrm((RW_GATE_LORA, C), RW_GATE_LORA ** -0.5)
    p['d_k_k'] = 0.85 + nrm((C,), 0.02)
    p['d_k_a'] = gain((C,))
    p['d_r_k'] = nrm((RW_HEADS, d), 0.1)
    p['d_ln_g'] = gain((C,))
    p['d_ln_b'] = nrm((C,), 0.02)
    p['d_w0'] = jax.random.uniform(next(ks), (2, C), F32, -6.5, -1.5)
    p['d_w1'] = nrm((2, C, RW_DECAY_LORA), C ** -0.5)
    p['d_w2'] = nrm((2, RW_DECAY_LORA, C), 0.1 * RW_DECAY_LORA ** -0.5)
    p['d_a0'] = nrm((2, C), 0.1)
    p['d_a1'] = nrm((2, C, RW_AAA_LORA), C ** -0.5)
    p['d_a2'] = nrm((2, RW_AAA_LORA, C), 0.1 * RW_AAA_LORA ** -0.5)
    for i in range(DEPTH):
        p[f'f{i}_w_in'] = nrm((C, 2 * FFN_HIDDEN), C ** -0.5)
        p[f'f{i}_conv_w'] = nrm((CONV_WIDTH, 2 * FFN_HIDDEN), CONV_WIDTH ** -0.5)
        p[f'f{i}_conv_b'] = nrm((2 * FFN_HIDDEN,), 0.02)
        p[f'f{i}_w_out'] = nrm((FFN_HIDDEN, C), FFN_HIDDEN ** -0.5)
    return p


def reference(x_prompt, x_sample,
              n0_attn, n0_ffn, n1_attn, n1_ffn, n2_attn, n2_ffn, n3_attn, n3_ffn,
              a_w_qkv, a_q_norm, a_k_norm, a_lq1, a_lk1, a_lq2, a_lk2, a_subln, a_w_o,
              b_w_qkv, b_q_norm, b_k_norm, b_w_o,
              c_w_qkv, c_q_norm, c_k_norm, c_rel_bias, c_w_o,
              d_mu, d_w_r, d_w_k, d_w_v, d_w_o, d_g1, d_g2, d_k_k, d_k_a, d_r_k, d_ln_g, d_ln_b,
              d_w0, d_w1, d_w2, d_a0, d_a1, d_a2,
              f0_w_in, f0_conv_w, f0_conv_b, f0_w_out,
              f1_w_in, f1_conv_w, f1_conv_b, f1_w_out,
              f2_w_in, f2_conv_w, f2_conv_b, f2_w_out,
              f3_w_in, f3_conv_w, f3_conv_b, f3_w_out):
    norms = ((n0_attn, n0_ffn), (n1_attn, n1_ffn), (n2_attn, n2_ffn), (n3_attn, n3_ffn))
    ffns = ((f0_w_in, f0_conv_w, f0_conv_b, f0_w_out),
            (f1_w_in, f1_conv_w, f1_conv_b, f1_w_out),
            (f2_w_in, f2_conv_w, f2_conv_b, f2_w_out),
            (f3_w_in, f3_conv_w, f3_conv_b, f3_w_out))
    mix_a = (a_w_qkv, a_q_norm, a_k_norm, a_lq1, a_lk1, a_lq2, a_lk2, a_subln, a_w_o)
    mix_b = (b_w_qkv, b_q_norm, b_k_norm, b_w_o)
    mix_c = (c_w_qkv, c_q_norm, c_k_norm, c_rel_bias, c_w_o)
    mix_d = (d_mu, d_w_r, d_w_k, d_w_v, d_w_o, d_g1, d_g2, d_k_k, d_k_a, d_r_k, d_ln_g, d_ln_b,
             d_w0, d_w1, d_w2, d_a0, d_a1, d_a2)

    def encode(x):
        for i in range(DEPTH):
            h = _rms_norm(x, norms[i][0])
            kind = i % N_MIXERS
            if kind == 0:
                h = _diff_attention(h, *mix_a, lambda_init=_lambda_init(i))
            elif kind == 1:
                h = _gqa_axial(h, *mix_b)
            elif kind == 2:
                h = _neighborhood_attention(h, *mix_c)
            else:
                h = _rwkv7_time_mix(h, *mix_d)
            x = x + h
            x = x + _conv_ffn(_rms_norm(x, norms[i][1]), *ffns[i])
        return x

    y_prompt = encode(x_prompt)
    y_sample = encode(x_sample)
    return (y_prompt, y_sample)
```

```python
import math
import contextlib
import numpy as np
import ml_dtypes
import concourse.bass as bass
import concourse.mybir as mybir
from concourse.bass_utils import run_bass_kernel_spmd

F32 = mybir.dt.float32
BF16 = mybir.dt.bfloat16
AF = mybir.ActivationFunctionType
ALU = mybir.AluOpType
AX = mybir.AxisListType

D = 1024
S_LEN = 2048
NT = 16
KC = 8
FH = 2816
NCH = 22
N_CORES = 8
SEQ_PER_CORE = 5
NORM_EPS = 1e-6


class Dep:
    __slots__ = ("w", "r", "name")

    def __init__(self, name=""):
        self.w = None
        self.r = {}
        self.name = name


class Sch:
    COMPUTE = ("pe", "dve", "act", "pool")

    def __init__(self, nc, n_sp_ring=40, n_pool_ring=8, same_eng_sync=True):
        self.nc = nc
        self.q = {e: [] for e in ("pe", "dve", "act", "pool", "sp")}
        self.sems = {}
        self.cnt = {}
        for e in self.COMPUTE:
            self.sems[e] = nc.alloc_semaphore(f"s_{e}")
            self.cnt[e] = 0
        self.ring = {}
        for qn, n in (("sp", n_sp_ring), ("pool", n_pool_ring)):
            self.ring[qn] = dict(n=n, i=0)
            for i in range(n):
                self.sems[(qn, i)] = nc.alloc_semaphore(f"d_{qn}{i}")
        self.seen = {e: {} for e in self.q}
        self.same_eng_sync = same_eng_sync
        self.nops = 0

    def _collect(self, eng, reads, writes, is_dma):
        need = {}

        def add(k, v, peng):
            if peng == eng and not is_dma and k in self.COMPUTE:
                if eng == "pe" or not self.same_eng_sync:
                    return
            if need.get(k, 0) < v:
                need[k] = v
        for d in reads:
            if d.w is not None:
                add(*d.w)
        for d in writes:
            if d.w is not None:
                add(*d.w)
            for k, (v, peng) in d.r.items():
                add(k, v, peng)
        out = []
        seen = self.seen[eng]
        for k, v in need.items():
            if seen.get(k, 0) < v:
                seen[k] = v
                out.append((k, v))
        return out

    def op(self, eng, fn, reads=(), writes=()):
        waits = self._collect(eng, reads, writes, False)
        self.cnt[eng] += 1
        v = self.cnt[eng]
        for d in reads:
            d.r[eng] = (v, eng)
        for d in writes:
            d.w = (eng, v, eng)
            d.r = {}
        self.q[eng].append((waits, fn, (eng, 1)))
        self.nops += 1

    def dma(self, qn, fn, reads=(), writes=()):
        ring = self.ring[qn]
        i = ring["i"]
        ring["i"] += 1
        n = ring["n"]
        slot, gen = i % n, i // n
        key = (qn, slot)
        waits = self._collect(qn, reads, writes, True)
        if gen > 0:
            seen = self.seen[qn]
            if seen.get(key, 0) < 16 * gen:
                seen[key] = 16 * gen
                waits.append((key, 16 * gen))
        v = 16 * (gen + 1)
        for d in reads:
            d.r[key] = (v, "dma")
        for d in writes:
            d.w = (key, v, "dma")
            d.r = {}
        self.q[qn].append((waits, fn, (key, 16)))
        self.nops += 1

    def _all_marks(self):
        marks = [(e, self.cnt[e]) for e in self.COMPUTE if self.cnt[e] > 0]
        for rq, ring in self.ring.items():
            n = ring["n"]
            for slot in range(n):
                c = (ring["i"] - slot + n - 1) // n
                if c > 0:
                    marks.append(((rq, slot), 16 * c))
        return marks

    def barrier(self):
        marks = self._all_marks()
        for e in self.q:
            waits = []
            seen = self.seen[e]
            for k, v in marks:
                if k == e:
                    continue
                if seen.get(k, 0) < v:
                    seen[k] = v
                    waits.append((k, v))
            if waits:
                self.q[e].append((waits, None, None))

    def final_wait(self, qn="pool"):
        waits = [(k, v) for k, v in self._all_marks() if k != qn]
        self.q[qn].append((waits, None, None))

    def emit(self):
        nc = self.nc
        sems = self.sems

        def replay(eng_obj, lst):
            for waits, fn, inc in lst:
                for k, v in waits:
                    eng_obj.wait_ge(sems[k], v)
                if fn is not None:
                    ins = fn(eng_obj)
                    ins.then_inc(sems[inc[0]], inc[1])

        with nc.Block() as block:
            @block.tensor
            def _(e):
                replay(e, self.q["pe"])

            @block.vector
            def _(e):
                replay(e, self.q["dve"])

            @block.scalar
            def _(e):
                replay(e, self.q["act"])

            @block.gpsimd
            def _(e):
                replay(e, self.q["pool"])

            @block.sync
            def _(e):
                replay(e, self.q["sp"])


def colchunk(w):
    K, N = w.shape
    return np.ascontiguousarray(w.reshape(K // 128, 128, N // 128, 128).transpose(2, 1, 0, 3))


class Packer:
    def __init__(self):
        self.parts = []
        self.off = {}
        self.n = 0

    def add(self, name, arr):
        a = np.ascontiguousarray(arr, dtype=np.float32).reshape(-1)
        self.off[name] = (self.n, arr.shape)
        self.parts.append(a)
        self.n += a.size

    def finish(self, align):
        pad = (-self.n) % align
        if pad:
            self.parts.append(np.zeros(pad, np.float32))
            self.n += pad
        return np.concatenate(self.parts)


class ColPacker:
    def __init__(self):
        self.parts = []
        self.off = {}
        self.n = 0

    def add(self, name, arr):
        a = np.ascontiguousarray(arr, dtype=np.float32).reshape(128, -1)
        self.off[name] = self.n
        self.parts.append(a)
        self.n += a.shape[1]

    def finish(self):
        return np.ascontiguousarray(np.concatenate(self.parts, axis=1))


def chan8(v):
    return np.ascontiguousarray(v.reshape(8, 128).T)


def rep(v):
    return np.ascontiguousarray(np.broadcast_to(v.reshape(1, -1), (128, v.size)))


def build_rope_tables():
    out = np.zeros((4, 128, S_LEN), np.float32)
    t = np.arange(S_LEN, dtype=np.float32)
    half = 32
    inv = (10000.0 ** (-np.arange(half, dtype=np.float32) / half)).astype(np.float32)
    ang = t[None, :] * inv[:, None]
    c0 = np.cos(ang).astype(np.float32)
    s0 = np.sin(ang).astype(np.float32)
    for p in range(128):
        i = (p % 64) % half
        out[0, p] = c0[i]
        out[1, p] = s0[i]
    half = 16
    inv = (10000.0 ** (-np.arange(half, dtype=np.float32) / half)).astype(np.float32)
    row = np.floor(t / 64.0).astype(np.float32)
    col = (t - row * 64.0).astype(np.float32)
    for p in range(128):
        dd = p % 64
        pos = row if dd < 32 else col
        i = (dd % 32) % half
        ang = pos * inv[i]
        out[2, p] = np.cos(ang).astype(np.float32)
        out[3, p] = np.sin(ang).astype(np.float32)
    return out


def rot_matrix_T(block):
    R = np.zeros((128, 128), np.float32)
    half = block // 2
    for p in range(128):
        i = p % block
        if i < half:
            R[p, p + half] = -1.0
        else:
            R[p, p - half] = 1.0
    return np.ascontiguousarray(R.T)


class Prog:
    def __init__(self, nseq, wbig_n, wbig_off, ws_n, ws_off, cb_n, cb_off, cf_n, cf_off,
                 mixers=(0, 1, 2, 3), ffn=True, nlayers=4):
        self.nseq = nseq
        self.mixers = mixers
        self.do_ffn = ffn
        self.nlayers = nlayers
        nc = bass.Bass("TRN2", target_bir_lowering=False)
        self.nc = nc
        self._uniq = 0
        self.S = Sch(nc)
        self.wo, self.so, self.cbo, self.cfo = wbig_off, ws_off, cb_off, cf_off
        self.x_in = nc.dram_tensor("x", [nseq, S_LEN, D], F32, kind="ExternalInput").ap()
        self.y_out = nc.dram_tensor("y", [nseq, S_LEN, D], F32, kind="ExternalOutput").ap()
        self.wbig = nc.dram_tensor("wbig", [wbig_n], F32, kind="ExternalInput").ap()
        self.wbf = nc.dram_tensor("wbf", [wbig_n], BF16, kind="Internal").ap()
        self.wsmall_d = nc.dram_tensor("wsmall", [128, ws_n], F32, kind="ExternalInput").ap()
        self.cbf_d = nc.dram_tensor("cbf", [128, cb_n], BF16, kind="ExternalInput").ap()
        self.cf_d = nc.dram_tensor("cf", [128, cf_n], F32, kind="ExternalInput").ap()
        self.rope_d = nc.dram_tensor("rope", [4, 128, S_LEN], BF16, kind="ExternalInput").ap()
        self.wbf_dep = Dep("wbf")
        self.xsp = nc.dram_tensor("xsp", [NT, 128, D], F32, kind="Internal").ap()
        self.xspd = [Dep() for _ in range(NT)]
        self.rkvsp = nc.dram_tensor("rkvsp", [3, NT, 128, D], BF16, kind="Internal").ap()
        self.rkvd = [[Dep() for _ in range(NT)] for _ in range(3)]
        self.x = nc.alloc_sbuf_tensor("xres", [128, NT, D], F32)
        self.xd = [Dep(f"x{t}") for t in range(NT)]
        self.hT = nc.alloc_sbuf_tensor("hT", [128, KC, S_LEN], BF16)
        self.hTd = [Dep(f"hT{t}") for t in range(NT)]
        self.ws = nc.alloc_sbuf_tensor("ws", [128, ws_n], F32)
        self.wsd = Dep("ws")
        self.cb = nc.alloc_sbuf_tensor("cb", [128, cb_n], BF16)
        self.cbd = Dep("cb")
        self.ps = nc.alloc_psum_tensor("ps", [128, 8, 512], F32)
        self.psd = [Dep(f"ps{b}") for b in range(8)]
        self.ident = self.cb[:, cb_off["ident"]:cb_off["ident"] + 128]

    def sbt(self, name, shape, dt):
        self._uniq += 1
        return self.nc.sbuf_tensor(f"{name}_{self._uniq}", shape, dt)

    def mm(self, out, lhsT, rhs, start, stop, r, w):
        self.S.op("pe", lambda e: e.matmul(out, lhsT, rhs, start=start, stop=stop,
                                           skip_group_check=True), r, w)

    def tr(self, out, in_, r, w, ident=None):
        idn = self.ident if ident is None else ident
        self.S.op("pe", lambda e: e.transpose(out, in_, idn), r, w)

    def act(self, out, in_, func, r, w, scale=1.0, bias=None, accum=None):
        def f(e):
            kw = {}
            if bias is not None:
                kw["bias"] = bias
            if accum is not None:
                kw["accum_out"] = accum
            return e.activation(out=out, in_=in_, func=func, scale=scale, **kw)
        self.S.op("act", f, r, w)

    def ts(self, eng, out, in0, s1, s2, op0, op1, r, w):
        def f(e):
            if op1 is None:
                return e.tensor_scalar(out=out, in0=in0, scalar1=s1, scalar2=None, op0=op0)
            return e.tensor_scalar(out=out, in0=in0, scalar1=s1, scalar2=s2, op0=op0, op1=op1)
        self.S.op(eng, f, r, w)

    def tt(self, eng, out, in0, in1, op, r, w):
        self.S.op(eng, lambda e: e.tensor_tensor(out=out, in0=in0, in1=in1, op=op), r, w)

    def stt(self, out, in0, scalar, in1, op0, op1, r, w):
        self.S.op("dve", lambda e: e.scalar_tensor_tensor(out=out, in0=in0, scalar=scalar, in1=in1,
                                                          op0=op0, op1=op1), r, w)

    def cp(self, eng, out, in_, r, w):
        if eng == "act":
            self.S.op("act", lambda e: e.copy(out=out, in_=in_), r, w)
        else:
            self.S.op(eng, lambda e: e.tensor_copy(out=out, in_=in_), r, w)

    def memset(self, eng, ap, val, w):
        self.S.op(eng, lambda e: e.memset(ap, val), (), w)

    def load(self, out, in_, r, w, q="sp"):
        self.S.dma(q, lambda e: e.dma_start(out=out, in_=in_), r, w)

    def wsc(self, name, j=0, n=1):
        o = self.so[name] + j
        return self.ws[:, o:o + n]

    def setup(self):
        S = self.S
        self.load(self.ws[:], self.wsmall_d, (), [self.wsd])
        self.load(self.cb[:], self.cbf_d, (), [self.cbd])
        n = self.wbig.shape[0]
        blk = 128 * 2048
        src = self.wbig.rearrange("(b p f) -> b p f", p=128, f=2048)
        dst = self.wbf.rearrange("(b p f) -> b p f", p=128, f=2048)
        for b in range(n // blk):
            self.load(dst[b], src[b], (), [self.wbf_dep], q="pool")

    def wview(self, name):
        off, shape = self.wo[name]
        n = int(np.prod(shape))
        flat = self.wbf[off:off + n]
        if len(shape) == 3:
            return flat.rearrange("(a b c) -> a b c", b=shape[1], c=shape[2])
        if len(shape) == 4:
            return flat.rearrange("(a b c d) -> a b c d", b=shape[1], c=shape[2], d=shape[3])
        return flat.rearrange("(a b) -> a b", b=shape[1])

    def load_x(self, s):
        for t in range(NT):
            self.load(self.x[:, t, :], self.x_in[s, t * 128:(t + 1) * 128, :], (), [self.xd[t]])

    def store_x(self, s):
        for t in range(NT):
            self.load(self.y_out[s, t * 128:(t + 1) * 128, :], self.x[:, t, :], [self.xd[t]], [Dep()], q="pool")

    def rmsnorm_T(self, gname, st):
        nc = self.nc
        junk = st.enter_context(self.sbt("nrm_junk", [128, D], BF16))
        hn = st.enter_context(self.sbt("nrm_hn", [128, 2, D], BF16))
        ss = st.enter_context(self.sbt("nrm_ss", [128, 2, 2], F32))
        junkd = Dep()
        hnd = [Dep(), Dep()]
        ssd = [Dep(), Dep()]
        g = self.wsc(gname, 0, 8)
        for t in range(NT):
            b = t % 2
            self.act(junk[:], self.x[:, t, :], AF.Square, [self.xd[t]], [junkd, ssd[b]],
                     accum=ss[:, b, 0:1])
            self.act(ss[:, b, 1:2], ss[:, b, 0:1], AF.Sqrt, [ssd[b]], [ssd[b]],
                     scale=1.0 / D, bias=self.wsc("eps6"))
            self.S.op("dve", (lambda b=b: (lambda e: e.reciprocal(out=ss[:, b, 1:2], in_=ss[:, b, 1:2])))(),
                      [ssd[b]], [ssd[b]])
            self.ts("dve", hn[:, b, :], self.x[:, t, :], ss[:, b, 1:2], None, ALU.mult, None,
                    [self.xd[t], ssd[b]], [hnd[b]])
            pb = 6 + (t % 2)
            pst = self.ps[:, pb, :].bitcast(BF16)
            for kc in range(KC):
                self.tr(pst[:, kc * 128:(kc + 1) * 128], hn[:, b, kc * 128:(kc + 1) * 128],
                        [hnd[b], self.cbd], [self.psd[pb]])
            self.tt("dve", self.hT[:, :, t * 128:(t + 1) * 128],
                    pst.rearrange("p (k c) -> p k c", k=KC),
                    g.unsqueeze(2).to_broadcast([128, KC, 128]), ALU.mult,
                    [self.psd[pb], self.wsd], [self.hTd[t]])

    def ffn(self, L):
        nc, S = self.nc, self.S
        S.barrier()
        with contextlib.ExitStack() as st0:
            with contextlib.ExitStack() as st:
                self.rmsnorm_T(f"n{L}_ffn", st)
            S.barrier()
            st = st0
            wi = st.enter_context(self.sbt("ffn_wi", [128, 2, 4, 1024], BF16))
            wo = st.enter_context(self.sbt("ffn_wo", [128, 2, 2, 1024], BF16))
            gT = st.enter_context(self.sbt("ffn_gT", [128, 2, 2, S_LEN], BF16))
            upad = st.enter_context(self.sbt("ffn_upad", [128, 2, 2, S_LEN + 2], F32))
            acc = st.enter_context(self.sbt("ffn_acc", [128, 2, S_LEN], F32))
            wid = [Dep(), Dep()]
            wod = [Dep(), Dep()]
            gTd = [Dep(), Dep()]
            upd = [[Dep(), Dep()], [Dep(), Dep()]]
            accd = [Dep(), Dep()]
            allh = list(self.hTd)
            for w_ in range(2):
                for b in range(2):
                    self.memset("pool", upad[:, w_, b, 0:1], 0.0, [upd[w_][b]])
                    self.memset("pool", upad[:, w_, b, S_LEN + 1:S_LEN + 2], 0.0, [upd[w_][b]])
            win = self.wview(f"f{L}_w_in")
            wout = self.wview(f"f{L}_w_out")
            cw = self.so[f"f{L}_conv_w"]
            cbo = self.so[f"f{L}_conv_b"]
            NP = NCH // 2
            ubuf = 0

            def out_partial(p, tiles):
                pb_ = p % 2
                for t in tiles:
                    bank = 4 + 2 * (t % 2)
                    for nh in range(2):
                        for j in range(2):
                            self.mm(self.ps[:, bank + nh, :], gT[:, pb_, j, t * 128:(t + 1) * 128],
                                    wo[:, pb_, j, nh * 512:(nh + 1) * 512], j == 0, j == 1,
                                    [gTd[pb_], wod[pb_]], [self.psd[bank + nh]])
                    self.tt("dve", self.x[:, t, :], self.x[:, t, :],
                            self.ps[:, bank:bank + 2, :].rearrange("p a b -> p (a b)"), ALU.add,
                            [self.psd[bank], self.psd[bank + 1], self.xd[t]], [self.xd[t]])

            for p in range(NP + 1):
                if p < NP:
                    pb = p % 2
                    self.load(wi[:, pb, 0:2, :],
                              win[2 * p:2 * p + 2].rearrange("a p k c -> p a (k c)"),
                              [self.wbf_dep], [wid[pb]])
                    self.load(wi[:, pb, 2:4, :],
                              win[NCH + 2 * p:NCH + 2 * p + 2].rearrange("a p k c -> p a (k c)"),
                              [self.wbf_dep], [wid[pb]])
                    self.load(wo[:, pb, :, :], wout[2 * p:2 * p + 2].rearrange("a p n -> p a n"),
                              [self.wbf_dep], [wod[pb]])
                for j in range(2):
                    if p < NP:
                        for which in range(2):
                            ci = 2 * p + j + which * NCH
                            ub = ubuf % 2
                            for half in range(2):
                                bank = 2 * half
                                for tq in range(2):
                                    c0 = half * 1024 + tq * 512
                                    for kc in range(KC):
                                        self.mm(self.ps[:, bank + tq, :],
                                                wi[:, pb, which * 2 + j, kc * 128:(kc + 1) * 128],
                                                self.hT[:, kc, c0:c0 + 512], kc == 0, kc == KC - 1,
                                                [wid[pb]] + allh[c0 // 128:c0 // 128 + 4],
                                                [self.psd[bank + tq]])
                                self.cp("act", upad[:, which, ub, 1 + half * 1024:1 + (half + 1) * 1024],
                                        self.ps[:, bank:bank + 2, :].rearrange("p a b -> p (a b)"),
                                        [self.psd[bank], self.psd[bank + 1]], [upd[which][ub]])
                            u = upad[:, which, ub, :]
                            a_ = acc[:, which, :]
                            self.act(a_, u[:, 0:S_LEN], AF.Identity, [upd[which][ub], self.wsd], [accd[which]],
                                     scale=self.ws[:, cw + ci * 3:cw + ci * 3 + 1],
                                     bias=self.ws[:, cbo + ci:cbo + ci + 1])
                            self.stt(a_, u[:, 1:S_LEN + 1], self.ws[:, cw + ci * 3 + 1:cw + ci * 3 + 2], a_,
                                     ALU.mult, ALU.add, [upd[which][ub], accd[which]], [accd[which]])
                            self.stt(a_, u[:, 2:S_LEN + 2], self.ws[:, cw + ci * 3 + 2:cw + ci * 3 + 3], a_,
                                     ALU.mult, ALU.add, [upd[which][ub], accd[which]], [accd[which]])
                        ubuf += 1
                        self.act(acc[:, 0, :], acc[:, 0, :], AF.Silu, [accd[0]], [accd[0]])
                        self.tt("dve", gT[:, pb, j, :], acc[:, 0, :], acc[:, 1, :], ALU.mult,
                                [accd[0], accd[1]], [gTd[pb]])
                    if p > 0:
                        out_partial(p - 1, range(j * 8, j * 8 + 8))

    def run_seq(self, s):
        self.load_x(s)
        if getattr(self, "debug", None) == "hT":
            with contextlib.ExitStack() as st:
                self.rmsnorm_T("n0_ffn", st)
            xv = self.x[:].rearrange("p a b -> p (a b)").rearrange("p (k t) -> p k t", k=KC)
            self.cp("act", xv, self.hT[:], self.hTd + self.xd, self.xd)
            self.store_x(s)
            return
        for L in range(self.nlayers):
            if L in self.mixers:
                getattr(self, f"mixer{L}")(L)
            if self.do_ffn:
                self.ffn(L)
        self.store_x(s)

    def build(self):
        self.setup()
        for s in range(self.nseq):
            self.run_seq(s)
        self.S.final_wait("pool")
        print("ops per engine", {e: len(v) for e, v in self.S.q.items()}, "cnt", self.S.cnt, flush=True)
        self.S.emit()
        return self.nc


def pack_params(p):
    big = Packer()
    sm = ColPacker()
    cb = ColPacker()
    cf = ColPacker()
    f32 = np.float32
    cb.add("ident", np.eye(128, dtype=f32))
    bo = np.zeros((128, 128), f32)
    bo[:64, :64] = 1.0 / 64
    bo[64:, 64:] = 1.0 / 64
    cb.add("blockmean", bo)
    cb.add("rotT64", rot_matrix_T(64))
    cb.add("rotT32", rot_matrix_T(32))
    cb.add("ones", np.ones((128, 128), f32))
    sm.add("eps6", np.full((128, 1), 1e-6, f32))
    sm.add("eps5", np.full((128, 1), 1e-5, f32))
    sm.add("epsgn", np.full((128, 1), 64e-5, f32))
    sm.add("zero", np.zeros((128, 1), f32))
    for L in range(4):
        sm.add(f"n{L}_attn", chan8(p[f"n{L}_attn"]))
        sm.add(f"n{L}_ffn", chan8(p[f"n{L}_ffn"]))
        cw = p[f"f{L}_conv_w"]
        sm.add(f"f{L}_conv_w", cw.T.reshape(44, 128, 3).transpose(1, 0, 2).reshape(128, 132))
        sm.add(f"f{L}_conv_b", p[f"f{L}_conv_b"].reshape(44, 128).T)
        big.add(f"f{L}_w_in", colchunk(p[f"f{L}_w_in"]))
        big.add(f"f{L}_w_out", p[f"f{L}_w_out"].reshape(22, 128, 1024))
    pack_mixers(p, big, sm, cb, cf)
    wbig = big.finish(128 * 2048)
    return (wbig, big.off, sm.finish(), sm.off, cb.finish().astype(ml_dtypes.bfloat16), cb.off,
            cf.finish(), cf.off)


def pack_mixers(p, big, sm, cb, cf):
    cf.add("dummy", np.zeros((128, 1), np.float32))
    f32 = np.float32
    idx64 = np.arange(128) % 64
    big.add("a_wqkv", colchunk(p["a_w_qkv"]))
    big.add("a_wo", p["a_w_o"].reshape(8, 128, 1024))
    sm.add("a_qg", p["a_q_norm"][idx64].reshape(128, 1))
    sm.add("a_kg", p["a_k_norm"][idx64].reshape(128, 1))
    sm.add("a_subln", rep(p["a_subln"]))
    for nm in ("a_lq1", "a_lk1", "a_lq2", "a_lk2"):
        sm.add(nm, rep(p[nm]))
    wb = p["b_w_qkv"]
    big.add("b_wqkv", colchunk(wb))
    kd = np.stack([np.concatenate([wb[:, 1024 + g * 64:1024 + (g + 1) * 64]] * 2, axis=1) for g in range(4)], 0)
    big.add("b_wkdup", np.stack([colchunk(kd[g])[0] for g in range(4)], 0))
    big.add("b_wo", p["b_w_o"].reshape(8, 128, 1024))
    sm.add("b_qg", p["b_q_norm"][idx64].reshape(128, 1))
    sm.add("b_kg", p["b_k_norm"][idx64].reshape(128, 1))
    big.add("c_wqkv", colchunk(p["c_w_qkv"]))
    big.add("c_wo", p["c_w_o"].reshape(8, 128, 1024))
    sm.add("c_qg", p["c_q_norm"][idx64].reshape(128, 1))
    sm.add("c_kg", p["c_k_norm"][idx64].reshape(128, 1))
    ho = np.zeros((128, 2), f32)
    ho[:64, 0] = -30000.0
    ho[64:, 1] = -30000.0
    sm.add("hoff", ho)
    rb = p["c_rel_bias"]
    pp = np.arange(128)
    j2 = pp // 64
    kc_ = pp % 64
    qq = np.arange(64)
    dc = np.clip(kc_[:, None] - qq[None, :] + 15, 0, 30)
    base = np.arange(14)
    dr = base[None, :] + j2[:, None]
    T = rb[:, dr[:, :, None], dc[:, None, :]]
    T = T.reshape(8, 2, 128, 14, 64).transpose(0, 2, 1, 3, 4)
    big.add("c_bias_f32", np.ascontiguousarray(T).reshape(8, 128, 2 * 14 * 64))
    cs_ = np.clip(qq - 8, 0, 48)
    inwin = (kc_[:, None] >= cs_[None, :]) & (kc_[:, None] < cs_[None, :] + 16)
    cf.add("nb_mask", np.where(inwin, 0.0, -30000.0).astype(f32))
    for nm in ("d_w_r", "d_w_k", "d_w_v", "d_w_o"):
        big.add(nm, p[nm].reshape(8, 128, 1024))
    big.add("d_w1s", np.concatenate([p["d_w1"][0], p["d_w1"][1]], axis=1).reshape(8, 128, 128))
    big.add("d_a1s", np.concatenate([p["d_a1"][0], p["d_a1"][1]], axis=1).reshape(8, 128, 128))
    big.add("d_g1", p["d_g1"].reshape(8, 128, 128))
    big.add("d_w2s", p["d_w2"].reshape(128, 1024))
    big.add("d_a2s", p["d_a2"].reshape(128, 1024))
    big.add("d_g2", p["d_g2"].reshape(128, 1024))
    sm.add("d_mu", np.concatenate([chan8(p["d_mu"][j]) for j in range(6)], axis=1))
    NEGC = -math.exp(-0.5)
    sm.add("negc", np.full((128, 2), NEGC, f32))
    ii = np.arange(128)
    le = (ii[:, None] <= ii[None, :]).astype(f32)
    lt = (ii[:, None] < ii[None, :]).astype(f32)
    ge = (ii[:, None] >= ii[None, :]).astype(f32)
    gt = (ii[:, None] > ii[None, :]).astype(f32)
    cf.add("tri", np.concatenate([le * NEGC, lt * NEGC, ge * NEGC, gt * NEGC], axis=1))
    cf.add("onesf", np.ones((128, 128), f32))
    cf.add("identf", np.eye(128, dtype=f32))
    lv = [(ii[:, None] // 8 == ii[None, :] // 8).astype(f32)]
    for bsz in (16, 32, 64, 128):
        lv.append(((ii[:, None] // bsz == ii[None, :] // bsz) & (ii[:, None] // (bsz // 2) != ii[None, :] // (bsz // 2))).astype(f32))
    cf.add("lvlmask", np.concatenate(lv, axis=1))
    cb.add("masks", np.concatenate([lt, le, gt, ge], axis=1))
    brow = np.zeros((128, 2048), f32)
    brow[0, :1024] = p["d_w0"][0]
    brow[64, :1024] = p["d_w0"][1]
    brow[0, 1024:] = p["d_a0"][0]
    brow[64, 1024:] = p["d_a0"][1]
    cf.add("brow", brow)
    cf.add("d_k_k", rep(p["d_k_k"]))
    cf.add("d_k_a", rep(p["d_k_a"]))
    cf.add("d_r_k", rep(p["d_r_k"].reshape(-1)))
    cf.add("d_ln_g", rep(p["d_ln_g"]))
    cf.add("d_ln_b", rep(p["d_ln_b"]))


_ROPE = None


def run(inputs, nseq_per_core=SEQ_PER_CORE, n_cores=N_CORES, **kw):
    global _ROPE
    xs = np.concatenate([np.asarray(inputs["x_prompt"]), np.asarray(inputs["x_sample"])], axis=0)
    p = {k: np.asarray(v) for k, v in inputs.items() if not k.startswith("x_")}
    wbig, woff, ws, soff, cbv, cboff, cfv, cfoff = pack_params(p)
    if _ROPE is None:
        _ROPE = build_rope_tables()
    prog = Prog(nseq_per_core, wbig.size, woff, ws.shape[1], soff, cbv.shape[1], cboff,
                cfv.shape[1], cfoff, **kw)
    nc = prog.build()
    in_maps = []
    for c in range(n_cores):
        in_maps.append({
            "x": np.ascontiguousarray(xs[c * nseq_per_core:(c + 1) * nseq_per_core]),
            "wbig": wbig, "wsmall": ws, "cbf": cbv, "cf": cfv, "rope": _ROPE.astype(ml_dtypes.bfloat16),
        })
    res = run_bass_kernel_spmd(nc, in_maps, core_ids=list(range(n_cores)))
    return np.concatenate([r["y"] for r in res.results], axis=0)


def kernel(**inputs):
    y = run(inputs)
    nb = np.asarray(inputs["x_prompt"]).shape[0]
    return (np.ascontiguousarray(y[:nb]), np.ascontiguousarray(y[nb:]))


def _attn_full(self, L):
    nc, S = self.nc, self.S
    diff = (L == 0)
    pre = "a" if diff else "b"
    E = 128 if diff else 64
    EW = 130 if diff else 66
    lam_init = 0.8 - 0.6 * math.exp(-0.3 * L)
    S.barrier()
    with contextlib.ExitStack() as st0:
        with contextlib.ExitStack() as st:
            self.rmsnorm_T(f"n{L}_attn", st)
        S.barrier()
        st = st0
        al = lambda n, sh, dt: st.enter_context(self.sbt(n, sh, dt))
        wq = al("at_wq", [128, 3, 1024], BF16)
        cs = al("at_cs", [128, 2, S_LEN], BF16)
        qT = al("at_qT", [128, 2, S_LEN], BF16)
        kT = al("at_kT", [128, 2, S_LEN], BF16)
        V = al("at_V", [128, 2, NT, EW], BF16)
        OT = al("at_OT", [128, KC, S_LEN], BF16)
        PT = al("at_PT", [128, 2, 512], BF16)
        sq = al("at_sq", [128, 512], BF16)
        qg = al("at_qg", [128, 512], BF16)
        t1 = al("at_t1", [128, 512], F32)
        t2 = al("at_t2", [128, 512], F32)
        t3 = al("at_t3", [128, 512], F32)
        oacc = al("at_oacc", [128, 4, 128], F32)
        osq = al("at_osq", [128, 4, 128], F32)
        osb = al("at_osb", [128, 4, 128], BF16)
        sm_ = al("at_sm", [128, 16], F32)
        wqd = [Dep(), Dep(), Dep()]
        csd = Dep()
        qTd = [Dep(), Dep()]
        kTd = [Dep(), Dep()]
        Vd = [Dep(), Dep()]
        OTd = Dep()
        PTd = [Dep(), Dep()]
        sqd, qgd, t1d, t2d, t3d, oaccd, osqd, osbd, smd = (Dep() for _ in range(9))
        allh = list(self.hTd)
        psd = self.psd
        ps = self.ps
        self.load(cs[:], self.rope_d[2 * L:2 * L + 2].rearrange("a p t -> p a t"), (), [csd])
        for b in range(2):
            self.memset("pool", V[:, b, :, E:E + 1], 1.0, [Vd[b]])
        cbo = self.cbo
        blockmean = self.cb[:, cbo["blockmean"]:cbo["blockmean"] + 128]
        rotT = self.cb[:, cbo["rotT64" if diff else "rotT32"]:cbo["rotT64" if diff else "rotT32"] + 128]
        wqkv = self.wview(f"{pre}_wqkv")
        if diff:
            for i, (a_, b_) in enumerate((("a_lq1", "a_lk1"), ("a_lq2", "a_lk2"))):
                self.tt("dve", t1[:, 0:64], self.wsc(a_, 0, 64), self.wsc(b_, 0, 64), ALU.mult, [self.wsd], [t1d])
                self.S.op("dve", (lambda i=i: lambda e: e.tensor_reduce(out=sm_[:, 8 + i:9 + i], in_=t1[:, 0:64],
                                                                       axis=AX.X, op=ALU.add))(), [t1d], [smd])
            self.act(sm_[:, 8:10], sm_[:, 8:10], AF.Exp, [smd], [smd])
            self.tt("dve", sm_[:, 10:11], sm_[:, 8:9], sm_[:, 9:10], ALU.subtract, [smd], [smd])
            self.ts("dve", sm_[:, 11:12], sm_[:, 10:11], lam_init, -1.0, ALU.add, ALU.mult, [smd], [smd])
        neglam = sm_[:, 11:12]

        def load_w(hc):
            if diff:
                idx = (hc, 8 + hc, 16 + hc)
            else:
                idx = (hc, None, 10 + hc // 4)
            for i, ci in enumerate(idx):
                if ci is None:
                    src = self.wview("b_wkdup")[hc // 2]
                else:
                    src = wqkv[ci]
                if (not diff) and i > 0 and hc % 2 == 1:
                    continue
                self.load(wq[:, i, :], src.rearrange("p k c -> p (k c)"), [self.wbf_dep], [wqd[i]])

        def proj_qk(hc, which, tc):
            b = hc % 2 if (diff or which == 0) else (hc // 2) % 2
            dst, dstd = (qT, qTd) if which == 0 else (kT, kTd)
            gain = self.wsc(f"{pre}_qg" if which == 0 else f"{pre}_kg")
            c0 = tc * 512
            for kc in range(KC):
                self.mm(ps[:, 4, :], wq[:, which, kc * 128:(kc + 1) * 128], self.hT[:, kc, c0:c0 + 512],
                        kc == 0, kc == KC - 1, [wqd[which]] + allh[tc * 4:tc * 4 + 4], [psd[4]])
            self.act(sq[:], ps[:, 4, :], AF.Square, [psd[4]], [sqd])
            self.act(qg[:], ps[:, 4, :], AF.Identity, [psd[4], self.wsd], [qgd], scale=gain)
            self.mm(ps[:, 5, :], blockmean, sq[:], True, True, [sqd, self.cbd], [psd[5]])
            self.mm(ps[:, 6, :], rotT, qg[:], True, True, [qgd, self.cbd], [psd[6]])
            self.act(t1[:], ps[:, 5, :], AF.Sqrt, [psd[5], self.wsd], [t1d], bias=self.wsc("eps6"))
            self.S.op("dve", lambda e: e.reciprocal(out=t1[:], in_=t1[:]), [t1d], [t1d])
            self.tt("dve", t2[:], qg[:], cs[:, 0, c0:c0 + 512], ALU.mult, [qgd, csd], [t2d])
            self.tt("dve", t3[:], ps[:, 6, :], cs[:, 1, c0:c0 + 512], ALU.mult, [psd[6], csd], [t3d])
            self.tt("pool", t2[:], t2[:], t3[:], ALU.add, [t2d, t3d], [t2d])
            self.tt("dve", dst[:, b, c0:c0 + 512], t2[:], t1[:], ALU.mult, [t2d, t1d], [dstd[b]])

        def proj_v(hc, tg):
            b = hc % 2 if diff else (hc // 2) % 2
            voff = 0 if diff else ((hc // 2) % 2) * 64
            for i in range(4):
                t = tg * 4 + i
                for kc in range(KC):
                    self.mm(ps[:, 7, i * E:(i + 1) * E], self.hT[:, kc, t * 128:(t + 1) * 128],
                            wq[:, 2, kc * 128 + voff:kc * 128 + voff + E], kc == 0 and i == 0, kc == KC - 1,
                            [wqd[2], allh[t]], [psd[7]])
            self.cp("act", V[:, b, tg * 4:tg * 4 + 4, 0:E],
                    ps[:, 7, 0:4 * E].rearrange("p (a e) -> p a e", a=4), [psd[7]], [Vd[b]])

        def proj_units(hc):
            units = []
            kv_new = diff or hc % 2 == 0
            for tc in range(4):
                units.append((lambda tc=tc: proj_qk(hc, 0, tc)))
                if kv_new:
                    units.append((lambda tc=tc: proj_qk(hc, 1, tc)))
                    units.append((lambda tc=tc: proj_v(hc, tc)))
            return units

        def attn_block(hc, qc, c):
            bq = hc % 2
            bk = hc % 2 if diff else (hc // 2) % 2
            pr = slice(c * 64, (c + 1) * 64)
            W = E + 1
            nb = 2 if diff else 1
            for kt in range(NT):
                sb = kt % 2
                self.mm(ps[:, sb, :], kT[pr, bk, kt * 128:(kt + 1) * 128], qT[pr, bq, qc * 512:(qc + 1) * 512],
                        True, True, [kTd[bk], qTd[bq]], [psd[sb]])
                self.act(PT[:, sb, :], ps[:, sb, :], AF.Exp, [psd[sb]], [PTd[sb]], scale=0.125)
                for qs in range(4):
                    if diff:
                        bank, col = 2 + qs // 2, (qs % 2) * W
                        first = (kt == 0 and qs % 2 == 0)
                    else:
                        bank, col = 2 + c, qs * W
                        first = (kt == 0 and qs == 0)
                    self.mm(ps[:, bank, col:col + W], PT[:, sb, qs * 128:(qs + 1) * 128], V[:, bk, kt, 0:W],
                            first, kt == NT - 1, [PTd[sb], Vd[bk]], [psd[bank]])
            if diff:
                for hb in range(2):
                    bank = 2 + hb
                    pv = ps[:, bank, 0:2 * W].rearrange("p (a w) -> p a w", a=2)
                    rz = sm_[:, hb * 2:hb * 2 + 2]
                    self.S.op("dve", (lambda rz=rz, pv=pv: lambda e: e.reciprocal(out=rz.unsqueeze(2), in_=pv[:, :, E:E + 1]))(),
                              [psd[bank]], [smd])
                    if c == 0:
                        self.tt("dve", oacc[:, hb * 2:hb * 2 + 2, :], pv[:, :, 0:E],
                                rz.unsqueeze(2).to_broadcast([128, 2, E]), ALU.mult, [psd[bank], smd], [oaccd])
                    else:
                        self.ts("dve", rz, rz, neglam, None, ALU.mult, None, [smd], [smd])
                        self.tt("dve", osq[:, hb * 2:hb * 2 + 2, :], pv[:, :, 0:E],
                                rz.unsqueeze(2).to_broadcast([128, 2, E]), ALU.mult, [psd[bank], smd], [osqd])
                        self.tt("pool", oacc[:, hb * 2:hb * 2 + 2, :], oacc[:, hb * 2:hb * 2 + 2, :],
                                osq[:, hb * 2:hb * 2 + 2, :], ALU.add, [osqd, oaccd], [oaccd])
                if c == 1:
                    self.tt("dve", osq[:], oacc[:], oacc[:], ALU.mult, [oaccd], [osqd])
                    self.S.op("dve", lambda e: e.tensor_reduce(out=sm_[:, 4:8], in_=osq[:], axis=AX.X, op=ALU.add),
                              [osqd], [smd])
                    self.act(sm_[:, 4:8], sm_[:, 4:8], AF.Sqrt, [smd, self.wsd], [smd], scale=1.0 / 128,
                             bias=self.wsc("eps5"))
                    self.S.op("dve", lambda e: e.reciprocal(out=sm_[:, 4:8], in_=sm_[:, 4:8]), [smd], [smd])
                    self.tt("dve", osq[:], oacc[:], sm_[:, 4:8].unsqueeze(2).to_broadcast([128, 4, 128]), ALU.mult,
                            [oaccd, smd], [osqd])
                    self.stt(osb[:], osq[:], 1.0 - lam_init,
                             self.wsc("a_subln", 0, 128).unsqueeze(1).to_broadcast([128, 4, 128]),
                             ALU.mult, ALU.mult, [osqd, self.wsd], [osbd])
            else:
                bank = 2 + c
                pv = ps[:, bank, 0:4 * W].rearrange("p (a w) -> p a w", a=4)
                rz = sm_[:, c * 4:c * 4 + 4]
                self.S.op("dve", lambda e: e.reciprocal(out=rz.unsqueeze(2), in_=pv[:, :, E:E + 1]), [psd[bank]], [smd])
                self.tt("dve", osb[:, :, c * 64:(c + 1) * 64], pv[:, :, 0:E],
                        rz.unsqueeze(2).to_broadcast([128, 4, E]), ALU.mult, [psd[bank], smd], [osbd])
            if c == 1:
                pst = ps[:, 7, :].bitcast(BF16)
                for qs in range(4):
                    self.tr(pst[:, qs * 128:(qs + 1) * 128], osb[:, qs, :], [osbd, self.cbd], [psd[7]])
                self.cp("act", OT[:, hc, qc * 512:(qc + 1) * 512], pst[:, 0:512], [psd[7]], [OTd])

        load_w(0)
        for u in proj_units(0):
            u()
        for hc in range(8):
            units = []
            if hc + 1 < 8:
                load_w(hc + 1)
                units = proj_units(hc + 1)
            ui = 0
            for qc in range(4):
                for c in range(2):
                    attn_block(hc, qc, c)
                    n_take = (len(units) - ui + (7 - (qc * 2 + c))) // (8 - (qc * 2 + c))
                    for _ in range(n_take):
                        units[ui]()
                        ui += 1
        wo = al("at_wo", [128, KC, 512], BF16)
        wod = Dep()
        wov = self.wview(f"{pre}_wo")
        for nh in range(2):
            self.load(wo[:], wov[:, :, nh * 512:(nh + 1) * 512].rearrange("k p n -> p k n"), [self.wbf_dep], [wod])
            for t in range(NT):
                bank = 4 + (t % 2)
                for kc in range(KC):
                    self.mm(ps[:, bank, :], OT[:, kc, t * 128:(t + 1) * 128], wo[:, kc, :], kc == 0, kc == KC - 1,
                            [OTd, wod], [psd[bank]])
                self.tt("dve", self.x[:, t, nh * 512:(nh + 1) * 512], self.x[:, t, nh * 512:(nh + 1) * 512],
                        ps[:, bank, :], ALU.add, [psd[bank], self.xd[t]], [self.xd[t]])


Prog._attn_full = _attn_full
Prog.mixer0 = _attn_full
Prog.mixer1 = _attn_full


def _attn_nbr(self, L):
    nc, S = self.nc, self.S
    S.barrier()
    with contextlib.ExitStack() as st0:
        with contextlib.ExitStack() as st:
            self.rmsnorm_T(f"n{L}_attn", st)
        S.barrier()
        st = st0
        al = lambda n, sh, dt: st.enter_context(self.sbt(n, sh, dt))
        wq = al("nb_wq", [128, 3, 1024], BF16)
        qT = al("nb_qT", [128, 2, S_LEN], BF16)
        kT = al("nb_kT", [128, 2, S_LEN], BF16)
        V = al("nb_V", [128, 2, NT, 2, 66], BF16)
        OT = al("nb_OT", [128, KC, S_LEN], BF16)
        Tb = al("nb_T", [128, 2, 2, 14, 64], F32)
        msk = al("nb_msk", [128, 64], F32)
        sb = al("nb_sb", [128, 2, 5, 64], F32)
        PT = al("nb_PT", [128, 2, 5, 64], BF16)
        sq = al("nb_sq", [128, 512], BF16)
        t1 = al("nb_t1", [128, 512], F32)
        osb = al("nb_osb", [128, 2, 128], BF16)
        sm_ = al("nb_sm", [128, 8], F32)
        wo = al("nb_wo", [128, KC, 512], BF16)
        wqd = [Dep(), Dep(), Dep()]
        qTd, kTd, Vd, Td = [Dep(), Dep()], [Dep(), Dep()], [Dep(), Dep()], [Dep(), Dep()]
        sbd, PTd, osbd = [Dep(), Dep()], [Dep(), Dep()], [Dep(), Dep()]
        OTd, mskd, sqd, t1d, smd, wod = (Dep() for _ in range(6))
        allh = list(self.hTd)
        psd, ps = self.psd, self.ps
        cbo = self.cbo
        blockmean = self.cb[:, cbo["blockmean"]:cbo["blockmean"] + 128]
        wqkv = self.wview("c_wqkv")
        o_b, shp = self.wo["c_bias_f32"]
        bias_d = self.wbig[o_b:o_b + 8 * 128 * 1792].rearrange("(a p n) -> a p n", p=128, n=1792)
        self.load(msk[:], self.cf_d[:, self.cfo["nb_mask"]:self.cfo["nb_mask"] + 64], (), [mskd])
        for b in range(2):
            self.memset("pool", V[:, b, :, :, 64:65], 1.0, [Vd[b]])
        self.ts("dve", sm_[:, 0:1], self.wsc("c_qg"), 0.125, None, ALU.mult, None, [self.wsd], [smd])

        def load_w(hc):
            b = hc % 2
            for i, ci in enumerate((hc, 8 + hc, 16 + hc)):
                self.load(wq[:, i, :], wqkv[ci].rearrange("p k c -> p (k c)"), [self.wbf_dep], [wqd[i]])
            self.load(Tb[:, b].rearrange("p c a q -> p (c a q)"), bias_d[hc], (), [Td[b]])
            self.tt("pool", Tb[:, b].rearrange("p c a q -> p (c a) q"), Tb[:, b].rearrange("p c a q -> p (c a) q"),
                    msk[:].unsqueeze(1).to_broadcast([128, 28, 64]), ALU.add, [Td[b], mskd], [Td[b]])

        def proj_qk(hc, which, tc):
            b = hc % 2
            dst, dstd = (qT, qTd) if which == 0 else (kT, kTd)
            gain = sm_[:, 0:1] if which == 0 else self.wsc("c_kg")
            c0 = tc * 512
            for kc in range(KC):
                self.mm(ps[:, 4, :], wq[:, which, kc * 128:(kc + 1) * 128], self.hT[:, kc, c0:c0 + 512],
                        kc == 0, kc == KC - 1, [wqd[which]] + allh[tc * 4:tc * 4 + 4], [psd[4]])
            self.act(sq[:], ps[:, 4, :], AF.Square, [psd[4]], [sqd])
            self.mm(ps[:, 5, :], blockmean, sq[:], True, True, [sqd, self.cbd], [psd[5]])
            self.act(t1[:], ps[:, 5, :], AF.Sqrt, [psd[5], self.wsd], [t1d], bias=self.wsc("eps6"))
            self.S.op("dve", lambda e: e.reciprocal(out=t1[:], in_=t1[:]), [t1d], [t1d])
            self.stt(dst[:, b, c0:c0 + 512], ps[:, 4, :], gain, t1[:], ALU.mult, ALU.mult,
                     [psd[4], t1d, smd, self.wsd], [dstd[b]])

        def proj_v(hc, tg):
            b = hc % 2
            for i in range(4):
                t = tg * 4 + i
                for kc in range(KC):
                    self.mm(ps[:, 7, i * 128:(i + 1) * 128], self.hT[:, kc, t * 128:(t + 1) * 128],
                            wq[:, 2, kc * 128:(kc + 1) * 128], kc == 0 and i == 0, kc == KC - 1,
                            [wqd[2], allh[t]], [psd[7]])
            self.cp("act", V[:, b, tg * 4:tg * 4 + 4, :, 0:64],
                    ps[:, 7, :].rearrange("p (a c e) -> p a c e", a=4, c=2), [psd[7]], [Vd[b]])

        def proj_units(hc):
            units = []
            for tc in range(4):
                units.append((lambda tc=tc: proj_qk(hc, 0, tc)))
                units.append((lambda tc=tc: proj_qk(hc, 1, tc)))
                units.append((lambda tc=tc: proj_v(hc, tc)))
            return units

        def row_block(hc, r):
            b = hc % 2
            rs = min(max(r - 4, 0), 24)
            a0, a1 = rs // 2, (rs + 7) // 2
            ns = a1 - a0 + 1
            ph = (r % 2) * 64
            tb = (r // 2) % 2
            for c in range(2):
                pr = slice(c * 64, (c + 1) * 64)
                sbank = c
                for si in range(ns):
                    a = a0 + si
                    self.mm(ps[:, sbank, si * 64:(si + 1) * 64], kT[pr, b, a * 128:(a + 1) * 128],
                            qT[pr, b, r * 64:(r + 1) * 64], si == 0, True, [kTd[b], qTd[b]], [psd[sbank]])
                base0 = 2 * a0 - r + 7
                self.tt("dve", sb[:, c, 0:ns, :], ps[:, sbank, 0:ns * 64].rearrange("p (s q) -> p s q", s=ns),
                        Tb[:, b, c, base0:base0 + 2 * ns - 1:2, :], ALU.add, [psd[sbank], Td[b]], [sbd[c]])
                if rs % 2 == 0:
                    self.act(PT[:, c, 0:ns, :], sb[:, c, 0:ns, :], AF.Exp, [sbd[c]], [PTd[c]])
                else:
                    self.act(PT[:, c, 0:1, :], sb[:, c, 0:1, :], AF.Exp, [sbd[c], self.wsd], [PTd[c]],
                             bias=self.wsc("hoff", 0, 1))
                    self.act(PT[:, c, 1:ns - 1, :], sb[:, c, 1:ns - 1, :], AF.Exp, [sbd[c]], [PTd[c]])
                    self.act(PT[:, c, ns - 1:ns, :], sb[:, c, ns - 1:ns, :], AF.Exp, [sbd[c], self.wsd], [PTd[c]],
                             bias=self.wsc("hoff", 1, 1))
                for si in range(ns):
                    a = a0 + si
                    self.mm(ps[ph:ph + 64, 2, c * 65:(c + 1) * 65], PT[:, c, si, :], V[:, b, a, c, 0:65],
                            si == 0, si == ns - 1, [PTd[c], Vd[b]], [psd[2]])
            pv = ps[ph:ph + 64, 2, 0:130].rearrange("p (c w) -> p c w", c=2)
            rz = sm_[ph:ph + 64, 2:4]
            self.S.op("dve", lambda e: e.reciprocal(out=rz.unsqueeze(2), in_=pv[:, :, 64:65]), [psd[2]], [smd])
            self.tt("dve", osb[ph:ph + 64, tb, :].rearrange("p (c e) -> p c e", c=2), pv[:, :, 0:64],
                    rz.unsqueeze(2).to_broadcast([64, 2, 64]), ALU.mult, [psd[2], smd], [osbd[tb]])
            if r % 2 == 1:
                t = r // 2
                pst = ps[:, 3, :].bitcast(BF16)
                self.tr(pst[:, 0:128], osb[:, tb, :], [osbd[tb], self.cbd], [psd[3]])
                self.cp("act", OT[:, hc, t * 128:(t + 1) * 128], pst[:, 0:128], [psd[3]], [OTd])

        load_w(0)
        for u in proj_units(0):
            u()
        for hc in range(8):
            units = []
            if hc + 1 < 8:
                load_w(hc + 1)
                units = proj_units(hc + 1)
            ui = 0
            for r in range(32):
                row_block(hc, r)
                n_take = (len(units) - ui + (31 - r)) // (32 - r)
                for _ in range(n_take):
                    units[ui]()
                    ui += 1
        wov = self.wview("c_wo")
        for nh in range(2):
            self.load(wo[:], wov[:, :, nh * 512:(nh + 1) * 512].rearrange("k p n -> p k n"), [self.wbf_dep], [wod])
            for t in range(NT):
                bank = 4 + (t % 2)
                for kc in range(KC):
                    self.mm(ps[:, bank, :], OT[:, kc, t * 128:(t + 1) * 128], wo[:, kc, :], kc == 0, kc == KC - 1,
                            [OTd, wod], [psd[bank]])
                self.tt("dve", self.x[:, t, nh * 512:(nh + 1) * 512], self.x[:, t, nh * 512:(nh + 1) * 512],
                        ps[:, bank, :], ALU.add, [psd[bank], self.xd[t]], [self.xd[t]])


Prog.mixer2 = _attn_nbr


def _rwkv(self, L):
    nc, S = self.nc, self.S
    ps, psd = self.ps, self.psd
    cbo, cfo = self.cbo, self.cfo
    bank_i = [0]

    def nb():
        b = bank_i[0] % 8
        bank_i[0] += 1
        return b

    def nb2():
        s0 = (bank_i[0] + 1) // 2 * 2
        bank_i[0] = s0 + 2
        return s0 % 8

    def ps2(b):
        return ps[:, b:b + 2, :].rearrange("p a b -> p (a b)")

    def pd2(b):
        return [psd[b], psd[b + 1]]

    def reduce_x(out, in_, r, w):
        self.S.op("dve", lambda e: e.tensor_reduce(out=out, in_=in_, axis=AX.X, op=ALU.add), r, w)

    def recip(ap, r, w):
        self.S.op("dve", lambda e: e.reciprocal(out=ap, in_=ap), r, w)

    S.barrier()
    with contextlib.ExitStack() as st_outer:
        alo = lambda n, sh, dt: st_outer.enter_context(self.sbt(n, sh, dt))
        lT = alo("rw_lT", [128, 3, S_LEN], BF16)
        lTd = Dep()
        allh = list(self.hTd)
        xd = self.xd
        with contextlib.ExitStack() as st:
            self.rmsnorm_T(f"n{L}_attn", st)
        for t in range(NT):
            self.load(self.xsp[t], self.x[:, t, :], [xd[t]], [self.xspd[t]], q="pool")
        S.barrier()
        with contextlib.ExitStack() as st:
            al = lambda n, sh, dt: st.enter_context(self.sbt(n, sh, dt))
            tmp = al("rwa_tmp", [128, S_LEN], F32)
            W = al("rwa_W", [128, KC, 1024], BF16)
            stage = al("rwa_stage", [128, 2, 1024], BF16)
            lw = al("rwa_lw", [128, KC, 128], BF16)
            coef = al("rwa_coef", [128, 2, 48], F32)
            tmpd, Wd, lwd, coefd, shd, mixd = (Dep() for _ in range(6))
            staged = [Dep(), Dep()]
            xb = self.x[:].rearrange("p a b -> p (a b)").bitcast(BF16)
            sh = xb[:, 0:KC * S_LEN].rearrange("p (k t) -> p k t", k=KC)
            mix = xb[:, KC * S_LEN:2 * KC * S_LEN].rearrange("p (k t) -> p k t", k=KC)
            mu = self.wsc("d_mu", 0, 48)
            self.ts("dve", coef[:, 0, :], mu, -1.0, 1.0, ALU.mult, ALU.add, [self.wsd], [coefd])
            self.ts("dve", coef[:, 1, :], mu, 0.5, None, ALU.mult, None, [self.wsd], [coefd])
            for kc in range(KC):
                self.tt("dve", sh[:, kc, 1:S_LEN - 1], self.hT[:, kc, 0:S_LEN - 2],
                        self.hT[:, kc, 2:S_LEN], ALU.add, allh, [shd] + xd)
            self.cp("dve", sh[:, :, 0:1], self.hT[:, :, 1:2], allh, [shd] + xd)
            self.cp("dve", sh[:, :, S_LEN - 1:S_LEN], self.hT[:, :, S_LEN - 2:S_LEN - 1], allh, [shd] + xd)

            def make_mix(j):
                for kc in range(KC):
                    self.act(tmp[:], self.hT[:, kc, :], AF.Identity, allh + [coefd], [tmpd],
                             scale=coef[:, 0, j * 8 + kc:j * 8 + kc + 1])
                    self.stt(mix[:, kc, :], sh[:, kc, :], coef[:, 1, j * 8 + kc:j * 8 + kc + 1], tmp[:],
                             ALU.mult, ALU.add, [shd, tmpd, coefd], [mixd] + xd)

            def proj_tm(wname, idx):
                self.load(W[:], self.wview(wname).rearrange("k p n -> p k n"), [self.wbf_dep], [Wd])
                for t in range(NT):
                    b2 = nb2()
                    for nh in range(2):
                        for kc in range(KC):
                            self.mm(ps[:, b2 + nh, :], mix[:, kc, t * 128:(t + 1) * 128],
                                    W[:, kc, nh * 512:(nh + 1) * 512], kc == 0, kc == KC - 1, [mixd, Wd],
                                    [psd[b2 + nh]])
                    sb_ = t % 2
                    self.cp("act", stage[:, sb_, :], ps2(b2), pd2(b2), [staged[sb_]])
                    self.load(self.rkvsp[idx, t], stage[:, sb_, :], [staged[sb_]], [self.rkvd[idx][t]], q="pool")

            def proj_lora(wname, li, func):
                self.load(lw[:], self.wview(wname).rearrange("k p n -> p k n"), [self.wbf_dep], [lwd])
                for tc in range(4):
                    b = nb()
                    for kc in range(KC):
                        self.mm(ps[:, b, :], lw[:, kc, :], mix[:, kc, tc * 512:(tc + 1) * 512], kc == 0,
                                kc == KC - 1, [mixd, lwd], [psd[b]])
                    self.act(lT[:, li, tc * 512:(tc + 1) * 512], ps[:, b, :], func, [psd[b]], [lTd])

            make_mix(0)
            proj_tm("d_w_r", 0)
            make_mix(1)
            proj_lora("d_w1s", 0, AF.Tanh)
            make_mix(2)
            proj_tm("d_w_k", 1)
            make_mix(3)
            proj_tm("d_w_v", 2)
            make_mix(4)
            proj_lora("d_a1s", 1, AF.Identity)
            make_mix(5)
            proj_lora("d_g1", 2, AF.Sigmoid)
        S.barrier()
        import os
        stop = os.environ.get("RW_STOP", "")
        if stop:
            nbt = int(os.environ.get("RW_NT", "16"))
        if stop == "A":
            for t in range(NT):
                self.load(self.x[:, t, :], self.xsp[t], [self.xspd[t]], [xd[t]])
            return
        hTf = self.hT[:].rearrange("p k t -> p (k t)")
        with contextlib.ExitStack() as st:
            al = lambda n, sh_, dt: st.enter_context(self.sbt(n, sh_, dt))
            tri = al("rwb_tri", [128, 4, 128], F32)
            onesf = al("rwb_onesf", [128, 128], F32)
            brow = al("rwb_brow", [128, 2048], F32)
            kkb = al("rwb_kkb", [128, 2, 1024], F32)
            w2a2 = al("rwb_w2a2", [128, 2, 1024], BF16)
            rkv = al("rwb_rkv", [128, 3, 1024], BF16)
            F = al("rwb_F", [128, 6, 1024], F32)
            Bt = al("rwb_Bt", [128, 4, 1024], BF16)
            ARc = al("rwb_ARc", [128, 8, 2, 128], BF16)
            Bc = al("rwb_Bc", [128, 8, 128], BF16)
            Kc = al("rwb_Kc", [128, 8, 128], BF16)
            IB = al("rwb_IB", [128, 6, 4, 128], F32)
            lvlm = al("rwb_lvlm", [128, 5, 128], F32)
            identf = al("rwb_identf", [128, 128], F32)
            RU = al("rwb_RU", [128, 2, 1024], BF16)
            S32 = al("rwb_S32", [128, 8, 64], F32)
            Sb = al("rwb_Sb", [128, 8, 64], BF16)
            WL = al("rwb_WL", [128, 8], F32)
            small = al("rwb_small", [128, 16], F32)
            cstd, rkvsd, Fd, Btd, cmd, NAd, NTd, RUd, S32d, Sbd, WLd, smd = (
                Dep(), [Dep(), Dep(), Dep()], [Dep() for _ in range(6)], [Dep() for _ in range(4)],
                Dep(), Dep(), Dep(), [Dep(), Dep()], Dep(), Dep(), Dep(), Dep())
            NA3 = hTf[:, 0:6144].rearrange("p (h a t) -> p h a t", h=16, a=3)
            Xb = hTf[:, 6144:8192].rearrange("p (h t) -> p h t", h=16)
            Nf = hTf[:, 8192:12288].bitcast(F32).rearrange("p (h t) -> p h t", h=16)
            NTf = hTf[:, 12288:16384].bitcast(F32).rearrange("p (h t) -> p h t", h=16)
            Nfd, NTfd, Xbd = Dep(), Dep(), Dep()
            ibd = [Dep() for _ in range(6)]
            ibdB = [Dep() for _ in range(6)]
            masks = self.cb[:, cbo["masks"]:cbo["masks"] + 512].rearrange("p (a t) -> p a t", a=4)
            identb = self.ident
            self.load(tri[:].rearrange("p a t -> p (a t)"), self.cf_d[:, cfo["tri"]:cfo["tri"] + 512], (), [cstd])
            self.load(onesf[:], self.cf_d[:, cfo["onesf"]:cfo["onesf"] + 128], (), [cstd])
            self.load(identf[:], self.cf_d[:, cfo["identf"]:cfo["identf"] + 128], (), [cstd])
            self.load(lvlm[:].rearrange("p a t -> p (a t)"), self.cf_d[:, cfo["lvlmask"]:cfo["lvlmask"] + 640], (), [cstd])
            self.load(brow[:], self.cf_d[:, cfo["brow"]:cfo["brow"] + 2048], (), [cstd])
            self.load(kkb[:, 0, :], self.cf_d[:, cfo["d_k_k"]:cfo["d_k_k"] + 1024], (), [cstd])
            self.load(kkb[:, 1, :], self.cf_d[:, cfo["d_k_a"]:cfo["d_k_a"] + 1024], (), [cstd])
            self.load(w2a2[:, 0, :], self.wview("d_w2s"), [self.wbf_dep], [cstd])
            self.load(w2a2[:, 1, :], self.wview("d_a2s"), [self.wbf_dep], [cstd])
            negc2 = self.wsc("negc", 0, 2)

            def lora_sig(d, which, n, dstF, dstd):
                prd = slice(d * 64, (d + 1) * 64)
                rowp = d * 64
                b2 = nb2()
                for nh in range(2):
                    self.mm(ps[:, b2 + nh, :], lT[prd, which, n * 128:(n + 1) * 128],
                            w2a2[prd, which, nh * 512:(nh + 1) * 512], True, False, [lTd, cstd], [psd[b2 + nh]])
                    c0_ = which * 1024 + nh * 512
                    self.mm(ps[:, b2 + nh, :], onesf[rowp:rowp + 1, :], brow[rowp:rowp + 1, c0_:c0_ + 512],
                            False, True, [cstd], [psd[b2 + nh]])
                self.act(dstF, ps2(b2), AF.Sigmoid, pd2(b2), [dstd])

            def scan_tile(d, n):
                for i in range(3):
                    self.load(rkv[:, i, :], self.rkvsp[i, n], [self.rkvd[i][n]], [rkvsd[i]])
                F0, F1, F2, F3, F4, F5 = (F[:, i, :] for i in range(6))
                v3 = lambda ap: ap.rearrange("p (h j) -> p h j", h=16)
                lora_sig(d, 0, n, F0, Fd[0])
                lora_sig(d, 1, n, F1, Fd[1])
                cut = int(os.environ.get("RW_CUT", "99"))
                if cut <= 1:
                    return
                self.tt("dve", F5, rkv[:, 1, :], kkb[:, 0, :], ALU.mult, [rkvsd[1], cstd], [Fd[5]])
                self.tt("pool", F2, F5, F5, ALU.mult, [Fd[5]], [Fd[2]])
                reduce_x(small[:, 0:16], v3(F2), [Fd[2]], [smd])
                self.act(small[:, 0:16], small[:, 0:16], AF.Sqrt, [smd], [smd])
                self.ts("dve", small[:, 0:16], small[:, 0:16], 1e-12, None, ALU.max, None, [smd], [smd])
                recip(small[:, 0:16], [smd], [smd])
                self.tt("dve", v3(F5), v3(F5), small[:, 0:16].unsqueeze(2).to_broadcast([128, 16, 64]), ALU.mult,
                        [Fd[5], smd], [Fd[5]])
                if cut <= 2:
                    return
                bi, be = nb2(), nb2()
                for nh in range(2):
                    self.mm(ps[:, bi + nh, :], tri[:, 2 * d, :], F0[:, nh * 512:(nh + 1) * 512], True, True,
                            [cstd, Fd[0]], [psd[bi + nh]])
                    self.mm(ps[:, be + nh, :], tri[:, 2 * d + 1, :], F0[:, nh * 512:(nh + 1) * 512], True, True,
                            [cstd, Fd[0]], [psd[be + nh]])
                self.act(F2, ps2(bi), AF.Exp, pd2(bi), [Fd[2]])
                self.act(F3, ps2(bi), AF.Exp, pd2(bi), [Fd[3]], scale=-1.0)
                self.act(F4, ps2(be), AF.Exp, pd2(be), [Fd[4]])
                bw = nb()
                for hp in range(8):
                    self.mm(ps[:, bw, hp * 2:hp * 2 + 2], F0[:, hp * 128:(hp + 1) * 128], negc2, hp == 0, True,
                            [Fd[0], self.wsd], [psd[bw]])
                self.act(WL[:], ps[:, bw, 0:16:2], AF.Exp, [psd[bw]], [WLd])
                if cut <= 3:
                    return
                self.stt(Bt[:, 0, :], F5, -1.0, F4, ALU.mult, ALU.mult, [Fd[5], Fd[4]], [Btd[0]])
                self.tt("pool", F4, F5, F1, ALU.mult, [Fd[5], Fd[1]], [Fd[4]])
                self.tt("dve", Bt[:, 1, :], F4, F3, ALU.mult, [Fd[4], Fd[3]], [Btd[1]])
                self.stt(F4, F1, -1.0, kkb[:, 1, :], ALU.add, ALU.mult, [Fd[1], cstd], [Fd[4]])
                self.stt(F5, F4, 1.0, rkv[:, 1, :], ALU.add, ALU.mult, [Fd[4], rkvsd[1]], [Fd[5]])
                self.tt("dve", Bt[:, 2, :], F5, F3, ALU.mult, [Fd[5], Fd[3]], [Btd[2]])
                self.tt("dve", Bt[:, 3, :], rkv[:, 0, :], F2, ALU.mult, [rkvsd[0], Fd[2]], [Btd[3]])
                if cut <= 4:
                    return
                for si, dst, eng in ((0, ARc[:, :, 0, :], "act"), (3, ARc[:, :, 1, :], "dve"), (1, Bc[:], "act"),
                                     (2, Kc[:], "dve")):
                    b = nb()
                    pst = ps[:, b, :].bitcast(BF16)
                    for hp in range(8):
                        self.tr(pst[:, hp * 128:(hp + 1) * 128], Bt[:, si, hp * 128:(hp + 1) * 128],
                                [Btd[si], self.cbd], [psd[b]])
                    self.cp(eng, dst, pst.rearrange("p (h t) -> p h t", h=8), [psd[b]], [cmd])
                if os.environ.get("RW_DUMP"):
                    dd = [Dep() for _ in range(NT)]
                    for i in range(6):
                        self.cp("act", self.x[:, i, :], F[:, i, :], [Fd[i]], [xd[i]])
                    for i in range(4):
                        self.cp("act", self.x[:, 6 + i, :], Bt[:, i, :], [Btd[i]], [xd[6 + i]])
                    for i in range(3):
                        self.cp("act", self.x[:, 10 + i, :], rkv[:, i, :], [rkvsd[i]], [xd[10 + i]])
                    return
                if cut <= 5:
                    return
                m12 = masks[:, 0:2, :] if d == 0 else masks[:, 2:4, :]
                m3 = masks[:, 2, :] if d == 0 else masks[:, 0, :]
                for h in range(16):
                    hp, pr = h // 2, slice((h % 2) * 64, (h % 2) * 64 + 64)
                    b = nb()
                    arv = ARc[pr, hp, :, :].rearrange("p a t -> p (a t)")
                    self.mm(ps[:, b, 0:256], Bc[pr, hp, :], arv, True, True, [cmd], [psd[b]])
                    self.mm(ps[:, b, 256:512], Kc[pr, hp, :], arv, False, True, [cmd], [psd[b]])
                    self.tt("dve", Nf[:, h, :], ps[:, b, 0:128], m12[:, 0, :], ALU.mult, [psd[b], self.cbd],
                            [Nfd] + allh)
                    self.tt("dve", NA3[:, h, 0, :], ps[:, b, 128:256], m12[:, 1, :], ALU.mult, [psd[b], self.cbd],
                            [NAd] + allh)
                    self.tt("dve", NA3[:, h, 1:3, :], ps[:, b, 256:512].rearrange("p (a t) -> p a t", a=2),
                            m12, ALU.mult, [psd[b], self.cbd], [NAd] + allh)
                for g8 in range(2):
                    bb = nb2()
                    for par in range(2):
                        pr = slice(par * 64, par * 64 + 64)
                        for i4 in range(4):
                            hp = g8 * 4 + i4
                            self.mm(ps[:, bb + par, i4 * 128:(i4 + 1) * 128], ARc[pr, hp, 0, :], Bc[pr, hp, :],
                                    i4 == 0, True, [cmd], [psd[bb + par]])
                        self.tt("dve", NTf[:, g8 * 8 + par:g8 * 8 + 8:2, :],
                                ps[:, bb + par, :].rearrange("p (h t) -> p h t", h=4),
                                m3.unsqueeze(1).to_broadcast([128, 4, 128]), ALU.mult, [psd[bb + par], self.cbd],
                                [NTfd] + allh)
                if cut <= 6:
                    return
                bc4 = lambda ap: ap.unsqueeze(1).to_broadcast([128, 4, 128])
                p4 = lambda b_: ps[:, b_, :].rearrange("p (h t) -> p h t", h=4)
                Fv = F[:, 0:3, :].rearrange("p a n -> p (a n)").rearrange("p (i h t) -> p i h t", i=6, h=4)
                bufsets = [([IB[:, i] for i in range(6)], ibd, [], []),
                           ([Fv[:, i] for i in range(6)], ibdB, list(Fd[0:3]), list(Fd[0:3]))]

                def inv_group(g4, bset):
                    bufs_, deps_, xr, xw = bset
                    hs = slice(g4 * 4, g4 * 4 + 4)
                    P1, PT1, Q, P2, PT2, Y1 = bufs_
                    dP1, dPT1, dQ, dP2, dPT2, dY1 = deps_

                    def mm4(bank, lhs, rhs, r):
                        for hh in range(4):
                            self.mm(ps[:, bank, hh * 128:(hh + 1) * 128], lhs[:, hh, :], rhs[:, hh, :], hh == 0, True,
                                    r + xr, [psd[bank]])
                    self.tt("dve", P1, Nf[:, hs, :], bc4(lvlm[:, 0, :]), ALU.mult, [Nfd, cstd] + xr, [dP1] + xw)
                    self.tt("pool", PT1, NTf[:, hs, :], bc4(lvlm[:, 0, :]), ALU.mult, [NTfd, cstd] + xr, [dPT1] + xw)
                    self.tt("pool", Q, P1, bc4(identf[:]), ALU.add, [dP1, cstd] + xr, [dQ] + xw)
                    yield
                    b1, b2_ = nb(), nb()
                    mm4(b1, PT1, P1, [dP1, dPT1])
                    mm4(b2_, P1, PT1, [dP1, dPT1])
                    yield
                    self.cp("act", P2, p4(b1), [psd[b1]] + xr, [dP2])
                    self.cp("act", PT2, p4(b2_), [psd[b2_]] + xr, [dPT2])
                    yield
                    b3, b4 = nb(), nb()
                    mm4(b3, PT2, Q, [dPT2, dQ])
                    mm4(b4, P2, PT2, [dP2, dPT2])
                    yield
                    self.tt("dve", Q, Q, p4(b3), ALU.add, [psd[b3], dQ] + xr, [dQ])
                    self.cp("act", PT1, p4(b4), [psd[b4]] + xr, [dPT1])
                    yield
                    b5 = nb()
                    mm4(b5, PT1, Q, [dPT1, dQ])
                    yield
                    self.tt("dve", Q, Q, p4(b5), ALU.add, [psd[b5], dQ] + xr, [dQ])
                    yield
                    b6 = nb()
                    for hh in range(4):
                        self.tr(ps[:, b6, hh * 128:(hh + 1) * 128], Q[:, hh, :], [dQ, cstd] + xr, [psd[b6]],
                                ident=identf[:])
                    Xt, dXt = P2, dP2
                    X_, dX = Q, dQ
                    NTm, dNTm = PT2, dPT2
                    self.tt("pool", NTm, NTf[:, hs, :], bc4(lvlm[:, 1, :]), ALU.mult, [NTfd, cstd] + xr, [dNTm])
                    yield
                    self.cp("act", Xt, p4(b6), [psd[b6]] + xr, [dXt])
                    for li in range(1, 5):
                        lastl = li == 4
                        by = nb()
                        mm4(by, NTm, X_, [dNTm, dX])
                        yield
                        self.cp("act", Y1, p4(by), [psd[by]] + xr, [dY1])
                        if not lastl:
                            self.tt("pool", NTm, NTf[:, hs, :], bc4(lvlm[:, li + 1, :]), ALU.mult,
                                    [NTfd, cstd] + xr, [dNTm])
                        yield
                        bx = nb()
                        mm4(bx, Xt, Y1, [dXt, dY1])
                        if not lastl:
                            bxt = nb()
                            mm4(bxt, Y1, Xt, [dY1, dXt])
                        yield
                        if not lastl:
                            self.tt("dve", X_, X_, p4(bx), ALU.add, [psd[bx], dX] + xr, [dX])
                            self.tt("dve", Xt, Xt, p4(bxt), ALU.add, [psd[bxt], dXt] + xr, [dXt])
                            yield
                        else:
                            self.tt("dve", Xb[:, hs, :], X_, p4(bx), ALU.add, [psd[bx], dX] + xr, [Xbd] + allh)

                for rnd in range(2):
                    gens = [inv_group(2 * rnd, bufsets[0]), inv_group(2 * rnd + 1, bufsets[1])]
                    alive = list(gens)
                    while alive:
                        nxt = []
                        for g_ in alive:
                            try:
                                next(g_)
                                nxt.append(g_)
                            except StopIteration:
                                pass
                        alive = nxt
                X, Xd = Xb, Xbd
                if cut <= 7:
                    return
                Vt = rkv[:, 2, :]
                bR = nb2()
                for h in range(16):
                    hp, pr = h // 2, slice((h % 2) * 64, (h % 2) * 64 + 64)
                    o = ps[:, bR + h % 2, hp * 64:hp * 64 + 64]
                    self.mm(o, ARc[pr, hp, 0, :], Sb[pr, hp, :], True, False, [cmd, Sbd], [psd[bR + h % 2]])
                    self.mm(o, NA3[:, h, 1, :], Vt[:, h * 64:(h + 1) * 64], False, True, [NAd, rkvsd[2]],
                            [psd[bR + h % 2]])
                self.cp("act", RU[:, 0, :].rearrange("p (h a i) -> p h a i", h=8, a=2),
                        ps2(bR).rearrange("p (a h i) -> p h a i", a=2, h=8), pd2(bR), [RUd[0]])
                bU = nb2()
                for h in range(16):
                    o = ps[:, bU + h // 8, (h % 8) * 64:(h % 8) * 64 + 64]
                    self.mm(o, X[:, h, :], RU[:, 0, h * 64:(h + 1) * 64], True, True, [Xd, RUd[0]], [psd[bU + h // 8]])
                self.cp("dve", RU[:, 1, :], ps2(bU), pd2(bU), [RUd[1]])
                bY = nb2()
                for h in range(16):
                    hp, pr = h // 2, slice((h % 2) * 64, (h % 2) * 64 + 64)
                    o = ps[:, bY + h % 2, hp * 64:hp * 64 + 64]
                    self.mm(o, ARc[pr, hp, 1, :], Sb[pr, hp, :], True, False, [cmd, Sbd], [psd[bY + h % 2]])
                    self.mm(o, NA3[:, h, 0, :], RU[:, 1, h * 64:(h + 1) * 64], False, False, [NAd, RUd[1]],
                            [psd[bY + h % 2]])
                    self.mm(o, NA3[:, h, 2, :], Vt[:, h * 64:(h + 1) * 64], False, True, [NAd, rkvsd[2]],
                            [psd[bY + h % 2]])
                yv = self.x[:, n, :].rearrange("p (h a i) -> p h a i", h=8, a=2)
                pyv = ps2(bY).rearrange("p (a h i) -> p h a i", a=2, h=8)
                if d == 0:
                    self.cp("act", yv, pyv, pd2(bY), [xd[n]])
                else:
                    self.tt("dve", yv, yv, pyv, ALU.add, pd2(bY) + [xd[n]], [xd[n]])
                bS = nb()
                for h in range(16):
                    hp, pr = h // 2, slice((h % 2) * 64, (h % 2) * 64 + 64)
                    o = ps[pr, bS, hp * 64:(hp + 1) * 64]
                    self.mm(o, Bt[:, 1, h * 64:(h + 1) * 64], RU[:, 1, h * 64:(h + 1) * 64], True, False,
                            [Btd[1], RUd[1]], [psd[bS]])
                    self.mm(o, Bt[:, 2, h * 64:(h + 1) * 64], Vt[:, h * 64:(h + 1) * 64], False, True,
                            [Btd[2], rkvsd[2]], [psd[bS]])
                self.tt("dve", S32[:], S32[:], ps[:, bS, :].rearrange("p (h i) -> p h i", h=8), ALU.add,
                        [psd[bS], S32d], [S32d])
                self.tt("dve", S32[:], S32[:], WL[:].unsqueeze(2).to_broadcast([128, 8, 64]), ALU.mult,
                        [S32d, WLd], [S32d])
                self.cp("act", Sb[:], S32[:], [S32d], [Sbd])

            for d in range(2):
                self.memset("pool", S32[:], 0.0, [S32d])
                self.memset("pool", Sb[:], 0.0, [Sbd])
                order = list(range(NT)) if d == 0 else list(range(NT - 1, -1, -1))
                if stop:
                    order = order[:nbt]
                for n in order:
                    scan_tile(d, n)
        S.barrier()
        if stop == "BY":
            return
        if stop == "B":
            for t in range(NT):
                self.load(self.x[:, t, :], self.xsp[t], [self.xspd[t]], [xd[t]])
            return
        with contextlib.ExitStack() as st:
            al = lambda n, sh_, dt: st.enter_context(self.sbt(n, sh_, dt))
            onesf = al("rwc_onesf", [128, 128], F32)
            brow = al("rwc_brow", [128, 2048], F32)
            bc4 = al("rwc_bc4", [128, 4, 1024], F32)
            w2a2 = al("rwc_w2a2", [128, 2, 1024], BF16)
            g2 = al("rwc_g2", [128, 1024], BF16)
            Wo = al("rwc_Wo", [128, KC, 1024], BF16)
            rkv = al("rwc_rkv", [128, 3, 1024], BF16)
            xt = al("rwc_xt", [128, 1024], F32)
            F = al("rwc_F", [128, 5, 1024], F32)
            ob = al("rwc_ob", [128, 1024], BF16)
            oT = al("rwc_oT", [128, KC, 128], BF16)
            small = al("rwc_small", [128, 48], F32)
            cstd, xtd, obd, oTd, smd = (Dep() for _ in range(5))
            rkvsd = [Dep(), Dep(), Dep()]
            Fd = [Dep() for _ in range(5)]
            self.load(onesf[:], self.cf_d[:, cfo["onesf"]:cfo["onesf"] + 128], (), [cstd])
            self.load(brow[:], self.cf_d[:, cfo["brow"]:cfo["brow"] + 2048], (), [cstd])
            for i, nm in enumerate(("d_k_a", "d_r_k", "d_ln_g", "d_ln_b")):
                self.load(bc4[:, i, :], self.cf_d[:, cfo[nm]:cfo[nm] + 1024], (), [cstd])
            self.load(w2a2[:, 0, :], self.wview("d_w2s"), [self.wbf_dep], [cstd])
            self.load(w2a2[:, 1, :], self.wview("d_a2s"), [self.wbf_dep], [cstd])
            self.load(g2[:], self.wview("d_g2"), [self.wbf_dep], [cstd])
            self.load(Wo[:], self.wview("d_w_o").rearrange("k p n -> p k n"), [self.wbf_dep], [cstd])
            v3 = lambda ap: ap.rearrange("p (h j) -> p h j", h=16)
            bc16 = lambda ap: ap.unsqueeze(2).to_broadcast([128, 16, 64])
            for n in range(NT):
                for i in range(3):
                    self.load(rkv[:, i, :], self.rkvsp[i, n], [self.rkvd[i][n]], [rkvsd[i]])
                self.load(xt[:], self.xsp[n], [self.xspd[n]], [xtd])
                F0, F1, F2, F3, F4 = (F[:, i, :] for i in range(5))
                for d in range(2):
                    prd = slice(d * 64, (d + 1) * 64)
                    rowp = d * 64
                    b2 = nb2()
                    for nh in range(2):
                        self.mm(ps[:, b2 + nh, :], lT[prd, 1, n * 128:(n + 1) * 128],
                                w2a2[prd, 1, nh * 512:(nh + 1) * 512], True, False, [lTd, cstd], [psd[b2 + nh]])
                        self.mm(ps[:, b2 + nh, :], onesf[rowp:rowp + 1, :],
                                brow[rowp:rowp + 1, 1024 + nh * 512:1024 + (nh + 1) * 512], False, True, [cstd],
                                [psd[b2 + nh]])
                    self.act(F[:, d, :], ps2(b2), AF.Sigmoid, pd2(b2), [Fd[d]])
                bG = nb2()
                for nh in range(2):
                    self.mm(ps[:, bG + nh, :], lT[:, 2, n * 128:(n + 1) * 128], g2[:, nh * 512:(nh + 1) * 512],
                            True, True, [lTd, cstd], [psd[bG + nh]])
                self.cp("act", F2, ps2(bG), pd2(bG), [Fd[2]])
                y = self.x[:, n, :]
                reduce_x(small[:, 0:16], v3(y), [xd[n]], [smd])
                self.ts("dve", small[:, 0:16], small[:, 0:16], 1.0 / 64, None, ALU.mult, None, [smd], [smd])
                self.tt("dve", v3(F3), v3(y), bc16(small[:, 0:16]), ALU.subtract, [xd[n], smd], [Fd[3]])
                self.tt("pool", F4, F3, F3, ALU.mult, [Fd[3]], [Fd[4]])
                reduce_x(small[:, 16:32], v3(F4), [Fd[4]], [smd])
                self.act(small[:, 16:32], small[:, 16:32], AF.Sqrt, [smd, self.wsd], [smd], scale=1.0 / 64,
                         bias=self.wsc("epsgn"))
                recip(small[:, 16:32], [smd], [smd])
                self.tt("dve", v3(F3), v3(F3), bc16(small[:, 16:32]), ALU.mult, [Fd[3], smd], [Fd[3]])
                self.tt("pool", F3, F3, bc4[:, 2, :], ALU.mult, [Fd[3], cstd], [Fd[3]])
                self.tt("pool", F3, F3, bc4[:, 3, :], ALU.add, [Fd[3], cstd], [Fd[3]])
                self.tt("dve", F0, F0, F1, ALU.add, [Fd[0], Fd[1]], [Fd[0]])
                self.ts("dve", F0, F0, 0.5, -1.0, ALU.mult, ALU.add, [Fd[0]], [Fd[0]])
                self.tt("dve", F0, F0, bc4[:, 0, :], ALU.mult, [Fd[0], cstd], [Fd[0]])
                self.stt(F0, F0, 1.0, rkv[:, 1, :], ALU.add, ALU.mult, [Fd[0], rkvsd[1]], [Fd[0]])
                self.tt("dve", F0, F0, rkv[:, 0, :], ALU.mult, [Fd[0], rkvsd[0]], [Fd[0]])
                self.tt("pool", F0, F0, bc4[:, 1, :], ALU.mult, [Fd[0], cstd], [Fd[0]])
                reduce_x(small[:, 32:48], v3(F0), [Fd[0]], [smd])
                self.tt("dve", v3(F4), v3(rkv[:, 2, :]), bc16(small[:, 32:48]), ALU.mult, [rkvsd[2], smd], [Fd[4]])
                self.tt("pool", F3, F3, F4, ALU.add, [Fd[3], Fd[4]], [Fd[3]])
                self.tt("dve", ob[:], F3, F2, ALU.mult, [Fd[3], Fd[2]], [obd])
                b = nb()
                pst = ps[:, b, :].bitcast(BF16)
                for kc in range(KC):
                    self.tr(pst[:, kc * 128:(kc + 1) * 128], ob[:, kc * 128:(kc + 1) * 128], [obd, self.cbd], [psd[b]])
                self.cp("act", oT[:], pst.rearrange("p (k t) -> p k t", k=KC), [psd[b]], [oTd])
                bO = nb2()
                for nh in range(2):
                    for kc in range(KC):
                        self.mm(ps[:, bO + nh, :], oT[:, kc, :], Wo[:, kc, nh * 512:(nh + 1) * 512], kc == 0,
                                kc == KC - 1, [oTd, cstd], [psd[bO + nh]])
                self.tt("dve", self.x[:, n, :], xt[:], ps2(bO), ALU.add, pd2(bO) + [xtd, xd[n]], [xd[n]])
    S.barrier()


Prog.mixer3 = _rwkv
```

```python
import math
import contextlib
import numpy as np
import ml_dtypes
import concourse.bass as bass
import concourse.mybir as mybir
from concourse.bass_utils import run_bass_kernel_spmd

F32 = mybir.dt.float32
BF16 = mybir.dt.bfloat16
AF = mybir.ActivationFunctionType
ALU = mybir.AluOpType
AX = mybir.AxisListType

D = 1024
S_LEN = 2048
NT = 16
KC = 8
FH = 2816
NCH = 22
N_CORES = 8
SEQ_PER_CORE = 5
NORM_EPS = 1e-6


class Dep:
    __slots__ = ("w", "r", "name")

    def __init__(self, name=""):
        self.w = None
        self.r = {}
        self.name = name


class Sch:
    COMPUTE = ("pe", "dve", "act", "pool")

    def __init__(self, nc, n_sp_ring=40, n_pool_ring=8, same_eng_sync=True):
        self.nc = nc
        self.q = {e: [] for e in ("pe", "dve", "act", "pool", "sp")}
        self.sems = {}
        self.cnt = {}
        for e in self.COMPUTE:
            self.sems[e] = nc.alloc_semaphore(f"s_{e}")
            self.cnt[e] = 0
        self.ring = {}
        for qn, n in (("sp", n_sp_ring), ("pool", n_pool_ring)):
            self.ring[qn] = dict(n=n, i=0)
            for i in range(n):
                self.sems[(qn, i)] = nc.alloc_semaphore(f"d_{qn}{i}")
        self.seen = {e: {} for e in self.q}
        self.same_eng_sync = same_eng_sync
        self.nops = 0

    def _collect(self, eng, reads, writes, is_dma):
        need = {}

        def add(k, v, peng):
            if peng == eng and not is_dma and k in self.COMPUTE:
                if eng == "pe" or not self.same_eng_sync:
                    return
            if need.get(k, 0) < v:
                need[k] = v
        for d in reads:
            if d.w is not None:
                add(*d.w)
        for d in writes:
            if d.w is not None:
                add(*d.w)
            for k, (v, peng) in d.r.items():
                add(k, v, peng)
        out = []
        seen = self.seen[eng]
        for k, v in need.items():
            if seen.get(k, 0) < v:
                seen[k] = v
                out.append((k, v))
        return out

    def op(self, eng, fn, reads=(), writes=()):
        waits = self._collect(eng, reads, writes, False)
        self.cnt[eng] += 1
        v = self.cnt[eng]
        for d in reads:
            d.r[eng] = (v, eng)
        for d in writes:
            d.w = (eng, v, eng)
            d.r = {}
        self.q[eng].append((waits, fn, (eng, 1)))
        self.nops += 1

    def dma(self, qn, fn, reads=(), writes=()):
        ring = self.ring[qn]
        i = ring["i"]
        ring["i"] += 1
        n = ring["n"]
        slot, gen = i % n, i // n
        key = (qn, slot)
        waits = self._collect(qn, reads, writes, True)
        if gen > 0:
            seen = self.seen[qn]
            if seen.get(key, 0) < 16 * gen:
                seen[key] = 16 * gen
                waits.append((key, 16 * gen))
        v = 16 * (gen + 1)
        for d in reads:
            d.r[key] = (v, "dma")
        for d in writes:
            d.w = (key, v, "dma")
            d.r = {}
        self.q[qn].append((waits, fn, (key, 16)))
        self.nops += 1

    def _all_marks(self):
        marks = [(e, self.cnt[e]) for e in self.COMPUTE if self.cnt[e] > 0]
        for rq, ring in self.ring.items():
            n = ring["n"]
            for slot in range(n):
                c = (ring["i"] - slot + n - 1) // n
                if c > 0:
                    marks.append(((rq, slot), 16 * c))
        return marks

    def barrier(self):
        marks = self._all_marks()
        for e in self.q:
            waits = []
            seen = self.seen[e]
            for k, v in marks:
                if k == e:
                    continue
                if seen.get(k, 0) < v:
                    seen[k] = v
                    waits.append((k, v))
            if waits:
                self.q[e].append((waits, None, None))

    def final_wait(self, qn="pool"):
        waits = [(k, v) for k, v in self._all_marks() if k != qn]
        self.q[qn].append((waits, None, None))

    def emit(self):
        nc = self.nc
        sems = self.sems

        def replay(eng_obj, lst):
            for waits, fn, inc in lst:
                for k, v in waits:
                    eng_obj.wait_ge(sems[k], v)
                if fn is not None:
                    ins = fn(eng_obj)
                    ins.then_inc(sems[inc[0]], inc[1])

        with nc.Block() as block:
            @block.tensor
            def _(e):
                replay(e, self.q["pe"])

            @block.vector
            def _(e):
                replay(e, self.q["dve"])

            @block.scalar
            def _(e):
                replay(e, self.q["act"])

            @block.gpsimd
            def _(e):
                replay(e, self.q["pool"])

            @block.sync
            def _(e):
                replay(e, self.q["sp"])


def colchunk(w):
    K, N = w.shape
    return np.ascontiguousarray(w.reshape(K // 128, 128, N // 128, 128).transpose(2, 1, 0, 3))


class Packer:
    def __init__(self):
        self.parts = []
        self.off = {}
        self.n = 0

    def add(self, name, arr):
        a = np.ascontiguousarray(arr, dtype=np.float32).reshape(-1)
        self.off[name] = (self.n, arr.shape)
        self.parts.append(a)
        self.n += a.size

    def finish(self, align):
        pad = (-self.n) % align
        if pad:
            self.parts.append(np.zeros(pad, np.float32))
            self.n += pad
        return np.concatenate(self.parts)


class ColPacker:
    def __init__(self):
        self.parts = []
        self.off = {}
        self.n = 0

    def add(self, name, arr):
        a = np.ascontiguousarray(arr, dtype=np.float32).reshape(128, -1)
        self.off[name] = self.n
        self.parts.append(a)
        self.n += a.shape[1]

    def finish(self):
        return np.ascontiguousarray(np.concatenate(self.parts, axis=1))


def chan8(v):
    return np.ascontiguousarray(v.reshape(8, 128).T)


def rep(v):
    return np.ascontiguousarray(np.broadcast_to(v.reshape(1, -1), (128, v.size)))


def build_rope_tables():
    out = np.zeros((4, 128, S_LEN), np.float32)
    t = np.arange(S_LEN, dtype=np.float32)
    half = 32
    inv = (10000.0 ** (-np.arange(half, dtype=np.float32) / half)).astype(np.float32)
    ang = t[None, :] * inv[:, None]
    c0 = np.cos(ang).astype(np.float32)
    s0 = np.sin(ang).astype(np.float32)
    for p in range(128):
        i = (p % 64) % half
        out[0, p] = c0[i]
        out[1, p] = s0[i]
    half = 16
    inv = (10000.0 ** (-np.arange(half, dtype=np.float32) / half)).astype(np.float32)
    row = np.floor(t / 64.0).astype(np.float32)
    col = (t - row * 64.0).astype(np.float32)
    for p in range(128):
        dd = p % 64
        pos = row if dd < 32 else col
        i = (dd % 32) % half
        ang = pos * inv[i]
        out[2, p] = np.cos(ang).astype(np.float32)
        out[3, p] = np.sin(ang).astype(np.float32)
    return out


def rot_matrix_T(block):
    R = np.zeros((128, 128), np.float32)
    half = block // 2
    for p in range(128):
        i = p % block
        if i < half:
            R[p, p + half] = -1.0
        else:
            R[p, p - half] = 1.0
    return np.ascontiguousarray(R.T)


class Prog:
    def __init__(self, nseq, wbig_n, wbig_off, ws_n, ws_off, cb_n, cb_off, cf_n, cf_off,
                 mixers=(0, 1, 2, 3), ffn=True, nlayers=4):
        self.nseq = nseq
        self.mixers = mixers
        self.do_ffn = ffn
        self.nlayers = nlayers
        nc = bass.Bass("TRN2", target_bir_lowering=False)
        self.nc = nc
        self._uniq = 0
        self.S = Sch(nc)
        self.wo, self.so, self.cbo, self.cfo = wbig_off, ws_off, cb_off, cf_off
        self.x_in = nc.dram_tensor("x", [nseq, S_LEN, D], F32, kind="ExternalInput").ap()
        self.y_out = nc.dram_tensor("y", [nseq, S_LEN, D], F32, kind="ExternalOutput").ap()
        self.wbig = nc.dram_tensor("wbig", [wbig_n], F32, kind="ExternalInput").ap()
        self.wbf = nc.dram_tensor("wbf", [wbig_n], BF16, kind="Internal").ap()
        self.wsmall_d = nc.dram_tensor("wsmall", [128, ws_n], F32, kind="ExternalInput").ap()
        self.cbf_d = nc.dram_tensor("cbf", [128, cb_n], BF16, kind="ExternalInput").ap()
        self.cf_d = nc.dram_tensor("cf", [128, cf_n], F32, kind="ExternalInput").ap()
        self.rope_d = nc.dram_tensor("rope", [4, 128, S_LEN], BF16, kind="ExternalInput").ap()
        self.wbf_dep = Dep("wbf")
        self.wbf_blocks = [Dep() for _ in range(wbig_n // (128 * 2048))]
        self._wdeps_acc = []
        self.xsp = nc.dram_tensor("xsp", [NT, 128, D], F32, kind="Internal").ap()
        self.xspd = [Dep() for _ in range(NT)]
        self.rkvsp = nc.dram_tensor("rkvsp", [3, NT, 128, D], BF16, kind="Internal").ap()
        self.rkvd = [[Dep() for _ in range(NT)] for _ in range(3)]
        self.x = nc.alloc_sbuf_tensor("xres", [128, NT, D], F32)
        self.xd = [Dep(f"x{t}") for t in range(NT)]
        self.hT = nc.alloc_sbuf_tensor("hT", [128, KC, S_LEN], BF16)
        self.hTd = [Dep(f"hT{t}") for t in range(NT)]
        self.ws = nc.alloc_sbuf_tensor("ws", [128, ws_n], F32)
        self.wsd = Dep("ws")
        self.cb = nc.alloc_sbuf_tensor("cb", [128, cb_n], BF16)
        self.cbd = Dep("cb")
        self.ps = nc.alloc_psum_tensor("ps", [128, 8, 512], F32)
        self.psd = [Dep(f"ps{b}") for b in range(8)]
        self.ident = self.cb[:, cb_off["ident"]:cb_off["ident"] + 128]

    def sbt(self, name, shape, dt):
        self._uniq += 1
        return self.nc.sbuf_tensor(f"{name}_{self._uniq}", shape, dt)

    def mm(self, out, lhsT, rhs, start, stop, r, w):
        self.S.op("pe", lambda e: e.matmul(out, lhsT, rhs, start=start, stop=stop,
                                           skip_group_check=True), r, w)

    def tr(self, out, in_, r, w, ident=None):
        idn = self.ident if ident is None else ident
        self.S.op("pe", lambda e: e.transpose(out, in_, idn), r, w)

    def act(self, out, in_, func, r, w, scale=1.0, bias=None, accum=None):
        def f(e):
            kw = {}
            if bias is not None:
                kw["bias"] = bias
            if accum is not None:
                kw["accum_out"] = accum
            return e.activation(out=out, in_=in_, func=func, scale=scale, **kw)
        self.S.op("act", f, r, w)

    def ts(self, eng, out, in0, s1, s2, op0, op1, r, w):
        def f(e):
            if op1 is None:
                return e.tensor_scalar(out=out, in0=in0, scalar1=s1, scalar2=None, op0=op0)
            return e.tensor_scalar(out=out, in0=in0, scalar1=s1, scalar2=s2, op0=op0, op1=op1)
        self.S.op(eng, f, r, w)

    def tt(self, eng, out, in0, in1, op, r, w):
        self.S.op(eng, lambda e: e.tensor_tensor(out=out, in0=in0, in1=in1, op=op), r, w)

    def stt(self, out, in0, scalar, in1, op0, op1, r, w):
        self.S.op("dve", lambda e: e.scalar_tensor_tensor(out=out, in0=in0, scalar=scalar, in1=in1,
                                                          op0=op0, op1=op1), r, w)

    def cp(self, eng, out, in_, r, w):
        if eng == "act":
            self.S.op("act", lambda e: e.copy(out=out, in_=in_), r, w)
        else:
            self.S.op(eng, lambda e: e.tensor_copy(out=out, in_=in_), r, w)

    def memset(self, eng, ap, val, w):
        self.S.op(eng, lambda e: e.memset(ap, val), (), w)

    def load(self, out, in_, r, w, q="sp"):
        r = list(r)
        if self.wbf_dep in r:
            r.remove(self.wbf_dep)
            r += list(self._wdeps_acc)
        self.S.dma(q, lambda e: e.dma_start(out=out, in_=in_), r, w)

    def wsc(self, name, j=0, n=1):
        o = self.so[name] + j
        return self.ws[:, o:o + n]

    def setup(self):
        S = self.S
        self.load(self.ws[:], self.wsmall_d, (), [self.wsd])
        self.load(self.cb[:], self.cbf_d, (), [self.cbd])
        n = self.wbig.shape[0]
        blk = 128 * 2048
        src = self.wbig.rearrange("(b p f) -> b p f", p=128, f=2048)
        dst = self.wbf.rearrange("(b p f) -> b p f", p=128, f=2048)
        for b in range(n // blk):
            self.load(dst[b], src[b], (), [self.wbf_blocks[b]], q="pool")

    def wview(self, name):
        off, shape = self.wo[name]
        n = int(np.prod(shape))
        blk = 128 * 2048
        for d_ in self.wbf_blocks[off // blk:(off + n - 1) // blk + 1]:
            if d_ not in self._wdeps_acc:
                self._wdeps_acc.append(d_)
        flat = self.wbf[off:off + n]
        if len(shape) == 3:
            return flat.rearrange("(a b c) -> a b c", b=shape[1], c=shape[2])
        if len(shape) == 4:
            return flat.rearrange("(a b c d) -> a b c d", b=shape[1], c=shape[2], d=shape[3])
        return flat.rearrange("(a b) -> a b", b=shape[1])

    def load_x(self, s):
        for t in range(NT):
            self.load(self.x[:, t, :], self.x_in[s, t * 128:(t + 1) * 128, :], (), [self.xd[t]])

    def store_x(self, s):
        for t in range(NT):
            self.load(self.y_out[s, t * 128:(t + 1) * 128, :], self.x[:, t, :], [self.xd[t]], [Dep()], q="pool")

    def rmsnorm_T(self, gname, st):
        nc = self.nc
        junk = st.enter_context(self.sbt("nrm_junk", [128, D], BF16))
        hn = st.enter_context(self.sbt("nrm_hn", [128, 2, D], BF16))
        ss = st.enter_context(self.sbt("nrm_ss", [128, 2, 2], F32))
        junkd = Dep()
        hnd = [Dep(), Dep()]
        ssd = [Dep(), Dep()]
        g = self.wsc(gname, 0, 8)
        for t in range(NT):
            b = t % 2
            self.act(junk[:], self.x[:, t, :], AF.Square, [self.xd[t]], [junkd, ssd[b]],
                     accum=ss[:, b, 0:1])
            self.act(ss[:, b, 1:2], ss[:, b, 0:1], AF.Sqrt, [ssd[b]], [ssd[b]],
                     scale=1.0 / D, bias=self.wsc("eps6"))
            self.S.op("dve", (lambda b=b: (lambda e: e.reciprocal(out=ss[:, b, 1:2], in_=ss[:, b, 1:2])))(),
                      [ssd[b]], [ssd[b]])
            self.ts("dve", hn[:, b, :], self.x[:, t, :], ss[:, b, 1:2], None, ALU.mult, None,
                    [self.xd[t], ssd[b]], [hnd[b]])
            pb = 6 + (t % 2)
            pst = self.ps[:, pb, :].bitcast(BF16)
            for kc in range(KC):
                self.tr(pst[:, kc * 128:(kc + 1) * 128], hn[:, b, kc * 128:(kc + 1) * 128],
                        [hnd[b], self.cbd], [self.psd[pb]])
            self.tt("dve", self.hT[:, :, t * 128:(t + 1) * 128],
                    pst.rearrange("p (k c) -> p k c", k=KC),
                    g.unsqueeze(2).to_broadcast([128, KC, 128]), ALU.mult,
                    [self.psd[pb], self.wsd], [self.hTd[t]])

    def ffn(self, L):
        nc, S = self.nc, self.S
        S.barrier()
        with contextlib.ExitStack() as st0:
            with contextlib.ExitStack() as st:
                self.rmsnorm_T(f"n{L}_ffn", st)
            S.barrier()
            st = st0
            wi = st.enter_context(self.sbt("ffn_wi", [128, 2, 4, 1024], BF16))
            wo = st.enter_context(self.sbt("ffn_wo", [128, 2, 2, 1024], BF16))
            gT = st.enter_context(self.sbt("ffn_gT", [128, 2, 2, S_LEN], BF16))
            upad = st.enter_context(self.sbt("ffn_upad", [128, 2, 2, S_LEN + 2], F32))
            acc = st.enter_context(self.sbt("ffn_acc", [128, 2, S_LEN], F32))
            wid = [Dep(), Dep()]
            wod = [Dep(), Dep()]
            gTd = [Dep(), Dep()]
            upd = [[Dep(), Dep()], [Dep(), Dep()]]
            accd = [Dep(), Dep()]
            allh = list(self.hTd)
            for w_ in range(2):
                for b in range(2):
                    self.memset("pool", upad[:, w_, b, 0:1], 0.0, [upd[w_][b]])
                    self.memset("pool", upad[:, w_, b, S_LEN + 1:S_LEN + 2], 0.0, [upd[w_][b]])
            win = self.wview(f"f{L}_w_in")
            wout = self.wview(f"f{L}_w_out")
            cw = self.so[f"f{L}_conv_w"]
            cbo = self.so[f"f{L}_conv_b"]
            NP = NCH // 2
            ubuf = 0

            def out_partial(p, tiles):
                pb_ = p % 2
                for t in tiles:
                    bank = 4 + 2 * (t % 2)
                    for nh in range(2):
                        for j in range(2):
                            self.mm(self.ps[:, bank + nh, :], gT[:, pb_, j, t * 128:(t + 1) * 128],
                                    wo[:, pb_, j, nh * 512:(nh + 1) * 512], j == 0, j == 1,
                                    [gTd[pb_], wod[pb_]], [self.psd[bank + nh]])
                    self.tt("dve", self.x[:, t, :], self.x[:, t, :],
                            self.ps[:, bank:bank + 2, :].rearrange("p a b -> p (a b)"), ALU.add,
                            [self.psd[bank], self.psd[bank + 1], self.xd[t]], [self.xd[t]])

            for p in range(NP + 1):
                if p < NP:
                    pb = p % 2
                    self.load(wi[:, pb, 0:2, :],
                              win[2 * p:2 * p + 2].rearrange("a p k c -> p a (k c)"),
                              [self.wbf_dep], [wid[pb]])
                    self.load(wi[:, pb, 2:4, :],
                              win[NCH + 2 * p:NCH + 2 * p + 2].rearrange("a p k c -> p a (k c)"),
                              [self.wbf_dep], [wid[pb]])
                    self.load(wo[:, pb, :, :], wout[2 * p:2 * p + 2].rearrange("a p n -> p a n"),
                              [self.wbf_dep], [wod[pb]])
                for j in range(2):
                    if p < NP:
                        for which in range(2):
                            ci = 2 * p + j + which * NCH
                            ub = ubuf % 2
                            for half in range(2):
                                bank = 2 * half
                                for tq in range(2):
                                    c0 = half * 1024 + tq * 512
                                    for kc in range(KC):
                                        self.mm(self.ps[:, bank + tq, :],
                                                wi[:, pb, which * 2 + j, kc * 128:(kc + 1) * 128],
                                                self.hT[:, kc, c0:c0 + 512], kc == 0, kc == KC - 1,
                                                [wid[pb]] + allh[c0 // 128:c0 // 128 + 4],
                                                [self.psd[bank + tq]])
                                self.cp("act", upad[:, which, ub, 1 + half * 1024:1 + (half + 1) * 1024],
                                        self.ps[:, bank:bank + 2, :].rearrange("p a b -> p (a b)"),
                                        [self.psd[bank], self.psd[bank + 1]], [upd[which][ub]])
                            u = upad[:, which, ub, :]
                            a_ = acc[:, which, :]
                            self.act(a_, u[:, 0:S_LEN], AF.Identity, [upd[which][ub], self.wsd], [accd[which]],
                                     scale=self.ws[:, cw + ci * 3:cw + ci * 3 + 1],
                                     bias=self.ws[:, cbo + ci:cbo + ci + 1])
                            self.stt(a_, u[:, 1:S_LEN + 1], self.ws[:, cw + ci * 3 + 1:cw + ci * 3 + 2], a_,
                                     ALU.mult, ALU.add, [upd[which][ub], accd[which]], [accd[which]])
                            self.stt(a_, u[:, 2:S_LEN + 2], self.ws[:, cw + ci * 3 + 2:cw + ci * 3 + 3], a_,
                                     ALU.mult, ALU.add, [upd[which][ub], accd[which]], [accd[which]])
                        ubuf += 1
                        self.act(acc[:, 0, :], acc[:, 0, :], AF.Silu, [accd[0]], [accd[0]])
                        self.tt("dve", gT[:, pb, j, :], acc[:, 0, :], acc[:, 1, :], ALU.mult,
                                [accd[0], accd[1]], [gTd[pb]])
                    if p > 0:
                        out_partial(p - 1, range(j * 8, j * 8 + 8))

    def run_seq(self, s):
        self.load_x(s)
        if getattr(self, "debug", None) == "hT":
            with contextlib.ExitStack() as st:
                self.rmsnorm_T("n0_ffn", st)
            xv = self.x[:].rearrange("p a b -> p (a b)").rearrange("p (k t) -> p k t", k=KC)
            self.cp("act", xv, self.hT[:], self.hTd + self.xd, self.xd)
            self.store_x(s)
            return
        for L in range(self.nlayers):
            if L in self.mixers:
                getattr(self, f"mixer{L}")(L)
            if self.do_ffn:
                self.ffn(L)
        self.store_x(s)

    def build(self):
        self.setup()
        for s in range(self.nseq):
            self.run_seq(s)
        self.S.final_wait("pool")
        print("ops per engine", {e: len(v) for e, v in self.S.q.items()}, "cnt", self.S.cnt, flush=True)
        self.S.emit()
        return self.nc


def pack_params(p):
    big = Packer()
    sm = ColPacker()
    cb = ColPacker()
    cf = ColPacker()
    f32 = np.float32
    cb.add("ident", np.eye(128, dtype=f32))
    bo = np.zeros((128, 128), f32)
    bo[:64, :64] = 1.0 / 64
    bo[64:, 64:] = 1.0 / 64
    cb.add("blockmean", bo)
    cb.add("rotT64", rot_matrix_T(64))
    cb.add("rotT32", rot_matrix_T(32))
    cb.add("ones", np.ones((128, 128), f32))
    sm.add("eps6", np.full((128, 1), 1e-6, f32))
    sm.add("eps5", np.full((128, 1), 1e-5, f32))
    sm.add("epsgn", np.full((128, 1), 64e-5, f32))
    sm.add("zero", np.zeros((128, 1), f32))
    pack_mixers(p, big, sm, cb, cf)
    for L in range(4):
        sm.add(f"n{L}_attn", chan8(p[f"n{L}_attn"]))
        sm.add(f"n{L}_ffn", chan8(p[f"n{L}_ffn"]))
        cw = p[f"f{L}_conv_w"]
        sm.add(f"f{L}_conv_w", cw.T.reshape(44, 128, 3).transpose(1, 0, 2).reshape(128, 132))
        sm.add(f"f{L}_conv_b", p[f"f{L}_conv_b"].reshape(44, 128).T)
        big.add(f"f{L}_w_in", colchunk(p[f"f{L}_w_in"]))
        big.add(f"f{L}_w_out", p[f"f{L}_w_out"].reshape(22, 128, 1024))
    wbig = big.finish(128 * 2048)
    return (wbig, big.off, sm.finish(), sm.off, cb.finish().astype(ml_dtypes.bfloat16), cb.off,
            cf.finish(), cf.off)


def pack_mixers(p, big, sm, cb, cf):
    cf.add("dummy", np.zeros((128, 1), np.float32))
    f32 = np.float32
    idx64 = np.arange(128) % 64
    big.add("a_wqkv", colchunk(p["a_w_qkv"]))
    big.add("a_wo", p["a_w_o"].reshape(8, 128, 1024))
    sm.add("a_qg", p["a_q_norm"][idx64].reshape(128, 1))
    sm.add("a_kg", p["a_k_norm"][idx64].reshape(128, 1))
    sm.add("a_subln", rep(p["a_subln"]))
    for nm in ("a_lq1", "a_lk1", "a_lq2", "a_lk2"):
        sm.add(nm, rep(p[nm]))
    wb = p["b_w_qkv"]
    big.add("b_wqkv", colchunk(wb))
    kd = np.stack([np.concatenate([wb[:, 1024 + g * 64:1024 + (g + 1) * 64]] * 2, axis=1) for g in range(4)], 0)
    big.add("b_wkdup", np.stack([colchunk(kd[g])[0] for g in range(4)], 0))
    big.add("b_wo", p["b_w_o"].reshape(8, 128, 1024))
    sm.add("b_qg", p["b_q_norm"][idx64].reshape(128, 1))
    sm.add("b_kg", p["b_k_norm"][idx64].reshape(128, 1))
    big.add("c_wqkv", colchunk(p["c_w_qkv"]))
    big.add("c_wo", p["c_w_o"].reshape(8, 128, 1024))
    sm.add("c_qg", p["c_q_norm"][idx64].reshape(128, 1))
    sm.add("c_kg", p["c_k_norm"][idx64].reshape(128, 1))
    ho = np.zeros((128, 2), f32)
    ho[:64, 0] = -30000.0
    ho[64:, 1] = -30000.0
    sm.add("hoff", ho)
    rb = p["c_rel_bias"]
    pp = np.arange(128)
    j2 = pp // 64
    kc_ = pp % 64
    qq = np.arange(64)
    dc = np.clip(kc_[:, None] - qq[None, :] + 15, 0, 30)
    base = np.arange(14)
    dr = base[None, :] + j2[:, None]
    T = rb[:, dr[:, :, None], dc[:, None, :]]
    T = T.reshape(8, 2, 128, 14, 64).transpose(0, 2, 1, 3, 4)
    big.add("c_bias_f32", np.ascontiguousarray(T).reshape(8, 128, 2 * 14 * 64))
    cs_ = np.clip(qq - 8, 0, 48)
    inwin = (kc_[:, None] >= cs_[None, :]) & (kc_[:, None] < cs_[None, :] + 16)
    cf.add("nb_mask", np.where(inwin, 0.0, -30000.0).astype(f32))
    for nm in ("d_w_r", "d_w_k", "d_w_v", "d_w_o"):
        big.add(nm, p[nm].reshape(8, 128, 1024))
    big.add("d_w1s", np.concatenate([p["d_w1"][0], p["d_w1"][1]], axis=1).reshape(8, 128, 128))
    big.add("d_a1s", np.concatenate([p["d_a1"][0], p["d_a1"][1]], axis=1).reshape(8, 128, 128))
    big.add("d_g1", p["d_g1"].reshape(8, 128, 128))
    big.add("d_w2s", p["d_w2"].reshape(128, 1024))
    big.add("d_a2s", p["d_a2"].reshape(128, 1024))
    big.add("d_g2", p["d_g2"].reshape(128, 1024))
    sm.add("d_mu", np.concatenate([chan8(p["d_mu"][j]) for j in range(6)], axis=1))
    NEGC = -math.exp(-0.5)
    sm.add("negc", np.full((128, 2), NEGC, f32))
    ii = np.arange(128)
    le = (ii[:, None] <= ii[None, :]).astype(f32)
    lt = (ii[:, None] < ii[None, :]).astype(f32)
    ge = (ii[:, None] >= ii[None, :]).astype(f32)
    gt = (ii[:, None] > ii[None, :]).astype(f32)
    cf.add("tri", np.concatenate([le * NEGC, lt * NEGC, ge * NEGC, gt * NEGC], axis=1))
    cf.add("onesf", np.ones((128, 128), f32))
    cf.add("identf", np.eye(128, dtype=f32))
    lv = [(ii[:, None] // 8 == ii[None, :] // 8).astype(f32)]
    for bsz in (16, 32, 64, 128):
        lv.append(((ii[:, None] // bsz == ii[None, :] // bsz) & (ii[:, None] // (bsz // 2) != ii[None, :] // (bsz // 2))).astype(f32))
    cf.add("lvlmask", np.concatenate(lv, axis=1))
    cb.add("masks", np.concatenate([lt, le, gt, ge], axis=1))
    brow = np.zeros((128, 2048), f32)
    brow[0, :1024] = p["d_w0"][0]
    brow[64, :1024] = p["d_w0"][1]
    brow[0, 1024:] = p["d_a0"][0]
    brow[64, 1024:] = p["d_a0"][1]
    cf.add("brow", brow)
    cf.add("d_k_k", rep(p["d_k_k"]))
    cf.add("d_k_a", rep(p["d_k_a"]))
    cf.add("d_r_k", rep(p["d_r_k"].reshape(-1)))
    cf.add("d_ln_g", rep(p["d_ln_g"]))
    cf.add("d_ln_b", rep(p["d_ln_b"]))


_ROPE = None


def run(inputs, nseq_per_core=SEQ_PER_CORE, n_cores=N_CORES, **kw):
    global _ROPE
    xs = np.concatenate([np.asarray(inputs["x_prompt"]), np.asarray(inputs["x_sample"])], axis=0)
    p = {k: np.asarray(v) for k, v in inputs.items() if not k.startswith("x_")}
    wbig, woff, ws, soff, cbv, cboff, cfv, cfoff = pack_params(p)
    if _ROPE is None:
        _ROPE = build_rope_tables()
    prog = Prog(nseq_per_core, wbig.size, woff, ws.shape[1], soff, cbv.shape[1], cboff,
                cfv.shape[1], cfoff, **kw)
    nc = prog.build()
    in_maps = []
    for c in range(n_cores):
        in_maps.append({
            "x": np.ascontiguousarray(xs[c * nseq_per_core:(c + 1) * nseq_per_core]),
            "wbig": wbig, "wsmall": ws, "cbf": cbv, "cf": cfv, "rope": _ROPE.astype(ml_dtypes.bfloat16),
        })
    res = run_bass_kernel_spmd(nc, in_maps, core_ids=list(range(n_cores)))
    return np.concatenate([r["y"] for r in res.results], axis=0)


def kernel(**inputs):
    y = run(inputs)
    nb = np.asarray(inputs["x_prompt"]).shape[0]
    return (np.ascontiguousarray(y[:nb]), np.ascontiguousarray(y[nb:]))


def _attn_full(self, L):
    nc, S = self.nc, self.S
    diff = (L == 0)
    pre = "a" if diff else "b"
    E = 128 if diff else 64
    EW = 130 if diff else 66
    lam_init = 0.8 - 0.6 * math.exp(-0.3 * L)
    S.barrier()
    with contextlib.ExitStack() as st0:
        with contextlib.ExitStack() as st:
            self.rmsnorm_T(f"n{L}_attn", st)
        S.barrier()
        st = st0
        al = lambda n, sh, dt: st.enter_context(self.sbt(n, sh, dt))
        wq = al("at_wq", [128, 3, 1024], BF16)
        cs = al("at_cs", [128, 2, S_LEN], BF16)
        qT = al("at_qT", [128, 2, S_LEN], BF16)
        kT = al("at_kT", [128, 2, S_LEN], BF16)
        V = al("at_V", [128, 2, NT, EW], BF16)
        OT = al("at_OT", [128, KC, S_LEN], BF16)
        PT = al("at_PT", [128, 2, 512], BF16)
        sq = al("at_sq", [128, 512], BF16)
        qg = al("at_qg", [128, 512], BF16)
        t1 = al("at_t1", [128, 512], F32)
        t2 = al("at_t2", [128, 512], F32)
        t3 = al("at_t3", [128, 512], F32)
        oacc = al("at_oacc", [128, 4, 128], F32)
        osq = al("at_osq", [128, 4, 128], F32)
        osb = al("at_osb", [128, 4, 128], BF16)
        sm_ = al("at_sm", [128, 16], F32)
        wqd = [Dep(), Dep(), Dep()]
        csd = Dep()
        qTd = [Dep(), Dep()]
        kTd = [Dep(), Dep()]
        Vd = [Dep(), Dep()]
        OTd = Dep()
        PTd = [Dep(), Dep()]
        sqd, qgd, t1d, t2d, t3d, oaccd, osqd, osbd, smd = (Dep() for _ in range(9))
        allh = list(self.hTd)
        psd = self.psd
        ps = self.ps
        self.load(cs[:], self.rope_d[2 * L:2 * L + 2].rearrange("a p t -> p a t"), (), [csd])
        for b in range(2):
            self.memset("pool", V[:, b, :, E:E + 1], 1.0, [Vd[b]])
        cbo = self.cbo
        blockmean = self.cb[:, cbo["blockmean"]:cbo["blockmean"] + 128]
        rotT = self.cb[:, cbo["rotT64" if diff else "rotT32"]:cbo["rotT64" if diff else "rotT32"] + 128]
        wqkv = self.wview(f"{pre}_wqkv")
        if diff:
            for i, (a_, b_) in enumerate((("a_lq1", "a_lk1"), ("a_lq2", "a_lk2"))):
                self.tt("dve", t1[:, 0:64], self.wsc(a_, 0, 64), self.wsc(b_, 0, 64), ALU.mult, [self.wsd], [t1d])
                self.S.op("dve", (lambda i=i: lambda e: e.tensor_reduce(out=sm_[:, 8 + i:9 + i], in_=t1[:, 0:64],
                                                                       axis=AX.X, op=ALU.add))(), [t1d], [smd])
            self.act(sm_[:, 8:10], sm_[:, 8:10], AF.Exp, [smd], [smd])
            self.tt("dve", sm_[:, 10:11], sm_[:, 8:9], sm_[:, 9:10], ALU.subtract, [smd], [smd])
            self.ts("dve", sm_[:, 11:12], sm_[:, 10:11], lam_init, -1.0, ALU.add, ALU.mult, [smd], [smd])
        neglam = sm_[:, 11:12]

        def load_w(hc):
            if diff:
                idx = (hc, 8 + hc, 16 + hc)
            else:
                idx = (hc, None, 10 + hc // 4)
            for i, ci in enumerate(idx):
                if ci is None:
                    src = self.wview("b_wkdup")[hc // 2]
                else:
                    src = wqkv[ci]
                if (not diff) and i > 0 and hc % 2 == 1:
                    continue
                self.load(wq[:, i, :], src.rearrange("p k c -> p (k c)"), [self.wbf_dep], [wqd[i]])

        def proj_qk(hc, which, tc):
            b = hc % 2 if (diff or which == 0) else (hc // 2) % 2
            dst, dstd = (qT, qTd) if which == 0 else (kT, kTd)
            gain = self.wsc(f"{pre}_qg" if which == 0 else f"{pre}_kg")
            c0 = tc * 512
            for kc in range(KC):
                self.mm(ps[:, 4, :], wq[:, which, kc * 128:(kc + 1) * 128], self.hT[:, kc, c0:c0 + 512],
                        kc == 0, kc == KC - 1, [wqd[which]] + allh[tc * 4:tc * 4 + 4], [psd[4]])
            self.act(sq[:], ps[:, 4, :], AF.Square, [psd[4]], [sqd])
            self.act(qg[:], ps[:, 4, :], AF.Identity, [psd[4], self.wsd], [qgd], scale=gain)
            self.mm(ps[:, 5, :], blockmean, sq[:], True, True, [sqd, self.cbd], [psd[5]])
            self.mm(ps[:, 6, :], rotT, qg[:], True, True, [qgd, self.cbd], [psd[6]])
            self.act(t1[:], ps[:, 5, :], AF.Sqrt, [psd[5], self.wsd], [t1d], bias=self.wsc("eps6"))
            self.S.op("dve", lambda e: e.reciprocal(out=t1[:], in_=t1[:]), [t1d], [t1d])
            self.tt("dve", t2[:], qg[:], cs[:, 0, c0:c0 + 512], ALU.mult, [qgd, csd], [t2d])
            self.tt("dve", t3[:], ps[:, 6, :], cs[:, 1, c0:c0 + 512], ALU.mult, [psd[6], csd], [t3d])
            self.tt("pool", t2[:], t2[:], t3[:], ALU.add, [t2d, t3d], [t2d])
            self.tt("dve", dst[:, b, c0:c0 + 512], t2[:], t1[:], ALU.mult, [t2d, t1d], [dstd[b]])

        def proj_v(hc, tg):
            b = hc % 2 if diff else (hc // 2) % 2
            voff = 0 if diff else ((hc // 2) % 2) * 64
            for i in range(4):
                t = tg * 4 + i
                for kc in range(KC):
                    self.mm(ps[:, 7, i * E:(i + 1) * E], self.hT[:, kc, t * 128:(t + 1) * 128],
                            wq[:, 2, kc * 128 + voff:kc * 128 + voff + E], kc == 0 and i == 0, kc == KC - 1,
                            [wqd[2], allh[t]], [psd[7]])
            self.cp("act", V[:, b, tg * 4:tg * 4 + 4, 0:E],
                    ps[:, 7, 0:4 * E].rearrange("p (a e) -> p a e", a=4), [psd[7]], [Vd[b]])

        def proj_units(hc):
            units = []
            kv_new = diff or hc % 2 == 0
            for tc in range(4):
                units.append((lambda tc=tc: proj_qk(hc, 0, tc)))
                if kv_new:
                    units.append((lambda tc=tc: proj_qk(hc, 1, tc)))
                    units.append((lambda tc=tc: proj_v(hc, tc)))
            return units

        def attn_block(hc, qc, c):
            bq = hc % 2
            bk = hc % 2 if diff else (hc // 2) % 2
            pr = slice(c * 64, (c + 1) * 64)
            W = E + 1
            nb = 2 if diff else 1
            for kt in range(NT):
                sb = kt % 2
                self.mm(ps[:, sb, :], kT[pr, bk, kt * 128:(kt + 1) * 128], qT[pr, bq, qc * 512:(qc + 1) * 512],
                        True, True, [kTd[bk], qTd[bq]], [psd[sb]])
                self.act(PT[:, sb, :], ps[:, sb, :], AF.Exp, [psd[sb]], [PTd[sb]], scale=0.125)
                for qs in range(4):
                    if diff:
                        bank, col = 2 + qs // 2, (qs % 2) * W
                        first = (kt == 0 and qs % 2 == 0)
                    else:
                        bank, col = 2 + c, qs * W
                        first = (kt == 0 and qs == 0)
                    self.mm(ps[:, bank, col:col + W], PT[:, sb, qs * 128:(qs + 1) * 128], V[:, bk, kt, 0:W],
                            first, kt == NT - 1, [PTd[sb], Vd[bk]], [psd[bank]])
            if diff:
                for hb in range(2):
                    bank = 2 + hb
                    pv = ps[:, bank, 0:2 * W].rearrange("p (a w) -> p a w", a=2)
                    rz = sm_[:, hb * 2:hb * 2 + 2]
                    self.S.op("dve", (lambda rz=rz, pv=pv: lambda e: e.reciprocal(out=rz.unsqueeze(2), in_=pv[:, :, E:E + 1]))(),
                              [psd[bank]], [smd])
                    if c == 0:
                        self.tt("dve", oacc[:, hb * 2:hb * 2 + 2, :], pv[:, :, 0:E],
                                rz.unsqueeze(2).to_broadcast([128, 2, E]), ALU.mult, [psd[bank], smd], [oaccd])
                    else:
                        self.ts("dve", rz, rz, neglam, None, ALU.mult, None, [smd], [smd])
                        self.tt("dve", osq[:, hb * 2:hb * 2 + 2, :], pv[:, :, 0:E],
                                rz.unsqueeze(2).to_broadcast([128, 2, E]), ALU.mult, [psd[bank], smd], [osqd])
                        self.tt("pool", oacc[:, hb * 2:hb * 2 + 2, :], oacc[:, hb * 2:hb * 2 + 2, :],
                                osq[:, hb * 2:hb * 2 + 2, :], ALU.add, [osqd, oaccd], [oaccd])
                if c == 1:
                    self.tt("dve", osq[:], oacc[:], oacc[:], ALU.mult, [oaccd], [osqd])
                    self.S.op("dve", lambda e: e.tensor_reduce(out=sm_[:, 4:8], in_=osq[:], axis=AX.X, op=ALU.add),
                              [osqd], [smd])
                    self.act(sm_[:, 4:8], sm_[:, 4:8], AF.Sqrt, [smd, self.wsd], [smd], scale=1.0 / 128,
                             bias=self.wsc("eps5"))
                    self.S.op("dve", lambda e: e.reciprocal(out=sm_[:, 4:8], in_=sm_[:, 4:8]), [smd], [smd])
                    self.tt("dve", osq[:], oacc[:], sm_[:, 4:8].unsqueeze(2).to_broadcast([128, 4, 128]), ALU.mult,
                            [oaccd, smd], [osqd])
                    self.stt(osb[:], osq[:], 1.0 - lam_init,
                             self.wsc("a_subln", 0, 128).unsqueeze(1).to_broadcast([128, 4, 128]),
                             ALU.mult, ALU.mult, [osqd, self.wsd], [osbd])
            else:
                bank = 2 + c
                pv = ps[:, bank, 0:4 * W].rearrange("p (a w) -> p a w", a=4)
                rz = sm_[:, c * 4:c * 4 + 4]
                self.S.op("dve", lambda e: e.reciprocal(out=rz.unsqueeze(2), in_=pv[:, :, E:E + 1]), [psd[bank]], [smd])
                self.tt("dve", osb[:, :, c * 64:(c + 1) * 64], pv[:, :, 0:E],
                        rz.unsqueeze(2).to_broadcast([128, 4, E]), ALU.mult, [psd[bank], smd], [osbd])
            if c == 1:
                pst = ps[:, 7, :].bitcast(BF16)
                for qs in range(4):
                    self.tr(pst[:, qs * 128:(qs + 1) * 128], osb[:, qs, :], [osbd, self.cbd], [psd[7]])
                self.cp("act", OT[:, hc, qc * 512:(qc + 1) * 512], pst[:, 0:512], [psd[7]], [OTd])

        load_w(0)
        for u in proj_units(0):
            u()
        for hc in range(8):
            units = []
            if hc + 1 < 8:
                load_w(hc + 1)
                units = proj_units(hc + 1)
            ui = 0
            for qc in range(4):
                for c in range(2):
                    attn_block(hc, qc, c)
                    n_take = (len(units) - ui + (7 - (qc * 2 + c))) // (8 - (qc * 2 + c))
                    for _ in range(n_take):
                        units[ui]()
                        ui += 1
        wo = al("at_wo", [128, KC, 512], BF16)
        wod = Dep()
        wov = self.wview(f"{pre}_wo")
        for nh in range(2):
            self.load(wo[:], wov[:, :, nh * 512:(nh + 1) * 512].rearrange("k p n -> p k n"), [self.wbf_dep], [wod])
            for t in range(NT):
                bank = 4 + (t % 2)
                for kc in range(KC):
                    self.mm(ps[:, bank, :], OT[:, kc, t * 128:(t + 1) * 128], wo[:, kc, :], kc == 0, kc == KC - 1,
                            [OTd, wod], [psd[bank]])
                self.tt("dve", self.x[:, t, nh * 512:(nh + 1) * 512], self.x[:, t, nh * 512:(nh + 1) * 512],
                        ps[:, bank, :], ALU.add, [psd[bank], self.xd[t]], [self.xd[t]])


Prog._attn_full = _attn_full
Prog.mixer0 = _attn_full
Prog.mixer1 = _attn_full


def _attn_nbr(self, L):
    nc, S = self.nc, self.S
    S.barrier()
    with contextlib.ExitStack() as st0:
        with contextlib.ExitStack() as st:
            self.rmsnorm_T(f"n{L}_attn", st)
        S.barrier()
        st = st0
        al = lambda n, sh, dt: st.enter_context(self.sbt(n, sh, dt))
        wq = al("nb_wq", [128, 3, 1024], BF16)
        qT = al("nb_qT", [128, 2, S_LEN], BF16)
        kT = al("nb_kT", [128, 2, S_LEN], BF16)
        V = al("nb_V", [128, 2, NT, 2, 66], BF16)
        OT = al("nb_OT", [128, KC, S_LEN], BF16)
        Tb = al("nb_T", [128, 2, 2, 14, 64], F32)
        msk = al("nb_msk", [128, 64], F32)
        sb = al("nb_sb", [128, 2, 5, 64], F32)
        PT = al("nb_PT", [128, 2, 5, 64], BF16)
        sq = al("nb_sq", [128, 512], BF16)
        t1 = al("nb_t1", [128, 512], F32)
        osb = al("nb_osb", [128, 2, 128], BF16)
        sm_ = al("nb_sm", [128, 8], F32)
        wo = al("nb_wo", [128, KC, 512], BF16)
        wqd = [Dep(), Dep(), Dep()]
        qTd, kTd, Vd, Td = [Dep(), Dep()], [Dep(), Dep()], [Dep(), Dep()], [Dep(), Dep()]
        sbd, PTd, osbd = [Dep(), Dep()], [Dep(), Dep()], [Dep(), Dep()]
        OTd, mskd, sqd, t1d, smd, wod = (Dep() for _ in range(6))
        allh = list(self.hTd)
        psd, ps = self.psd, self.ps
        cbo = self.cbo
        blockmean = self.cb[:, cbo["blockmean"]:cbo["blockmean"] + 128]
        wqkv = self.wview("c_wqkv")
        o_b, shp = self.wo["c_bias_f32"]
        bias_d = self.wbig[o_b:o_b + 8 * 128 * 1792].rearrange("(a p n) -> a p n", p=128, n=1792)
        self.load(msk[:], self.cf_d[:, self.cfo["nb_mask"]:self.cfo["nb_mask"] + 64], (), [mskd])
        for b in range(2):
            self.memset("pool", V[:, b, :, :, 64:65], 1.0, [Vd[b]])
        self.ts("dve", sm_[:, 0:1], self.wsc("c_qg"), 0.125, None, ALU.mult, None, [self.wsd], [smd])

        def load_w(hc):
            b = hc % 2
            for i, ci in enumerate((hc, 8 + hc, 16 + hc)):
                self.load(wq[:, i, :], wqkv[ci].rearrange("p k c -> p (k c)"), [self.wbf_dep], [wqd[i]])
            self.load(Tb[:, b].rearrange("p c a q -> p (c a q)"), bias_d[hc], (), [Td[b]])
            self.tt("pool", Tb[:, b].rearrange("p c a q -> p (c a) q"), Tb[:, b].rearrange("p c a q -> p (c a) q"),
                    msk[:].unsqueeze(1).to_broadcast([128, 28, 64]), ALU.add, [Td[b], mskd], [Td[b]])

        def proj_qk(hc, which, tc):
            b = hc % 2
            dst, dstd = (qT, qTd) if which == 0 else (kT, kTd)
            gain = sm_[:, 0:1] if which == 0 else self.wsc("c_kg")
            c0 = tc * 512
            for kc in range(KC):
                self.mm(ps[:, 4, :], wq[:, which, kc * 128:(kc + 1) * 128], self.hT[:, kc, c0:c0 + 512],
                        kc == 0, kc == KC - 1, [wqd[which]] + allh[tc * 4:tc * 4 + 4], [psd[4]])
            self.act(sq[:], ps[:, 4, :], AF.Square, [psd[4]], [sqd])
            self.mm(ps[:, 5, :], blockmean, sq[:], True, True, [sqd, self.cbd], [psd[5]])
            self.act(t1[:], ps[:, 5, :], AF.Sqrt, [psd[5], self.wsd], [t1d], bias=self.wsc("eps6"))
            self.S.op("dve", lambda e: e.reciprocal(out=t1[:], in_=t1[:]), [t1d], [t1d])
            self.stt(dst[:, b, c0:c0 + 512], ps[:, 4, :], gain, t1[:], ALU.mult, ALU.mult,
                     [psd[4], t1d, smd, self.wsd], [dstd[b]])

        def proj_v(hc, tg):
            b = hc % 2
            for i in range(4):
                t = tg * 4 + i
                for kc in range(KC):
                    self.mm(ps[:, 7, i * 128:(i + 1) * 128], self.hT[:, kc, t * 128:(t + 1) * 128],
                            wq[:, 2, kc * 128:(kc + 1) * 128], kc == 0 and i == 0, kc == KC - 1,
                            [wqd[2], allh[t]], [psd[7]])
            self.cp("act", V[:, b, tg * 4:tg * 4 + 4, :, 0:64],
                    ps[:, 7, :].rearrange("p (a c e) -> p a c e", a=4, c=2), [psd[7]], [Vd[b]])

        def proj_units(hc):
            units = []
            for tc in range(4):
                units.append((lambda tc=tc: proj_qk(hc, 0, tc)))
                units.append((lambda tc=tc: proj_qk(hc, 1, tc)))
                units.append((lambda tc=tc: proj_v(hc, tc)))
            return units

        def row_block(hc, r):
            b = hc % 2
            rs = min(max(r - 4, 0), 24)
            a0, a1 = rs // 2, (rs + 7) // 2
            ns = a1 - a0 + 1
            ph = (r % 2) * 64
            tb = (r // 2) % 2
            for c in range(2):
                pr = slice(c * 64, (c + 1) * 64)
                sbank = c
                for si in range(ns):
                    a = a0 + si
                    self.mm(ps[:, sbank, si * 64:(si + 1) * 64], kT[pr, b, a * 128:(a + 1) * 128],
                            qT[pr, b, r * 64:(r + 1) * 64], si == 0, True, [kTd[b], qTd[b]], [psd[sbank]])
                base0 = 2 * a0 - r + 7
                self.tt("dve", sb[:, c, 0:ns, :], ps[:, sbank, 0:ns * 64].rearrange("p (s q) -> p s q", s=ns),
                        Tb[:, b, c, base0:base0 + 2 * ns - 1:2, :], ALU.add, [psd[sbank], Td[b]], [sbd[c]])
                if rs % 2 == 0:
                    self.act(PT[:, c, 0:ns, :], sb[:, c, 0:ns, :], AF.Exp, [sbd[c]], [PTd[c]])
                else:
                    self.act(PT[:, c, 0:1, :], sb[:, c, 0:1, :], AF.Exp, [sbd[c], self.wsd], [PTd[c]],
                             bias=self.wsc("hoff", 0, 1))
                    self.act(PT[:, c, 1:ns - 1, :], sb[:, c, 1:ns - 1, :], AF.Exp, [sbd[c]], [PTd[c]])
                    self.act(PT[:, c, ns - 1:ns, :], sb[:, c, ns - 1:ns, :], AF.Exp, [sbd[c], self.wsd], [PTd[c]],
                             bias=self.wsc("hoff", 1, 1))
                for si in range(ns):
                    a = a0 + si
                    self.mm(ps[ph:ph + 64, 2, c * 65:(c + 1) * 65], PT[:, c, si, :], V[:, b, a, c, 0:65],
                            si == 0, si == ns - 1, [PTd[c], Vd[b]], [psd[2]])
            pv = ps[ph:ph + 64, 2, 0:130].rearrange("p (c w) -> p c w", c=2)
            rz = sm_[ph:ph + 64, 2:4]
            self.S.op("dve", lambda e: e.reciprocal(out=rz.unsqueeze(2), in_=pv[:, :, 64:65]), [psd[2]], [smd])
            self.tt("dve", osb[ph:ph + 64, tb, :].rearrange("p (c e) -> p c e", c=2), pv[:, :, 0:64],
                    rz.unsqueeze(2).to_broadcast([64, 2, 64]), ALU.mult, [psd[2], smd], [osbd[tb]])
            if r % 2 == 1:
                t = r // 2
                pst = ps[:, 3, :].bitcast(BF16)
                self.tr(pst[:, 0:128], osb[:, tb, :], [osbd[tb], self.cbd], [psd[3]])
                self.cp("act", OT[:, hc, t * 128:(t + 1) * 128], pst[:, 0:128], [psd[3]], [OTd])

        load_w(0)
        for u in proj_units(0):
            u()
        for hc in range(8):
            units = []
            if hc + 1 < 8:
                load_w(hc + 1)
                units = proj_units(hc + 1)
            ui = 0
            for r in range(32):
                row_block(hc, r)
                n_take = (len(units) - ui + (31 - r)) // (32 - r)
                for _ in range(n_take):
                    units[ui]()
                    ui += 1
        wov = self.wview("c_wo")
        for nh in range(2):
            self.load(wo[:], wov[:, :, nh * 512:(nh + 1) * 512].rearrange("k p n -> p k n"), [self.wbf_dep], [wod])
            for t in range(NT):
                bank = 4 + (t % 2)
                for kc in range(KC):
                    self.mm(ps[:, bank, :], OT[:, kc, t * 128:(t + 1) * 128], wo[:, kc, :], kc == 0, kc == KC - 1,
                            [OTd, wod], [psd[bank]])
                self.tt("dve", self.x[:, t, nh * 512:(nh + 1) * 512], self.x[:, t, nh * 512:(nh + 1) * 512],
                        ps[:, bank, :], ALU.add, [psd[bank], self.xd[t]], [self.xd[t]])


Prog.mixer2 = _attn_nbr


def _rwkv(self, L):
    nc, S = self.nc, self.S
    ps, psd = self.ps, self.psd
    cbo, cfo = self.cbo, self.cfo
    bank_i = [0]

    def nb():
        b = bank_i[0] % 8
        bank_i[0] += 1
        return b

    def nb2():
        s0 = (bank_i[0] + 1) // 2 * 2
        bank_i[0] = s0 + 2
        return s0 % 8

    def ps2(b):
        return ps[:, b:b + 2, :].rearrange("p a b -> p (a b)")

    def pd2(b):
        return [psd[b], psd[b + 1]]

    def reduce_x(out, in_, r, w):
        self.S.op("dve", lambda e: e.tensor_reduce(out=out, in_=in_, axis=AX.X, op=ALU.add), r, w)

    def recip(ap, r, w):
        self.S.op("dve", lambda e: e.reciprocal(out=ap, in_=ap), r, w)

    S.barrier()
    with contextlib.ExitStack() as st_outer:
        alo = lambda n, sh, dt: st_outer.enter_context(self.sbt(n, sh, dt))
        lT = alo("rw_lT", [128, 3, S_LEN], BF16)
        lTd = Dep()
        allh = list(self.hTd)
        xd = self.xd
        with contextlib.ExitStack() as st:
            self.rmsnorm_T(f"n{L}_attn", st)
        for t in range(NT):
            self.load(self.xsp[t], self.x[:, t, :], [xd[t]], [self.xspd[t]], q="pool")
        S.barrier()
        with contextlib.ExitStack() as st:
            al = lambda n, sh, dt: st.enter_context(self.sbt(n, sh, dt))
            tmp = al("rwa_tmp", [128, S_LEN], F32)
            W = al("rwa_W", [128, KC, 1024], BF16)
            stage = al("rwa_stage", [128, 2, 1024], BF16)
            lw = al("rwa_lw", [128, KC, 128], BF16)
            coef = al("rwa_coef", [128, 2, 48], F32)
            tmpd, Wd, lwd, coefd, shd, mixd = (Dep() for _ in range(6))
            staged = [Dep(), Dep()]
            xb = self.x[:].rearrange("p a b -> p (a b)").bitcast(BF16)
            sh = xb[:, 0:KC * S_LEN].rearrange("p (k t) -> p k t", k=KC)
            mix = xb[:, KC * S_LEN:2 * KC * S_LEN].rearrange("p (k t) -> p k t", k=KC)
            mu = self.wsc("d_mu", 0, 48)
            self.ts("dve", coef[:, 0, :], mu, -1.0, 1.0, ALU.mult, ALU.add, [self.wsd], [coefd])
            self.ts("dve", coef[:, 1, :], mu, 0.5, None, ALU.mult, None, [self.wsd], [coefd])
            for kc in range(KC):
                self.tt("dve", sh[:, kc, 1:S_LEN - 1], self.hT[:, kc, 0:S_LEN - 2],
                        self.hT[:, kc, 2:S_LEN], ALU.add, allh, [shd] + xd)
            self.cp("dve", sh[:, :, 0:1], self.hT[:, :, 1:2], allh, [shd] + xd)
            self.cp("dve", sh[:, :, S_LEN - 1:S_LEN], self.hT[:, :, S_LEN - 2:S_LEN - 1], allh, [shd] + xd)

            def make_mix(j):
                for kc in range(KC):
                    self.act(tmp[:], self.hT[:, kc, :], AF.Identity, allh + [coefd], [tmpd],
                             scale=coef[:, 0, j * 8 + kc:j * 8 + kc + 1])
                    self.stt(mix[:, kc, :], sh[:, kc, :], coef[:, 1, j * 8 + kc:j * 8 + kc + 1], tmp[:],
                             ALU.mult, ALU.add, [shd, tmpd, coefd], [mixd] + xd)

            def proj_tm(wname, idx):
                self.load(W[:], self.wview(wname).rearrange("k p n -> p k n"), [self.wbf_dep], [Wd])
                for t in range(NT):
                    b2 = nb2()
                    for nh in range(2):
                        for kc in range(KC):
                            self.mm(ps[:, b2 + nh, :], mix[:, kc, t * 128:(t + 1) * 128],
                                    W[:, kc, nh * 512:(nh + 1) * 512], kc == 0, kc == KC - 1, [mixd, Wd],
                                    [psd[b2 + nh]])
                    sb_ = t % 2
                    self.cp("act", stage[:, sb_, :], ps2(b2), pd2(b2), [staged[sb_]])
                    self.load(self.rkvsp[idx, t], stage[:, sb_, :], [staged[sb_]], [self.rkvd[idx][t]], q="pool")

            def proj_lora(wname, li, func):
                self.load(lw[:], self.wview(wname).rearrange("k p n -> p k n"), [self.wbf_dep], [lwd])
                for tc in range(4):
                    b = nb()
                    for kc in range(KC):
                        self.mm(ps[:, b, :], lw[:, kc, :], mix[:, kc, tc * 512:(tc + 1) * 512], kc == 0,
                                kc == KC - 1, [mixd, lwd], [psd[b]])
                    self.act(lT[:, li, tc * 512:(tc + 1) * 512], ps[:, b, :], func, [psd[b]], [lTd])

            make_mix(0)
            proj_tm("d_w_r", 0)
            make_mix(1)
            proj_lora("d_w1s", 0, AF.Tanh)
            make_mix(2)
            proj_tm("d_w_k", 1)
            make_mix(3)
            proj_tm("d_w_v", 2)
            make_mix(4)
            proj_lora("d_a1s", 1, AF.Identity)
            make_mix(5)
            proj_lora("d_g1", 2, AF.Sigmoid)
        S.barrier()
        import os
        stop = os.environ.get("RW_STOP", "")
        if stop:
            nbt = int(os.environ.get("RW_NT", "16"))
        if stop == "A":
            for t in range(NT):
                self.load(self.x[:, t, :], self.xsp[t], [self.xspd[t]], [xd[t]])
            return
        hTf = self.hT[:].rearrange("p k t -> p (k t)")
        with contextlib.ExitStack() as st:
            al = lambda n, sh_, dt: st.enter_context(self.sbt(n, sh_, dt))
            tri = al("rwb_tri", [128, 4, 128], F32)
            onesf = al("rwb_onesf", [128, 128], F32)
            brow = al("rwb_brow", [128, 2048], F32)
            kkb = al("rwb_kkb", [128, 2, 1024], F32)
            w2a2 = al("rwb_w2a2", [128, 2, 1024], BF16)
            rkv = al("rwb_rkv", [128, 3, 1024], BF16)
            F = al("rwb_F", [128, 6, 1024], F32)
            Bt = al("rwb_Bt", [128, 4, 1024], BF16)
            ARc = al("rwb_ARc", [128, 8, 2, 128], BF16)
            Bc = al("rwb_Bc", [128, 8, 128], BF16)
            Kc = al("rwb_Kc", [128, 8, 128], BF16)
            IB = al("rwb_IB", [128, 6, 4, 128], F32)
            lvlm = al("rwb_lvlm", [128, 5, 128], F32)
            identf = al("rwb_identf", [128, 128], F32)
            RU = al("rwb_RU", [128, 2, 1024], BF16)
            S32 = al("rwb_S32", [128, 8, 64], F32)
            Sb = al("rwb_Sb", [128, 8, 64], BF16)
            WL = al("rwb_WL", [128, 8], F32)
            small = al("rwb_small", [128, 16], F32)
            cstd, rkvsd, Fd, Btd, cmd, NAd, NTd, RUd, S32d, Sbd, WLd, smd = (
                Dep(), [Dep(), Dep(), Dep()], [Dep() for _ in range(6)], [Dep() for _ in range(4)],
                Dep(), Dep(), Dep(), [Dep(), Dep()], Dep(), Dep(), Dep(), Dep())
            NA3 = hTf[:, 0:6144].rearrange("p (h a t) -> p h a t", h=16, a=3)
            Xb = hTf[:, 6144:8192].rearrange("p (h t) -> p h t", h=16)
            Nf = hTf[:, 8192:12288].bitcast(F32).rearrange("p (h t) -> p h t", h=16)
            NTf = hTf[:, 12288:16384].bitcast(F32).rearrange("p (h t) -> p h t", h=16)
            Nfd, NTfd, Xbd = Dep(), Dep(), Dep()
            ibd = [Dep() for _ in range(6)]
            ibdB = [Dep() for _ in range(6)]
            masks = self.cb[:, cbo["masks"]:cbo["masks"] + 512].rearrange("p (a t) -> p a t", a=4)
            identb = self.ident
            self.load(tri[:].rearrange("p a t -> p (a t)"), self.cf_d[:, cfo["tri"]:cfo["tri"] + 512], (), [cstd])
            self.load(onesf[:], self.cf_d[:, cfo["onesf"]:cfo["onesf"] + 128], (), [cstd])
            self.load(identf[:], self.cf_d[:, cfo["identf"]:cfo["identf"] + 128], (), [cstd])
            self.load(lvlm[:].rearrange("p a t -> p (a t)"), self.cf_d[:, cfo["lvlmask"]:cfo["lvlmask"] + 640], (), [cstd])
            self.load(brow[:], self.cf_d[:, cfo["brow"]:cfo["brow"] + 2048], (), [cstd])
            self.load(kkb[:, 0, :], self.cf_d[:, cfo["d_k_k"]:cfo["d_k_k"] + 1024], (), [cstd])
            self.load(kkb[:, 1, :], self.cf_d[:, cfo["d_k_a"]:cfo["d_k_a"] + 1024], (), [cstd])
            self.load(w2a2[:, 0, :], self.wview("d_w2s"), [self.wbf_dep], [cstd])
            self.load(w2a2[:, 1, :], self.wview("d_a2s"), [self.wbf_dep], [cstd])
            negc2 = self.wsc("negc", 0, 2)

            def lora_sig(d, which, n, dstF, dstd):
                prd = slice(d * 64, (d + 1) * 64)
                rowp = d * 64
                b2 = nb2()
                for nh in range(2):
                    self.mm(ps[:, b2 + nh, :], lT[prd, which, n * 128:(n + 1) * 128],
                            w2a2[prd, which, nh * 512:(nh + 1) * 512], True, False, [lTd, cstd], [psd[b2 + nh]])
                    c0_ = which * 1024 + nh * 512
                    self.mm(ps[:, b2 + nh, :], onesf[rowp:rowp + 1, :], brow[rowp:rowp + 1, c0_:c0_ + 512],
                            False, True, [cstd], [psd[b2 + nh]])
                self.act(dstF, ps2(b2), AF.Sigmoid, pd2(b2), [dstd])

            def scan_tile(d, n):
                for i in range(3):
                    self.load(rkv[:, i, :], self.rkvsp[i, n], [self.rkvd[i][n]], [rkvsd[i]])
                F0, F1, F2, F3, F4, F5 = (F[:, i, :] for i in range(6))
                v3 = lambda ap: ap.rearrange("p (h j) -> p h j", h=16)
                lora_sig(d, 0, n, F0, Fd[0])
                lora_sig(d, 1, n, F1, Fd[1])
                cut = int(os.environ.get("RW_CUT", "99"))
                if cut <= 1:
                    return
                self.tt("dve", F5, rkv[:, 1, :], kkb[:, 0, :], ALU.mult, [rkvsd[1], cstd], [Fd[5]])
                self.tt("pool", F2, F5, F5, ALU.mult, [Fd[5]], [Fd[2]])
                reduce_x(small[:, 0:16], v3(F2), [Fd[2]], [smd])
                self.act(small[:, 0:16], small[:, 0:16], AF.Sqrt, [smd], [smd])
                self.ts("dve", small[:, 0:16], small[:, 0:16], 1e-12, None, ALU.max, None, [smd], [smd])
                recip(small[:, 0:16], [smd], [smd])
                self.tt("dve", v3(F5), v3(F5), small[:, 0:16].unsqueeze(2).to_broadcast([128, 16, 64]), ALU.mult,
                        [Fd[5], smd], [Fd[5]])
                if cut <= 2:
                    return
                bi, be = nb2(), nb2()
                for nh in range(2):
                    self.mm(ps[:, bi + nh, :], tri[:, 2 * d, :], F0[:, nh * 512:(nh + 1) * 512], True, True,
                            [cstd, Fd[0]], [psd[bi + nh]])
                    self.mm(ps[:, be + nh, :], tri[:, 2 * d + 1, :], F0[:, nh * 512:(nh + 1) * 512], True, True,
                            [cstd, Fd[0]], [psd[be + nh]])
                self.act(F2, ps2(bi), AF.Exp, pd2(bi), [Fd[2]])
                self.act(F3, ps2(bi), AF.Exp, pd2(bi), [Fd[3]], scale=-1.0)
                self.act(F4, ps2(be), AF.Exp, pd2(be), [Fd[4]])
                bw = nb()
                for hp in range(8):
                    self.mm(ps[:, bw, hp * 2:hp * 2 + 2], F0[:, hp * 128:(hp + 1) * 128], negc2, hp == 0, True,
                            [Fd[0], self.wsd], [psd[bw]])
                self.act(WL[:], ps[:, bw, 0:16:2], AF.Exp, [psd[bw]], [WLd])
                if cut <= 3:
                    return
                self.stt(Bt[:, 0, :], F5, -1.0, F4, ALU.mult, ALU.mult, [Fd[5], Fd[4]], [Btd[0]])
                self.tt("pool", F4, F5, F1, ALU.mult, [Fd[5], Fd[1]], [Fd[4]])
                self.tt("dve", Bt[:, 1, :], F4, F3, ALU.mult, [Fd[4], Fd[3]], [Btd[1]])
                self.stt(F4, F1, -1.0, kkb[:, 1, :], ALU.add, ALU.mult, [Fd[1], cstd], [Fd[4]])
                self.stt(F5, F4, 1.0, rkv[:, 1, :], ALU.add, ALU.mult, [Fd[4], rkvsd[1]], [Fd[5]])
                self.tt("dve", Bt[:, 2, :], F5, F3, ALU.mult, [Fd[5], Fd[3]], [Btd[2]])
                self.tt("dve", Bt[:, 3, :], rkv[:, 0, :], F2, ALU.mult, [rkvsd[0], Fd[2]], [Btd[3]])
                if cut <= 4:
                    return
                for si, dst, eng in ((0, ARc[:, :, 0, :], "act"), (3, ARc[:, :, 1, :], "dve"), (1, Bc[:], "act"),
                                     (2, Kc[:], "dve")):
                    b = nb()
                    pst = ps[:, b, :].bitcast(BF16)
                    for hp in range(8):
                        self.tr(pst[:, hp * 128:(hp + 1) * 128], Bt[:, si, hp * 128:(hp + 1) * 128],
                                [Btd[si], self.cbd], [psd[b]])
                    self.cp(eng, dst, pst.rearrange("p (h t) -> p h t", h=8), [psd[b]], [cmd])
                if os.environ.get("RW_DUMP"):
                    dd = [Dep() for _ in range(NT)]
                    for i in range(6):
                        self.cp("act", self.x[:, i, :], F[:, i, :], [Fd[i]], [xd[i]])
                    for i in range(4):
                        self.cp("act", self.x[:, 6 + i, :], Bt[:, i, :], [Btd[i]], [xd[6 + i]])
                    for i in range(3):
                        self.cp("act", self.x[:, 10 + i, :], rkv[:, i, :], [rkvsd[i]], [xd[10 + i]])
                    return
                if cut <= 5:
                    return
                m12 = masks[:, 0:2, :] if d == 0 else masks[:, 2:4, :]
                m3 = masks[:, 2, :] if d == 0 else masks[:, 0, :]
                for h in range(16):
                    hp, pr = h // 2, slice((h % 2) * 64, (h % 2) * 64 + 64)
                    b = nb()
                    arv = ARc[pr, hp, :, :].rearrange("p a t -> p (a t)")
                    self.mm(ps[:, b, 0:256], Bc[pr, hp, :], arv, True, True, [cmd], [psd[b]])
                    self.mm(ps[:, b, 256:512], Kc[pr, hp, :], arv, False, True, [cmd], [psd[b]])
                    self.tt("dve", Nf[:, h, :], ps[:, b, 0:128], m12[:, 0, :], ALU.mult, [psd[b], self.cbd],
                            [Nfd] + allh)
                    self.tt("dve", NA3[:, h, 0, :], ps[:, b, 128:256], m12[:, 1, :], ALU.mult, [psd[b], self.cbd],
                            [NAd] + allh)
                    self.tt("dve", NA3[:, h, 1:3, :], ps[:, b, 256:512].rearrange("p (a t) -> p a t", a=2),
                            m12, ALU.mult, [psd[b], self.cbd], [NAd] + allh)
                for g8 in range(2):
                    bb = nb2()
                    for par in range(2):
                        pr = slice(par * 64, par * 64 + 64)
                        for i4 in range(4):
                            hp = g8 * 4 + i4
                            self.mm(ps[:, bb + par, i4 * 128:(i4 + 1) * 128], ARc[pr, hp, 0, :], Bc[pr, hp, :],
                                    i4 == 0, True, [cmd], [psd[bb + par]])
                        self.tt("dve", NTf[:, g8 * 8 + par:g8 * 8 + 8:2, :],
                                ps[:, bb + par, :].rearrange("p (h t) -> p h t", h=4),
                                m3.unsqueeze(1).to_broadcast([128, 4, 128]), ALU.mult, [psd[bb + par], self.cbd],
                                [NTfd] + allh)
                if cut <= 6:
                    return
                bc4 = lambda ap: ap.unsqueeze(1).to_broadcast([128, 4, 128])
                p4 = lambda b_: ps[:, b_, :].rearrange("p (h t) -> p h t", h=4)
                Fv = F[:, 0:3, :].rearrange("p a n -> p (a n)").rearrange("p (i h t) -> p i h t", i=6, h=4)
                bufsets = [([IB[:, i] for i in range(6)], ibd, [], []),
                           ([Fv[:, i] for i in range(6)], ibdB, list(Fd[0:3]), list(Fd[0:3]))]

                def inv_group(g4, bset):
                    bufs_, deps_, xr, xw = bset
                    hs = slice(g4 * 4, g4 * 4 + 4)
                    P1, PT1, Q, P2, PT2, Y1 = bufs_
                    dP1, dPT1, dQ, dP2, dPT2, dY1 = deps_

                    def mm4(bank, lhs, rhs, r):
                        for hh in range(4):
                            self.mm(ps[:, bank, hh * 128:(hh + 1) * 128], lhs[:, hh, :], rhs[:, hh, :], hh == 0, True,
                                    r + xr, [psd[bank]])
                    self.tt("dve", P1, Nf[:, hs, :], bc4(lvlm[:, 0, :]), ALU.mult, [Nfd, cstd] + xr, [dP1] + xw)
                    self.tt("pool", PT1, NTf[:, hs, :], bc4(lvlm[:, 0, :]), ALU.mult, [NTfd, cstd] + xr, [dPT1] + xw)
                    self.tt("pool", Q, P1, bc4(identf[:]), ALU.add, [dP1, cstd] + xr, [dQ] + xw)
                    yield
                    b1, b2_ = nb(), nb()
                    mm4(b1, PT1, P1, [dP1, dPT1])
                    mm4(b2_, P1, PT1, [dP1, dPT1])
                    yield
                    self.cp("act", P2, p4(b1), [psd[b1]] + xr, [dP2])
                    self.cp("act", PT2, p4(b2_), [psd[b2_]] + xr, [dPT2])
                    yield
                    b3, b4 = nb(), nb()
                    mm4(b3, PT2, Q, [dPT2, dQ])
                    mm4(b4, P2, PT2, [dP2, dPT2])
                    yield
                    self.tt("dve", Q, Q, p4(b3), ALU.add, [psd[b3], dQ] + xr, [dQ])
                    self.cp("act", PT1, p4(b4), [psd[b4]] + xr, [dPT1])
                    yield
                    b5 = nb()
                    mm4(b5, PT1, Q, [dPT1, dQ])
                    yield
                    self.tt("dve", Q, Q, p4(b5), ALU.add, [psd[b5], dQ] + xr, [dQ])
                    yield
                    b6 = nb()
                    for hh in range(4):
                        self.tr(ps[:, b6, hh * 128:(hh + 1) * 128], Q[:, hh, :], [dQ, cstd] + xr, [psd[b6]],
                                ident=identf[:])
                    Xt, dXt = P2, dP2
                    X_, dX = Q, dQ
                    NTm, dNTm = PT2, dPT2
                    self.tt("pool", NTm, NTf[:, hs, :], bc4(lvlm[:, 1, :]), ALU.mult, [NTfd, cstd] + xr, [dNTm])
                    yield
                    self.cp("act", Xt, p4(b6), [psd[b6]] + xr, [dXt])
                    for li in range(1, 5):
                        lastl = li == 4
                        by = nb()
                        mm4(by, NTm, X_, [dNTm, dX])
                        yield
                        self.cp("act", Y1, p4(by), [psd[by]] + xr, [dY1])
                        if not lastl:
                            self.tt("pool", NTm, NTf[:, hs, :], bc4(lvlm[:, li + 1, :]), ALU.mult,
                                    [NTfd, cstd] + xr, [dNTm])
                        yield
                        bx = nb()
                        mm4(bx, Xt, Y1, [dXt, dY1])
                        if not lastl:
                            bxt = nb()
                            mm4(bxt, Y1, Xt, [dY1, dXt])
                        yield
                        if not lastl:
                            self.tt("dve", X_, X_, p4(bx), ALU.add, [psd[bx], dX] + xr, [dX])
                            self.tt("dve", Xt, Xt, p4(bxt), ALU.add, [psd[bxt], dXt] + xr, [dXt])
                            yield
                        else:
                            self.tt("dve", Xb[:, hs, :], X_, p4(bx), ALU.add, [psd[bx], dX] + xr, [Xbd] + allh)

                for rnd in range(2):
                    gens = [inv_group(2 * rnd, bufsets[0]), inv_group(2 * rnd + 1, bufsets[1])]
                    alive = list(gens)
                    while alive:
                        nxt = []
                        for g_ in alive:
                            try:
                                next(g_)
                                nxt.append(g_)
                            except StopIteration:
                                pass
                        alive = nxt
                X, Xd = Xb, Xbd
                if cut <= 7:
                    return
                Vt = rkv[:, 2, :]
                bR = nb2()
                for h in range(16):
                    hp, pr = h // 2, slice((h % 2) * 64, (h % 2) * 64 + 64)
                    o = ps[:, bR + h % 2, hp * 64:hp * 64 + 64]
                    self.mm(o, ARc[pr, hp, 0, :], Sb[pr, hp, :], True, False, [cmd, Sbd], [psd[bR + h % 2]])
                    self.mm(o, NA3[:, h, 1, :], Vt[:, h * 64:(h + 1) * 64], False, True, [NAd, rkvsd[2]],
                            [psd[bR + h % 2]])
                self.cp("act", RU[:, 0, :].rearrange("p (h a i) -> p h a i", h=8, a=2),
                        ps2(bR).rearrange("p (a h i) -> p h a i", a=2, h=8), pd2(bR), [RUd[0]])
                bU = nb2()
                for h in range(16):
                    o = ps[:, bU + h // 8, (h % 8) * 64:(h % 8) * 64 + 64]
                    self.mm(o, X[:, h, :], RU[:, 0, h * 64:(h + 1) * 64], True, True, [Xd, RUd[0]], [psd[bU + h // 8]])
                self.cp("dve", RU[:, 1, :], ps2(bU), pd2(bU), [RUd[1]])
                bY = nb2()
                for h in range(16):
                    hp, pr = h // 2, slice((h % 2) * 64, (h % 2) * 64 + 64)
                    o = ps[:, bY + h % 2, hp * 64:hp * 64 + 64]
                    self.mm(o, ARc[pr, hp, 1, :], Sb[pr, hp, :], True, False, [cmd, Sbd], [psd[bY + h % 2]])
                    self.mm(o, NA3[:, h, 0, :], RU[:, 1, h * 64:(h + 1) * 64], False, False, [NAd, RUd[1]],
                            [psd[bY + h % 2]])
                    self.mm(o, NA3[:, h, 2, :], Vt[:, h * 64:(h + 1) * 64], False, True, [NAd, rkvsd[2]],
                            [psd[bY + h % 2]])
                yv = self.x[:, n, :].rearrange("p (h a i) -> p h a i", h=8, a=2)
                pyv = ps2(bY).rearrange("p (a h i) -> p h a i", a=2, h=8)
                if d == 0:
                    self.cp("act", yv, pyv, pd2(bY), [xd[n]])
                else:
                    self.tt("dve", yv, yv, pyv, ALU.add, pd2(bY) + [xd[n]], [xd[n]])
                bS = nb()
                for h in range(16):
                    hp, pr = h // 2, slice((h % 2) * 64, (h % 2) * 64 + 64)
                    o = ps[pr, bS, hp * 64:(hp + 1) * 64]
                    self.mm(o, Bt[:, 1, h * 64:(h + 1) * 64], RU[:, 1, h * 64:(h + 1) * 64], True, False,
                            [Btd[1], RUd[1]], [psd[bS]])
                    self.mm(o, Bt[:, 2, h * 64:(h + 1) * 64], Vt[:, h * 64:(h + 1) * 64], False, True,
                            [Btd[2], rkvsd[2]], [psd[bS]])
                self.tt("dve", S32[:], S32[:], ps[:, bS, :].rearrange("p (h i) -> p h i", h=8), ALU.add,
                        [psd[bS], S32d], [S32d])
                self.tt("dve", S32[:], S32[:], WL[:].unsqueeze(2).to_broadcast([128, 8, 64]), ALU.mult,
                        [S32d, WLd], [S32d])
                self.cp("act", Sb[:], S32[:], [S32d], [Sbd])

            for d in range(2):
                self.memset("pool", S32[:], 0.0, [S32d])
                self.memset("pool", Sb[:], 0.0, [Sbd])
                order = list(range(NT)) if d == 0 else list(range(NT - 1, -1, -1))
                if stop:
                    order = order[:nbt]
                for n in order:
                    scan_tile(d, n)
        S.barrier()
        if stop == "BY":
            return
        if stop == "B":
            for t in range(NT):
                self.load(self.x[:, t, :], self.xsp[t], [self.xspd[t]], [xd[t]])
            return
        with contextlib.ExitStack() as st:
            al = lambda n, sh_, dt: st.enter_context(self.sbt(n, sh_, dt))
            onesf = al("rwc_onesf", [128, 128], F32)
            brow = al("rwc_brow", [128, 2048], F32)
            bc4 = al("rwc_bc4", [128, 4, 1024], F32)
            w2a2 = al("rwc_w2a2", [128, 2, 1024], BF16)
            g2 = al("rwc_g2", [128, 1024], BF16)
            Wo = al("rwc_Wo", [128, KC, 1024], BF16)
            rkv = al("rwc_rkv", [128, 3, 1024], BF16)
            xt = al("rwc_xt", [128, 1024], F32)
            F = al("rwc_F", [128, 5, 1024], F32)
            ob = al("rwc_ob", [128, 1024], BF16)
            oT = al("rwc_oT", [128, KC, 128], BF16)
            small = al("rwc_small", [128, 48], F32)
            cstd, xtd, obd, oTd, smd = (Dep() for _ in range(5))
            rkvsd = [Dep(), Dep(), Dep()]
            Fd = [Dep() for _ in range(5)]
            self.load(onesf[:], self.cf_d[:, cfo["onesf"]:cfo["onesf"] + 128], (), [cstd])
            self.load(brow[:], self.cf_d[:, cfo["brow"]:cfo["brow"] + 2048], (), [cstd])
            for i, nm in enumerate(("d_k_a", "d_r_k", "d_ln_g", "d_ln_b")):
                self.load(bc4[:, i, :], self.cf_d[:, cfo[nm]:cfo[nm] + 1024], (), [cstd])
            self.load(w2a2[:, 0, :], self.wview("d_w2s"), [self.wbf_dep], [cstd])
            self.load(w2a2[:, 1, :], self.wview("d_a2s"), [self.wbf_dep], [cstd])
            self.load(g2[:], self.wview("d_g2"), [self.wbf_dep], [cstd])
            self.load(Wo[:], self.wview("d_w_o").rearrange("k p n -> p k n"), [self.wbf_dep], [cstd])
            v3 = lambda ap: ap.rearrange("p (h j) -> p h j", h=16)
            bc16 = lambda ap: ap.unsqueeze(2).to_broadcast([128, 16, 64])
            for n in range(NT):
                for i in range(3):
                    self.load(rkv[:, i, :], self.rkvsp[i, n], [self.rkvd[i][n]], [rkvsd[i]])
                self.load(xt[:], self.xsp[n], [self.xspd[n]], [xtd])
                F0, F1, F2, F3, F4 = (F[:, i, :] for i in range(5))
                for d in range(2):
                    prd = slice(d * 64, (d + 1) * 64)
                    rowp = d * 64
                    b2 = nb2()
                    for nh in range(2):
                        self.mm(ps[:, b2 + nh, :], lT[prd, 1, n * 128:(n + 1) * 128],
                                w2a2[prd, 1, nh * 512:(nh + 1) * 512], True, False, [lTd, cstd], [psd[b2 + nh]])
                        self.mm(ps[:, b2 + nh, :], onesf[rowp:rowp + 1, :],
                                brow[rowp:rowp + 1, 1024 + nh * 512:1024 + (nh + 1) * 512], False, True, [cstd],
                                [psd[b2 + nh]])
                    self.act(F[:, d, :], ps2(b2), AF.Sigmoid, pd2(b2), [Fd[d]])
                bG = nb2()
                for nh in range(2):
                    self.mm(ps[:, bG + nh, :], lT[:, 2, n * 128:(n + 1) * 128], g2[:, nh * 512:(nh + 1) * 512],
                            True, True, [lTd, cstd], [psd[bG + nh]])
                self.cp("act", F2, ps2(bG), pd2(bG), [Fd[2]])
                y = self.x[:, n, :]
                reduce_x(small[:, 0:16], v3(y), [xd[n]], [smd])
                self.ts("dve", small[:, 0:16], small[:, 0:16], 1.0 / 64, None, ALU.mult, None, [smd], [smd])
                self.tt("dve", v3(F3), v3(y), bc16(small[:, 0:16]), ALU.subtract, [xd[n], smd], [Fd[3]])
                self.tt("pool", F4, F3, F3, ALU.mult, [Fd[3]], [Fd[4]])
                reduce_x(small[:, 16:32], v3(F4), [Fd[4]], [smd])
                self.act(small[:, 16:32], small[:, 16:32], AF.Sqrt, [smd, self.wsd], [smd], scale=1.0 / 64,
                         bias=self.wsc("epsgn"))
                recip(small[:, 16:32], [smd], [smd])
                self.tt("dve", v3(F3), v3(F3), bc16(small[:, 16:32]), ALU.mult, [Fd[3], smd], [Fd[3]])
                self.tt("pool", F3, F3, bc4[:, 2, :], ALU.mult, [Fd[3], cstd], [Fd[3]])
                self.tt("pool", F3, F3, bc4[:, 3, :], ALU.add, [Fd[3], cstd], [Fd[3]])
                self.tt("dve", F0, F0, F1, ALU.add, [Fd[0], Fd[1]], [Fd[0]])
                self.ts("dve", F0, F0, 0.5, -1.0, ALU.mult, ALU.add, [Fd[0]], [Fd[0]])
                self.tt("dve", F0, F0, bc4[:, 0, :], ALU.mult, [Fd[0], cstd], [Fd[0]])
                self.stt(F0, F0, 1.0, rkv[:, 1, :], ALU.add, ALU.mult, [Fd[0], rkvsd[1]], [Fd[0]])
                self.tt("dve", F0, F0, rkv[:, 0, :], ALU.mult, [Fd[0], rkvsd[0]], [Fd[0]])
                self.tt("pool", F0, F0, bc4[:, 1, :], ALU.mult, [Fd[0], cstd], [Fd[0]])
                reduce_x(small[:, 32:48], v3(F0), [Fd[0]], [smd])
                self.tt("dve", v3(F4), v3(rkv[:, 2, :]), bc16(small[:, 32:48]), ALU.mult, [rkvsd[2], smd], [Fd[4]])
                self.tt("pool", F3, F3, F4, ALU.add, [Fd[3], Fd[4]], [Fd[3]])
                self.tt("dve", ob[:], F3, F2, ALU.mult, [Fd[3], Fd[2]], [obd])
                b = nb()
                pst = ps[:, b, :].bitcast(BF16)
                for kc in range(KC):
                    self.tr(pst[:, kc * 128:(kc + 1) * 128], ob[:, kc * 128:(kc + 1) * 128], [obd, self.cbd], [psd[b]])
                self.cp("act", oT[:], pst.rearrange("p (k t) -> p k t", k=KC), [psd[b]], [oTd])
                bO = nb2()
                for nh in range(2):
                    for kc in range(KC):
                        self.mm(ps[:, bO + nh, :], oT[:, kc, :], Wo[:, kc, nh * 512:(nh + 1) * 512], kc == 0,
                                kc == KC - 1, [oTd, cstd], [psd[bO + nh]])
                self.tt("dve", self.x[:, n, :], xt[:], ps2(bO), ALU.add, pd2(bO) + [xtd, xd[n]], [xd[n]])
    S.barrier()


Prog.mixer3 = _rwkv
```

```python
import math
import contextlib
import numpy as np
import ml_dtypes
import concourse.bass as bass
import concourse.mybir as mybir
from concourse.bass_utils import run_bass_kernel_spmd

F32 = mybir.dt.float32
BF16 = mybir.dt.bfloat16
AF = mybir.ActivationFunctionType
ALU = mybir.AluOpType
AX = mybir.AxisListType

D = 1024
S_LEN = 2048
NT = 16
KC = 8
FH = 2816
NCH = 22
N_CORES = 8
SEQ_PER_CORE = 5
NORM_EPS = 1e-6


class Dep:
    __slots__ = ("w", "r", "name")

    def __init__(self, name=""):
        self.w = None
        self.r = {}
        self.name = name


class Sch:
    COMPUTE = ("pe", "dve", "act", "pool")

    def __init__(self, nc, n_sp_ring=40, n_pool_ring=8, same_eng_sync=True):
        self.nc = nc
        self.q = {e: [] for e in ("pe", "dve", "act", "pool", "sp")}
        self.sems = {}
        self.cnt = {}
        for e in self.COMPUTE:
            self.sems[e] = nc.alloc_semaphore(f"s_{e}")
            self.cnt[e] = 0
        self.ring = {}
        for qn, n in (("sp", n_sp_ring), ("pool", n_pool_ring)):
            self.ring[qn] = dict(n=n, i=0)
            for i in range(n):
                self.sems[(qn, i)] = nc.alloc_semaphore(f"d_{qn}{i}")
        self.seen = {e: {} for e in self.q}
        self.same_eng_sync = same_eng_sync
        self.nops = 0

    def _collect(self, eng, reads, writes, is_dma):
        need = {}

        def add(k, v, peng):
            if peng == eng and not is_dma and k in self.COMPUTE:
                if eng == "pe" or not self.same_eng_sync:
                    return
            if need.get(k, 0) < v:
                need[k] = v
        for d in reads:
            if d.w is not None:
                add(*d.w)
        for d in writes:
            if d.w is not None:
                add(*d.w)
            for k, (v, peng) in d.r.items():
                add(k, v, peng)
        out = []
        seen = self.seen[eng]
        for k, v in need.items():
            if seen.get(k, 0) < v:
                seen[k] = v
                out.append((k, v))
        return out

    def op(self, eng, fn, reads=(), writes=()):
        waits = self._collect(eng, reads, writes, False)
        self.cnt[eng] += 1
        v = self.cnt[eng]
        for d in reads:
            d.r[eng] = (v, eng)
        for d in writes:
            d.w = (eng, v, eng)
            d.r = {}
        self.q[eng].append((waits, fn, (eng, 1)))
        self.nops += 1

    def dma(self, qn, fn, reads=(), writes=()):
        ring = self.ring[qn]
        i = ring["i"]
        ring["i"] += 1
        n = ring["n"]
        slot, gen = i % n, i // n
        key = (qn, slot)
        waits = self._collect(qn, reads, writes, True)
        if gen > 0:
            seen = self.seen[qn]
            if seen.get(key, 0) < 16 * gen:
                seen[key] = 16 * gen
                waits.append((key, 16 * gen))
        v = 16 * (gen + 1)
        for d in reads:
            d.r[key] = (v, "dma")
        for d in writes:
            d.w = (key, v, "dma")
            d.r = {}
        self.q[qn].append((waits, fn, (key, 16)))
        self.nops += 1

    def _all_marks(self):
        marks = [(e, self.cnt[e]) for e in self.COMPUTE if self.cnt[e] > 0]
        for rq, ring in self.ring.items():
            n = ring["n"]
            for slot in range(n):
                c = (ring["i"] - slot + n - 1) // n
                if c > 0:
                    marks.append(((rq, slot), 16 * c))
        return marks

    def barrier(self):
        marks = self._all_marks()
        for e in self.q:
            waits = []
            seen = self.seen[e]
            for k, v in marks:
                if k == e:
                    continue
                if seen.get(k, 0) < v:
                    seen[k] = v
                    waits.append((k, v))
            if waits:
                self.q[e].append((waits, None, None))

    def final_wait(self, qn="pool"):
        waits = [(k, v) for k, v in self._all_marks() if k != qn]
        self.q[qn].append((waits, None, None))

    def emit(self):
        nc = self.nc
        sems = self.sems

        def replay(eng_obj, lst):
            for waits, fn, inc in lst:
                for k, v in waits:
                    eng_obj.wait_ge(sems[k], v)
                if fn is not None:
                    ins = fn(eng_obj)
                    ins.then_inc(sems[inc[0]], inc[1])

        with nc.Block() as block:
            @block.tensor
            def _(e):
                replay(e, self.q["pe"])

            @block.vector
            def _(e):
                replay(e, self.q["dve"])

            @block.scalar
            def _(e):
                replay(e, self.q["act"])

            @block.gpsimd
            def _(e):
                replay(e, self.q["pool"])

            @block.sync
            def _(e):
                replay(e, self.q["sp"])


def colchunk(w):
    K, N = w.shape
    return np.ascontiguousarray(w.reshape(K // 128, 128, N // 128, 128).transpose(2, 1, 0, 3))


class Packer:
    def __init__(self):
        self.parts = []
        self.off = {}
        self.n = 0

    def add(self, name, arr):
        a = np.ascontiguousarray(arr, dtype=np.float32).reshape(-1)
        self.off[name] = (self.n, arr.shape)
        self.parts.append(a)
        self.n += a.size

    def finish(self, align):
        pad = (-self.n) % align
        if pad:
            self.parts.append(np.zeros(pad, np.float32))
            self.n += pad
        return np.concatenate(self.parts)


class ColPacker:
    def __init__(self):
        self.parts = []
        self.off = {}
        self.n = 0

    def add(self, name, arr):
        a = np.ascontiguousarray(arr, dtype=np.float32).reshape(128, -1)
        self.off[name] = self.n
        self.parts.append(a)
        self.n += a.shape[1]

    def finish(self):
        return np.ascontiguousarray(np.concatenate(self.parts, axis=1))


def chan8(v):
    return np.ascontiguousarray(v.reshape(8, 128).T)


def rep(v):
    return np.ascontiguousarray(np.broadcast_to(v.reshape(1, -1), (128, v.size)))


def build_rope_tables():
    out = np.zeros((4, 128, S_LEN), np.float32)
    t = np.arange(S_LEN, dtype=np.float32)
    half = 32
    inv = (10000.0 ** (-np.arange(half, dtype=np.float32) / half)).astype(np.float32)
    ang = t[None, :] * inv[:, None]
    c0 = np.cos(ang).astype(np.float32)
    s0 = np.sin(ang).astype(np.float32)
    for p in range(128):
        i = (p % 64) % half
        out[0, p] = c0[i]
        out[1, p] = s0[i]
    half = 16
    inv = (10000.0 ** (-np.arange(half, dtype=np.float32) / half)).astype(np.float32)
    row = np.floor(t / 64.0).astype(np.float32)
    col = (t - row * 64.0).astype(np.float32)
    for p in range(128):
        dd = p % 64
        pos = row if dd < 32 else col
        i = (dd % 32) % half
        ang = pos * inv[i]
        out[2, p] = np.cos(ang).astype(np.float32)
        out[3, p] = np.sin(ang).astype(np.float32)
    return out


def rot_matrix_T(block):
    R = np.zeros((128, 128), np.float32)
    half = block // 2
    for p in range(128):
        i = p % block
        if i < half:
            R[p, p + half] = -1.0
        else:
            R[p, p - half] = 1.0
    return np.ascontiguousarray(R.T)


class Prog:
    def __init__(self, nseq, wbig_n, wbig_off, ws_n, ws_off, cb_n, cb_off, cf_n, cf_off,
                 mixers=(0, 1, 2, 3), ffn=True, nlayers=4):
        self.nseq = nseq
        self.mixers = mixers
        self.do_ffn = ffn
        self.nlayers = nlayers
        nc = bass.Bass("TRN2", target_bir_lowering=False)
        self.nc = nc
        self._uniq = 0
        self.S = Sch(nc)
        self.wo, self.so, self.cbo, self.cfo = wbig_off, ws_off, cb_off, cf_off
        self.x_in = nc.dram_tensor("x", [nseq, S_LEN, D], F32, kind="ExternalInput").ap()
        self.y_out = nc.dram_tensor("y", [nseq, S_LEN, D], F32, kind="ExternalOutput").ap()
        self.wbig = nc.dram_tensor("wbig", [wbig_n], F32, kind="ExternalInput").ap()
        self.wbf = nc.dram_tensor("wbf", [wbig_n], BF16, kind="Internal").ap()
        self.wsmall_d = nc.dram_tensor("wsmall", [128, ws_n], F32, kind="ExternalInput").ap()
        self.cbf_d = nc.dram_tensor("cbf", [128, cb_n], BF16, kind="ExternalInput").ap()
        self.cf_d = nc.dram_tensor("cf", [128, cf_n], F32, kind="ExternalInput").ap()
        self.rope_d = nc.dram_tensor("rope", [4, 128, S_LEN], BF16, kind="ExternalInput").ap()
        self.wbf_dep = Dep("wbf")
        self.wbf_blocks = [Dep() for _ in range(wbig_n // (128 * 2048))]
        self._wdeps_acc = []
        self.xsp = nc.dram_tensor("xsp", [NT, 128, D], F32, kind="Internal").ap()
        self.xspd = [Dep() for _ in range(NT)]
        self.rkvsp = nc.dram_tensor("rkvsp", [3, NT, 128, D], BF16, kind="Internal").ap()
        self.rkvd = [[Dep() for _ in range(NT)] for _ in range(3)]
        self.x = nc.alloc_sbuf_tensor("xres", [128, NT, D], F32)
        self.xd = [Dep(f"x{t}") for t in range(NT)]
        self.hT = nc.alloc_sbuf_tensor("hT", [128, KC, S_LEN], BF16)
        self.hTd = [Dep(f"hT{t}") for t in range(NT)]
        self.ws = nc.alloc_sbuf_tensor("ws", [128, ws_n], F32)
        self.wsd = Dep("ws")
        self.cb = nc.alloc_sbuf_tensor("cb", [128, cb_n], BF16)
        self.cbd = Dep("cb")
        self.ps = nc.alloc_psum_tensor("ps", [128, 8, 512], F32)
        self.psd = [Dep(f"ps{b}") for b in range(8)]
        self.ident = self.cb[:, cb_off["ident"]:cb_off["ident"] + 128]

    def sbt(self, name, shape, dt):
        self._uniq += 1
        return self.nc.sbuf_tensor(f"{name}_{self._uniq}", shape, dt)

    def mm(self, out, lhsT, rhs, start, stop, r, w):
        self.S.op("pe", lambda e: e.matmul(out, lhsT, rhs, start=start, stop=stop,
                                           skip_group_check=True), r, w)

    def tr(self, out, in_, r, w, ident=None):
        idn = self.ident if ident is None else ident
        self.S.op("pe", lambda e: e.transpose(out, in_, idn), r, w)

    def act(self, out, in_, func, r, w, scale=1.0, bias=None, accum=None):
        def f(e):
            kw = {}
            if bias is not None:
                kw["bias"] = bias
            if accum is not None:
                kw["accum_out"] = accum
            return e.activation(out=out, in_=in_, func=func, scale=scale, **kw)
        self.S.op("act", f, r, w)

    def ts(self, eng, out, in0, s1, s2, op0, op1, r, w):
        def f(e):
            if op1 is None:
                return e.tensor_scalar(out=out, in0=in0, scalar1=s1, scalar2=None, op0=op0)
            return e.tensor_scalar(out=out, in0=in0, scalar1=s1, scalar2=s2, op0=op0, op1=op1)
        self.S.op(eng, f, r, w)

    def tt(self, eng, out, in0, in1, op, r, w):
        self.S.op(eng, lambda e: e.tensor_tensor(out=out, in0=in0, in1=in1, op=op), r, w)

    def stt(self, out, in0, scalar, in1, op0, op1, r, w):
        self.S.op("dve", lambda e: e.scalar_tensor_tensor(out=out, in0=in0, scalar=scalar, in1=in1,
                                                          op0=op0, op1=op1), r, w)

    def cp(self, eng, out, in_, r, w):
        if eng == "act":
            self.S.op("act", lambda e: e.copy(out=out, in_=in_), r, w)
        else:
            self.S.op(eng, lambda e: e.tensor_copy(out=out, in_=in_), r, w)

    def memset(self, eng, ap, val, w):
        self.S.op(eng, lambda e: e.memset(ap, val), (), w)

    def load(self, out, in_, r, w, q="sp"):
        r = list(r)
        if self.wbf_dep in r:
            r.remove(self.wbf_dep)
            r += list(self._wdeps_acc)
        self.S.dma(q, lambda e: e.dma_start(out=out, in_=in_), r, w)

    def wsc(self, name, j=0, n=1):
        o = self.so[name] + j
        return self.ws[:, o:o + n]

    def setup(self):
        S = self.S
        self.load(self.ws[:], self.wsmall_d, (), [self.wsd])
        self.load(self.cb[:], self.cbf_d, (), [self.cbd])
        n = self.wbig.shape[0]
        blk = 128 * 2048
        src = self.wbig.rearrange("(b p f) -> b p f", p=128, f=2048)
        dst = self.wbf.rearrange("(b p f) -> b p f", p=128, f=2048)
        for b in range(n // blk):
            self.load(dst[b], src[b], (), [self.wbf_blocks[b]], q="pool")

    def wview(self, name):
        off, shape = self.wo[name]
        n = int(np.prod(shape))
        blk = 128 * 2048
        for d_ in self.wbf_blocks[off // blk:(off + n - 1) // blk + 1]:
            if d_ not in self._wdeps_acc:
                self._wdeps_acc.append(d_)
        flat = self.wbf[off:off + n]
        if len(shape) == 3:
            return flat.rearrange("(a b c) -> a b c", b=shape[1], c=shape[2])
        if len(shape) == 4:
            return flat.rearrange("(a b c d) -> a b c d", b=shape[1], c=shape[2], d=shape[3])
        return flat.rearrange("(a b) -> a b", b=shape[1])

    def load_x(self, s):
        for t in range(NT):
            self.load(self.x[:, t, :], self.x_in[s, t * 128:(t + 1) * 128, :], (), [self.xd[t]])

    def store_x(self, s):
        for t in range(NT):
            self.load(self.y_out[s, t * 128:(t + 1) * 128, :], self.x[:, t, :], [self.xd[t]], [Dep()], q="pool")

    def rmsnorm_T(self, gname, st):
        nc = self.nc
        junk = st.enter_context(self.sbt("nrm_junk", [128, D], BF16))
        hn = st.enter_context(self.sbt("nrm_hn", [128, 2, D], BF16))
        ss = st.enter_context(self.sbt("nrm_ss", [128, 2, 2], F32))
        junkd = Dep()
        hnd = [Dep(), Dep()]
        ssd = [Dep(), Dep()]
        g = self.wsc(gname, 0, 8)
        for t in range(NT):
            b = t % 2
            self.act(junk[:], self.x[:, t, :], AF.Square, [self.xd[t]], [junkd, ssd[b]],
                     accum=ss[:, b, 0:1])
            self.act(ss[:, b, 1:2], ss[:, b, 0:1], AF.Sqrt, [ssd[b]], [ssd[b]],
                     scale=1.0 / D, bias=self.wsc("eps6"))
            self.S.op("dve", (lambda b=b: (lambda e: e.reciprocal(out=ss[:, b, 1:2], in_=ss[:, b, 1:2])))(),
                      [ssd[b]], [ssd[b]])
            self.ts("dve", hn[:, b, :], self.x[:, t, :], ss[:, b, 1:2], None, ALU.mult, None,
                    [self.xd[t], ssd[b]], [hnd[b]])
            pb = 6 + (t % 2)
            pst = self.ps[:, pb, :].bitcast(BF16)
            for kc in range(KC):
                self.tr(pst[:, kc * 128:(kc + 1) * 128], hn[:, b, kc * 128:(kc + 1) * 128],
                        [hnd[b], self.cbd], [self.psd[pb]])
            self.tt("dve", self.hT[:, :, t * 128:(t + 1) * 128],
                    pst.rearrange("p (k c) -> p k c", k=KC),
                    g.unsqueeze(2).to_broadcast([128, KC, 128]), ALU.mult,
                    [self.psd[pb], self.wsd], [self.hTd[t]])

    def ffn(self, L):
        nc, S = self.nc, self.S
        S.barrier()
        with contextlib.ExitStack() as st0:
            self.rmsnorm_T(f"n{L}_ffn", st0)
            st = st0
            wi = st.enter_context(self.sbt("ffn_wi", [128, 2, 4, 1024], BF16))
            wo = st.enter_context(self.sbt("ffn_wo", [128, 2, 2, 1024], BF16))
            gT = st.enter_context(self.sbt("ffn_gT", [128, 2, 2, S_LEN], BF16))
            upad = st.enter_context(self.sbt("ffn_upad", [128, 2, 2, S_LEN + 2], F32))
            acc = st.enter_context(self.sbt("ffn_acc", [128, 2, S_LEN], F32))
            wid = [Dep(), Dep()]
            wod = [Dep(), Dep()]
            gTd = [Dep(), Dep()]
            upd = [[Dep(), Dep()], [Dep(), Dep()]]
            accd = [Dep(), Dep()]
            allh = list(self.hTd)
            for w_ in range(2):
                for b in range(2):
                    self.memset("pool", upad[:, w_, b, 0:1], 0.0, [upd[w_][b]])
                    self.memset("pool", upad[:, w_, b, S_LEN + 1:S_LEN + 2], 0.0, [upd[w_][b]])
            win = self.wview(f"f{L}_w_in")
            wout = self.wview(f"f{L}_w_out")
            cw = self.so[f"f{L}_conv_w"]
            cbo = self.so[f"f{L}_conv_b"]
            NP = NCH // 2
            ubuf = 0

            def out_partial(p, tiles):
                pb_ = p % 2
                for t in tiles:
                    bank = 4 + 2 * (t % 2)
                    for nh in range(2):
                        for j in range(2):
                            self.mm(self.ps[:, bank + nh, :], gT[:, pb_, j, t * 128:(t + 1) * 128],
                                    wo[:, pb_, j, nh * 512:(nh + 1) * 512], j == 0, j == 1,
                                    [gTd[pb_], wod[pb_]], [self.psd[bank + nh]])
                    self.tt("dve", self.x[:, t, :], self.x[:, t, :],
                            self.ps[:, bank:bank + 2, :].rearrange("p a b -> p (a b)"), ALU.add,
                            [self.psd[bank], self.psd[bank + 1], self.xd[t]], [self.xd[t]])

            for p in range(NP + 1):
                if p < NP:
                    pb = p % 2
                    self.load(wi[:, pb, 0:2, :],
                              win[2 * p:2 * p + 2].rearrange("a p k c -> p a (k c)"),
                              [self.wbf_dep], [wid[pb]])
                    self.load(wi[:, pb, 2:4, :],
                              win[NCH + 2 * p:NCH + 2 * p + 2].rearrange("a p k c -> p a (k c)"),
                              [self.wbf_dep], [wid[pb]])
                    self.load(wo[:, pb, :, :], wout[2 * p:2 * p + 2].rearrange("a p n -> p a n"),
                              [self.wbf_dep], [wod[pb]])
                for j in range(2):
                    if p < NP:
                        for which in range(2):
                            ci = 2 * p + j + which * NCH
                            ub = ubuf % 2
                            for half in range(2):
                                bank = 2 * half
                                for tq in range(2):
                                    c0 = half * 1024 + tq * 512
                                    for kc in range(KC):
                                        self.mm(self.ps[:, bank + tq, :],
                                                wi[:, pb, which * 2 + j, kc * 128:(kc + 1) * 128],
                                                self.hT[:, kc, c0:c0 + 512], kc == 0, kc == KC - 1,
                                                [wid[pb]] + allh[c0 // 128:c0 // 128 + 4],
                                                [self.psd[bank + tq]])
                                self.cp("act", upad[:, which, ub, 1 + half * 1024:1 + (half + 1) * 1024],
                                        self.ps[:, bank:bank + 2, :].rearrange("p a b -> p (a b)"),
                                        [self.psd[bank], self.psd[bank + 1]], [upd[which][ub]])
                            u = upad[:, which, ub, :]
                            a_ = acc[:, which, :]
                            self.act(a_, u[:, 0:S_LEN], AF.Identity, [upd[which][ub], self.wsd], [accd[which]],
                                     scale=self.ws[:, cw + ci * 3:cw + ci * 3 + 1],
                                     bias=self.ws[:, cbo + ci:cbo + ci + 1])
                            self.stt(a_, u[:, 1:S_LEN + 1], self.ws[:, cw + ci * 3 + 1:cw + ci * 3 + 2], a_,
                                     ALU.mult, ALU.add, [upd[which][ub], accd[which]], [accd[which]])
                            self.stt(a_, u[:, 2:S_LEN + 2], self.ws[:, cw + ci * 3 + 2:cw + ci * 3 + 3], a_,
                                     ALU.mult, ALU.add, [upd[which][ub], accd[which]], [accd[which]])
                        ubuf += 1
                        self.act(acc[:, 0, :], acc[:, 0, :], AF.Silu, [accd[0]], [accd[0]])
                        self.tt("dve", gT[:, pb, j, :], acc[:, 0, :], acc[:, 1, :], ALU.mult,
                                [accd[0], accd[1]], [gTd[pb]])
                    if p > 0:
                        out_partial(p - 1, range(j * 8, j * 8 + 8))

    def run_seq(self, s):
        self.load_x(s)
        if getattr(self, "debug", None) == "hT":
            with contextlib.ExitStack() as st:
                self.rmsnorm_T("n0_ffn", st)
            xv = self.x[:].rearrange("p a b -> p (a b)").rearrange("p (k t) -> p k t", k=KC)
            self.cp("act", xv, self.hT[:], self.hTd + self.xd, self.xd)
            self.store_x(s)
            return
        for L in range(self.nlayers):
            if L in self.mixers:
                getattr(self, f"mixer{L}")(L)
            if self.do_ffn:
                self.ffn(L)
        self.store_x(s)

    def build(self):
        self.setup()
        for s in range(self.nseq):
            self.run_seq(s)
        self.S.final_wait("pool")
        print("ops per engine", {e: len(v) for e, v in self.S.q.items()}, "cnt", self.S.cnt, flush=True)
        self.S.emit()
        return self.nc


def pack_params(p):
    big = Packer()
    sm = ColPacker()
    cb = ColPacker()
    cf = ColPacker()
    f32 = np.float32
    cb.add("ident", np.eye(128, dtype=f32))
    bo = np.zeros((128, 128), f32)
    bo[:64, :64] = 1.0 / 64
    bo[64:, 64:] = 1.0 / 64
    cb.add("blockmean", bo)
    cb.add("rotT64", rot_matrix_T(64))
    cb.add("rotT32", rot_matrix_T(32))
    cb.add("ones", np.ones((128, 128), f32))
    sm.add("eps6", np.full((128, 1), 1e-6, f32))
    sm.add("eps5", np.full((128, 1), 1e-5, f32))
    sm.add("epsgn", np.full((128, 1), 64e-5, f32))
    sm.add("zero", np.zeros((128, 1), f32))
    pack_mixers(p, big, sm, cb, cf)
    for L in range(4):
        sm.add(f"n{L}_attn", chan8(p[f"n{L}_attn"]))
        sm.add(f"n{L}_ffn", chan8(p[f"n{L}_ffn"]))
        cw = p[f"f{L}_conv_w"]
        sm.add(f"f{L}_conv_w", cw.T.reshape(44, 128, 3).transpose(1, 0, 2).reshape(128, 132))
        sm.add(f"f{L}_conv_b", p[f"f{L}_conv_b"].reshape(44, 128).T)
        big.add(f"f{L}_w_in", colchunk(p[f"f{L}_w_in"]))
        big.add(f"f{L}_w_out", p[f"f{L}_w_out"].reshape(22, 128, 1024))
    wbig = big.finish(128 * 2048)
    return (wbig, big.off, sm.finish(), sm.off, cb.finish().astype(ml_dtypes.bfloat16), cb.off,
            cf.finish(), cf.off)


def pack_mixers(p, big, sm, cb, cf):
    cf.add("dummy", np.zeros((128, 1), np.float32))
    f32 = np.float32
    idx64 = np.arange(128) % 64
    big.add("a_wqkv", colchunk(p["a_w_qkv"]))
    big.add("a_wo", p["a_w_o"].reshape(8, 128, 1024))
    sm.add("a_qg", p["a_q_norm"][idx64].reshape(128, 1))
    sm.add("a_kg", p["a_k_norm"][idx64].reshape(128, 1))
    sm.add("a_subln", rep(p["a_subln"]))
    for nm in ("a_lq1", "a_lk1", "a_lq2", "a_lk2"):
        sm.add(nm, rep(p[nm]))
    wb = p["b_w_qkv"]
    big.add("b_wqkv", colchunk(wb))
    kd = np.stack([np.concatenate([wb[:, 1024 + g * 64:1024 + (g + 1) * 64]] * 2, axis=1) for g in range(4)], 0)
    big.add("b_wkdup", np.stack([colchunk(kd[g])[0] for g in range(4)], 0))
    big.add("b_wo", p["b_w_o"].reshape(8, 128, 1024))
    sm.add("b_qg", p["b_q_norm"][idx64].reshape(128, 1))
    sm.add("b_kg", p["b_k_norm"][idx64].reshape(128, 1))
    big.add("c_wqkv", colchunk(p["c_w_qkv"]))
    big.add("c_wo", p["c_w_o"].reshape(8, 128, 1024))
    sm.add("c_qg", p["c_q_norm"][idx64].reshape(128, 1))
    sm.add("c_kg", p["c_k_norm"][idx64].reshape(128, 1))
    ho = np.zeros((128, 2), f32)
    ho[:64, 0] = -30000.0
    ho[64:, 1] = -30000.0
    sm.add("hoff", ho)
    rb = p["c_rel_bias"]
    pp = np.arange(128)
    j2 = pp // 64
    kc_ = pp % 64
    qq = np.arange(64)
    dc = np.clip(kc_[:, None] - qq[None, :] + 15, 0, 30)
    base = np.arange(14)
    dr = base[None, :] + j2[:, None]
    T = rb[:, dr[:, :, None], dc[:, None, :]]
    T = T.reshape(8, 2, 128, 14, 64).transpose(0, 2, 1, 3, 4)
    big.add("c_bias_f32", np.ascontiguousarray(T).reshape(8, 128, 2 * 14 * 64))
    cs_ = np.clip(qq - 8, 0, 48)
    inwin = (kc_[:, None] >= cs_[None, :]) & (kc_[:, None] < cs_[None, :] + 16)
    cf.add("nb_mask", np.where(inwin, 0.0, -30000.0).astype(f32))
    for nm in ("d_w_r", "d_w_k", "d_w_v", "d_w_o"):
        big.add(nm, p[nm].reshape(8, 128, 1024))
    big.add("d_w1s", np.concatenate([p["d_w1"][0], p["d_w1"][1]], axis=1).reshape(8, 128, 128))
    big.add("d_a1s", np.concatenate([p["d_a1"][0], p["d_a1"][1]], axis=1).reshape(8, 128, 128))
    big.add("d_g1", p["d_g1"].reshape(8, 128, 128))
    big.add("d_w2s", p["d_w2"].reshape(128, 1024))
    big.add("d_a2s", p["d_a2"].reshape(128, 1024))
    big.add("d_g2", p["d_g2"].reshape(128, 1024))
    sm.add("d_mu", np.concatenate([chan8(p["d_mu"][j]) for j in range(6)], axis=1))
    NEGC = -math.exp(-0.5)
    sm.add("negc", np.full((128, 2), NEGC, f32))
    ii = np.arange(128)
    le = (ii[:, None] <= ii[None, :]).astype(f32)
    lt = (ii[:, None] < ii[None, :]).astype(f32)
    ge = (ii[:, None] >= ii[None, :]).astype(f32)
    gt = (ii[:, None] > ii[None, :]).astype(f32)
    cf.add("tri", np.concatenate([le * NEGC, lt * NEGC, ge * NEGC, gt * NEGC], axis=1))
    cf.add("onesf", np.ones((128, 128), f32))
    cf.add("identf", np.eye(128, dtype=f32))
    lv = [(ii[:, None] // 8 == ii[None, :] // 8).astype(f32)]
    for bsz in (16, 32, 64, 128):
        lv.append(((ii[:, None] // bsz == ii[None, :] // bsz) & (ii[:, None] // (bsz // 2) != ii[None, :] // (bsz // 2))).astype(f32))
    cf.add("lvlmask", np.concatenate(lv, axis=1))
    cb.add("masks", np.concatenate([lt, le, gt, ge], axis=1))
    brow = np.zeros((128, 2048), f32)
    brow[0, :1024] = p["d_w0"][0]
    brow[64, :1024] = p["d_w0"][1]
    brow[0, 1024:] = p["d_a0"][0]
    brow[64, 1024:] = p["d_a0"][1]
    cf.add("brow", brow)
    cf.add("d_k_k", rep(p["d_k_k"]))
    cf.add("d_k_a", rep(p["d_k_a"]))
    cf.add("d_r_k", rep(p["d_r_k"].reshape(-1)))
    cf.add("d_ln_g", rep(p["d_ln_g"]))
    cf.add("d_ln_b", rep(p["d_ln_b"]))


_ROPE = None


def run(inputs, nseq_per_core=SEQ_PER_CORE, n_cores=N_CORES, **kw):
    global _ROPE
    xs = np.concatenate([np.asarray(inputs["x_prompt"]), np.asarray(inputs["x_sample"])], axis=0)
    p = {k: np.asarray(v) for k, v in inputs.items() if not k.startswith("x_")}
    wbig, woff, ws, soff, cbv, cboff, cfv, cfoff = pack_params(p)
    if _ROPE is None:
        _ROPE = build_rope_tables()
    prog = Prog(nseq_per_core, wbig.size, woff, ws.shape[1], soff, cbv.shape[1], cboff,
                cfv.shape[1], cfoff, **kw)
    nc = prog.build()
    in_maps = []
    for c in range(n_cores):
        in_maps.append({
            "x": np.ascontiguousarray(xs[c * nseq_per_core:(c + 1) * nseq_per_core]),
            "wbig": wbig, "wsmall": ws, "cbf": cbv, "cf": cfv, "rope": _ROPE.astype(ml_dtypes.bfloat16),
        })
    res = run_bass_kernel_spmd(nc, in_maps, core_ids=list(range(n_cores)))
    return np.concatenate([r["y"] for r in res.results], axis=0)


def kernel(**inputs):
    y = run(inputs)
    nb = np.asarray(inputs["x_prompt"]).shape[0]
    return (np.ascontiguousarray(y[:nb]), np.ascontiguousarray(y[nb:]))


def _attn_full(self, L):
    nc, S = self.nc, self.S
    diff = (L == 0)
    pre = "a" if diff else "b"
    E = 128 if diff else 64
    EW = 130 if diff else 66
    lam_init = 0.8 - 0.6 * math.exp(-0.3 * L)
    S.barrier()
    with contextlib.ExitStack() as st0:
        self.rmsnorm_T(f"n{L}_attn", st0)
        st = st0
        al = lambda n, sh, dt: st.enter_context(self.sbt(n, sh, dt))
        wq = al("at_wq", [128, 3, 1024], BF16)
        cs = al("at_cs", [128, 2, S_LEN], BF16)
        qT = al("at_qT", [128, 2, S_LEN], BF16)
        kT = al("at_kT", [128, 2, S_LEN], BF16)
        V = al("at_V", [128, 2, NT, EW], BF16)
        OT = al("at_OT", [128, KC, S_LEN], BF16)
        PT = al("at_PT", [128, 2, 512], BF16)
        sq = al("at_sq", [128, 512], BF16)
        qg = al("at_qg", [128, 512], BF16)
        t1 = al("at_t1", [128, 512], F32)
        t2 = al("at_t2", [128, 512], F32)
        t3 = al("at_t3", [128, 512], F32)
        oacc = al("at_oacc", [128, 4, 128], F32)
        osq = al("at_osq", [128, 4, 128], F32)
        osb = al("at_osb", [128, 4, 128], BF16)
        sm_ = al("at_sm", [128, 16], F32)
        wqd = [Dep(), Dep(), Dep()]
        csd = Dep()
        qTd = [Dep(), Dep()]
        kTd = [Dep(), Dep()]
        Vd = [Dep(), Dep()]
        OTd = Dep()
        PTd = [Dep(), Dep()]
        sqd, qgd, t1d, t2d, t3d, oaccd, osqd, osbd, smd = (Dep() for _ in range(9))
        allh = list(self.hTd)
        psd = self.psd
        ps = self.ps
        self.load(cs[:], self.rope_d[2 * L:2 * L + 2].rearrange("a p t -> p a t"), (), [csd])
        for b in range(2):
            self.memset("pool", V[:, b, :, E:E + 1], 1.0, [Vd[b]])
        cbo = self.cbo
        blockmean = self.cb[:, cbo["blockmean"]:cbo["blockmean"] + 128]
        rotT = self.cb[:, cbo["rotT64" if diff else "rotT32"]:cbo["rotT64" if diff else "rotT32"] + 128]
        wqkv = self.wview(f"{pre}_wqkv")
        if diff:
            for i, (a_, b_) in enumerate((("a_lq1", "a_lk1"), ("a_lq2", "a_lk2"))):
                self.tt("dve", t1[:, 0:64], self.wsc(a_, 0, 64), self.wsc(b_, 0, 64), ALU.mult, [self.wsd], [t1d])
                self.S.op("dve", (lambda i=i: lambda e: e.tensor_reduce(out=sm_[:, 8 + i:9 + i], in_=t1[:, 0:64],
                                                                       axis=AX.X, op=ALU.add))(), [t1d], [smd])
            self.act(sm_[:, 8:10], sm_[:, 8:10], AF.Exp, [smd], [smd])
            self.tt("dve", sm_[:, 10:11], sm_[:, 8:9], sm_[:, 9:10], ALU.subtract, [smd], [smd])
            self.ts("dve", sm_[:, 11:12], sm_[:, 10:11], lam_init, -1.0, ALU.add, ALU.mult, [smd], [smd])
        neglam = sm_[:, 11:12]

        def load_w(hc):
            if diff:
                idx = (hc, 8 + hc, 16 + hc)
            else:
                idx = (hc, None, 10 + hc // 4)
            for i, ci in enumerate(idx):
                if ci is None:
                    src = self.wview("b_wkdup")[hc // 2]
                else:
                    src = wqkv[ci]
                if (not diff) and i > 0 and hc % 2 == 1:
                    continue
                self.load(wq[:, i, :], src.rearrange("p k c -> p (k c)"), [self.wbf_dep], [wqd[i]])

        def proj_qk(hc, which, tc):
            b = hc % 2 if (diff or which == 0) else (hc // 2) % 2
            dst, dstd = (qT, qTd) if which == 0 else (kT, kTd)
            gain = self.wsc(f"{pre}_qg" if which == 0 else f"{pre}_kg")
            c0 = tc * 512
            for kc in range(KC):
                self.mm(ps[:, 4, :], wq[:, which, kc * 128:(kc + 1) * 128], self.hT[:, kc, c0:c0 + 512],
                        kc == 0, kc == KC - 1, [wqd[which]] + allh[tc * 4:tc * 4 + 4], [psd[4]])
            self.act(sq[:], ps[:, 4, :], AF.Square, [psd[4]], [sqd])
            self.act(qg[:], ps[:, 4, :], AF.Identity, [psd[4], self.wsd], [qgd], scale=gain)
            self.mm(ps[:, 5, :], blockmean, sq[:], True, True, [sqd, self.cbd], [psd[5]])
            self.mm(ps[:, 6, :], rotT, qg[:], True, True, [qgd, self.cbd], [psd[6]])
            self.act(t1[:], ps[:, 5, :], AF.Sqrt, [psd[5], self.wsd], [t1d], bias=self.wsc("eps6"))
            self.S.op("dve", lambda e: e.reciprocal(out=t1[:], in_=t1[:]), [t1d], [t1d])
            self.tt("dve", t2[:], qg[:], cs[:, 0, c0:c0 + 512], ALU.mult, [qgd, csd], [t2d])
            self.tt("dve", t3[:], ps[:, 6, :], cs[:, 1, c0:c0 + 512], ALU.mult, [psd[6], csd], [t3d])
            self.tt("pool", t2[:], t2[:], t3[:], ALU.add, [t2d, t3d], [t2d])
            self.tt("dve", dst[:, b, c0:c0 + 512], t2[:], t1[:], ALU.mult, [t2d, t1d], [dstd[b]])

        def proj_v(hc, tg):
            b = hc % 2 if diff else (hc // 2) % 2
            voff = 0 if diff else ((hc // 2) % 2) * 64
            for i in range(4):
                t = tg * 4 + i
                for kc in range(KC):
                    self.mm(ps[:, 7, i * E:(i + 1) * E], self.hT[:, kc, t * 128:(t + 1) * 128],
                            wq[:, 2, kc * 128 + voff:kc * 128 + voff + E], kc == 0 and i == 0, kc == KC - 1,
                            [wqd[2], allh[t]], [psd[7]])
            self.cp("act", V[:, b, tg * 4:tg * 4 + 4, 0:E],
                    ps[:, 7, 0:4 * E].rearrange("p (a e) -> p a e", a=4), [psd[7]], [Vd[b]])

        def proj_units(hc):
            units = []
            kv_new = diff or hc % 2 == 0
            for tc in range(4):
                units.append((lambda tc=tc: proj_qk(hc, 0, tc)))
                if kv_new:
                    units.append((lambda tc=tc: proj_qk(hc, 1, tc)))
                    units.append((lambda tc=tc: proj_v(hc, tc)))
            return units

        def attn_block(hc, qc, c):
            bq = hc % 2
            bk = hc % 2 if diff else (hc // 2) % 2
            pr = slice(c * 64, (c + 1) * 64)
            W = E + 1
            nb = 2 if diff else 1
            for kt in range(NT):
                sb = kt % 2
                self.mm(ps[:, sb, :], kT[pr, bk, kt * 128:(kt + 1) * 128], qT[pr, bq, qc * 512:(qc + 1) * 512],
                        True, True, [kTd[bk], qTd[bq]], [psd[sb]])
                self.act(PT[:, sb, :], ps[:, sb, :], AF.Exp, [psd[sb]], [PTd[sb]], scale=0.125)
                for qs in range(4):
                    if diff:
                        bank, col = 2 + qs // 2, (qs % 2) * W
                        first = (kt == 0 and qs % 2 == 0)
                    else:
                        bank, col = 2 + c, qs * W
                        first = (kt == 0 and qs == 0)
                    self.mm(ps[:, bank, col:col + W], PT[:, sb, qs * 128:(qs + 1) * 128], V[:, bk, kt, 0:W],
                            first, kt == NT - 1, [PTd[sb], Vd[bk]], [psd[bank]])
            if diff:
                for hb in range(2):
                    bank = 2 + hb
                    pv = ps[:, bank, 0:2 * W].rearrange("p (a w) -> p a w", a=2)
                    rz = sm_[:, hb * 2:hb * 2 + 2]
                    self.S.op("dve", (lambda rz=rz, pv=pv: lambda e: e.reciprocal(out=rz.unsqueeze(2), in_=pv[:, :, E:E + 1]))(),
                              [psd[bank]], [smd])
                    if c == 0:
                        self.tt("dve", oacc[:, hb * 2:hb * 2 + 2, :], pv[:, :, 0:E],
                                rz.unsqueeze(2).to_broadcast([128, 2, E]), ALU.mult, [psd[bank], smd], [oaccd])
                    else:
                        self.ts("dve", rz, rz, neglam, None, ALU.mult, None, [smd], [smd])
                        self.tt("dve", osq[:, hb * 2:hb * 2 + 2, :], pv[:, :, 0:E],
                                rz.unsqueeze(2).to_broadcast([128, 2, E]), ALU.mult, [psd[bank], smd], [osqd])
                        self.tt("pool", oacc[:, hb * 2:hb * 2 + 2, :], oacc[:, hb * 2:hb * 2 + 2, :],
                                osq[:, hb * 2:hb * 2 + 2, :], ALU.add, [osqd, oaccd], [oaccd])
                if c == 1:
                    self.tt("dve", osq[:], oacc[:], oacc[:], ALU.mult, [oaccd], [osqd])
                    self.S.op("dve", lambda e: e.tensor_reduce(out=sm_[:, 4:8], in_=osq[:], axis=AX.X, op=ALU.add),
                              [osqd], [smd])
                    self.act(sm_[:, 4:8], sm_[:, 4:8], AF.Sqrt, [smd, self.wsd], [smd], scale=1.0 / 128,
                             bias=self.wsc("eps5"))
                    self.S.op("dve", lambda e: e.reciprocal(out=sm_[:, 4:8], in_=sm_[:, 4:8]), [smd], [smd])
                    self.tt("dve", osq[:], oacc[:], sm_[:, 4:8].unsqueeze(2).to_broadcast([128, 4, 128]), ALU.mult,
                            [oaccd, smd], [osqd])
                    self.stt(osb[:], osq[:], 1.0 - lam_init,
                             self.wsc("a_subln", 0, 128).unsqueeze(1).to_broadcast([128, 4, 128]),
                             ALU.mult, ALU.mult, [osqd, self.wsd], [osbd])
            else:
                bank = 2 + c
                pv = ps[:, bank, 0:4 * W].rearrange("p (a w) -> p a w", a=4)
                rz = sm_[:, c * 4:c * 4 + 4]
                self.S.op("dve", lambda e: e.reciprocal(out=rz.unsqueeze(2), in_=pv[:, :, E:E + 1]), [psd[bank]], [smd])
                self.tt("dve", osb[:, :, c * 64:(c + 1) * 64], pv[:, :, 0:E],
                        rz.unsqueeze(2).to_broadcast([128, 4, E]), ALU.mult, [psd[bank], smd], [osbd])
            if c == 1:
                pst = ps[:, 7, :].bitcast(BF16)
                for qs in range(4):
                    self.tr(pst[:, qs * 128:(qs + 1) * 128], osb[:, qs, :], [osbd, self.cbd], [psd[7]])
                self.cp("act", OT[:, hc, qc * 512:(qc + 1) * 512], pst[:, 0:512], [psd[7]], [OTd])

        load_w(0)
        for u in proj_units(0):
            u()
        for hc in range(8):
            units = []
            if hc + 1 < 8:
                load_w(hc + 1)
                units = proj_units(hc + 1)
            ui = 0
            for qc in range(4):
                for c in range(2):
                    attn_block(hc, qc, c)
                    n_take = (len(units) - ui + (7 - (qc * 2 + c))) // (8 - (qc * 2 + c))
                    for _ in range(n_take):
                        units[ui]()
                        ui += 1
        wo = al("at_wo", [128, KC, 512], BF16)
        wod = Dep()
        wov = self.wview(f"{pre}_wo")
        for nh in range(2):
            self.load(wo[:], wov[:, :, nh * 512:(nh + 1) * 512].rearrange("k p n -> p k n"), [self.wbf_dep], [wod])
            for t in range(NT):
                bank = 4 + (t % 2)
                for kc in range(KC):
                    self.mm(ps[:, bank, :], OT[:, kc, t * 128:(t + 1) * 128], wo[:, kc, :], kc == 0, kc == KC - 1,
                            [OTd, wod], [psd[bank]])
                self.tt("dve", self.x[:, t, nh * 512:(nh + 1) * 512], self.x[:, t, nh * 512:(nh + 1) * 512],
                        ps[:, bank, :], ALU.add, [psd[bank], self.xd[t]], [self.xd[t]])


Prog._attn_full = _attn_full
Prog.mixer0 = _attn_full
Prog.mixer1 = _attn_full


def _attn_nbr(self, L):
    nc, S = self.nc, self.S
    S.barrier()
    with contextlib.ExitStack() as st0:
        self.rmsnorm_T(f"n{L}_attn", st0)
        st = st0
        al = lambda n, sh, dt: st.enter_context(self.sbt(n, sh, dt))
        wq = al("nb_wq", [128, 3, 1024], BF16)
        qT = al("nb_qT", [128, 2, S_LEN], BF16)
        kT = al("nb_kT", [128, 2, S_LEN], BF16)
        V = al("nb_V", [128, 2, NT, 2, 66], BF16)
        OT = al("nb_OT", [128, KC, S_LEN], BF16)
        Tb = al("nb_T", [128, 2, 2, 14, 64], F32)
        msk = al("nb_msk", [128, 64], F32)
        sb = al("nb_sb", [128, 2, 5, 64], F32)
        PT = al("nb_PT", [128, 2, 5, 64], BF16)
        sq = al("nb_sq", [128, 512], BF16)
        t1 = al("nb_t1", [128, 512], F32)
        osb = al("nb_osb", [128, 2, 128], BF16)
        sm_ = al("nb_sm", [128, 8], F32)
        wo = al("nb_wo", [128, KC, 512], BF16)
        wqd = [Dep(), Dep(), Dep()]
        qTd, kTd, Vd, Td = [Dep(), Dep()], [Dep(), Dep()], [Dep(), Dep()], [Dep(), Dep()]
        sbd, PTd, osbd = [Dep(), Dep()], [Dep(), Dep()], [Dep(), Dep()]
        OTd, mskd, sqd, t1d, smd, wod = (Dep() for _ in range(6))
        allh = list(self.hTd)
        psd, ps = self.psd, self.ps
        cbo = self.cbo
        blockmean = self.cb[:, cbo["blockmean"]:cbo["blockmean"] + 128]
        wqkv = self.wview("c_wqkv")
        o_b, shp = self.wo["c_bias_f32"]
        bias_d = self.wbig[o_b:o_b + 8 * 128 * 1792].rearrange("(a p n) -> a p n", p=128, n=1792)
        self.load(msk[:], self.cf_d[:, self.cfo["nb_mask"]:self.cfo["nb_mask"] + 64], (), [mskd])
        for b in range(2):
            self.memset("pool", V[:, b, :, :, 64:65], 1.0, [Vd[b]])
        self.ts("dve", sm_[:, 0:1], self.wsc("c_qg"), 0.125, None, ALU.mult, None, [self.wsd], [smd])

        def load_w(hc):
            b = hc % 2
            for i, ci in enumerate((hc, 8 + hc, 16 + hc)):
                self.load(wq[:, i, :], wqkv[ci].rearrange("p k c -> p (k c)"), [self.wbf_dep], [wqd[i]])
            self.load(Tb[:, b].rearrange("p c a q -> p (c a q)"), bias_d[hc], (), [Td[b]])
            self.tt("pool", Tb[:, b].rearrange("p c a q -> p (c a) q"), Tb[:, b].rearrange("p c a q -> p (c a) q"),
                    msk[:].unsqueeze(1).to_broadcast([128, 28, 64]), ALU.add, [Td[b], mskd], [Td[b]])

        def proj_qk(hc, which, tc):
            b = hc % 2
            dst, dstd = (qT, qTd) if which == 0 else (kT, kTd)
            gain = sm_[:, 0:1] if which == 0 else self.wsc("c_kg")
            c0 = tc * 512
            for kc in range(KC):
                self.mm(ps[:, 4, :], wq[:, which, kc * 128:(kc + 1) * 128], self.hT[:, kc, c0:c0 + 512],
                        kc == 0, kc == KC - 1, [wqd[which]] + allh[tc * 4:tc * 4 + 4], [psd[4]])
            self.act(sq[:], ps[:, 4, :], AF.Square, [psd[4]], [sqd])
            self.mm(ps[:, 5, :], blockmean, sq[:], True, True, [sqd, self.cbd], [psd[5]])
            self.act(t1[:], ps[:, 5, :], AF.Sqrt, [psd[5], self.wsd], [t1d], bias=self.wsc("eps6"))
            self.S.op("dve", lambda e: e.reciprocal(out=t1[:], in_=t1[:]), [t1d], [t1d])
            self.stt(dst[:, b, c0:c0 + 512], ps[:, 4, :], gain, t1[:], ALU.mult, ALU.mult,
                     [psd[4], t1d, smd, self.wsd], [dstd[b]])

        def proj_v(hc, tg):
            b = hc % 2
            for i in range(4):
                t = tg * 4 + i
                for kc in range(KC):
                    self.mm(ps[:, 7, i * 128:(i + 1) * 128], self.hT[:, kc, t * 128:(t + 1) * 128],
                            wq[:, 2, kc * 128:(kc + 1) * 128], kc == 0 and i == 0, kc == KC - 1,
                            [wqd[2], allh[t]], [psd[7]])
            self.cp("act", V[:, b, tg * 4:tg * 4 + 4, :, 0:64],
                    ps[:, 7, :].rearrange("p (a c e) -> p a c e", a=4, c=2), [psd[7]], [Vd[b]])

        def proj_units(hc):
            units = []
            for tc in range(4):
                units.append((lambda tc=tc: proj_qk(hc, 0, tc)))
                units.append((lambda tc=tc: proj_qk(hc, 1, tc)))
                units.append((lambda tc=tc: proj_v(hc, tc)))
            return units

        def row_block(hc, r):
            b = hc % 2
            rs = min(max(r - 4, 0), 24)
            a0, a1 = rs // 2, (rs + 7) // 2
            ns = a1 - a0 + 1
            ph = (r % 2) * 64
            tb = (r // 2) % 2
            for c in range(2):
                pr = slice(c * 64, (c + 1) * 64)
                sbank = c
                for si in range(ns):
                    a = a0 + si
                    self.mm(ps[:, sbank, si * 64:(si + 1) * 64], kT[pr, b, a * 128:(a + 1) * 128],
                            qT[pr, b, r * 64:(r + 1) * 64], si == 0, True, [kTd[b], qTd[b]], [psd[sbank]])
                base0 = 2 * a0 - r + 7
                self.tt("dve", sb[:, c, 0:ns, :], ps[:, sbank, 0:ns * 64].rearrange("p (s q) -> p s q", s=ns),
                        Tb[:, b, c, base0:base0 + 2 * ns - 1:2, :], ALU.add, [psd[sbank], Td[b]], [sbd[c]])
                if rs % 2 == 0:
                    self.act(PT[:, c, 0:ns, :], sb[:, c, 0:ns, :], AF.Exp, [sbd[c]], [PTd[c]])
                else:
                    self.act(PT[:, c, 0:1, :], sb[:, c, 0:1, :], AF.Exp, [sbd[c], self.wsd], [PTd[c]],
                             bias=self.wsc("hoff", 0, 1))
                    self.act(PT[:, c, 1:ns - 1, :], sb[:, c, 1:ns - 1, :], AF.Exp, [sbd[c]], [PTd[c]])
                    self.act(PT[:, c, ns - 1:ns, :], sb[:, c, ns - 1:ns, :], AF.Exp, [sbd[c], self.wsd], [PTd[c]],
                             bias=self.wsc("hoff", 1, 1))
                for si in range(ns):
                    a = a0 + si
                    self.mm(ps[ph:ph + 64, 2, c * 65:(c + 1) * 65], PT[:, c, si, :], V[:, b, a, c, 0:65],
                            si == 0, si == ns - 1, [PTd[c], Vd[b]], [psd[2]])
            pv = ps[ph:ph + 64, 2, 0:130].rearrange("p (c w) -> p c w", c=2)
            rz = sm_[ph:ph + 64, 2:4]
            self.S.op("dve", lambda e: e.reciprocal(out=rz.unsqueeze(2), in_=pv[:, :, 64:65]), [psd[2]], [smd])
            self.tt("dve", osb[ph:ph + 64, tb, :].rearrange("p (c e) -> p c e", c=2), pv[:, :, 0:64],
                    rz.unsqueeze(2).to_broadcast([64, 2, 64]), ALU.mult, [psd[2], smd], [osbd[tb]])
            if r % 2 == 1:
                t = r // 2
                pst = ps[:, 3, :].bitcast(BF16)
                self.tr(pst[:, 0:128], osb[:, tb, :], [osbd[tb], self.cbd], [psd[3]])
                self.cp("act", OT[:, hc, t * 128:(t + 1) * 128], pst[:, 0:128], [psd[3]], [OTd])

        load_w(0)
        for u in proj_units(0):
            u()
        for hc in range(8):
            units = []
            if hc + 1 < 8:
                load_w(hc + 1)
                units = proj_units(hc + 1)
            ui = 0
            for r in range(32):
                row_block(hc, r)
                n_take = (len(units) - ui + (31 - r)) // (32 - r)
                for _ in range(n_take):
                    units[ui]()
                    ui += 1
        wov = self.wview("c_wo")
        for nh in range(2):
            self.load(wo[:], wov[:, :, nh * 512:(nh + 1) * 512].rearrange("k p n -> p k n"), [self.wbf_dep], [wod])
            for t in range(NT):
                bank = 4 + (t % 2)
                for kc in range(KC):
                    self.mm(ps[:, bank, :], OT[:, kc, t * 128:(t + 1) * 128], wo[:, kc, :], kc == 0, kc == KC - 1,
                            [OTd, wod], [psd[bank]])
                self.tt("dve", self.x[:, t, nh * 512:(nh + 1) * 512], self.x[:, t, nh * 512:(nh + 1) * 512],
                        ps[:, bank, :], ALU.add, [psd[bank], self.xd[t]], [self.xd[t]])


Prog.mixer2 = _attn_nbr


def _rwkv(self, L):
    nc, S = self.nc, self.S
    ps, psd = self.ps, self.psd
    cbo, cfo = self.cbo, self.cfo
    bank_i = [0]

    def nb():
        b = bank_i[0] % 8
        bank_i[0] += 1
        return b

    def nb2():
        s0 = (bank_i[0] + 1) // 2 * 2
        bank_i[0] = s0 + 2
        return s0 % 8

    def ps2(b):
        return ps[:, b:b + 2, :].rearrange("p a b -> p (a b)")

    def pd2(b):
        return [psd[b], psd[b + 1]]

    def reduce_x(out, in_, r, w):
        self.S.op("dve", lambda e: e.tensor_reduce(out=out, in_=in_, axis=AX.X, op=ALU.add), r, w)

    def recip(ap, r, w):
        self.S.op("dve", lambda e: e.reciprocal(out=ap, in_=ap), r, w)

    S.barrier()
    with contextlib.ExitStack() as st_outer:
        alo = lambda n, sh, dt: st_outer.enter_context(self.sbt(n, sh, dt))
        lT = alo("rw_lT", [128, 3, S_LEN], BF16)
        lTd = Dep()
        allh = list(self.hTd)
        xd = self.xd
        with contextlib.ExitStack() as st:
            self.rmsnorm_T(f"n{L}_attn", st)
        for t in range(NT):
            self.load(self.xsp[t], self.x[:, t, :], [xd[t]], [self.xspd[t]], q="pool")
        S.barrier()
        with contextlib.ExitStack() as st:
            al = lambda n, sh, dt: st.enter_context(self.sbt(n, sh, dt))
            tmp = al("rwa_tmp", [128, S_LEN], F32)
            W = al("rwa_W", [128, KC, 1024], BF16)
            stage = al("rwa_stage", [128, 2, 1024], BF16)
            lw = al("rwa_lw", [128, KC, 128], BF16)
            coef = al("rwa_coef", [128, 2, 48], F32)
            tmpd, Wd, lwd, coefd, shd, mixd = (Dep() for _ in range(6))
            staged = [Dep(), Dep()]
            xb = self.x[:].rearrange("p a b -> p (a b)").bitcast(BF16)
            sh = xb[:, 0:KC * S_LEN].rearrange("p (k t) -> p k t", k=KC)
            mix = xb[:, KC * S_LEN:2 * KC * S_LEN].rearrange("p (k t) -> p k t", k=KC)
            mu = self.wsc("d_mu", 0, 48)
            self.ts("dve", coef[:, 0, :], mu, -1.0, 1.0, ALU.mult, ALU.add, [self.wsd], [coefd])
            self.ts("dve", coef[:, 1, :], mu, 0.5, None, ALU.mult, None, [self.wsd], [coefd])
            for kc in range(KC):
                self.tt("dve", sh[:, kc, 1:S_LEN - 1], self.hT[:, kc, 0:S_LEN - 2],
                        self.hT[:, kc, 2:S_LEN], ALU.add, allh, [shd] + xd)
            self.cp("dve", sh[:, :, 0:1], self.hT[:, :, 1:2], allh, [shd] + xd)
            self.cp("dve", sh[:, :, S_LEN - 1:S_LEN], self.hT[:, :, S_LEN - 2:S_LEN - 1], allh, [shd] + xd)

            def make_mix(j):
                for kc in range(KC):
                    self.act(tmp[:], self.hT[:, kc, :], AF.Identity, allh + [coefd], [tmpd],
                             scale=coef[:, 0, j * 8 + kc:j * 8 + kc + 1])
                    self.stt(mix[:, kc, :], sh[:, kc, :], coef[:, 1, j * 8 + kc:j * 8 + kc + 1], tmp[:],
                             ALU.mult, ALU.add, [shd, tmpd, coefd], [mixd] + xd)

            def proj_tm(wname, idx):
                self.load(W[:], self.wview(wname).rearrange("k p n -> p k n"), [self.wbf_dep], [Wd])
                for t in range(NT):
                    b2 = nb2()
                    for nh in range(2):
                        for kc in range(KC):
                            self.mm(ps[:, b2 + nh, :], mix[:, kc, t * 128:(t + 1) * 128],
                                    W[:, kc, nh * 512:(nh + 1) * 512], kc == 0, kc == KC - 1, [mixd, Wd],
                                    [psd[b2 + nh]])
                    sb_ = t % 2
                    self.cp("act", stage[:, sb_, :], ps2(b2), pd2(b2), [staged[sb_]])
                    self.load(self.rkvsp[idx, t], stage[:, sb_, :], [staged[sb_]], [self.rkvd[idx][t]], q="pool")

            def proj_lora(wname, li, func):
                self.load(lw[:], self.wview(wname).rearrange("k p n -> p k n"), [self.wbf_dep], [lwd])
                for tc in range(4):
                    b = nb()
                    for kc in range(KC):
                        self.mm(ps[:, b, :], lw[:, kc, :], mix[:, kc, tc * 512:(tc + 1) * 512], kc == 0,
                                kc == KC - 1, [mixd, lwd], [psd[b]])
                    self.act(lT[:, li, tc * 512:(tc + 1) * 512], ps[:, b, :], func, [psd[b]], [lTd])

            make_mix(0)
            proj_tm("d_w_r", 0)
            make_mix(1)
            proj_lora("d_w1s", 0, AF.Tanh)
            make_mix(2)
            proj_tm("d_w_k", 1)
            make_mix(3)
            proj_tm("d_w_v", 2)
            make_mix(4)
            proj_lora("d_a1s", 1, AF.Identity)
            make_mix(5)
            proj_lora("d_g1", 2, AF.Sigmoid)
        S.barrier()
        import os
        stop = os.environ.get("RW_STOP", "")
        if stop:
            nbt = int(os.environ.get("RW_NT", "16"))
        if stop == "A":
            for t in range(NT):
                self.load(self.x[:, t, :], self.xsp[t], [self.xspd[t]], [xd[t]])
            return
        hTf = self.hT[:].rearrange("p k t -> p (k t)")
        with contextlib.ExitStack() as st:
            al = lambda n, sh_, dt: st.enter_context(self.sbt(n, sh_, dt))
            tri = al("rwb_tri", [128, 4, 128], F32)
            onesf = al("rwb_onesf", [128, 128], F32)
            brow = al("rwb_brow", [128, 2048], F32)
            kkb = al("rwb_kkb", [128, 2, 1024], F32)
            w2a2 = al("rwb_w2a2", [128, 2, 1024], BF16)
            rkv = al("rwb_rkv", [128, 3, 1024], BF16)
            F = al("rwb_F", [128, 6, 1024], F32)
            Bt = al("rwb_Bt", [128, 4, 1024], BF16)
            ARc = al("rwb_ARc", [128, 8, 2, 128], BF16)
            Bc = al("rwb_Bc", [128, 8, 128], BF16)
            Kc = al("rwb_Kc", [128, 8, 128], BF16)
            IB = al("rwb_IB", [128, 6, 4, 128], F32)
            lvlm = al("rwb_lvlm", [128, 5, 128], F32)
            identf = al("rwb_identf", [128, 128], F32)
            RU = al("rwb_RU", [128, 2, 1024], BF16)
            S32 = al("rwb_S32", [128, 8, 64], F32)
            Sb = al("rwb_Sb", [128, 8, 64], BF16)
            WL = al("rwb_WL", [128, 8], F32)
            small = al("rwb_small", [128, 16], F32)
            cstd, rkvsd, Fd, Btd, cmd, NAd, NTd, RUd, S32d, Sbd, WLd, smd = (
                Dep(), [Dep(), Dep(), Dep()], [Dep() for _ in range(6)], [Dep() for _ in range(4)],
                Dep(), Dep(), Dep(), [Dep(), Dep()], Dep(), Dep(), Dep(), Dep())
            NA3 = hTf[:, 0:6144].rearrange("p (h a t) -> p h a t", h=16, a=3)
            Xb = hTf[:, 6144:8192].rearrange("p (h t) -> p h t", h=16)
            Nf = hTf[:, 8192:12288].bitcast(F32).rearrange("p (h t) -> p h t", h=16)
            NTf = hTf[:, 12288:16384].bitcast(F32).rearrange("p (h t) -> p h t", h=16)
            Nfd, NTfd, Xbd = Dep(), Dep(), Dep()
            ibd = [Dep() for _ in range(6)]
            ibdB = [Dep() for _ in range(6)]
            masks = self.cb[:, cbo["masks"]:cbo["masks"] + 512].rearrange("p (a t) -> p a t", a=4)
            identb = self.ident
            self.load(tri[:].rearrange("p a t -> p (a t)"), self.cf_d[:, cfo["tri"]:cfo["tri"] + 512], (), [cstd])
            self.load(onesf[:], self.cf_d[:, cfo["onesf"]:cfo["onesf"] + 128], (), [cstd])
            self.load(identf[:], self.cf_d[:, cfo["identf"]:cfo["identf"] + 128], (), [cstd])
            self.load(lvlm[:].rearrange("p a t -> p (a t)"), self.cf_d[:, cfo["lvlmask"]:cfo["lvlmask"] + 640], (), [cstd])
            self.load(brow[:], self.cf_d[:, cfo["brow"]:cfo["brow"] + 2048], (), [cstd])
            self.load(kkb[:, 0, :], self.cf_d[:, cfo["d_k_k"]:cfo["d_k_k"] + 1024], (), [cstd])
            self.load(kkb[:, 1, :], self.cf_d[:, cfo["d_k_a"]:cfo["d_k_a"] + 1024], (), [cstd])
            self.load(w2a2[:, 0, :], self.wview("d_w2s"), [self.wbf_dep], [cstd])
            self.load(w2a2[:, 1, :], self.wview("d_a2s"), [self.wbf_dep], [cstd])
            negc2 = self.wsc("negc", 0, 2)

            def lora_sig(d, which, n, dstF, dstd):
                prd = slice(d * 64, (d + 1) * 64)
                rowp = d * 64
                b2 = nb2()
                for nh in range(2):
                    self.mm(ps[:, b2 + nh, :], lT[prd, which, n * 128:(n + 1) * 128],
                            w2a2[prd, which, nh * 512:(nh + 1) * 512], True, False, [lTd, cstd], [psd[b2 + nh]])
                    c0_ = which * 1024 + nh * 512
                    self.mm(ps[:, b2 + nh, :], onesf[rowp:rowp + 1, :], brow[rowp:rowp + 1, c0_:c0_ + 512],
                            False, True, [cstd], [psd[b2 + nh]])
                self.act(dstF, ps2(b2), AF.Sigmoid, pd2(b2), [dstd])

            def scan_tile(d, n):
                for i in range(3):
                    self.load(rkv[:, i, :], self.rkvsp[i, n], [self.rkvd[i][n]], [rkvsd[i]])
                F0, F1, F2, F3, F4, F5 = (F[:, i, :] for i in range(6))
                v3 = lambda ap: ap.rearrange("p (h j) -> p h j", h=16)
                lora_sig(d, 0, n, F0, Fd[0])
                lora_sig(d, 1, n, F1, Fd[1])
                cut = int(os.environ.get("RW_CUT", "99"))
                if cut <= 1:
                    return
                self.tt("dve", F5, rkv[:, 1, :], kkb[:, 0, :], ALU.mult, [rkvsd[1], cstd], [Fd[5]])
                self.tt("pool", F2, F5, F5, ALU.mult, [Fd[5]], [Fd[2]])
                reduce_x(small[:, 0:16], v3(F2), [Fd[2]], [smd])
                self.act(small[:, 0:16], small[:, 0:16], AF.Sqrt, [smd], [smd])
                self.ts("dve", small[:, 0:16], small[:, 0:16], 1e-12, None, ALU.max, None, [smd], [smd])
                recip(small[:, 0:16], [smd], [smd])
                self.tt("dve", v3(F5), v3(F5), small[:, 0:16].unsqueeze(2).to_broadcast([128, 16, 64]), ALU.mult,
                        [Fd[5], smd], [Fd[5]])
                if cut <= 2:
                    return
                bi, be = nb2(), nb2()
                for nh in range(2):
                    self.mm(ps[:, bi + nh, :], tri[:, 2 * d, :], F0[:, nh * 512:(nh + 1) * 512], True, True,
                            [cstd, Fd[0]], [psd[bi + nh]])
                    self.mm(ps[:, be + nh, :], tri[:, 2 * d + 1, :], F0[:, nh * 512:(nh + 1) * 512], True, True,
                            [cstd, Fd[0]], [psd[be + nh]])
                self.act(F2, ps2(bi), AF.Exp, pd2(bi), [Fd[2]])
                self.act(F3, ps2(bi), AF.Exp, pd2(bi), [Fd[3]], scale=-1.0)
                self.act(F4, ps2(be), AF.Exp, pd2(be), [Fd[4]])
                bw = nb()
                for hp in range(8):
                    self.mm(ps[:, bw, hp * 2:hp * 2 + 2], F0[:, hp * 128:(hp + 1) * 128], negc2, hp == 0, True,
                            [Fd[0], self.wsd], [psd[bw]])
                self.act(WL[:], ps[:, bw, 0:16:2], AF.Exp, [psd[bw]], [WLd])
                if cut <= 3:
                    return
                self.stt(Bt[:, 0, :], F5, -1.0, F4, ALU.mult, ALU.mult, [Fd[5], Fd[4]], [Btd[0]])
                self.tt("pool", F4, F5, F1, ALU.mult, [Fd[5], Fd[1]], [Fd[4]])
                self.tt("dve", Bt[:, 1, :], F4, F3, ALU.mult, [Fd[4], Fd[3]], [Btd[1]])
                self.stt(F4, F1, -1.0, kkb[:, 1, :], ALU.add, ALU.mult, [Fd[1], cstd], [Fd[4]])
                self.stt(F5, F4, 1.0, rkv[:, 1, :], ALU.add, ALU.mult, [Fd[4], rkvsd[1]], [Fd[5]])
                self.tt("dve", Bt[:, 2, :], F5, F3, ALU.mult, [Fd[5], Fd[3]], [Btd[2]])
                self.tt("dve", Bt[:, 3, :], rkv[:, 0, :], F2, ALU.mult, [rkvsd[0], Fd[2]], [Btd[3]])
                if cut <= 4:
                    return
                for si, dst, eng in ((0, ARc[:, :, 0, :], "act"), (3, ARc[:, :, 1, :], "dve"), (1, Bc[:], "act"),
                                     (2, Kc[:], "dve")):
                    b = nb()
                    pst = ps[:, b, :].bitcast(BF16)
                    for hp in range(8):
                        self.tr(pst[:, hp * 128:(hp + 1) * 128], Bt[:, si, hp * 128:(hp + 1) * 128],
                                [Btd[si], self.cbd], [psd[b]])
                    self.cp(eng, dst, pst.rearrange("p (h t) -> p h t", h=8), [psd[b]], [cmd])
                if os.environ.get("RW_DUMP"):
                    dd = [Dep() for _ in range(NT)]
                    for i in range(6):
                        self.cp("act", self.x[:, i, :], F[:, i, :], [Fd[i]], [xd[i]])
                    for i in range(4):
                        self.cp("act", self.x[:, 6 + i, :], Bt[:, i, :], [Btd[i]], [xd[6 + i]])
                    for i in range(3):
                        self.cp("act", self.x[:, 10 + i, :], rkv[:, i, :], [rkvsd[i]], [xd[10 + i]])
                    return
                if cut <= 5:
                    return
                m12 = masks[:, 0:2, :] if d == 0 else masks[:, 2:4, :]
                m3 = masks[:, 2, :] if d == 0 else masks[:, 0, :]
                for h in range(16):
                    hp, pr = h // 2, slice((h % 2) * 64, (h % 2) * 64 + 64)
                    b = nb()
                    arv = ARc[pr, hp, :, :].rearrange("p a t -> p (a t)")
                    self.mm(ps[:, b, 0:256], Bc[pr, hp, :], arv, True, True, [cmd], [psd[b]])
                    self.mm(ps[:, b, 256:512], Kc[pr, hp, :], arv, False, True, [cmd], [psd[b]])
                    self.tt("dve", Nf[:, h, :], ps[:, b, 0:128], m12[:, 0, :], ALU.mult, [psd[b], self.cbd],
                            [Nfd] + allh)
                    self.tt("dve", NA3[:, h, 0, :], ps[:, b, 128:256], m12[:, 1, :], ALU.mult, [psd[b], self.cbd],
                            [NAd] + allh)
                    self.tt("dve", NA3[:, h, 1:3, :], ps[:, b, 256:512].rearrange("p (a t) -> p a t", a=2),
                            m12, ALU.mult, [psd[b], self.cbd], [NAd] + allh)
                for g8 in range(2):
                    bb = nb2()
                    for par in range(2):
                        pr = slice(par * 64, par * 64 + 64)
                        for i4 in range(4):
                            hp = g8 * 4 + i4
                            self.mm(ps[:, bb + par, i4 * 128:(i4 + 1) * 128], ARc[pr, hp, 0, :], Bc[pr, hp, :],
                                    i4 == 0, True, [cmd], [psd[bb + par]])
                        self.tt("dve", NTf[:, g8 * 8 + par:g8 * 8 + 8:2, :],
                                ps[:, bb + par, :].rearrange("p (h t) -> p h t", h=4),
                                m3.unsqueeze(1).to_broadcast([128, 4, 128]), ALU.mult, [psd[bb + par], self.cbd],
                                [NTfd] + allh)
                if cut <= 6:
                    return
                bc4 = lambda ap: ap.unsqueeze(1).to_broadcast([128, 4, 128])
                p4 = lambda b_: ps[:, b_, :].rearrange("p (h t) -> p h t", h=4)
                Fv = F[:, 0:3, :].rearrange("p a n -> p (a n)").rearrange("p (i h t) -> p i h t", i=6, h=4)
                bufsets = [([IB[:, i] for i in range(6)], ibd, [], []),
                           ([Fv[:, i] for i in range(6)], ibdB, list(Fd[0:3]), list(Fd[0:3]))]

                def inv_group(g4, bset):
                    bufs_, deps_, xr, xw = bset
                    hs = slice(g4 * 4, g4 * 4 + 4)
                    P1, PT1, Q, P2, PT2, Y1 = bufs_
                    dP1, dPT1, dQ, dP2, dPT2, dY1 = deps_

                    def mm4(bank, lhs, rhs, r):
                        for hh in range(4):
                            self.mm(ps[:, bank, hh * 128:(hh + 1) * 128], lhs[:, hh, :], rhs[:, hh, :], hh == 0, True,
                                    r + xr, [psd[bank]])
                    self.tt("dve", P1, Nf[:, hs, :], bc4(lvlm[:, 0, :]), ALU.mult, [Nfd, cstd] + xr, [dP1] + xw)
                    self.tt("pool", PT1, NTf[:, hs, :], bc4(lvlm[:, 0, :]), ALU.mult, [NTfd, cstd] + xr, [dPT1] + xw)
                    self.tt("pool", Q, P1, bc4(identf[:]), ALU.add, [dP1, cstd] + xr, [dQ] + xw)
                    yield
                    b1, b2_ = nb(), nb()
                    mm4(b1, PT1, P1, [dP1, dPT1])
                    mm4(b2_, P1, PT1, [dP1, dPT1])
                    yield
                    self.cp("act", P2, p4(b1), [psd[b1]] + xr, [dP2])
                    self.cp("act", PT2, p4(b2_), [psd[b2_]] + xr, [dPT2])
                    yield
                    b3, b4 = nb(), nb()
                    mm4(b3, PT2, Q, [dPT2, dQ])
                    mm4(b4, P2, PT2, [dP2, dPT2])
                    yield
                    self.tt("dve", Q, Q, p4(b3), ALU.add, [psd[b3], dQ] + xr, [dQ])
                    self.cp("act", PT1, p4(b4), [psd[b4]] + xr, [dPT1])
                    yield
                    b5 = nb()
                    mm4(b5, PT1, Q, [dPT1, dQ])
                    yield
                    self.tt("dve", Q, Q, p4(b5), ALU.add, [psd[b5], dQ] + xr, [dQ])
                    yield
                    b6 = nb()
                    for hh in range(4):
                        self.tr(ps[:, b6, hh * 128:(hh + 1) * 128], Q[:, hh, :], [dQ, cstd] + xr, [psd[b6]],
                                ident=identf[:])
                    Xt, dXt = P2, dP2
                    X_, dX = Q, dQ
                    NTm, dNTm = PT2, dPT2
                    self.tt("pool", NTm, NTf[:, hs, :], bc4(lvlm[:, 1, :]), ALU.mult, [NTfd, cstd] + xr, [dNTm])
                    yield
                    self.cp("act", Xt, p4(b6), [psd[b6]] + xr, [dXt])
                    for li in range(1, 5):
                        lastl = li == 4
                        by = nb()
                        mm4(by, NTm, X_, [dNTm, dX])
                        yield
                        self.cp("act", Y1, p4(by), [psd[by]] + xr, [dY1])
                        if not lastl:
                            self.tt("pool", NTm, NTf[:, hs, :], bc4(lvlm[:, li + 1, :]), ALU.mult,
                                    [NTfd, cstd] + xr, [dNTm])
                        yield
                        bx = nb()
                        mm4(bx, Xt, Y1, [dXt, dY1])
                        if not lastl:
                            bxt = nb()
                            mm4(bxt, Y1, Xt, [dY1, dXt])
                        yield
                        if not lastl:
                            self.tt("dve", X_, X_, p4(bx), ALU.add, [psd[bx], dX] + xr, [dX])
                            self.tt("dve", Xt, Xt, p4(bxt), ALU.add, [psd[bxt], dXt] + xr, [dXt])
                            yield
                        else:
                            self.tt("dve", Xb[:, hs, :], X_, p4(bx), ALU.add, [psd[bx], dX] + xr, [Xbd] + allh)

                for rnd in range(2):
                    gens = [inv_group(2 * rnd, bufsets[0]), inv_group(2 * rnd + 1, bufsets[1])]
                    alive = list(gens)
                    while alive:
                        nxt = []
                        for g_ in alive:
                            try:
                                next(g_)
                                nxt.append(g_)
                            except StopIteration:
                                pass
                        alive = nxt
                X, Xd = Xb, Xbd
                if cut <= 7:
                    return
                Vt = rkv[:, 2, :]
                bR = nb2()
                for h in range(16):
                    hp, pr = h // 2, slice((h % 2) * 64, (h % 2) * 64 + 64)
                    o = ps[:, bR + h % 2, hp * 64:hp * 64 + 64]
                    self.mm(o, ARc[pr, hp, 0, :], Sb[pr, hp, :], True, False, [cmd, Sbd], [psd[bR + h % 2]])
                    self.mm(o, NA3[:, h, 1, :], Vt[:, h * 64:(h + 1) * 64], False, True, [NAd, rkvsd[2]],
                            [psd[bR + h % 2]])
                self.cp("act", RU[:, 0, :].rearrange("p (h a i) -> p h a i", h=8, a=2),
                        ps2(bR).rearrange("p (a h i) -> p h a i", a=2, h=8), pd2(bR), [RUd[0]])
                bU = nb2()
                for h in range(16):
                    o = ps[:, bU + h // 8, (h % 8) * 64:(h % 8) * 64 + 64]
                    self.mm(o, X[:, h, :], RU[:, 0, h * 64:(h + 1) * 64], True, True, [Xd, RUd[0]], [psd[bU + h // 8]])
                self.cp("dve", RU[:, 1, :], ps2(bU), pd2(bU), [RUd[1]])
                bY = nb2()
                for h in range(16):
                    hp, pr = h // 2, slice((h % 2) * 64, (h % 2) * 64 + 64)
                    o = ps[:, bY + h % 2, hp * 64:hp * 64 + 64]
                    self.mm(o, ARc[pr, hp, 1, :], Sb[pr, hp, :], True, False, [cmd, Sbd], [psd[bY + h % 2]])
                    self.mm(o, NA3[:, h, 0, :], RU[:, 1, h * 64:(h + 1) * 64], False, False, [NAd, RUd[1]],
                            [psd[bY + h % 2]])
                    self.mm(o, NA3[:, h, 2, :], Vt[:, h * 64:(h + 1) * 64], False, True, [NAd, rkvsd[2]],
                            [psd[bY + h % 2]])
                yv = self.x[:, n, :].rearrange("p (h a i) -> p h a i", h=8, a=2)
                pyv = ps2(bY).rearrange("p (a h i) -> p h a i", a=2, h=8)
                if d == 0:
                    self.cp("act", yv, pyv, pd2(bY), [xd[n]])
                else:
                    self.tt("dve", yv, yv, pyv, ALU.add, pd2(bY) + [xd[n]], [xd[n]])
                bS = nb()
                for h in range(16):
                    hp, pr = h // 2, slice((h % 2) * 64, (h % 2) * 64 + 64)
                    o = ps[pr, bS, hp * 64:(hp + 1) * 64]
                    self.mm(o, Bt[:, 1, h * 64:(h + 1) * 64], RU[:, 1, h * 64:(h + 1) * 64], True, False,
                            [Btd[1], RUd[1]], [psd[bS]])
                    self.mm(o, Bt[:, 2, h * 64:(h + 1) * 64], Vt[:, h * 64:(h + 1) * 64], False, True,
                            [Btd[2], rkvsd[2]], [psd[bS]])
                self.tt("dve", S32[:], S32[:], ps[:, bS, :].rearrange("p (h i) -> p h i", h=8), ALU.add,
                        [psd[bS], S32d], [S32d])
                self.tt("dve", S32[:], S32[:], WL[:].unsqueeze(2).to_broadcast([128, 8, 64]), ALU.mult,
                        [S32d, WLd], [S32d])
                self.cp("act", Sb[:], S32[:], [S32d], [Sbd])

            for d in range(2):
                self.memset("pool", S32[:], 0.0, [S32d])
                self.memset("pool", Sb[:], 0.0, [Sbd])
                order = list(range(NT)) if d == 0 else list(range(NT - 1, -1, -1))
                if stop:
                    order = order[:nbt]
                for n in order:
                    scan_tile(d, n)
        S.barrier()
        if stop == "BY":
            return
        if stop == "B":
            for t in range(NT):
                self.load(self.x[:, t, :], self.xsp[t], [self.xspd[t]], [xd[t]])
            return
        with contextlib.ExitStack() as st:
            al = lambda n, sh_, dt: st.enter_context(self.sbt(n, sh_, dt))
            onesf = al("rwc_onesf", [128, 128], F32)
            brow = al("rwc_brow", [128, 2048], F32)
            bc4 = al("rwc_bc4", [128, 4, 1024], F32)
            w2a2 = al("rwc_w2a2", [128, 2, 1024], BF16)
            g2 = al("rwc_g2", [128, 1024], BF16)
            Wo = al("rwc_Wo", [128, KC, 1024], BF16)
            rkv = al("rwc_rkv", [128, 3, 1024], BF16)
            xt = al("rwc_xt", [128, 1024], F32)
            F = al("rwc_F", [128, 5, 1024], F32)
            ob = al("rwc_ob", [128, 1024], BF16)
            oT = al("rwc_oT", [128, KC, 128], BF16)
            small = al("rwc_small", [128, 48], F32)
            cstd, xtd, obd, oTd, smd = (Dep() for _ in range(5))
            rkvsd = [Dep(), Dep(), Dep()]
            Fd = [Dep() for _ in range(5)]
            self.load(onesf[:], self.cf_d[:, cfo["onesf"]:cfo["onesf"] + 128], (), [cstd])
            self.load(brow[:], self.cf_d[:, cfo["brow"]:cfo["brow"] + 2048], (), [cstd])
            for i, nm in enumerate(("d_k_a", "d_r_k", "d_ln_g", "d_ln_b")):
                self.load(bc4[:, i, :], self.cf_d[:, cfo[nm]:cfo[nm] + 1024], (), [cstd])
            self.load(w2a2[:, 0, :], self.wview("d_w2s"), [self.wbf_dep], [cstd])
            self.load(w2a2[:, 1, :], self.wview("d_a2s"), [self.wbf_dep], [cstd])
            self.load(g2[:], self.wview("d_g2"), [self.wbf_dep], [cstd])
            self.load(Wo[:], self.wview("d_w_o").rearrange("k p n -> p k n"), [self.wbf_dep], [cstd])
            v3 = lambda ap: ap.rearrange("p (h j) -> p h j", h=16)
            bc16 = lambda ap: ap.unsqueeze(2).to_broadcast([128, 16, 64])
            for n in range(NT):
                for i in range(3):
                    self.load(rkv[:, i, :], self.rkvsp[i, n], [self.rkvd[i][n]], [rkvsd[i]])
                self.load(xt[:], self.xsp[n], [self.xspd[n]], [xtd])
                F0, F1, F2, F3, F4 = (F[:, i, :] for i in range(5))
                for d in range(2):
                    prd = slice(d * 64, (d + 1) * 64)
                    rowp = d * 64
                    b2 = nb2()
                    for nh in range(2):
                        self.mm(ps[:, b2 + nh, :], lT[prd, 1, n * 128:(n + 1) * 128],
                                w2a2[prd, 1, nh * 512:(nh + 1) * 512], True, False, [lTd, cstd], [psd[b2 + nh]])
                        self.mm(ps[:, b2 + nh, :], onesf[rowp:rowp + 1, :],
                                brow[rowp:rowp + 1, 1024 + nh * 512:1024 + (nh + 1) * 512], False, True, [cstd],
                                [psd[b2 + nh]])
                    self.act(F[:, d, :], ps2(b2), AF.Sigmoid, pd2(b2), [Fd[d]])
                bG = nb2()
                for nh in range(2):
                    self.mm(ps[:, bG + nh, :], lT[:, 2, n * 128:(n + 1) * 128], g2[:, nh * 512:(nh + 1) * 512],
                            True, True, [lTd, cstd], [psd[bG + nh]])
                self.cp("act", F2, ps2(bG), pd2(bG), [Fd[2]])
                y = self.x[:, n, :]
                reduce_x(small[:, 0:16], v3(y), [xd[n]], [smd])
                self.ts("dve", small[:, 0:16], small[:, 0:16], 1.0 / 64, None, ALU.mult, None, [smd], [smd])
                self.tt("dve", v3(F3), v3(y), bc16(small[:, 0:16]), ALU.subtract, [xd[n], smd], [Fd[3]])
                self.tt("pool", F4, F3, F3, ALU.mult, [Fd[3]], [Fd[4]])
                reduce_x(small[:, 16:32], v3(F4), [Fd[4]], [smd])
                self.act(small[:, 16:32], small[:, 16:32], AF.Sqrt, [smd, self.wsd], [smd], scale=1.0 / 64,
                         bias=self.wsc("epsgn"))
                recip(small[:, 16:32], [smd], [smd])
                self.tt("dve", v3(F3), v3(F3), bc16(small[:, 16:32]), ALU.mult, [Fd[3], smd], [Fd[3]])
                self.tt("pool", F3, F3, bc4[:, 2, :], ALU.mult, [Fd[3], cstd], [Fd[3]])
                self.tt("pool", F3, F3, bc4[:, 3, :], ALU.add, [Fd[3], cstd], [Fd[3]])
                self.tt("dve", F0, F0, F1, ALU.add, [Fd[0], Fd[1]], [Fd[0]])
                self.ts("dve", F0, F0, 0.5, -1.0, ALU.mult, ALU.add, [Fd[0]], [Fd[0]])
                self.tt("dve", F0, F0, bc4[:, 0, :], ALU.mult, [Fd[0], cstd], [Fd[0]])
                self.stt(F0, F0, 1.0, rkv[:, 1, :], ALU.add, ALU.mult, [Fd[0], rkvsd[1]], [Fd[0]])
                self.tt("dve", F0, F0, rkv[:, 0, :], ALU.mult, [Fd[0], rkvsd[0]], [Fd[0]])
                self.tt("pool", F0, F0, bc4[:, 1, :], ALU.mult, [Fd[0], cstd], [Fd[0]])
                reduce_x(small[:, 32:48], v3(F0), [Fd[0]], [smd])
                self.tt("dve", v3(F4), v3(rkv[:, 2, :]), bc16(small[:, 32:48]), ALU.mult, [rkvsd[2], smd], [Fd[4]])
                self.tt("pool", F3, F3, F4, ALU.add, [Fd[3], Fd[4]], [Fd[3]])
                self.tt("dve", ob[:], F3, F2, ALU.mult, [Fd[3], Fd[2]], [obd])
                b = nb()
                pst = ps[:, b, :].bitcast(BF16)
                for kc in range(KC):
                    self.tr(pst[:, kc * 128:(kc + 1) * 128], ob[:, kc * 128:(kc + 1) * 128], [obd, self.cbd], [psd[b]])
                self.cp("act", oT[:], pst.rearrange("p (k t) -> p k t", k=KC), [psd[b]], [oTd])
                bO = nb2()
                for nh in range(2):
                    for kc in range(KC):
                        self.mm(ps[:, bO + nh, :], oT[:, kc, :], Wo[:, kc, nh * 512:(nh + 1) * 512], kc == 0,
                                kc == KC - 1, [oTd, cstd], [psd[bO + nh]])
                self.tt("dve", self.x[:, n, :], xt[:], ps2(bO), ALU.add, pd2(bO) + [xtd, xd[n]], [xd[n]])
    S.barrier()


Prog.mixer3 = _rwkv
```

```python
import math
import contextlib
import numpy as np
import ml_dtypes
import concourse.bass as bass
import concourse.mybir as mybir
from concourse.bass_utils import run_bass_kernel_spmd

F32 = mybir.dt.float32
BF16 = mybir.dt.bfloat16
AF = mybir.ActivationFunctionType
ALU = mybir.AluOpType
AX = mybir.AxisListType

D = 1024
S_LEN = 2048
NT = 16
KC = 8
FH = 2816
NCH = 22
N_CORES = 8
SEQ_PER_CORE = 5
NORM_EPS = 1e-6


class Dep:
    __slots__ = ("w", "r", "name")

    def __init__(self, name=""):
        self.w = None
        self.r = {}
        self.name = name


class Sch:
    COMPUTE = ("pe", "dve", "act", "pool")

    def __init__(self, nc, n_sp_ring=40, n_pool_ring=8, same_eng_sync=True):
        self.nc = nc
        self.q = {e: [] for e in ("pe", "dve", "act", "pool", "sp")}
        self.sems = {}
        self.cnt = {}
        for e in self.COMPUTE:
            self.sems[e] = nc.alloc_semaphore(f"s_{e}")
            self.cnt[e] = 0
        self.ring = {}
        for qn, n in (("sp", n_sp_ring), ("pool", n_pool_ring)):
            self.ring[qn] = dict(n=n, i=0)
            for i in range(n):
                self.sems[(qn, i)] = nc.alloc_semaphore(f"d_{qn}{i}")
        self.seen = {e: {} for e in self.q}
        self.same_eng_sync = same_eng_sync
        self.nops = 0

    def _collect(self, eng, reads, writes, is_dma):
        need = {}

        def add(k, v, peng):
            if peng == eng and not is_dma and k in self.COMPUTE:
                if eng == "pe" or not self.same_eng_sync:
                    return
            if need.get(k, 0) < v:
                need[k] = v
        for d in reads:
            if d.w is not None:
                add(*d.w)
        for d in writes:
            if d.w is not None:
                add(*d.w)
            for k, (v, peng) in d.r.items():
                add(k, v, peng)
        out = []
        seen = self.seen[eng]
        for k, v in need.items():
            if seen.get(k, 0) < v:
                seen[k] = v
                out.append((k, v))
        return out

    def op(self, eng, fn, reads=(), writes=()):
        waits = self._collect(eng, reads, writes, False)
        self.cnt[eng] += 1
        v = self.cnt[eng]
        for d in reads:
            d.r[eng] = (v, eng)
        for d in writes:
            d.w = (eng, v, eng)
            d.r = {}
        self.q[eng].append((waits, fn, (eng, 1)))
        self.nops += 1

    def dma(self, qn, fn, reads=(), writes=()):
        ring = self.ring[qn]
        i = ring["i"]
        ring["i"] += 1
        n = ring["n"]
        slot, gen = i % n, i // n
        key = (qn, slot)
        waits = self._collect(qn, reads, writes, True)
        if gen > 0:
            seen = self.seen[qn]
            if seen.get(key, 0) < 16 * gen:
                seen[key] = 16 * gen
                waits.append((key, 16 * gen))
        v = 16 * (gen + 1)
        for d in reads:
            d.r[key] = (v, "dma")
        for d in writes:
            d.w = (key, v, "dma")
            d.r = {}
        self.q[qn].append((waits, fn, (key, 16)))
        self.nops += 1

    def _all_marks(self):
        marks = [(e, self.cnt[e]) for e in self.COMPUTE if self.cnt[e] > 0]
        for rq, ring in self.ring.items():
            n = ring["n"]
            for slot in range(n):
                c = (ring["i"] - slot + n - 1) // n
                if c > 0:
                    marks.append(((rq, slot), 16 * c))
        return marks

    def barrier(self):
        marks = self._all_marks()
        for e in self.q:
            waits = []
            seen = self.seen[e]
            for k, v in marks:
                if k == e:
                    continue
                if seen.get(k, 0) < v:
                    seen[k] = v
                    waits.append((k, v))
            if waits:
                self.q[e].append((waits, None, None))

    def final_wait(self, qn="pool"):
        waits = [(k, v) for k, v in self._all_marks() if k != qn]
        self.q[qn].append((waits, None, None))

    def emit(self):
        nc = self.nc
        sems = self.sems

        def replay(eng_obj, lst):
            for waits, fn, inc in lst:
                for k, v in waits:
                    eng_obj.wait_ge(sems[k], v)
                if fn is not None:
                    ins = fn(eng_obj)
                    ins.then_inc(sems[inc[0]], inc[1])

        with nc.Block() as block:
            @block.tensor
            def _(e):
                replay(e, self.q["pe"])

            @block.vector
            def _(e):
                replay(e, self.q["dve"])

            @block.scalar
            def _(e):
                replay(e, self.q["act"])

            @block.gpsimd
            def _(e):
                replay(e, self.q["pool"])

            @block.sync
            def _(e):
                replay(e, self.q["sp"])


def colchunk(w):
    K, N = w.shape
    return np.ascontiguousarray(w.reshape(K // 128, 128, N // 128, 128).transpose(2, 1, 0, 3))


class Packer:
    def __init__(self):
        self.parts = []
        self.off = {}
        self.n = 0

    def add(self, name, arr):
        a = np.ascontiguousarray(arr, dtype=np.float32).reshape(-1)
        self.off[name] = (self.n, arr.shape)
        self.parts.append(a)
        self.n += a.size

    def finish(self, align):
        pad = (-self.n) % align
        if pad:
            self.parts.append(np.zeros(pad, np.float32))
            self.n += pad
        return np.concatenate(self.parts)


class ColPacker:
    def __init__(self):
        self.parts = []
        self.off = {}
        self.n = 0

    def add(self, name, arr):
        a = np.ascontiguousarray(arr, dtype=np.float32).reshape(128, -1)
        self.off[name] = self.n
        self.parts.append(a)
        self.n += a.shape[1]

    def finish(self):
        return np.ascontiguousarray(np.concatenate(self.parts, axis=1))


def chan8(v):
    return np.ascontiguousarray(v.reshape(8, 128).T)


def rep(v):
    return np.ascontiguousarray(np.broadcast_to(v.reshape(1, -1), (128, v.size)))


def build_rope_tables():
    out = np.zeros((4, 128, S_LEN), np.float32)
    t = np.arange(S_LEN, dtype=np.float32)
    half = 32
    inv = (10000.0 ** (-np.arange(half, dtype=np.float32) / half)).astype(np.float32)
    ang = t[None, :] * inv[:, None]
    c0 = np.cos(ang).astype(np.float32)
    s0 = np.sin(ang).astype(np.float32)
    for p in range(128):
        i = (p % 64) % half
        out[0, p] = c0[i]
        out[1, p] = s0[i]
    half = 16
    inv = (10000.0 ** (-np.arange(half, dtype=np.float32) / half)).astype(np.float32)
    row = np.floor(t / 64.0).astype(np.float32)
    col = (t - row * 64.0).astype(np.float32)
    for p in range(128):
        dd = p % 64
        pos = row if dd < 32 else col
        i = (dd % 32) % half
        ang = pos * inv[i]
        out[2, p] = np.cos(ang).astype(np.float32)
        out[3, p] = np.sin(ang).astype(np.float32)
    return out


def rot_matrix_T(block):
    R = np.zeros((128, 128), np.float32)
    half = block // 2
    for p in range(128):
        i = p % block
        if i < half:
            R[p, p + half] = -1.0
        else:
            R[p, p - half] = 1.0
    return np.ascontiguousarray(R.T)


class Prog:
    def __init__(self, nseq, wbig_n, wbig_off, ws_n, ws_off, cb_n, cb_off, cf_n, cf_off,
                 mixers=(0, 1, 2, 3), ffn=True, nlayers=4):
        self.nseq = nseq
        self.mixers = mixers
        self.do_ffn = ffn
        self.nlayers = nlayers
        nc = bass.Bass("TRN2", target_bir_lowering=False)
        self.nc = nc
        self._uniq = 0
        self.S = Sch(nc)
        self.wo, self.so, self.cbo, self.cfo = wbig_off, ws_off, cb_off, cf_off
        self.x_in = nc.dram_tensor("x", [nseq, S_LEN, D], F32, kind="ExternalInput").ap()
        self.y_out = nc.dram_tensor("y", [nseq, S_LEN, D], F32, kind="ExternalOutput").ap()
        self.wbig = nc.dram_tensor("wbig", [wbig_n], F32, kind="ExternalInput").ap()
        self.wbf = nc.dram_tensor("wbf", [wbig_n], BF16, kind="Internal").ap()
        self.wsmall_d = nc.dram_tensor("wsmall", [128, ws_n], F32, kind="ExternalInput").ap()
        self.cbf_d = nc.dram_tensor("cbf", [128, cb_n], BF16, kind="ExternalInput").ap()
        self.cf_d = nc.dram_tensor("cf", [128, cf_n], F32, kind="ExternalInput").ap()
        self.rope_d = nc.dram_tensor("rope", [4, 128, S_LEN], BF16, kind="ExternalInput").ap()
        self.wbf_dep = Dep("wbf")
        self.wbf_blocks = [Dep() for _ in range(wbig_n // (128 * 2048))]
        self._wdeps_acc = []
        self.xsp = nc.dram_tensor("xsp", [NT, 128, D], F32, kind="Internal").ap()
        self.xspd = [Dep() for _ in range(NT)]
        self.rkvsp = nc.dram_tensor("rkvsp", [3, NT, 128, D], BF16, kind="Internal").ap()
        self.rkvd = [[Dep() for _ in range(NT)] for _ in range(3)]
        self.x = nc.alloc_sbuf_tensor("xres", [128, NT, D], F32)
        self.xd = [Dep(f"x{t}") for t in range(NT)]
        self.hT = nc.alloc_sbuf_tensor("hT", [128, KC, S_LEN], BF16)
        self.hTd = [Dep(f"hT{t}") for t in range(NT)]
        self.ws = nc.alloc_sbuf_tensor("ws", [128, ws_n], F32)
        self.wsd = Dep("ws")
        self.cb = nc.alloc_sbuf_tensor("cb", [128, cb_n], BF16)
        self.cbd = Dep("cb")
        self.ps = nc.alloc_psum_tensor("ps", [128, 8, 512], F32)
        self.psd = [Dep(f"ps{b}") for b in range(8)]
        self.ident = self.cb[:, cb_off["ident"]:cb_off["ident"] + 128]

    def sbt(self, name, shape, dt):
        self._uniq += 1
        return self.nc.sbuf_tensor(f"{name}_{self._uniq}", shape, dt)

    def mm(self, out, lhsT, rhs, start, stop, r, w):
        self.S.op("pe", lambda e: e.matmul(out, lhsT, rhs, start=start, stop=stop,
                                           skip_group_check=True), r, w)

    def tr(self, out, in_, r, w, ident=None):
        idn = self.ident if ident is None else ident
        self.S.op("pe", lambda e: e.transpose(out, in_, idn), r, w)

    def act(self, out, in_, func, r, w, scale=1.0, bias=None, accum=None):
        def f(e):
            kw = {}
            if bias is not None:
                kw["bias"] = bias
            if accum is not None:
                kw["accum_out"] = accum
            return e.activation(out=out, in_=in_, func=func, scale=scale, **kw)
        self.S.op("act", f, r, w)

    def ts(self, eng, out, in0, s1, s2, op0, op1, r, w):
        def f(e):
            if op1 is None:
                return e.tensor_scalar(out=out, in0=in0, scalar1=s1, scalar2=None, op0=op0)
            return e.tensor_scalar(out=out, in0=in0, scalar1=s1, scalar2=s2, op0=op0, op1=op1)
        self.S.op(eng, f, r, w)

    def tt(self, eng, out, in0, in1, op, r, w):
        self.S.op(eng, lambda e: e.tensor_tensor(out=out, in0=in0, in1=in1, op=op), r, w)

    def stt(self, out, in0, scalar, in1, op0, op1, r, w):
        self.S.op("dve", lambda e: e.scalar_tensor_tensor(out=out, in0=in0, scalar=scalar, in1=in1,
                                                          op0=op0, op1=op1), r, w)

    def cp(self, eng, out, in_, r, w):
        if eng == "act":
            self.S.op("act", lambda e: e.copy(out=out, in_=in_), r, w)
        else:
            self.S.op(eng, lambda e: e.tensor_copy(out=out, in_=in_), r, w)

    def memset(self, eng, ap, val, w):
        self.S.op(eng, lambda e: e.memset(ap, val), (), w)

    def load(self, out, in_, r, w, q="sp"):
        r = list(r)
        if self.wbf_dep in r:
            r.remove(self.wbf_dep)
            r += list(self._wdeps_acc)
        self.S.dma(q, lambda e: e.dma_start(out=out, in_=in_), r, w)

    def wsc(self, name, j=0, n=1):
        o = self.so[name] + j
        return self.ws[:, o:o + n]

    def setup(self):
        S = self.S
        self.load(self.ws[:], self.wsmall_d, (), [self.wsd])
        self.load(self.cb[:], self.cbf_d, (), [self.cbd])
        n = self.wbig.shape[0]
        blk = 128 * 2048
        src = self.wbig.rearrange("(b p f) -> b p f", p=128, f=2048)
        dst = self.wbf.rearrange("(b p f) -> b p f", p=128, f=2048)
        for b in range(n // blk):
            self.load(dst[b], src[b], (), [self.wbf_blocks[b]], q="pool")

    def wview(self, name):
        off, shape = self.wo[name]
        n = int(np.prod(shape))
        blk = 128 * 2048
        for d_ in self.wbf_blocks[off // blk:(off + n - 1) // blk + 1]:
            if d_ not in self._wdeps_acc:
                self._wdeps_acc.append(d_)
        flat = self.wbf[off:off + n]
        if len(shape) == 3:
            return flat.rearrange("(a b c) -> a b c", b=shape[1], c=shape[2])
        if len(shape) == 4:
            return flat.rearrange("(a b c d) -> a b c d", b=shape[1], c=shape[2], d=shape[3])
        return flat.rearrange("(a b) -> a b", b=shape[1])

    def load_x(self, s):
        for t in range(NT):
            self.load(self.x[:, t, :], self.x_in[s, t * 128:(t + 1) * 128, :], (), [self.xd[t]])

    def store_x(self, s):
        for t in range(NT):
            self.load(self.y_out[s, t * 128:(t + 1) * 128, :], self.x[:, t, :], [self.xd[t]], [Dep()], q="pool")

    def rmsnorm_T(self, gname, st):
        nc = self.nc
        junk = st.enter_context(self.sbt("nrm_junk", [128, D], BF16))
        hn = st.enter_context(self.sbt("nrm_hn", [128, 2, D], BF16))
        ss = st.enter_context(self.sbt("nrm_ss", [128, 2, 2], F32))
        junkd = Dep()
        hnd = [Dep(), Dep()]
        ssd = [Dep(), Dep()]
        g = self.wsc(gname, 0, 8)
        for t in range(NT):
            b = t % 2
            self.act(junk[:], self.x[:, t, :], AF.Square, [self.xd[t]], [junkd, ssd[b]],
                     accum=ss[:, b, 0:1])
            self.act(ss[:, b, 1:2], ss[:, b, 0:1], AF.Sqrt, [ssd[b]], [ssd[b]],
                     scale=1.0 / D, bias=self.wsc("eps6"))
            self.S.op("dve", (lambda b=b: (lambda e: e.reciprocal(out=ss[:, b, 1:2], in_=ss[:, b, 1:2])))(),
                      [ssd[b]], [ssd[b]])
            self.ts("dve", hn[:, b, :], self.x[:, t, :], ss[:, b, 1:2], None, ALU.mult, None,
                    [self.xd[t], ssd[b]], [hnd[b]])
            pb = 6 + (t % 2)
            pst = self.ps[:, pb, :].bitcast(BF16)
            for kc in range(KC):
                self.tr(pst[:, kc * 128:(kc + 1) * 128], hn[:, b, kc * 128:(kc + 1) * 128],
                        [hnd[b], self.cbd], [self.psd[pb]])
            self.tt("dve", self.hT[:, :, t * 128:(t + 1) * 128],
                    pst.rearrange("p (k c) -> p k c", k=KC),
                    g.unsqueeze(2).to_broadcast([128, KC, 128]), ALU.mult,
                    [self.psd[pb], self.wsd], [self.hTd[t]])

    def ffn(self, L):
        nc, S = self.nc, self.S
        S.barrier()
        with contextlib.ExitStack() as st0:
            self.rmsnorm_T(f"n{L}_ffn", st0)
            st = st0
            wi = st.enter_context(self.sbt("ffn_wi", [128, 2, 4, 1024], BF16))
            wo = st.enter_context(self.sbt("ffn_wo", [128, 2, 2, 1024], BF16))
            gT = st.enter_context(self.sbt("ffn_gT", [128, 2, 2, S_LEN], BF16))
            upad = st.enter_context(self.sbt("ffn_upad", [128, 2, 2, S_LEN + 2], F32))
            acc = st.enter_context(self.sbt("ffn_acc", [128, 2, S_LEN], F32))
            wid = [Dep(), Dep()]
            wod = [Dep(), Dep()]
            gTd = [Dep(), Dep()]
            upd = [[Dep(), Dep()], [Dep(), Dep()]]
            accd = [Dep(), Dep()]
            allh = list(self.hTd)
            for w_ in range(2):
                for b in range(2):
                    self.memset("pool", upad[:, w_, b, 0:1], 0.0, [upd[w_][b]])
                    self.memset("pool", upad[:, w_, b, S_LEN + 1:S_LEN + 2], 0.0, [upd[w_][b]])
            win = self.wview(f"f{L}_w_in")
            wout = self.wview(f"f{L}_w_out")
            cw = self.so[f"f{L}_conv_w"]
            cbo = self.so[f"f{L}_conv_b"]
            NP = NCH // 2
            ubuf = 0

            def out_partial(p, tiles):
                pb_ = p % 2
                for t in tiles:
                    bank = 4 + 2 * (t % 2)
                    for nh in range(2):
                        for j in range(2):
                            self.mm(self.ps[:, bank + nh, :], gT[:, pb_, j, t * 128:(t + 1) * 128],
                                    wo[:, pb_, j, nh * 512:(nh + 1) * 512], j == 0, j == 1,
                                    [gTd[pb_], wod[pb_]], [self.psd[bank + nh]])
                    self.tt("dve", self.x[:, t, :], self.x[:, t, :],
                            self.ps[:, bank:bank + 2, :].rearrange("p a b -> p (a b)"), ALU.add,
                            [self.psd[bank], self.psd[bank + 1], self.xd[t]], [self.xd[t]])

            for p in range(NP + 1):
                if p < NP:
                    pb = p % 2
                    self.load(wi[:, pb, 0:2, :],
                              win[2 * p:2 * p + 2].rearrange("a p k c -> p a (k c)"),
                              [self.wbf_dep], [wid[pb]])
                    self.load(wi[:, pb, 2:4, :],
                              win[NCH + 2 * p:NCH + 2 * p + 2].rearrange("a p k c -> p a (k c)"),
                              [self.wbf_dep], [wid[pb]])
                    self.load(wo[:, pb, :, :], wout[2 * p:2 * p + 2].rearrange("a p n -> p a n"),
                              [self.wbf_dep], [wod[pb]])
                for j in range(2):
                    if p < NP:
                        for which in range(2):
                            ci = 2 * p + j + which * NCH
                            ub = ubuf % 2
                            for half in range(2):
                                bank = 2 * half
                                for tq in range(2):
                                    c0 = half * 1024 + tq * 512
                                    for kc in range(KC):
                                        self.mm(self.ps[:, bank + tq, :],
                                                wi[:, pb, which * 2 + j, kc * 128:(kc + 1) * 128],
                                                self.hT[:, kc, c0:c0 + 512], kc == 0, kc == KC - 1,
                                                [wid[pb]] + allh[c0 // 128:c0 // 128 + 4],
                                                [self.psd[bank + tq]])
                                self.cp("act", upad[:, which, ub, 1 + half * 1024:1 + (half + 1) * 1024],
                                        self.ps[:, bank:bank + 2, :].rearrange("p a b -> p (a b)"),
                                        [self.psd[bank], self.psd[bank + 1]], [upd[which][ub]])
                            u = upad[:, which, ub, :]
                            a_ = acc[:, which, :]
                            self.act(a_, u[:, 0:S_LEN], AF.Identity, [upd[which][ub], self.wsd], [accd[which]],
                                     scale=self.ws[:, cw + ci * 3:cw + ci * 3 + 1],
                                     bias=self.ws[:, cbo + ci:cbo + ci + 1])
                            self.stt(a_, u[:, 1:S_LEN + 1], self.ws[:, cw + ci * 3 + 1:cw + ci * 3 + 2], a_,
                                     ALU.mult, ALU.add, [upd[which][ub], accd[which]], [accd[which]])
                            self.stt(a_, u[:, 2:S_LEN + 2], self.ws[:, cw + ci * 3 + 2:cw + ci * 3 + 3], a_,
                                     ALU.mult, ALU.add, [upd[which][ub], accd[which]], [accd[which]])
                        ubuf += 1
                        self.act(acc[:, 0, :], acc[:, 0, :], AF.Silu, [accd[0]], [accd[0]])
                        self.tt("dve", gT[:, pb, j, :], acc[:, 0, :], acc[:, 1, :], ALU.mult,
                                [accd[0], accd[1]], [gTd[pb]])
                    if p > 0:
                        out_partial(p - 1, range(j * 8, j * 8 + 8))

    def run_seq(self, s):
        self.load_x(s)
        if getattr(self, "debug", None) == "hT":
            with contextlib.ExitStack() as st:
                self.rmsnorm_T("n0_ffn", st)
            xv = self.x[:].rearrange("p a b -> p (a b)").rearrange("p (k t) -> p k t", k=KC)
            self.cp("act", xv, self.hT[:], self.hTd + self.xd, self.xd)
            self.store_x(s)
            return
        for L in range(self.nlayers):
            if L in self.mixers:
                getattr(self, f"mixer{L}")(L)
            if self.do_ffn:
                self.ffn(L)
        self.store_x(s)

    def build(self):
        self.setup()
        for s in range(self.nseq):
            self.run_seq(s)
        self.S.final_wait("pool")
        print("ops per engine", {e: len(v) for e, v in self.S.q.items()}, "cnt", self.S.cnt, flush=True)
        self.S.emit()
        return self.nc


def pack_params(p):
    big = Packer()
    sm = ColPacker()
    cb = ColPacker()
    cf = ColPacker()
    f32 = np.float32
    cb.add("ident", np.eye(128, dtype=f32))
    bo = np.zeros((128, 128), f32)
    bo[:64, :64] = 1.0 / 64
    bo[64:, 64:] = 1.0 / 64
    cb.add("blockmean", bo)
    cb.add("rotT64", rot_matrix_T(64))
    cb.add("rotT32", rot_matrix_T(32))
    cb.add("ones", np.ones((128, 128), f32))
    sm.add("eps6", np.full((128, 1), 1e-6, f32))
    sm.add("eps5", np.full((128, 1), 1e-5, f32))
    sm.add("epsgn", np.full((128, 1), 64e-5, f32))
    sm.add("zero", np.zeros((128, 1), f32))
    pack_mixers(p, big, sm, cb, cf)
    for L in range(4):
        sm.add(f"n{L}_attn", chan8(p[f"n{L}_attn"]))
        sm.add(f"n{L}_ffn", chan8(p[f"n{L}_ffn"]))
        cw = p[f"f{L}_conv_w"]
        sm.add(f"f{L}_conv_w", cw.T.reshape(44, 128, 3).transpose(1, 0, 2).reshape(128, 132))
        sm.add(f"f{L}_conv_b", p[f"f{L}_conv_b"].reshape(44, 128).T)
        big.add(f"f{L}_w_in", colchunk(p[f"f{L}_w_in"]))
        big.add(f"f{L}_w_out", p[f"f{L}_w_out"].reshape(22, 128, 1024))
    wbig = big.finish(128 * 2048)
    return (wbig, big.off, sm.finish(), sm.off, cb.finish().astype(ml_dtypes.bfloat16), cb.off,
            cf.finish(), cf.off)


def pack_mixers(p, big, sm, cb, cf):
    cf.add("dummy", np.zeros((128, 1), np.float32))
    f32 = np.float32
    idx64 = np.arange(128) % 64
    big.add("a_wqkv", colchunk(p["a_w_qkv"]))
    big.add("a_wo", p["a_w_o"].reshape(8, 128, 1024))
    sm.add("a_qg", p["a_q_norm"][idx64].reshape(128, 1))
    sm.add("a_kg", p["a_k_norm"][idx64].reshape(128, 1))
    sm.add("a_subln", rep(p["a_subln"]))
    for nm in ("a_lq1", "a_lk1", "a_lq2", "a_lk2"):
        sm.add(nm, rep(p[nm]))
    wb = p["b_w_qkv"]
    big.add("b_wqkv", colchunk(wb))
    kd = np.stack([np.concatenate([wb[:, 1024 + g * 64:1024 + (g + 1) * 64]] * 2, axis=1) for g in range(4)], 0)
    big.add("b_wkdup", np.stack([colchunk(kd[g])[0] for g in range(4)], 0))
    big.add("b_wo", p["b_w_o"].reshape(8, 128, 1024))
    sm.add("b_qg", p["b_q_norm"][idx64].reshape(128, 1))
    sm.add("b_kg", p["b_k_norm"][idx64].reshape(128, 1))
    big.add("c_wqkv", colchunk(p["c_w_qkv"]))
    big.add("c_wo", p["c_w_o"].reshape(8, 128, 1024))
    sm.add("c_qg", p["c_q_norm"][idx64].reshape(128, 1))
    sm.add("c_kg", p["c_k_norm"][idx64].reshape(128, 1))
    ho = np.zeros((128, 2), f32)
    ho[:64, 0] = -30000.0
    ho[64:, 1] = -30000.0
    sm.add("hoff", ho)
    rb = p["c_rel_bias"]
    pp = np.arange(128)
    j2 = pp // 64
    kc_ = pp % 64
    qq = np.arange(64)
    dc = np.clip(kc_[:, None] - qq[None, :] + 15, 0, 30)
    base = np.arange(14)
    dr = base[None, :] + j2[:, None]
    T = rb[:, dr[:, :, None], dc[:, None, :]]
    T = T.reshape(8, 2, 128, 14, 64).transpose(0, 2, 1, 3, 4)
    big.add("c_bias_f32", np.ascontiguousarray(T).reshape(8, 128, 2 * 14 * 64))
    cs_ = np.clip(qq - 8, 0, 48)
    inwin = (kc_[:, None] >= cs_[None, :]) & (kc_[:, None] < cs_[None, :] + 16)
    cf.add("nb_mask", np.where(inwin, 0.0, -30000.0).astype(f32))
    for nm in ("d_w_r", "d_w_k", "d_w_v", "d_w_o"):
        big.add(nm, p[nm].reshape(8, 128, 1024))
    big.add("d_w1s", np.concatenate([p["d_w1"][0], p["d_w1"][1]], axis=1).reshape(8, 128, 128))
    big.add("d_a1s", np.concatenate([p["d_a1"][0], p["d_a1"][1]], axis=1).reshape(8, 128, 128))
    big.add("d_g1", p["d_g1"].reshape(8, 128, 128))
    big.add("d_w2s", p["d_w2"].reshape(128, 1024))
    big.add("d_a2s", p["d_a2"].reshape(128, 1024))
    big.add("d_g2", p["d_g2"].reshape(128, 1024))
    sm.add("d_mu", np.concatenate([chan8(p["d_mu"][j]) for j in range(6)], axis=1))
    NEGC = -math.exp(-0.5)
    sm.add("negc", np.full((128, 2), NEGC, f32))
    ii = np.arange(128)
    le = (ii[:, None] <= ii[None, :]).astype(f32)
    lt = (ii[:, None] < ii[None, :]).astype(f32)
    ge = (ii[:, None] >= ii[None, :]).astype(f32)
    gt = (ii[:, None] > ii[None, :]).astype(f32)
    cf.add("tri", np.concatenate([le * NEGC, lt * NEGC, ge * NEGC, gt * NEGC], axis=1))
    cf.add("onesf", np.ones((128, 128), f32))
    cf.add("identf", np.eye(128, dtype=f32))
    lv = [(ii[:, None] // 8 == ii[None, :] // 8).astype(f32)]
    for bsz in (16, 32, 64, 128):
        lv.append(((ii[:, None] // bsz == ii[None, :] // bsz) & (ii[:, None] // (bsz // 2) != ii[None, :] // (bsz // 2))).astype(f32))
    cf.add("lvlmask", np.concatenate(lv, axis=1))
    cb.add("masks", np.concatenate([lt, le, gt, ge], axis=1))
    brow = np.zeros((128, 2048), f32)
    brow[0, :1024] = p["d_w0"][0]
    brow[64, :1024] = p["d_w0"][1]
    brow[0, 1024:] = p["d_a0"][0]
    brow[64, 1024:] = p["d_a0"][1]
    cf.add("brow", brow)
    cf.add("d_k_k", rep(p["d_k_k"]))
    cf.add("d_k_a", rep(p["d_k_a"]))
    cf.add("d_r_k", rep(p["d_r_k"].reshape(-1)))
    cf.add("d_ln_g", rep(p["d_ln_g"]))
    cf.add("d_ln_b", rep(p["d_ln_b"]))


_ROPE = None


def run(inputs, nseq_per_core=SEQ_PER_CORE, n_cores=N_CORES, **kw):
    global _ROPE
    xs = np.concatenate([np.asarray(inputs["x_prompt"]), np.asarray(inputs["x_sample"])], axis=0)
    p = {k: np.asarray(v) for k, v in inputs.items() if not k.startswith("x_")}
    wbig, woff, ws, soff, cbv, cboff, cfv, cfoff = pack_params(p)
    if _ROPE is None:
        _ROPE = build_rope_tables()
    prog = Prog(nseq_per_core, wbig.size, woff, ws.shape[1], soff, cbv.shape[1], cboff,
                cfv.shape[1], cfoff, **kw)
    nc = prog.build()
    in_maps = []
    for c in range(n_cores):
        in_maps.append({
            "x": np.ascontiguousarray(xs[c * nseq_per_core:(c + 1) * nseq_per_core]),
            "wbig": wbig, "wsmall": ws, "cbf": cbv, "cf": cfv, "rope": _ROPE.astype(ml_dtypes.bfloat16),
        })
    res = run_bass_kernel_spmd(nc, in_maps, core_ids=list(range(n_cores)))
    return np.concatenate([r["y"] for r in res.results], axis=0)


def kernel(**inputs):
    y = run(inputs)
    nb = np.asarray(inputs["x_prompt"]).shape[0]
    return (np.ascontiguousarray(y[:nb]), np.ascontiguousarray(y[nb:]))


def _attn_full(self, L):
    nc, S = self.nc, self.S
    diff = (L == 0)
    pre = "a" if diff else "b"
    E = 128 if diff else 64
    EW = 130 if diff else 66
    lam_init = 0.8 - 0.6 * math.exp(-0.3 * L)
    S.barrier()
    with contextlib.ExitStack() as st0:
        self.rmsnorm_T(f"n{L}_attn", st0)
        st = st0
        al = lambda n, sh, dt: st.enter_context(self.sbt(n, sh, dt))
        wq = al("at_wq", [128, 3, 1024], BF16)
        cs = al("at_cs", [128, 2, S_LEN], BF16)
        qT = al("at_qT", [128, 2, S_LEN], BF16)
        kT = al("at_kT", [128, 2, S_LEN], BF16)
        V = al("at_V", [128, 2, NT, EW], BF16)
        OT = al("at_OT", [128, KC, S_LEN], BF16)
        PT = al("at_PT", [128, 2, 512], BF16)
        sq = al("at_sq", [128, 512], BF16)
        qg = al("at_qg", [128, 512], BF16)
        t1 = al("at_t1", [128, 512], F32)
        t2 = al("at_t2", [128, 512], F32)
        t3 = al("at_t3", [128, 512], F32)
        oacc = al("at_oacc", [128, 4, 128], F32)
        osq = al("at_osq", [128, 4, 128], F32)
        osb = al("at_osb", [128, 4, 128], BF16)
        sm_ = al("at_sm", [128, 16], F32)
        wqd = [Dep(), Dep(), Dep()]
        csd = Dep()
        qTd = [Dep(), Dep()]
        kTd = [Dep(), Dep()]
        Vd = [Dep(), Dep()]
        OTd = Dep()
        PTd = [Dep(), Dep()]
        sqd, qgd, t1d, t2d, t3d, oaccd, osqd, osbd, smd = (Dep() for _ in range(9))
        allh = list(self.hTd)
        psd = self.psd
        ps = self.ps
        self.load(cs[:], self.rope_d[2 * L:2 * L + 2].rearrange("a p t -> p a t"), (), [csd])
        for b in range(2):
            self.memset("pool", V[:, b, :, E:E + 1], 1.0, [Vd[b]])
        cbo = self.cbo
        blockmean = self.cb[:, cbo["blockmean"]:cbo["blockmean"] + 128]
        rotT = self.cb[:, cbo["rotT64" if diff else "rotT32"]:cbo["rotT64" if diff else "rotT32"] + 128]
        wqkv = self.wview(f"{pre}_wqkv")
        if diff:
            for i, (a_, b_) in enumerate((("a_lq1", "a_lk1"), ("a_lq2", "a_lk2"))):
                self.tt("dve", t1[:, 0:64], self.wsc(a_, 0, 64), self.wsc(b_, 0, 64), ALU.mult, [self.wsd], [t1d])
                self.S.op("dve", (lambda i=i: lambda e: e.tensor_reduce(out=sm_[:, 8 + i:9 + i], in_=t1[:, 0:64],
                                                                       axis=AX.X, op=ALU.add))(), [t1d], [smd])
            self.act(sm_[:, 8:10], sm_[:, 8:10], AF.Exp, [smd], [smd])
            self.tt("dve", sm_[:, 10:11], sm_[:, 8:9], sm_[:, 9:10], ALU.subtract, [smd], [smd])
            self.ts("dve", sm_[:, 11:12], sm_[:, 10:11], lam_init, -1.0, ALU.add, ALU.mult, [smd], [smd])
        neglam = sm_[:, 11:12]

        def load_w(hc):
            if diff:
                idx = (hc, 8 + hc, 16 + hc)
            else:
                idx = (hc, None, 10 + hc // 4)
            for i, ci in enumerate(idx):
                if ci is None:
                    src = self.wview("b_wkdup")[hc // 2]
                else:
                    src = wqkv[ci]
                if (not diff) and i > 0 and hc % 2 == 1:
                    continue
                self.load(wq[:, i, :], src.rearrange("p k c -> p (k c)"), [self.wbf_dep], [wqd[i]])

        def proj_qk(hc, which, tc):
            b = hc % 2 if (diff or which == 0) else (hc // 2) % 2
            dst, dstd = (qT, qTd) if which == 0 else (kT, kTd)
            gain = self.wsc(f"{pre}_qg" if which == 0 else f"{pre}_kg")
            c0 = tc * 512
            for kc in range(KC):
                self.mm(ps[:, 4, :], wq[:, which, kc * 128:(kc + 1) * 128], self.hT[:, kc, c0:c0 + 512],
                        kc == 0, kc == KC - 1, [wqd[which]] + allh[tc * 4:tc * 4 + 4], [psd[4]])
            self.act(sq[:], ps[:, 4, :], AF.Square, [psd[4]], [sqd])
            self.act(qg[:], ps[:, 4, :], AF.Identity, [psd[4], self.wsd], [qgd], scale=gain)
            self.mm(ps[:, 5, :], blockmean, sq[:], True, True, [sqd, self.cbd], [psd[5]])
            self.mm(ps[:, 6, :], rotT, qg[:], True, True, [qgd, self.cbd], [psd[6]])
            self.act(t1[:], ps[:, 5, :], AF.Sqrt, [psd[5], self.wsd], [t1d], bias=self.wsc("eps6"))
            self.S.op("dve", lambda e: e.reciprocal(out=t1[:], in_=t1[:]), [t1d], [t1d])
            self.tt("dve", t2[:], qg[:], cs[:, 0, c0:c0 + 512], ALU.mult, [qgd, csd], [t2d])
            self.tt("dve", t3[:], ps[:, 6, :], cs[:, 1, c0:c0 + 512], ALU.mult, [psd[6], csd], [t3d])
            self.tt("pool", t2[:], t2[:], t3[:], ALU.add, [t2d, t3d], [t2d])
            self.tt("dve", dst[:, b, c0:c0 + 512], t2[:], t1[:], ALU.mult, [t2d, t1d], [dstd[b]])

        def proj_v(hc, tg):
            b = hc % 2 if diff else (hc // 2) % 2
            voff = 0 if diff else ((hc // 2) % 2) * 64
            for i in range(4):
                t = tg * 4 + i
                for kc in range(KC):
                    self.mm(ps[:, 7, i * E:(i + 1) * E], self.hT[:, kc, t * 128:(t + 1) * 128],
                            wq[:, 2, kc * 128 + voff:kc * 128 + voff + E], kc == 0 and i == 0, kc == KC - 1,
                            [wqd[2], allh[t]], [psd[7]])
            self.cp("act", V[:, b, tg * 4:tg * 4 + 4, 0:E],
                    ps[:, 7, 0:4 * E].rearrange("p (a e) -> p a e", a=4), [psd[7]], [Vd[b]])

        def proj_units(hc):
            units = []
            kv_new = diff or hc % 2 == 0
            for tc in range(4):
                units.append((lambda tc=tc: proj_qk(hc, 0, tc)))
                if kv_new:
                    units.append((lambda tc=tc: proj_qk(hc, 1, tc)))
                    units.append((lambda tc=tc: proj_v(hc, tc)))
            return units

        pending = []

        def attn_block(hc, qc, c):
            bq = hc % 2
            bk = hc % 2 if diff else (hc // 2) % 2
            pr = slice(c * 64, (c + 1) * 64)
            W = E + 1
            nb = 2 if diff else 1
            def s_exp(kt):
                sb = kt % 2
                self.mm(ps[:, sb, :], kT[pr, bk, kt * 128:(kt + 1) * 128], qT[pr, bq, qc * 512:(qc + 1) * 512],
                        True, True, [kTd[bk], qTd[bq]], [psd[sb]])
                self.act(PT[:, sb, :], ps[:, sb, :], AF.Exp, [psd[sb]], [PTd[sb]], scale=0.125)

            s_exp(0)
            for kt in range(NT):
                sb = kt % 2
                if kt + 1 < NT:
                    s_exp(kt + 1)
                if kt == 3 and pending:
                    pending.pop(0)()
                for qs in range(4):
                    if diff:
                        bank, col = 2 + qs // 2, (qs % 2) * W
                        first = (kt == 0 and qs % 2 == 0)
                    else:
                        bank, col = 2 + c, qs * W
                        first = (kt == 0 and qs == 0)
                    self.mm(ps[:, bank, col:col + W], PT[:, sb, qs * 128:(qs + 1) * 128], V[:, bk, kt, 0:W],
                            first, kt == NT - 1, [PTd[sb], Vd[bk]], [psd[bank]])
            if diff:
                for hb in range(2):
                    bank = 2 + hb
                    pv = ps[:, bank, 0:2 * W].rearrange("p (a w) -> p a w", a=2)
                    rz = sm_[:, hb * 2:hb * 2 + 2]
                    self.S.op("dve", (lambda rz=rz, pv=pv: lambda e: e.reciprocal(out=rz.unsqueeze(2), in_=pv[:, :, E:E + 1]))(),
                              [psd[bank]], [smd])
                    if c == 0:
                        self.tt("dve", oacc[:, hb * 2:hb * 2 + 2, :], pv[:, :, 0:E],
                                rz.unsqueeze(2).to_broadcast([128, 2, E]), ALU.mult, [psd[bank], smd], [oaccd])
                    else:
                        self.ts("dve", rz, rz, neglam, None, ALU.mult, None, [smd], [smd])
                        self.tt("dve", osq[:, hb * 2:hb * 2 + 2, :], pv[:, :, 0:E],
                                rz.unsqueeze(2).to_broadcast([128, 2, E]), ALU.mult, [psd[bank], smd], [osqd])
                        self.tt("pool", oacc[:, hb * 2:hb * 2 + 2, :], oacc[:, hb * 2:hb * 2 + 2, :],
                                osq[:, hb * 2:hb * 2 + 2, :], ALU.add, [osqd, oaccd], [oaccd])
                if c == 1:
                    self.tt("dve", osq[:], oacc[:], oacc[:], ALU.mult, [oaccd], [osqd])
                    self.S.op("dve", lambda e: e.tensor_reduce(out=sm_[:, 4:8], in_=osq[:], axis=AX.X, op=ALU.add),
                              [osqd], [smd])
                    self.act(sm_[:, 4:8], sm_[:, 4:8], AF.Sqrt, [smd, self.wsd], [smd], scale=1.0 / 128,
                             bias=self.wsc("eps5"))
                    self.S.op("dve", lambda e: e.reciprocal(out=sm_[:, 4:8], in_=sm_[:, 4:8]), [smd], [smd])
                    self.tt("dve", osq[:], oacc[:], sm_[:, 4:8].unsqueeze(2).to_broadcast([128, 4, 128]), ALU.mult,
                            [oaccd, smd], [osqd])
                    self.stt(osb[:], osq[:], 1.0 - lam_init,
                             self.wsc("a_subln", 0, 128).unsqueeze(1).to_broadcast([128, 4, 128]),
                             ALU.mult, ALU.mult, [osqd, self.wsd], [osbd])
            else:
                bank = 2 + c
                pv = ps[:, bank, 0:4 * W].rearrange("p (a w) -> p a w", a=4)
                rz = sm_[:, c * 4:c * 4 + 4]
                self.S.op("dve", lambda e: e.reciprocal(out=rz.unsqueeze(2), in_=pv[:, :, E:E + 1]), [psd[bank]], [smd])
                self.tt("dve", osb[:, :, c * 64:(c + 1) * 64], pv[:, :, 0:E],
                        rz.unsqueeze(2).to_broadcast([128, 4, E]), ALU.mult, [psd[bank], smd], [osbd])
            if c == 1:
                def flush(hc=hc, qc=qc):
                    pst = ps[:, 7, :].bitcast(BF16)
                    for qs in range(4):
                        self.tr(pst[:, qs * 128:(qs + 1) * 128], osb[:, qs, :], [osbd, self.cbd], [psd[7]])
                    self.cp("act", OT[:, hc, qc * 512:(qc + 1) * 512], pst[:, 0:512], [psd[7]], [OTd])
                pending.append(flush)

        load_w(0)
        for u in proj_units(0):
            u()
        for hc in range(8):
            units = []
            if hc + 1 < 8:
                load_w(hc + 1)
                units = proj_units(hc + 1)
            ui = 0
            for qc in range(4):
                for c in range(2):
                    attn_block(hc, qc, c)
                    n_take = (len(units) - ui + (7 - (qc * 2 + c))) // (8 - (qc * 2 + c))
                    for _ in range(n_take):
                        units[ui]()
                        ui += 1
        while pending:
            pending.pop(0)()
        wo = al("at_wo", [128, KC, 512], BF16)
        wod = Dep()
        wov = self.wview(f"{pre}_wo")
        for nh in range(2):
            self.load(wo[:], wov[:, :, nh * 512:(nh + 1) * 512].rearrange("k p n -> p k n"), [self.wbf_dep], [wod])
            for t in range(NT):
                bank = 4 + (t % 2)
                for kc in range(KC):
                    self.mm(ps[:, bank, :], OT[:, kc, t * 128:(t + 1) * 128], wo[:, kc, :], kc == 0, kc == KC - 1,
                            [OTd, wod], [psd[bank]])
                self.tt("dve", self.x[:, t, nh * 512:(nh + 1) * 512], self.x[:, t, nh * 512:(nh + 1) * 512],
                        ps[:, bank, :], ALU.add, [psd[bank], self.xd[t]], [self.xd[t]])


Prog._attn_full = _attn_full
Prog.mixer0 = _attn_full
Prog.mixer1 = _attn_full


def _attn_nbr(self, L):
    nc, S = self.nc, self.S
    S.barrier()
    with contextlib.ExitStack() as st0:
        self.rmsnorm_T(f"n{L}_attn", st0)
        st = st0
        al = lambda n, sh, dt: st.enter_context(self.sbt(n, sh, dt))
        wq = al("nb_wq", [128, 3, 1024], BF16)
        qT = al("nb_qT", [128, 2, S_LEN], BF16)
        kT = al("nb_kT", [128, 2, S_LEN], BF16)
        V = al("nb_V", [128, 2, NT, 2, 66], BF16)
        OT = al("nb_OT", [128, KC, S_LEN], BF16)
        Tb = al("nb_T", [128, 2, 2, 14, 64], F32)
        msk = al("nb_msk", [128, 64], F32)
        sb = al("nb_sb", [128, 2, 5, 64], F32)
        PT = al("nb_PT", [128, 2, 5, 64], BF16)
        sq = al("nb_sq", [128, 512], BF16)
        t1 = al("nb_t1", [128, 512], F32)
        osb = al("nb_osb", [128, 2, 128], BF16)
        sm_ = al("nb_sm", [128, 8], F32)
        wo = al("nb_wo", [128, KC, 512], BF16)
        wqd = [Dep(), Dep(), Dep()]
        qTd, kTd, Vd, Td = [Dep(), Dep()], [Dep(), Dep()], [Dep(), Dep()], [Dep(), Dep()]
        sbd, PTd, osbd = [Dep(), Dep()], [Dep(), Dep()], [Dep(), Dep()]
        OTd, mskd, sqd, t1d, smd, wod = (Dep() for _ in range(6))
        allh = list(self.hTd)
        psd, ps = self.psd, self.ps
        cbo = self.cbo
        blockmean = self.cb[:, cbo["blockmean"]:cbo["blockmean"] + 128]
        wqkv = self.wview("c_wqkv")
        o_b, shp = self.wo["c_bias_f32"]
        bias_d = self.wbig[o_b:o_b + 8 * 128 * 1792].rearrange("(a p n) -> a p n", p=128, n=1792)
        self.load(msk[:], self.cf_d[:, self.cfo["nb_mask"]:self.cfo["nb_mask"] + 64], (), [mskd])
        for b in range(2):
            self.memset("pool", V[:, b, :, :, 64:65], 1.0, [Vd[b]])
        self.ts("dve", sm_[:, 0:1], self.wsc("c_qg"), 0.125, None, ALU.mult, None, [self.wsd], [smd])

        def load_w(hc):
            b = hc % 2
            for i, ci in enumerate((hc, 8 + hc, 16 + hc)):
                self.load(wq[:, i, :], wqkv[ci].rearrange("p k c -> p (k c)"), [self.wbf_dep], [wqd[i]])
            self.load(Tb[:, b].rearrange("p c a q -> p (c a q)"), bias_d[hc], (), [Td[b]])
            self.tt("pool", Tb[:, b].rearrange("p c a q -> p (c a) q"), Tb[:, b].rearrange("p c a q -> p (c a) q"),
                    msk[:].unsqueeze(1).to_broadcast([128, 28, 64]), ALU.add, [Td[b], mskd], [Td[b]])

        def proj_qk(hc, which, tc):
            b = hc % 2
            dst, dstd = (qT, qTd) if which == 0 else (kT, kTd)
            gain = sm_[:, 0:1] if which == 0 else self.wsc("c_kg")
            c0 = tc * 512
            for kc in range(KC):
                self.mm(ps[:, 4, :], wq[:, which, kc * 128:(kc + 1) * 128], self.hT[:, kc, c0:c0 + 512],
                        kc == 0, kc == KC - 1, [wqd[which]] + allh[tc * 4:tc * 4 + 4], [psd[4]])
            self.act(sq[:], ps[:, 4, :], AF.Square, [psd[4]], [sqd])
            self.mm(ps[:, 5, :], blockmean, sq[:], True, True, [sqd, self.cbd], [psd[5]])
            self.act(t1[:], ps[:, 5, :], AF.Sqrt, [psd[5], self.wsd], [t1d], bias=self.wsc("eps6"))
            self.S.op("dve", lambda e: e.reciprocal(out=t1[:], in_=t1[:]), [t1d], [t1d])
            self.stt(dst[:, b, c0:c0 + 512], ps[:, 4, :], gain, t1[:], ALU.mult, ALU.mult,
                     [psd[4], t1d, smd, self.wsd], [dstd[b]])

        def proj_v(hc, tg):
            b = hc % 2
            for i in range(4):
                t = tg * 4 + i
                for kc in range(KC):
                    self.mm(ps[:, 7, i * 128:(i + 1) * 128], self.hT[:, kc, t * 128:(t + 1) * 128],
                            wq[:, 2, kc * 128:(kc + 1) * 128], kc == 0 and i == 0, kc == KC - 1,
                            [wqd[2], allh[t]], [psd[7]])
            self.cp("act", V[:, b, tg * 4:tg * 4 + 4, :, 0:64],
                    ps[:, 7, :].rearrange("p (a c e) -> p a c e", a=4, c=2), [psd[7]], [Vd[b]])

        def proj_units(hc):
            units = []
            for tc in range(4):
                units.append((lambda tc=tc: proj_qk(hc, 0, tc)))
                units.append((lambda tc=tc: proj_qk(hc, 1, tc)))
                units.append((lambda tc=tc: proj_v(hc, tc)))
            return units

        def row_block(hc, r):
            b = hc % 2
            rs = min(max(r - 4, 0), 24)
            a0, a1 = rs // 2, (rs + 7) // 2
            ns = a1 - a0 + 1
            ph = (r % 2) * 64
            tb = (r // 2) % 2
            for c in range(2):
                pr = slice(c * 64, (c + 1) * 64)
                sbank = c
                for si in range(ns):
                    a = a0 + si
                    self.mm(ps[:, sbank, si * 64:(si + 1) * 64], kT[pr, b, a * 128:(a + 1) * 128],
                            qT[pr, b, r * 64:(r + 1) * 64], si == 0, True, [kTd[b], qTd[b]], [psd[sbank]])
                base0 = 2 * a0 - r + 7
                self.tt("dve", sb[:, c, 0:ns, :], ps[:, sbank, 0:ns * 64].rearrange("p (s q) -> p s q", s=ns),
                        Tb[:, b, c, base0:base0 + 2 * ns - 1:2, :], ALU.add, [psd[sbank], Td[b]], [sbd[c]])
                if rs % 2 == 0:
                    self.act(PT[:, c, 0:ns, :], sb[:, c, 0:ns, :], AF.Exp, [sbd[c]], [PTd[c]])
                else:
                    self.act(PT[:, c, 0:1, :], sb[:, c, 0:1, :], AF.Exp, [sbd[c], self.wsd], [PTd[c]],
                             bias=self.wsc("hoff", 0, 1))
                    self.act(PT[:, c, 1:ns - 1, :], sb[:, c, 1:ns - 1, :], AF.Exp, [sbd[c]], [PTd[c]])
                    self.act(PT[:, c, ns - 1:ns, :], sb[:, c, ns - 1:ns, :], AF.Exp, [sbd[c], self.wsd], [PTd[c]],
                             bias=self.wsc("hoff", 1, 1))
            for c in range(2):
                for si in range(ns):
                    a = a0 + si
                    self.mm(ps[ph:ph + 64, 2, c * 65:(c + 1) * 65], PT[:, c, si, :], V[:, b, a, c, 0:65],
                            si == 0, si == ns - 1, [PTd[c], Vd[b]], [psd[2]])
            pv = ps[ph:ph + 64, 2, 0:130].rearrange("p (c w) -> p c w", c=2)
            rz = sm_[ph:ph + 64, 2:4]
            self.S.op("dve", lambda e: e.reciprocal(out=rz.unsqueeze(2), in_=pv[:, :, 64:65]), [psd[2]], [smd])
            self.tt("dve", osb[ph:ph + 64, tb, :].rearrange("p (c e) -> p c e", c=2), pv[:, :, 0:64],
                    rz.unsqueeze(2).to_broadcast([64, 2, 64]), ALU.mult, [psd[2], smd], [osbd[tb]])
            if r % 2 == 1:
                t = r // 2
                pst = ps[:, 3, :].bitcast(BF16)
                self.tr(pst[:, 0:128], osb[:, tb, :], [osbd[tb], self.cbd], [psd[3]])
                self.cp("act", OT[:, hc, t * 128:(t + 1) * 128], pst[:, 0:128], [psd[3]], [OTd])

        load_w(0)
        for u in proj_units(0):
            u()
        for hc in range(8):
            units = []
            if hc + 1 < 8:
                load_w(hc + 1)
                units = proj_units(hc + 1)
            ui = 0
            for r in range(32):
                row_block(hc, r)
                n_take = (len(units) - ui + (31 - r)) // (32 - r)
                for _ in range(n_take):
                    units[ui]()
                    ui += 1
        wov = self.wview("c_wo")
        for nh in range(2):
            self.load(wo[:], wov[:, :, nh * 512:(nh + 1) * 512].rearrange("k p n -> p k n"), [self.wbf_dep], [wod])
            for t in range(NT):
                bank = 4 + (t % 2)
                for kc in range(KC):
                    self.mm(ps[:, bank, :], OT[:, kc, t * 128:(t + 1) * 128], wo[:, kc, :], kc == 0, kc == KC - 1,
                            [OTd, wod], [psd[bank]])
                self.tt("dve", self.x[:, t, nh * 512:(nh + 1) * 512], self.x[:, t, nh * 512:(nh + 1) * 512],
                        ps[:, bank, :], ALU.add, [psd[bank], self.xd[t]], [self.xd[t]])


Prog.mixer2 = _attn_nbr


def _rwkv(self, L):
    nc, S = self.nc, self.S
    ps, psd = self.ps, self.psd
    cbo, cfo = self.cbo, self.cfo
    bank_i = [0]

    def nb():
        b = bank_i[0] % 8
        bank_i[0] += 1
        return b

    def nb2():
        s0 = (bank_i[0] + 1) // 2 * 2
        bank_i[0] = s0 + 2
        return s0 % 8

    def ps2(b):
        return ps[:, b:b + 2, :].rearrange("p a b -> p (a b)")

    def pd2(b):
        return [psd[b], psd[b + 1]]

    def reduce_x(out, in_, r, w):
        self.S.op("dve", lambda e: e.tensor_reduce(out=out, in_=in_, axis=AX.X, op=ALU.add), r, w)

    def recip(ap, r, w):
        self.S.op("dve", lambda e: e.reciprocal(out=ap, in_=ap), r, w)

    S.barrier()
    with contextlib.ExitStack() as st_outer:
        alo = lambda n, sh, dt: st_outer.enter_context(self.sbt(n, sh, dt))
        lT = alo("rw_lT", [128, 3, S_LEN], BF16)
        lTd = Dep()
        allh = list(self.hTd)
        xd = self.xd
        with contextlib.ExitStack() as st:
            self.rmsnorm_T(f"n{L}_attn", st)
        for t in range(NT):
            self.load(self.xsp[t], self.x[:, t, :], [xd[t]], [self.xspd[t]], q="pool")
        S.barrier()
        with contextlib.ExitStack() as st:
            al = lambda n, sh, dt: st.enter_context(self.sbt(n, sh, dt))
            tmp = al("rwa_tmp", [128, S_LEN], F32)
            W = al("rwa_W", [128, KC, 1024], BF16)
            stage = al("rwa_stage", [128, 2, 1024], BF16)
            lw = al("rwa_lw", [128, KC, 128], BF16)
            coef = al("rwa_coef", [128, 2, 48], F32)
            tmpd, Wd, lwd, coefd, shd, mixd = (Dep() for _ in range(6))
            staged = [Dep(), Dep()]
            xb = self.x[:].rearrange("p a b -> p (a b)").bitcast(BF16)
            sh = xb[:, 0:KC * S_LEN].rearrange("p (k t) -> p k t", k=KC)
            mix = xb[:, KC * S_LEN:2 * KC * S_LEN].rearrange("p (k t) -> p k t", k=KC)
            mu = self.wsc("d_mu", 0, 48)
            self.ts("dve", coef[:, 0, :], mu, -1.0, 1.0, ALU.mult, ALU.add, [self.wsd], [coefd])
            self.ts("dve", coef[:, 1, :], mu, 0.5, None, ALU.mult, None, [self.wsd], [coefd])
            for kc in range(KC):
                self.tt("dve", sh[:, kc, 1:S_LEN - 1], self.hT[:, kc, 0:S_LEN - 2],
                        self.hT[:, kc, 2:S_LEN], ALU.add, allh, [shd] + xd)
            self.cp("dve", sh[:, :, 0:1], self.hT[:, :, 1:2], allh, [shd] + xd)
            self.cp("dve", sh[:, :, S_LEN - 1:S_LEN], self.hT[:, :, S_LEN - 2:S_LEN - 1], allh, [shd] + xd)

            def make_mix(j):
                for kc in range(KC):
                    self.act(tmp[:], self.hT[:, kc, :], AF.Identity, allh + [coefd], [tmpd],
                             scale=coef[:, 0, j * 8 + kc:j * 8 + kc + 1])
                    self.stt(mix[:, kc, :], sh[:, kc, :], coef[:, 1, j * 8 + kc:j * 8 + kc + 1], tmp[:],
                             ALU.mult, ALU.add, [shd, tmpd, coefd], [mixd] + xd)

            def proj_tm(wname, idx):
                self.load(W[:], self.wview(wname).rearrange("k p n -> p k n"), [self.wbf_dep], [Wd])
                for t in range(NT):
                    b2 = nb2()
                    for nh in range(2):
                        for kc in range(KC):
                            self.mm(ps[:, b2 + nh, :], mix[:, kc, t * 128:(t + 1) * 128],
                                    W[:, kc, nh * 512:(nh + 1) * 512], kc == 0, kc == KC - 1, [mixd, Wd],
                                    [psd[b2 + nh]])
                    sb_ = t % 2
                    self.cp("act", stage[:, sb_, :], ps2(b2), pd2(b2), [staged[sb_]])
                    self.load(self.rkvsp[idx, t], stage[:, sb_, :], [staged[sb_]], [self.rkvd[idx][t]], q="pool")

            def proj_lora(wname, li, func):
                self.load(lw[:], self.wview(wname).rearrange("k p n -> p k n"), [self.wbf_dep], [lwd])
                for tc in range(4):
                    b = nb()
                    for kc in range(KC):
                        self.mm(ps[:, b, :], lw[:, kc, :], mix[:, kc, tc * 512:(tc + 1) * 512], kc == 0,
                                kc == KC - 1, [mixd, lwd], [psd[b]])
                    self.act(lT[:, li, tc * 512:(tc + 1) * 512], ps[:, b, :], func, [psd[b]], [lTd])

            make_mix(0)
            proj_tm("d_w_r", 0)
            make_mix(1)
            proj_lora("d_w1s", 0, AF.Tanh)
            make_mix(2)
            proj_tm("d_w_k", 1)
            make_mix(3)
            proj_tm("d_w_v", 2)
            make_mix(4)
            proj_lora("d_a1s", 1, AF.Identity)
            make_mix(5)
            proj_lora("d_g1", 2, AF.Sigmoid)
        S.barrier()
        import os
        stop = os.environ.get("RW_STOP", "")
        if stop:
            nbt = int(os.environ.get("RW_NT", "16"))
        if stop == "A":
            for t in range(NT):
                self.load(self.x[:, t, :], self.xsp[t], [self.xspd[t]], [xd[t]])
            return
        hTf = self.hT[:].rearrange("p k t -> p (k t)")
        with contextlib.ExitStack() as st:
            al = lambda n, sh_, dt: st.enter_context(self.sbt(n, sh_, dt))
            tri = al("rwb_tri", [128, 4, 128], F32)
            onesf = al("rwb_onesf", [128, 128], F32)
            brow = al("rwb_brow", [128, 2048], F32)
            kkb = al("rwb_kkb", [128, 2, 1024], F32)
            w2a2 = al("rwb_w2a2", [128, 2, 1024], BF16)
            rkv = al("rwb_rkv", [128, 3, 1024], BF16)
            F = al("rwb_F", [128, 6, 1024], F32)
            Bt = al("rwb_Bt", [128, 4, 1024], BF16)
            ARc = al("rwb_ARc", [128, 8, 2, 128], BF16)
            Bc = al("rwb_Bc", [128, 8, 128], BF16)
            Kc = al("rwb_Kc", [128, 8, 128], BF16)
            IB = al("rwb_IB", [128, 6, 4, 128], F32)
            lvlm = al("rwb_lvlm", [128, 5, 128], F32)
            identf = al("rwb_identf", [128, 128], F32)
            RU = al("rwb_RU", [128, 2, 1024], BF16)
            S32 = al("rwb_S32", [128, 8, 64], F32)
            Sb = al("rwb_Sb", [128, 8, 64], BF16)
            WL = al("rwb_WL", [128, 8], F32)
            small = al("rwb_small", [128, 16], F32)
            cstd, rkvsd, Fd, Btd, cmd, NAd, NTd, RUd, S32d, Sbd, WLd, smd = (
                Dep(), [Dep(), Dep(), Dep()], [Dep() for _ in range(6)], [Dep() for _ in range(4)],
                Dep(), Dep(), Dep(), [Dep(), Dep()], Dep(), Dep(), Dep(), Dep())
            NA3 = hTf[:, 0:6144].rearrange("p (h a t) -> p h a t", h=16, a=3)
            Xb = hTf[:, 6144:8192].rearrange("p (h t) -> p h t", h=16)
            Nf = hTf[:, 8192:12288].bitcast(F32).rearrange("p (h t) -> p h t", h=16)
            NTf = hTf[:, 12288:16384].bitcast(F32).rearrange("p (h t) -> p h t", h=16)
            Nfd, NTfd, Xbd = Dep(), Dep(), Dep()
            ibd = [Dep() for _ in range(6)]
            ibdB = [Dep() for _ in range(6)]
            masks = self.cb[:, cbo["masks"]:cbo["masks"] + 512].rearrange("p (a t) -> p a t", a=4)
            identb = self.ident
            self.load(tri[:].rearrange("p a t -> p (a t)"), self.cf_d[:, cfo["tri"]:cfo["tri"] + 512], (), [cstd])
            self.load(onesf[:], self.cf_d[:, cfo["onesf"]:cfo["onesf"] + 128], (), [cstd])
            self.load(identf[:], self.cf_d[:, cfo["identf"]:cfo["identf"] + 128], (), [cstd])
            self.load(lvlm[:].rearrange("p a t -> p (a t)"), self.cf_d[:, cfo["lvlmask"]:cfo["lvlmask"] + 640], (), [cstd])
            self.load(brow[:], self.cf_d[:, cfo["brow"]:cfo["brow"] + 2048], (), [cstd])
            self.load(kkb[:, 0, :], self.cf_d[:, cfo["d_k_k"]:cfo["d_k_k"] + 1024], (), [cstd])
            self.load(kkb[:, 1, :], self.cf_d[:, cfo["d_k_a"]:cfo["d_k_a"] + 1024], (), [cstd])
            self.load(w2a2[:, 0, :], self.wview("d_w2s"), [self.wbf_dep], [cstd])
            self.load(w2a2[:, 1, :], self.wview("d_a2s"), [self.wbf_dep], [cstd])
            negc2 = self.wsc("negc", 0, 2)

            def lora_sig(d, which, n, dstF, dstd):
                prd = slice(d * 64, (d + 1) * 64)
                rowp = d * 64
                b2 = nb2()
                for nh in range(2):
                    self.mm(ps[:, b2 + nh, :], lT[prd, which, n * 128:(n + 1) * 128],
                            w2a2[prd, which, nh * 512:(nh + 1) * 512], True, False, [lTd, cstd], [psd[b2 + nh]])
                    c0_ = which * 1024 + nh * 512
                    self.mm(ps[:, b2 + nh, :], onesf[rowp:rowp + 1, :], brow[rowp:rowp + 1, c0_:c0_ + 512],
                            False, True, [cstd], [psd[b2 + nh]])
                self.act(dstF, ps2(b2), AF.Sigmoid, pd2(b2), [dstd])

            def scan_tile(d, n):
                for i in range(3):
                    self.load(rkv[:, i, :], self.rkvsp[i, n], [self.rkvd[i][n]], [rkvsd[i]])
                F0, F1, F2, F3, F4, F5 = (F[:, i, :] for i in range(6))
                v3 = lambda ap: ap.rearrange("p (h j) -> p h j", h=16)
                lora_sig(d, 0, n, F0, Fd[0])
                lora_sig(d, 1, n, F1, Fd[1])
                cut = int(os.environ.get("RW_CUT", "99"))
                if cut <= 1:
                    return
                self.tt("dve", F5, rkv[:, 1, :], kkb[:, 0, :], ALU.mult, [rkvsd[1], cstd], [Fd[5]])
                self.tt("pool", F2, F5, F5, ALU.mult, [Fd[5]], [Fd[2]])
                reduce_x(small[:, 0:16], v3(F2), [Fd[2]], [smd])
                self.act(small[:, 0:16], small[:, 0:16], AF.Sqrt, [smd], [smd])
                self.ts("dve", small[:, 0:16], small[:, 0:16], 1e-12, None, ALU.max, None, [smd], [smd])
                recip(small[:, 0:16], [smd], [smd])
                self.tt("dve", v3(F5), v3(F5), small[:, 0:16].unsqueeze(2).to_broadcast([128, 16, 64]), ALU.mult,
                        [Fd[5], smd], [Fd[5]])
                if cut <= 2:
                    return
                bi, be = nb2(), nb2()
                for nh in range(2):
                    self.mm(ps[:, bi + nh, :], tri[:, 2 * d, :], F0[:, nh * 512:(nh + 1) * 512], True, True,
                            [cstd, Fd[0]], [psd[bi + nh]])
                    self.mm(ps[:, be + nh, :], tri[:, 2 * d + 1, :], F0[:, nh * 512:(nh + 1) * 512], True, True,
                            [cstd, Fd[0]], [psd[be + nh]])
                self.act(F2, ps2(bi), AF.Exp, pd2(bi), [Fd[2]])
                self.act(F3, ps2(bi), AF.Exp, pd2(bi), [Fd[3]], scale=-1.0)
                self.act(F4, ps2(be), AF.Exp, pd2(be), [Fd[4]])
                bw = nb()
                for hp in range(8):
                    self.mm(ps[:, bw, hp * 2:hp * 2 + 2], F0[:, hp * 128:(hp + 1) * 128], negc2, hp == 0, True,
                            [Fd[0], self.wsd], [psd[bw]])
                self.act(WL[:], ps[:, bw, 0:16:2], AF.Exp, [psd[bw]], [WLd])
                if cut <= 3:
                    return
                self.stt(Bt[:, 0, :], F5, -1.0, F4, ALU.mult, ALU.mult, [Fd[5], Fd[4]], [Btd[0]])
                self.tt("pool", F4, F5, F1, ALU.mult, [Fd[5], Fd[1]], [Fd[4]])
                self.tt("dve", Bt[:, 1, :], F4, F3, ALU.mult, [Fd[4], Fd[3]], [Btd[1]])
                self.stt(F4, F1, -1.0, kkb[:, 1, :], ALU.add, ALU.mult, [Fd[1], cstd], [Fd[4]])
                self.stt(F5, F4, 1.0, rkv[:, 1, :], ALU.add, ALU.mult, [Fd[4], rkvsd[1]], [Fd[5]])
                self.tt("dve", Bt[:, 2, :], F5, F3, ALU.mult, [Fd[5], Fd[3]], [Btd[2]])
                self.tt("dve", Bt[:, 3, :], rkv[:, 0, :], F2, ALU.mult, [rkvsd[0], Fd[2]], [Btd[3]])
                if cut <= 4:
                    return
                for si, dst, eng in ((0, ARc[:, :, 0, :], "act"), (3, ARc[:, :, 1, :], "dve"), (1, Bc[:], "act"),
                                     (2, Kc[:], "dve")):
                    b = nb()
                    pst = ps[:, b, :].bitcast(BF16)
                    for hp in range(8):
                        self.tr(pst[:, hp * 128:(hp + 1) * 128], Bt[:, si, hp * 128:(hp + 1) * 128],
                                [Btd[si], self.cbd], [psd[b]])
                    self.cp(eng, dst, pst.rearrange("p (h t) -> p h t", h=8), [psd[b]], [cmd])
                if os.environ.get("RW_DUMP"):
                    dd = [Dep() for _ in range(NT)]
                    for i in range(6):
                        self.cp("act", self.x[:, i, :], F[:, i, :], [Fd[i]], [xd[i]])
                    for i in range(4):
                        self.cp("act", self.x[:, 6 + i, :], Bt[:, i, :], [Btd[i]], [xd[6 + i]])
                    for i in range(3):
                        self.cp("act", self.x[:, 10 + i, :], rkv[:, i, :], [rkvsd[i]], [xd[10 + i]])
                    return
                if cut <= 5:
                    return
                m12 = masks[:, 0:2, :] if d == 0 else masks[:, 2:4, :]
                m3 = masks[:, 2, :] if d == 0 else masks[:, 0, :]
                for h in range(16):
                    hp, pr = h // 2, slice((h % 2) * 64, (h % 2) * 64 + 64)
                    b = nb()
                    arv = ARc[pr, hp, :, :].rearrange("p a t -> p (a t)")
                    self.mm(ps[:, b, 0:256], Bc[pr, hp, :], arv, True, True, [cmd], [psd[b]])
                    self.mm(ps[:, b, 256:512], Kc[pr, hp, :], arv, False, True, [cmd], [psd[b]])
                    self.tt("dve", Nf[:, h, :], ps[:, b, 0:128], m12[:, 0, :], ALU.mult, [psd[b], self.cbd],
                            [Nfd] + allh)
                    self.tt("dve", NA3[:, h, 0, :], ps[:, b, 128:256], m12[:, 1, :], ALU.mult, [psd[b], self.cbd],
                            [NAd] + allh)
                    self.tt("dve", NA3[:, h, 1:3, :], ps[:, b, 256:512].rearrange("p (a t) -> p a t", a=2),
                            m12, ALU.mult, [psd[b], self.cbd], [NAd] + allh)
                for g8 in range(2):
                    bb = nb2()
                    for par in range(2):
                        pr = slice(par * 64, par * 64 + 64)
                        for i4 in range(4):
                            hp = g8 * 4 + i4
                            self.mm(ps[:, bb + par, i4 * 128:(i4 + 1) * 128], ARc[pr, hp, 0, :], Bc[pr, hp, :],
                                    i4 == 0, True, [cmd], [psd[bb + par]])
                        self.tt("dve", NTf[:, g8 * 8 + par:g8 * 8 + 8:2, :],
                                ps[:, bb + par, :].rearrange("p (h t) -> p h t", h=4),
                                m3.unsqueeze(1).to_broadcast([128, 4, 128]), ALU.mult, [psd[bb + par], self.cbd],
                                [NTfd] + allh)
                if cut <= 6:
                    return
                bc4 = lambda ap: ap.unsqueeze(1).to_broadcast([128, 4, 128])
                p4 = lambda b_: ps[:, b_, :].rearrange("p (h t) -> p h t", h=4)
                Fv = F[:, 0:3, :].rearrange("p a n -> p (a n)").rearrange("p (i h t) -> p i h t", i=6, h=4)
                bufsets = [([IB[:, i] for i in range(6)], ibd, [], []),
                           ([Fv[:, i] for i in range(6)], ibdB, list(Fd[0:3]), list(Fd[0:3]))]

                def inv_group(g4, bset):
                    bufs_, deps_, xr, xw = bset
                    hs = slice(g4 * 4, g4 * 4 + 4)
                    P1, PT1, Q, P2, PT2, Y1 = bufs_
                    dP1, dPT1, dQ, dP2, dPT2, dY1 = deps_

                    def mm4(bank, lhs, rhs, r):
                        for hh in range(4):
                            self.mm(ps[:, bank, hh * 128:(hh + 1) * 128], lhs[:, hh, :], rhs[:, hh, :], hh == 0, True,
                                    r + xr, [psd[bank]])
                    self.tt("dve", P1, Nf[:, hs, :], bc4(lvlm[:, 0, :]), ALU.mult, [Nfd, cstd] + xr, [dP1] + xw)
                    self.tt("pool", PT1, NTf[:, hs, :], bc4(lvlm[:, 0, :]), ALU.mult, [NTfd, cstd] + xr, [dPT1] + xw)
                    self.tt("pool", Q, P1, bc4(identf[:]), ALU.add, [dP1, cstd] + xr, [dQ] + xw)
                    yield
                    b1, b2_ = nb(), nb()
                    mm4(b1, PT1, P1, [dP1, dPT1])
                    mm4(b2_, P1, PT1, [dP1, dPT1])
                    yield
                    self.cp("act", P2, p4(b1), [psd[b1]] + xr, [dP2])
                    self.cp("act", PT2, p4(b2_), [psd[b2_]] + xr, [dPT2])
                    yield
                    b3, b4 = nb(), nb()
                    mm4(b3, PT2, Q, [dPT2, dQ])
                    mm4(b4, P2, PT2, [dP2, dPT2])
                    yield
                    self.tt("dve", Q, Q, p4(b3), ALU.add, [psd[b3], dQ] + xr, [dQ])
                    self.cp("act", PT1, p4(b4), [psd[b4]] + xr, [dPT1])
                    yield
                    b5 = nb()
                    mm4(b5, PT1, Q, [dPT1, dQ])
                    yield
                    self.tt("dve", Q, Q, p4(b5), ALU.add, [psd[b5], dQ] + xr, [dQ])
                    yield
                    b6 = nb()
                    for hh in range(4):
                        self.tr(ps[:, b6, hh * 128:(hh + 1) * 128], Q[:, hh, :], [dQ, cstd] + xr, [psd[b6]],
                                ident=identf[:])
                    Xt, dXt = P2, dP2
                    X_, dX = Q, dQ
                    NTm, dNTm = PT2, dPT2
                    self.tt("pool", NTm, NTf[:, hs, :], bc4(lvlm[:, 1, :]), ALU.mult, [NTfd, cstd] + xr, [dNTm])
                    yield
                    self.cp("act", Xt, p4(b6), [psd[b6]] + xr, [dXt])
                    for li in range(1, 5):
                        lastl = li == 4
                        by = nb()
                        mm4(by, NTm, X_, [dNTm, dX])
                        yield
                        self.cp("act", Y1, p4(by), [psd[by]] + xr, [dY1])
                        if not lastl:
                            self.tt("pool", NTm, NTf[:, hs, :], bc4(lvlm[:, li + 1, :]), ALU.mult,
                                    [NTfd, cstd] + xr, [dNTm])
                        yield
                        bx = nb()
                        mm4(bx, Xt, Y1, [dXt, dY1])
                        if not lastl:
                            bxt = nb()
                            mm4(bxt, Y1, Xt, [dY1, dXt])
                        yield
                        if not lastl:
                            self.tt("dve", X_, X_, p4(bx), ALU.add, [psd[bx], dX] + xr, [dX])
                            self.tt("dve", Xt, Xt, p4(bxt), ALU.add, [psd[bxt], dXt] + xr, [dXt])
                            yield
                        else:
                            self.tt("dve", Xb[:, hs, :], X_, p4(bx), ALU.add, [psd[bx], dX] + xr, [Xbd] + allh)

                for rnd in range(2):
                    gens = [inv_group(2 * rnd, bufsets[0]), inv_group(2 * rnd + 1, bufsets[1])]
                    alive = list(gens)
                    while alive:
                        nxt = []
                        for g_ in alive:
                            try:
                                next(g_)
                                nxt.append(g_)
                            except StopIteration:
                                pass
                        alive = nxt
                X, Xd = Xb, Xbd
                if cut <= 7:
                    return
                Vt = rkv[:, 2, :]
                bR = nb2()
                for h in range(16):
                    hp, pr = h // 2, slice((h % 2) * 64, (h % 2) * 64 + 64)
                    o = ps[:, bR + h % 2, hp * 64:hp * 64 + 64]
                    self.mm(o, ARc[pr, hp, 0, :], Sb[pr, hp, :], True, False, [cmd, Sbd], [psd[bR + h % 2]])
                    self.mm(o, NA3[:, h, 1, :], Vt[:, h * 64:(h + 1) * 64], False, True, [NAd, rkvsd[2]],
                            [psd[bR + h % 2]])
                self.cp("act", RU[:, 0, :].rearrange("p (h a i) -> p h a i", h=8, a=2),
                        ps2(bR).rearrange("p (a h i) -> p h a i", a=2, h=8), pd2(bR), [RUd[0]])
                bU = nb2()
                for h in range(16):
                    o = ps[:, bU + h // 8, (h % 8) * 64:(h % 8) * 64 + 64]
                    self.mm(o, X[:, h, :], RU[:, 0, h * 64:(h + 1) * 64], True, True, [Xd, RUd[0]], [psd[bU + h // 8]])
                self.cp("dve", RU[:, 1, :], ps2(bU), pd2(bU), [RUd[1]])
                bY = nb2()
                for h in range(16):
                    hp, pr = h // 2, slice((h % 2) * 64, (h % 2) * 64 + 64)
                    o = ps[:, bY + h % 2, hp * 64:hp * 64 + 64]
                    self.mm(o, ARc[pr, hp, 1, :], Sb[pr, hp, :], True, False, [cmd, Sbd], [psd[bY + h % 2]])
                    self.mm(o, NA3[:, h, 0, :], RU[:, 1, h * 64:(h + 1) * 64], False, False, [NAd, RUd[1]],
                            [psd[bY + h % 2]])
                    self.mm(o, NA3[:, h, 2, :], Vt[:, h * 64:(h + 1) * 64], False, True, [NAd, rkvsd[2]],
                            [psd[bY + h % 2]])
                yv = self.x[:, n, :].rearrange("p (h a i) -> p h a i", h=8, a=2)
                pyv = ps2(bY).rearrange("p (a h i) -> p h a i", a=2, h=8)
                if d == 0:
                    self.cp("act", yv, pyv, pd2(bY), [xd[n]])
                else:
                    self.tt("dve", yv, yv, pyv, ALU.add, pd2(bY) + [xd[n]], [xd[n]])
                bS = nb()
                for h in range(16):
                    hp, pr = h // 2, slice((h % 2) * 64, (h % 2) * 64 + 64)
                    o = ps[pr, bS, hp * 64:(hp + 1) * 64]
                    self.mm(o, Bt[:, 1, h * 64:(h + 1) * 64], RU[:, 1, h * 64:(h + 1) * 64], True, False,
                            [Btd[1], RUd[1]], [psd[bS]])
                    self.mm(o, Bt[:, 2, h * 64:(h + 1) * 64], Vt[:, h * 64:(h + 1) * 64], False, True,
                            [Btd[2], rkvsd[2]], [psd[bS]])
                self.tt("dve", S32[:], S32[:], ps[:, bS, :].rearrange("p (h i) -> p h i", h=8), ALU.add,
                        [psd[bS], S32d], [S32d])
                self.tt("dve", S32[:], S32[:], WL[:].unsqueeze(2).to_broadcast([128, 8, 64]), ALU.mult,
                        [S32d, WLd], [S32d])
                self.cp("act", Sb[:], S32[:], [S32d], [Sbd])

            for d in range(2):
                self.memset("pool", S32[:], 0.0, [S32d])
                self.memset("pool", Sb[:], 0.0, [Sbd])
                order = list(range(NT)) if d == 0 else list(range(NT - 1, -1, -1))
                if stop:
                    order = order[:nbt]
                for n in order:
                    scan_tile(d, n)
        S.barrier()
        if stop == "BY":
            return
        if stop == "B":
            for t in range(NT):
                self.load(self.x[:, t, :], self.xsp[t], [self.xspd[t]], [xd[t]])
            return
        with contextlib.ExitStack() as st:
            al = lambda n, sh_, dt: st.enter_context(self.sbt(n, sh_, dt))
            onesf = al("rwc_onesf", [128, 128], F32)
            brow = al("rwc_brow", [128, 2048], F32)
            bc4 = al("rwc_bc4", [128, 4, 1024], F32)
            w2a2 = al("rwc_w2a2", [128, 2, 1024], BF16)
            g2 = al("rwc_g2", [128, 1024], BF16)
            Wo = al("rwc_Wo", [128, KC, 1024], BF16)
            rkv = al("rwc_rkv", [128, 3, 1024], BF16)
            xt = al("rwc_xt", [128, 1024], F32)
            F = al("rwc_F", [128, 5, 1024], F32)
            ob = al("rwc_ob", [128, 1024], BF16)
            oT = al("rwc_oT", [128, KC, 128], BF16)
            small = al("rwc_small", [128, 48], F32)
            cstd, xtd, obd, oTd, smd = (Dep() for _ in range(5))
            rkvsd = [Dep(), Dep(), Dep()]
            Fd = [Dep() for _ in range(5)]
            self.load(onesf[:], self.cf_d[:, cfo["onesf"]:cfo["onesf"] + 128], (), [cstd])
            self.load(brow[:], self.cf_d[:, cfo["brow"]:cfo["brow"] + 2048], (), [cstd])
            for i, nm in enumerate(("d_k_a", "d_r_k", "d_ln_g", "d_ln_b")):
                self.load(bc4[:, i, :], self.cf_d[:, cfo[nm]:cfo[nm] + 1024], (), [cstd])
            self.load(w2a2[:, 0, :], self.wview("d_w2s"), [self.wbf_dep], [cstd])
            self.load(w2a2[:, 1, :], self.wview("d_a2s"), [self.wbf_dep], [cstd])
            self.load(g2[:], self.wview("d_g2"), [self.wbf_dep], [cstd])
            self.load(Wo[:], self.wview("d_w_o").rearrange("k p n -> p k n"), [self.wbf_dep], [cstd])
            v3 = lambda ap: ap.rearrange("p (h j) -> p h j", h=16)
            bc16 = lambda ap: ap.unsqueeze(2).to_broadcast([128, 16, 64])
            for n in range(NT):
                for i in range(3):
                    self.load(rkv[:, i, :], self.rkvsp[i, n], [self.rkvd[i][n]], [rkvsd[i]])
                self.load(xt[:], self.xsp[n], [self.xspd[n]], [xtd])
                F0, F1, F2, F3, F4 = (F[:, i, :] for i in range(5))
                for d in range(2):
                    prd = slice(d * 64, (d + 1) * 64)
                    rowp = d * 64
                    b2 = nb2()
                    for nh in range(2):
                        self.mm(ps[:, b2 + nh, :], lT[prd, 1, n * 128:(n + 1) * 128],
                                w2a2[prd, 1, nh * 512:(nh + 1) * 512], True, False, [lTd, cstd], [psd[b2 + nh]])
                        self.mm(ps[:, b2 + nh, :], onesf[rowp:rowp + 1, :],
                                brow[rowp:rowp + 1, 1024 + nh * 512:1024 + (nh + 1) * 512], False, True, [cstd],
                                [psd[b2 + nh]])
                    self.act(F[:, d, :], ps2(b2), AF.Sigmoid, pd2(b2), [Fd[d]])
                bG = nb2()
                for nh in range(2):
                    self.mm(ps[:, bG + nh, :], lT[:, 2, n * 128:(n + 1) * 128], g2[:, nh * 512:(nh + 1) * 512],
                            True, True, [lTd, cstd], [psd[bG + nh]])
                self.cp("act", F2, ps2(bG), pd2(bG), [Fd[2]])
                y = self.x[:, n, :]
                reduce_x(small[:, 0:16], v3(y), [xd[n]], [smd])
                self.ts("dve", small[:, 0:16], small[:, 0:16], 1.0 / 64, None, ALU.mult, None, [smd], [smd])
                self.tt("dve", v3(F3), v3(y), bc16(small[:, 0:16]), ALU.subtract, [xd[n], smd], [Fd[3]])
                self.tt("pool", F4, F3, F3, ALU.mult, [Fd[3]], [Fd[4]])
                reduce_x(small[:, 16:32], v3(F4), [Fd[4]], [smd])
                self.act(small[:, 16:32], small[:, 16:32], AF.Sqrt, [smd, self.wsd], [smd], scale=1.0 / 64,
                         bias=self.wsc("epsgn"))
                recip(small[:, 16:32], [smd], [smd])
                self.tt("dve", v3(F3), v3(F3), bc16(small[:, 16:32]), ALU.mult, [Fd[3], smd], [Fd[3]])
                self.tt("pool", F3, F3, bc4[:, 2, :], ALU.mult, [Fd[3], cstd], [Fd[3]])
                self.tt("pool", F3, F3, bc4[:, 3, :], ALU.add, [Fd[3], cstd], [Fd[3]])
                self.tt("dve", F0, F0, F1, ALU.add, [Fd[0], Fd[1]], [Fd[0]])
                self.ts("dve", F0, F0, 0.5, -1.0, ALU.mult, ALU.add, [Fd[0]], [Fd[0]])
                self.tt("dve", F0, F0, bc4[:, 0, :], ALU.mult, [Fd[0], cstd], [Fd[0]])
                self.stt(F0, F0, 1.0, rkv[:, 1, :], ALU.add, ALU.mult, [Fd[0], rkvsd[1]], [Fd[0]])
                self.tt("dve", F0, F0, rkv[:, 0, :], ALU.mult, [Fd[0], rkvsd[0]], [Fd[0]])
                self.tt("pool", F0, F0, bc4[:, 1, :], ALU.mult, [Fd[0], cstd], [Fd[0]])
                reduce_x(small[:, 32:48], v3(F0), [Fd[0]], [smd])
                self.tt("dve", v3(F4), v3(rkv[:, 2, :]), bc16(small[:, 32:48]), ALU.mult, [rkvsd[2], smd], [Fd[4]])
                self.tt("pool", F3, F3, F4, ALU.add, [Fd[3], Fd[4]], [Fd[3]])
                self.tt("dve", ob[:], F3, F2, ALU.mult, [Fd[3], Fd[2]], [obd])
                b = nb()
                pst = ps[:, b, :].bitcast(BF16)
                for kc in range(KC):
                    self.tr(pst[:, kc * 128:(kc + 1) * 128], ob[:, kc * 128:(kc + 1) * 128], [obd, self.cbd], [psd[b]])
                self.cp("act", oT[:], pst.rearrange("p (k t) -> p k t", k=KC), [psd[b]], [oTd])
                bO = nb2()
                for nh in range(2):
                    for kc in range(KC):
                        self.mm(ps[:, bO + nh, :], oT[:, kc, :], Wo[:, kc, nh * 512:(nh + 1) * 512], kc == 0,
                                kc == KC - 1, [oTd, cstd], [psd[bO + nh]])
                self.tt("dve", self.x[:, n, :], xt[:], ps2(bO), ALU.add, pd2(bO) + [xtd, xd[n]], [xd[n]])
    S.barrier()


Prog.mixer3 = _rwkv
```

```python
import math
import contextlib
import numpy as np
import ml_dtypes
import concourse.bass as bass
import concourse.mybir as mybir
from concourse.bass_utils import run_bass_kernel_spmd

F32 = mybir.dt.float32
BF16 = mybir.dt.bfloat16
AF = mybir.ActivationFunctionType
ALU = mybir.AluOpType
AX = mybir.AxisListType

D = 1024
S_LEN = 2048
NT = 16
KC = 8
FH = 2816
NCH = 22
N_CORES = 8
SEQ_PER_CORE = 5
NORM_EPS = 1e-6


class Dep:
    __slots__ = ("w", "r", "name")

    def __init__(self, name=""):
        self.w = None
        self.r = {}
        self.name = name


class Sch:
    COMPUTE = ("pe", "dve", "act", "pool")

    def __init__(self, nc, n_sp_ring=40, n_pool_ring=8, same_eng_sync=True):
        self.nc = nc
        self.q = {e: [] for e in ("pe", "dve", "act", "pool", "sp")}
        self.sems = {}
        self.cnt = {}
        for e in self.COMPUTE:
            self.sems[e] = nc.alloc_semaphore(f"s_{e}")
            self.cnt[e] = 0
        self.ring = {}
        for qn, n in (("sp", n_sp_ring), ("pool", n_pool_ring)):
            self.ring[qn] = dict(n=n, i=0)
            for i in range(n):
                self.sems[(qn, i)] = nc.alloc_semaphore(f"d_{qn}{i}")
        self.seen = {e: {} for e in self.q}
        self.same_eng_sync = same_eng_sync
        self.nops = 0

    def _collect(self, eng, reads, writes, is_dma):
        need = {}

        def add(k, v, peng):
            if peng == eng and not is_dma and k in self.COMPUTE:
                if eng == "pe" or not self.same_eng_sync:
                    return
            if need.get(k, 0) < v:
                need[k] = v
        for d in reads:
            if d.w is not None:
                add(*d.w)
        for d in writes:
            if d.w is not None:
                add(*d.w)
            for k, (v, peng) in d.r.items():
                add(k, v, peng)
        out = []
        seen = self.seen[eng]
        for k, v in need.items():
            if seen.get(k, 0) < v:
                seen[k] = v
                out.append((k, v))
        return out

    def op(self, eng, fn, reads=(), writes=()):
        waits = self._collect(eng, reads, writes, False)
        self.cnt[eng] += 1
        v = self.cnt[eng]
        for d in reads:
            d.r[eng] = (v, eng)
        for d in writes:
            d.w = (eng, v, eng)
            d.r = {}
        self.q[eng].append((waits, fn, (eng, 1)))
        self.nops += 1

    def dma(self, qn, fn, reads=(), writes=()):
        ring = self.ring[qn]
        i = ring["i"]
        ring["i"] += 1
        n = ring["n"]
        slot, gen = i % n, i // n
        key = (qn, slot)
        waits = self._collect(qn, reads, writes, True)
        if gen > 0:
            seen = self.seen[qn]
            if seen.get(key, 0) < 16 * gen:
                seen[key] = 16 * gen
                waits.append((key, 16 * gen))
        v = 16 * (gen + 1)
        for d in reads:
            d.r[key] = (v, "dma")
        for d in writes:
            d.w = (key, v, "dma")
            d.r = {}
        self.q[qn].append((waits, fn, (key, 16)))
        self.nops += 1

    def _all_marks(self):
        marks = [(e, self.cnt[e]) for e in self.COMPUTE if self.cnt[e] > 0]
        for rq, ring in self.ring.items():
            n = ring["n"]
            for slot in range(n):
                c = (ring["i"] - slot + n - 1) // n
                if c > 0:
                    marks.append(((rq, slot), 16 * c))
        return marks

    def barrier(self):
        marks = self._all_marks()
        for e in self.q:
            waits = []
            seen = self.seen[e]
            for k, v in marks:
                if k == e:
                    continue
                if seen.get(k, 0) < v:
                    seen[k] = v
                    waits.append((k, v))
            if waits:
                self.q[e].append((waits, None, None))

    def final_wait(self, qn="pool"):
        waits = [(k, v) for k, v in self._all_marks() if k != qn]
        self.q[qn].append((waits, None, None))

    def emit(self):
        nc = self.nc
        sems = self.sems

        def replay(eng_obj, lst):
            for waits, fn, inc in lst:
                for k, v in waits:
                    eng_obj.wait_ge(sems[k], v)
                if fn is not None:
                    ins = fn(eng_obj)
                    ins.then_inc(sems[inc[0]], inc[1])

        with nc.Block() as block:
            @block.tensor
            def _(e):
                replay(e, self.q["pe"])

            @block.vector
            def _(e):
                replay(e, self.q["dve"])

            @block.scalar
            def _(e):
                replay(e, self.q["act"])

            @block.gpsimd
            def _(e):
                replay(e, self.q["pool"])

            @block.sync
            def _(e):
                replay(e, self.q["sp"])


def colchunk(w):
    K, N = w.shape
    return np.ascontiguousarray(w.reshape(K // 128, 128, N // 128, 128).transpose(2, 1, 0, 3))


class Packer:
    def __init__(self):
        self.parts = []
        self.off = {}
        self.n = 0

    def add(self, name, arr):
        a = np.ascontiguousarray(arr, dtype=np.float32).reshape(-1)
        self.off[name] = (self.n, arr.shape)
        self.parts.append(a)
        self.n += a.size

    def finish(self, align):
        pad = (-self.n) % align
        if pad:
            self.parts.append(np.zeros(pad, np.float32))
            self.n += pad
        return np.concatenate(self.parts)


class ColPacker:
    def __init__(self):
        self.parts = []
        self.off = {}
        self.n = 0

    def add(self, name, arr):
        a = np.ascontiguousarray(arr, dtype=np.float32).reshape(128, -1)
        self.off[name] = self.n
        self.parts.append(a)
        self.n += a.shape[1]

    def finish(self):
        return np.ascontiguousarray(np.concatenate(self.parts, axis=1))


def chan8(v):
    return np.ascontiguousarray(v.reshape(8, 128).T)


def rep(v):
    return np.ascontiguousarray(np.broadcast_to(v.reshape(1, -1), (128, v.size)))


def build_rope_tables():
    out = np.zeros((4, 128, S_LEN), np.float32)
    t = np.arange(S_LEN, dtype=np.float32)
    half = 32
    inv = (10000.0 ** (-np.arange(half, dtype=np.float32) / half)).astype(np.float32)
    ang = t[None, :] * inv[:, None]
    c0 = np.cos(ang).astype(np.float32)
    s0 = np.sin(ang).astype(np.float32)
    for p in range(128):
        i = (p % 64) % half
        out[0, p] = c0[i]
        out[1, p] = s0[i]
    half = 16
    inv = (10000.0 ** (-np.arange(half, dtype=np.float32) / half)).astype(np.float32)
    row = np.floor(t / 64.0).astype(np.float32)
    col = (t - row * 64.0).astype(np.float32)
    for p in range(128):
        dd = p % 64
        pos = row if dd < 32 else col
        i = (dd % 32) % half
        ang = pos * inv[i]
        out[2, p] = np.cos(ang).astype(np.float32)
        out[3, p] = np.sin(ang).astype(np.float32)
    return out


def rot_matrix_T(block):
    R = np.zeros((128, 128), np.float32)
    half = block // 2
    for p in range(128):
        i = p % block
        if i < half:
            R[p, p + half] = -1.0
        else:
            R[p, p - half] = 1.0
    return np.ascontiguousarray(R.T)


class Prog:
    def __init__(self, nseq, wbig_n, wbig_off, ws_n, ws_off, cb_n, cb_off, cf_n, cf_off,
                 mixers=(0, 1, 2, 3), ffn=True, nlayers=4):
        self.nseq = nseq
        self.mixers = mixers
        self.do_ffn = ffn
        self.nlayers = nlayers
        nc = bass.Bass("TRN2", target_bir_lowering=False)
        self.nc = nc
        self._uniq = 0
        self.S = Sch(nc)
        self.wo, self.so, self.cbo, self.cfo = wbig_off, ws_off, cb_off, cf_off
        self.x_in = nc.dram_tensor("x", [nseq, S_LEN, D], F32, kind="ExternalInput").ap()
        self.y_out = nc.dram_tensor("y", [nseq, S_LEN, D], F32, kind="ExternalOutput").ap()
        self.wbig = nc.dram_tensor("wbig", [wbig_n], F32, kind="ExternalInput").ap()
        self.wbf = nc.dram_tensor("wbf", [wbig_n], BF16, kind="Internal").ap()
        self.wsmall_d = nc.dram_tensor("wsmall", [128, ws_n], F32, kind="ExternalInput").ap()
        self.cbf_d = nc.dram_tensor("cbf", [128, cb_n], BF16, kind="ExternalInput").ap()
        self.cf_d = nc.dram_tensor("cf", [128, cf_n], F32, kind="ExternalInput").ap()
        self.rope_d = nc.dram_tensor("rope", [4, 128, S_LEN], BF16, kind="ExternalInput").ap()
        self.wbf_dep = Dep("wbf")
        self.wbf_blocks = [Dep() for _ in range(wbig_n // (128 * 2048))]
        self._wdeps_acc = []
        self.xsp = nc.dram_tensor("xsp", [NT, 128, D], F32, kind="Internal").ap()
        self.xspd = [Dep() for _ in range(NT)]
        self.rkvsp = nc.dram_tensor("rkvsp", [3, NT, 128, D], BF16, kind="Internal").ap()
        self.rkvd = [[Dep() for _ in range(NT)] for _ in range(3)]
        self.x = nc.alloc_sbuf_tensor("xres", [128, NT, D], F32)
        self.xd = [Dep(f"x{t}") for t in range(NT)]
        self.hT = nc.alloc_sbuf_tensor("hT", [128, KC, S_LEN], BF16)
        self.hTd = [Dep(f"hT{t}") for t in range(NT)]
        self.ws = nc.alloc_sbuf_tensor("ws", [128, ws_n], F32)
        self.wsd = Dep("ws")
        self.cb = nc.alloc_sbuf_tensor("cb", [128, cb_n], BF16)
        self.cbd = Dep("cb")
        self.ps = nc.alloc_psum_tensor("ps", [128, 8, 512], F32)
        self.psd = [Dep(f"ps{b}") for b in range(8)]
        self.ident = self.cb[:, cb_off["ident"]:cb_off["ident"] + 128]

    def sbt(self, name, shape, dt):
        self._uniq += 1
        return self.nc.sbuf_tensor(f"{name}_{self._uniq}", shape, dt)

    def mm(self, out, lhsT, rhs, start, stop, r, w):
        self.S.op("pe", lambda e: e.matmul(out, lhsT, rhs, start=start, stop=stop,
                                           skip_group_check=True), r, w)

    def tr(self, out, in_, r, w, ident=None):
        idn = self.ident if ident is None else ident
        self.S.op("pe", lambda e: e.transpose(out, in_, idn), r, w)

    def act(self, out, in_, func, r, w, scale=1.0, bias=None, accum=None):
        def f(e):
            kw = {}
            if bias is not None:
                kw["bias"] = bias
            if accum is not None:
                kw["accum_out"] = accum
            return e.activation(out=out, in_=in_, func=func, scale=scale, **kw)
        self.S.op("act", f, r, w)

    def ts(self, eng, out, in0, s1, s2, op0, op1, r, w):
        def f(e):
            if op1 is None:
                return e.tensor_scalar(out=out, in0=in0, scalar1=s1, scalar2=None, op0=op0)
            return e.tensor_scalar(out=out, in0=in0, scalar1=s1, scalar2=s2, op0=op0, op1=op1)
        self.S.op(eng, f, r, w)

    def tt(self, eng, out, in0, in1, op, r, w):
        self.S.op(eng, lambda e: e.tensor_tensor(out=out, in0=in0, in1=in1, op=op), r, w)

    def stt(self, out, in0, scalar, in1, op0, op1, r, w):
        self.S.op("dve", lambda e: e.scalar_tensor_tensor(out=out, in0=in0, scalar=scalar, in1=in1,
                                                          op0=op0, op1=op1), r, w)

    def cp(self, eng, out, in_, r, w):
        if eng == "act":
            self.S.op("act", lambda e: e.copy(out=out, in_=in_), r, w)
        else:
            self.S.op(eng, lambda e: e.tensor_copy(out=out, in_=in_), r, w)

    def memset(self, eng, ap, val, w):
        self.S.op(eng, lambda e: e.memset(ap, val), (), w)

    def load(self, out, in_, r, w, q="sp"):
        r = list(r)
        if self.wbf_dep in r:
            r.remove(self.wbf_dep)
            r += list(self._wdeps_acc)
        self.S.dma(q, lambda e: e.dma_start(out=out, in_=in_), r, w)

    def wsc(self, name, j=0, n=1):
        o = self.so[name] + j
        return self.ws[:, o:o + n]

    def setup(self):
        S = self.S
        self.load(self.ws[:], self.wsmall_d, (), [self.wsd])
        self.load(self.cb[:], self.cbf_d, (), [self.cbd])
        n = self.wbig.shape[0]
        blk = 128 * 2048
        src = self.wbig.rearrange("(b p f) -> b p f", p=128, f=2048)
        dst = self.wbf.rearrange("(b p f) -> b p f", p=128, f=2048)
        for b in range(n // blk):
            self.load(dst[b], src[b], (), [self.wbf_blocks[b]], q="pool")

    def wview(self, name):
        off, shape = self.wo[name]
        n = int(np.prod(shape))
        blk = 128 * 2048
        for d_ in self.wbf_blocks[off // blk:(off + n - 1) // blk + 1]:
            if d_ not in self._wdeps_acc:
                self._wdeps_acc.append(d_)
        flat = self.wbf[off:off + n]
        if len(shape) == 3:
            return flat.rearrange("(a b c) -> a b c", b=shape[1], c=shape[2])
        if len(shape) == 4:
            return flat.rearrange("(a b c d) -> a b c d", b=shape[1], c=shape[2], d=shape[3])
        return flat.rearrange("(a b) -> a b", b=shape[1])

    def load_x(self, s):
        for t in range(NT):
            self.load(self.x[:, t, :], self.x_in[s, t * 128:(t + 1) * 128, :], (), [self.xd[t]])

    def store_x(self, s):
        for t in range(NT):
            self.load(self.y_out[s, t * 128:(t + 1) * 128, :], self.x[:, t, :], [self.xd[t]], [Dep()], q="pool")

    def rmsnorm_T(self, gname, st):
        nc = self.nc
        junk = st.enter_context(self.sbt("nrm_junk", [128, D], BF16))
        hn = st.enter_context(self.sbt("nrm_hn", [128, 2, D], BF16))
        ss = st.enter_context(self.sbt("nrm_ss", [128, 2, 2], F32))
        junkd = Dep()
        hnd = [Dep(), Dep()]
        ssd = [Dep(), Dep()]
        g = self.wsc(gname, 0, 8)
        for t in range(NT):
            b = t % 2
            self.act(junk[:], self.x[:, t, :], AF.Square, [self.xd[t]], [junkd, ssd[b]],
                     accum=ss[:, b, 0:1])
            self.act(ss[:, b, 1:2], ss[:, b, 0:1], AF.Sqrt, [ssd[b]], [ssd[b]],
                     scale=1.0 / D, bias=self.wsc("eps6"))
            self.S.op("dve", (lambda b=b: (lambda e: e.reciprocal(out=ss[:, b, 1:2], in_=ss[:, b, 1:2])))(),
                      [ssd[b]], [ssd[b]])
            self.ts("dve", hn[:, b, :], self.x[:, t, :], ss[:, b, 1:2], None, ALU.mult, None,
                    [self.xd[t], ssd[b]], [hnd[b]])
            pb = 6 + (t % 2)
            pst = self.ps[:, pb, :].bitcast(BF16)
            for kc in range(KC):
                self.tr(pst[:, kc * 128:(kc + 1) * 128], hn[:, b, kc * 128:(kc + 1) * 128],
                        [hnd[b], self.cbd], [self.psd[pb]])
            self.tt("dve", self.hT[:, :, t * 128:(t + 1) * 128],
                    pst.rearrange("p (k c) -> p k c", k=KC),
                    g.unsqueeze(2).to_broadcast([128, KC, 128]), ALU.mult,
                    [self.psd[pb], self.wsd], [self.hTd[t]])

    def ffn(self, L):
        nc, S = self.nc, self.S
        S.barrier()
        with contextlib.ExitStack() as st0:
            self.rmsnorm_T(f"n{L}_ffn", st0)
            st = st0
            wi = st.enter_context(self.sbt("ffn_wi", [128, 2, 4, 1024], BF16))
            wo = st.enter_context(self.sbt("ffn_wo", [128, 2, 2, 1024], BF16))
            gT = st.enter_context(self.sbt("ffn_gT", [128, 2, 2, S_LEN], BF16))
            upad = st.enter_context(self.sbt("ffn_upad", [128, 2, 2, S_LEN + 2], F32))
            acc = st.enter_context(self.sbt("ffn_acc", [128, 2, S_LEN], F32))
            wid = [Dep(), Dep()]
            wod = [Dep(), Dep()]
            gTd = [Dep(), Dep()]
            upd = [[Dep(), Dep()], [Dep(), Dep()]]
            accd = [Dep(), Dep()]
            allh = list(self.hTd)
            for w_ in range(2):
                for b in range(2):
                    self.memset("pool", upad[:, w_, b, 0:1], 0.0, [upd[w_][b]])
                    self.memset("pool", upad[:, w_, b, S_LEN + 1:S_LEN + 2], 0.0, [upd[w_][b]])
            win = self.wview(f"f{L}_w_in")
            wout = self.wview(f"f{L}_w_out")
            cw = self.so[f"f{L}_conv_w"]
            cbo = self.so[f"f{L}_conv_b"]
            NP = NCH // 2
            ubuf = 0

            def out_partial(p, tiles):
                pb_ = p % 2
                for t in tiles:
                    bank = 4 + 2 * (t % 2)
                    for nh in range(2):
                        for j in range(2):
                            self.mm(self.ps[:, bank + nh, :], gT[:, pb_, j, t * 128:(t + 1) * 128],
                                    wo[:, pb_, j, nh * 512:(nh + 1) * 512], j == 0, j == 1,
                                    [gTd[pb_], wod[pb_]], [self.psd[bank + nh]])
                    self.tt("dve", self.x[:, t, :], self.x[:, t, :],
                            self.ps[:, bank:bank + 2, :].rearrange("p a b -> p (a b)"), ALU.add,
                            [self.psd[bank], self.psd[bank + 1], self.xd[t]], [self.xd[t]])

            for p in range(NP + 1):
                if p < NP:
                    pb = p % 2
                    self.load(wi[:, pb, 0:2, :],
                              win[2 * p:2 * p + 2].rearrange("a p k c -> p a (k c)"),
                              [self.wbf_dep], [wid[pb]])
                    self.load(wi[:, pb, 2:4, :],
                              win[NCH + 2 * p:NCH + 2 * p + 2].rearrange("a p k c -> p a (k c)"),
                              [self.wbf_dep], [wid[pb]])
                    self.load(wo[:, pb, :, :], wout[2 * p:2 * p + 2].rearrange("a p n -> p a n"),
                              [self.wbf_dep], [wod[pb]])
                for j in range(2):
                    if p < NP:
                        for which in range(2):
                            ci = 2 * p + j + which * NCH
                            ub = ubuf % 2
                            for half in range(2):
                                bank = 2 * half
                                for tq in range(2):
                                    c0 = half * 1024 + tq * 512
                                    for kc in range(KC):
                                        self.mm(self.ps[:, bank + tq, :],
                                                wi[:, pb, which * 2 + j, kc * 128:(kc + 1) * 128],
                                                self.hT[:, kc, c0:c0 + 512], kc == 0, kc == KC - 1,
                                                [wid[pb]] + allh[c0 // 128:c0 // 128 + 4],
                                                [self.psd[bank + tq]])
                                self.cp("act", upad[:, which, ub, 1 + half * 1024:1 + (half + 1) * 1024],
                                        self.ps[:, bank:bank + 2, :].rearrange("p a b -> p (a b)"),
                                        [self.psd[bank], self.psd[bank + 1]], [upd[which][ub]])
                            u = upad[:, which, ub, :]
                            a_ = acc[:, which, :]
                            self.act(a_, u[:, 0:S_LEN], AF.Identity, [upd[which][ub], self.wsd], [accd[which]],
                                     scale=self.ws[:, cw + ci * 3:cw + ci * 3 + 1],
                                     bias=self.ws[:, cbo + ci:cbo + ci + 1])
                            self.stt(a_, u[:, 1:S_LEN + 1], self.ws[:, cw + ci * 3 + 1:cw + ci * 3 + 2], a_,
                                     ALU.mult, ALU.add, [upd[which][ub], accd[which]], [accd[which]])
                            self.stt(a_, u[:, 2:S_LEN + 2], self.ws[:, cw + ci * 3 + 2:cw + ci * 3 + 3], a_,
                                     ALU.mult, ALU.add, [upd[which][ub], accd[which]], [accd[which]])
                        ubuf += 1
                        self.act(acc[:, 0, :], acc[:, 0, :], AF.Silu, [accd[0]], [accd[0]])
                        self.tt("dve", gT[:, pb, j, :], acc[:, 0, :], acc[:, 1, :], ALU.mult,
                                [accd[0], accd[1]], [gTd[pb]])
                    if p > 0:
                        out_partial(p - 1, range(j * 8, j * 8 + 8))

    def run_seq(self, s):
        self.load_x(s)
        if getattr(self, "debug", None) == "hT":
            with contextlib.ExitStack() as st:
                self.rmsnorm_T("n0_ffn", st)
            xv = self.x[:].rearrange("p a b -> p (a b)").rearrange("p (k t) -> p k t", k=KC)
            self.cp("act", xv, self.hT[:], self.hTd + self.xd, self.xd)
            self.store_x(s)
            return
        for L in range(self.nlayers):
            if L in self.mixers:
                getattr(self, f"mixer{L}")(L)
            if self.do_ffn:
                self.ffn(L)
        self.store_x(s)

    def build(self):
        self.setup()
        for s in range(self.nseq):
            self.run_seq(s)
        self.S.final_wait("pool")
        print("ops per engine", {e: len(v) for e, v in self.S.q.items()}, "cnt", self.S.cnt, flush=True)
        self.S.emit()
        return self.nc


def pack_params(p):
    big = Packer()
    sm = ColPacker()
    cb = ColPacker()
    cf = ColPacker()
    f32 = np.float32
    cb.add("ident", np.eye(128, dtype=f32))
    bo = np.zeros((128, 128), f32)
    bo[:64, :64] = 1.0 / 64
    bo[64:, 64:] = 1.0 / 64
    cb.add("blockmean", bo)
    cb.add("rotT64", rot_matrix_T(64))
    cb.add("rotT32", rot_matrix_T(32))
    cb.add("ones", np.ones((128, 128), f32))
    sm.add("eps6", np.full((128, 1), 1e-6, f32))
    sm.add("eps5", np.full((128, 1), 1e-5, f32))
    sm.add("epsgn", np.full((128, 1), 64e-5, f32))
    sm.add("zero", np.zeros((128, 1), f32))
    pack_mixers(p, big, sm, cb, cf)
    for L in range(4):
        sm.add(f"n{L}_attn", chan8(p[f"n{L}_attn"]))
        sm.add(f"n{L}_ffn", chan8(p[f"n{L}_ffn"]))
        cw = p[f"f{L}_conv_w"]
        sm.add(f"f{L}_conv_w", cw.T.reshape(44, 128, 3).transpose(1, 0, 2).reshape(128, 132))
        sm.add(f"f{L}_conv_b", p[f"f{L}_conv_b"].reshape(44, 128).T)
        big.add(f"f{L}_w_in", colchunk(p[f"f{L}_w_in"]))
        big.add(f"f{L}_w_out", p[f"f{L}_w_out"].reshape(22, 128, 1024))
    wbig = big.finish(128 * 2048)
    return (wbig, big.off, sm.finish(), sm.off, cb.finish().astype(ml_dtypes.bfloat16), cb.off,
            cf.finish(), cf.off)


def pack_mixers(p, big, sm, cb, cf):
    cf.add("dummy", np.zeros((128, 1), np.float32))
    f32 = np.float32
    idx64 = np.arange(128) % 64
    big.add("a_wqkv", colchunk(p["a_w_qkv"]))
    big.add("a_wo", p["a_w_o"].reshape(8, 128, 1024))
    sm.add("a_qg", p["a_q_norm"][idx64].reshape(128, 1))
    sm.add("a_kg", p["a_k_norm"][idx64].reshape(128, 1))
    sm.add("a_subln", rep(p["a_subln"]))
    for nm in ("a_lq1", "a_lk1", "a_lq2", "a_lk2"):
        sm.add(nm, rep(p[nm]))
    wb = p["b_w_qkv"]
    big.add("b_wqkv", colchunk(wb))
    kd = np.stack([np.concatenate([wb[:, 1024 + g * 64:1024 + (g + 1) * 64]] * 2, axis=1) for g in range(4)], 0)
    big.add("b_wkdup", np.stack([colchunk(kd[g])[0] for g in range(4)], 0))
    big.add("b_wo", p["b_w_o"].reshape(8, 128, 1024))
    sm.add("b_qg", p["b_q_norm"][idx64].reshape(128, 1))
    sm.add("b_kg", p["b_k_norm"][idx64].reshape(128, 1))
    big.add("c_wqkv", colchunk(p["c_w_qkv"]))
    big.add("c_wo", p["c_w_o"].reshape(8, 128, 1024))
    sm.add("c_qg", p["c_q_norm"][idx64].reshape(128, 1))
    sm.add("c_kg", p["c_k_norm"][idx64].reshape(128, 1))
    ho = np.zeros((128, 2), f32)
    ho[:64, 0] = -30000.0
    ho[64:, 1] = -30000.0
    sm.add("hoff", ho)
    rb = p["c_rel_bias"]
    pp = np.arange(128)
    j2 = pp // 64
    kc_ = pp % 64
    qq = np.arange(64)
    dc = np.clip(kc_[:, None] - qq[None, :] + 15, 0, 30)
    base = np.arange(14)
    dr = base[None, :] + j2[:, None]
    T = rb[:, dr[:, :, None], dc[:, None, :]]
    T = T.reshape(8, 2, 128, 14, 64).transpose(0, 2, 1, 3, 4)
    big.add("c_bias_f32", np.ascontiguousarray(T).reshape(8, 128, 2 * 14 * 64))
    cs_ = np.clip(qq - 8, 0, 48)
    inwin = (kc_[:, None] >= cs_[None, :]) & (kc_[:, None] < cs_[None, :] + 16)
    cf.add("nb_mask", np.where(inwin, 0.0, -30000.0).astype(f32))
    for nm in ("d_w_r", "d_w_k", "d_w_v", "d_w_o"):
        big.add(nm, p[nm].reshape(8, 128, 1024))
    big.add("d_w1s", np.concatenate([p["d_w1"][0], p["d_w1"][1]], axis=1).reshape(8, 128, 128))
    big.add("d_a1s", np.concatenate([p["d_a1"][0], p["d_a1"][1]], axis=1).reshape(8, 128, 128))
    big.add("d_g1", p["d_g1"].reshape(8, 128, 128))
    big.add("d_w2s", p["d_w2"].reshape(128, 1024))
    big.add("d_a2s", p["d_a2"].reshape(128, 1024))
    big.add("d_g2", p["d_g2"].reshape(128, 1024))
    sm.add("d_mu", np.concatenate([chan8(p["d_mu"][j]) for j in range(6)], axis=1))
    NEGC = -math.exp(-0.5)
    sm.add("negc", np.full((128, 2), NEGC, f32))
    ii = np.arange(128)
    le = (ii[:, None] <= ii[None, :]).astype(f32)
    lt = (ii[:, None] < ii[None, :]).astype(f32)
    ge = (ii[:, None] >= ii[None, :]).astype(f32)
    gt = (ii[:, None] > ii[None, :]).astype(f32)
    cf.add("tri", np.concatenate([le * NEGC, lt * NEGC, ge * NEGC, gt * NEGC], axis=1))
    cf.add("onesf", np.ones((128, 128), f32))
    cf.add("identf", np.eye(128, dtype=f32))
    lv = [(ii[:, None] // 8 == ii[None, :] // 8).astype(f32)]
    for bsz in (16, 32, 64, 128):
        lv.append(((ii[:, None] // bsz == ii[None, :] // bsz) & (ii[:, None] // (bsz // 2) != ii[None, :] // (bsz // 2))).astype(f32))
    cf.add("lvlmask", np.concatenate(lv, axis=1))
    cb.add("masks", np.concatenate([lt, le, gt, ge], axis=1))
    brow = np.zeros((128, 2048), f32)
    brow[0, :1024] = p["d_w0"][0]
    brow[64, :1024] = p["d_w0"][1]
    brow[0, 1024:] = p["d_a0"][0]
    brow[64, 1024:] = p["d_a0"][1]
    cf.add("brow", brow)
    cf.add("d_k_k", rep(p["d_k_k"]))
    cf.add("d_k_a", rep(p["d_k_a"]))
    cf.add("d_r_k", rep(p["d_r_k"].reshape(-1)))
    cf.add("d_ln_g", rep(p["d_ln_g"]))
    cf.add("d_ln_b", rep(p["d_ln_b"]))


_ROPE = None


def run(inputs, nseq_per_core=SEQ_PER_CORE, n_cores=N_CORES, **kw):
    global _ROPE
    xs = np.concatenate([np.asarray(inputs["x_prompt"]), np.asarray(inputs["x_sample"])], axis=0)
    p = {k: np.asarray(v) for k, v in inputs.items() if not k.startswith("x_")}
    wbig, woff, ws, soff, cbv, cboff, cfv, cfoff = pack_params(p)
    if _ROPE is None:
        _ROPE = build_rope_tables()
    prog = Prog(nseq_per_core, wbig.size, woff, ws.shape[1], soff, cbv.shape[1], cboff,
                cfv.shape[1], cfoff, **kw)
    nc = prog.build()
    in_maps = []
    for c in range(n_cores):
        in_maps.append({
            "x": np.ascontiguousarray(xs[c * nseq_per_core:(c + 1) * nseq_per_core]),
            "wbig": wbig, "wsmall": ws, "cbf": cbv, "cf": cfv, "rope": _ROPE.astype(ml_dtypes.bfloat16),
        })
    res = run_bass_kernel_spmd(nc, in_maps, core_ids=list(range(n_cores)))
    return np.concatenate([r["y"] for r in res.results], axis=0)


def kernel(**inputs):
    y = run(inputs)
    nb = np.asarray(inputs["x_prompt"]).shape[0]
    return (np.ascontiguousarray(y[:nb]), np.ascontiguousarray(y[nb:]))


def _attn_full(self, L):
    nc, S = self.nc, self.S
    diff = (L == 0)
    pre = "a" if diff else "b"
    E = 128 if diff else 64
    EW = 130 if diff else 66
    lam_init = 0.8 - 0.6 * math.exp(-0.3 * L)
    S.barrier()
    with contextlib.ExitStack() as st0:
        self.rmsnorm_T(f"n{L}_attn", st0)
        st = st0
        al = lambda n, sh, dt: st.enter_context(self.sbt(n, sh, dt))
        wq = al("at_wq", [128, 3, 1024], BF16)
        cs = al("at_cs", [128, 2, S_LEN], BF16)
        qT = al("at_qT", [128, 2, S_LEN], BF16)
        kT = al("at_kT", [128, 2, S_LEN], BF16)
        V = al("at_V", [128, 2, NT, EW], BF16)
        OT = al("at_OT", [128, KC, S_LEN], BF16)
        PT = al("at_PT", [128, 2, 512], BF16)
        sq = al("at_sq", [128, 512], BF16)
        qg = al("at_qg", [128, 512], BF16)
        t1 = al("at_t1", [128, 512], F32)
        t2 = al("at_t2", [128, 512], F32)
        t3 = al("at_t3", [128, 512], F32)
        oacc = al("at_oacc", [128, 4, 128], F32)
        osq = al("at_osq", [128, 4, 128], F32)
        osb = al("at_osb", [128, 4, 128], BF16)
        sm_ = al("at_sm", [128, 16], F32)
        wqd = [Dep(), Dep(), Dep()]
        csd = Dep()
        qTd = [Dep(), Dep()]
        kTd = [Dep(), Dep()]
        Vd = [Dep(), Dep()]
        OTd = Dep()
        PTd = [Dep(), Dep()]
        sqd, qgd, t1d, t2d, t3d, oaccd, osqd, osbd, smd = (Dep() for _ in range(9))
        allh = list(self.hTd)
        psd = self.psd
        ps = self.ps
        self.load(cs[:], self.rope_d[2 * L:2 * L + 2].rearrange("a p t -> p a t"), (), [csd])
        for b in range(2):
            self.memset("pool", V[:, b, :, E:E + 1], 1.0, [Vd[b]])
        cbo = self.cbo
        blockmean = self.cb[:, cbo["blockmean"]:cbo["blockmean"] + 128]
        rotT = self.cb[:, cbo["rotT64" if diff else "rotT32"]:cbo["rotT64" if diff else "rotT32"] + 128]
        wqkv = self.wview(f"{pre}_wqkv")
        if diff:
            for i, (a_, b_) in enumerate((("a_lq1", "a_lk1"), ("a_lq2", "a_lk2"))):
                self.tt("dve", t1[:, 0:64], self.wsc(a_, 0, 64), self.wsc(b_, 0, 64), ALU.mult, [self.wsd], [t1d])
                self.S.op("dve", (lambda i=i: lambda e: e.tensor_reduce(out=sm_[:, 8 + i:9 + i], in_=t1[:, 0:64],
                                                                       axis=AX.X, op=ALU.add))(), [t1d], [smd])
            self.act(sm_[:, 8:10], sm_[:, 8:10], AF.Exp, [smd], [smd])
            self.tt("dve", sm_[:, 10:11], sm_[:, 8:9], sm_[:, 9:10], ALU.subtract, [smd], [smd])
            self.ts("dve", sm_[:, 11:12], sm_[:, 10:11], lam_init, -1.0, ALU.add, ALU.mult, [smd], [smd])
        neglam = sm_[:, 11:12]

        def load_w(hc):
            if diff:
                idx = (hc, 8 + hc, 16 + hc)
            else:
                idx = (hc, None, 10 + hc // 4)
            for i, ci in enumerate(idx):
                if ci is None:
                    src = self.wview("b_wkdup")[hc // 2]
                else:
                    src = wqkv[ci]
                if (not diff) and i > 0 and hc % 2 == 1:
                    continue
                self.load(wq[:, i, :], src.rearrange("p k c -> p (k c)"), [self.wbf_dep], [wqd[i]])

        def proj_qk(hc, which, tc):
            b = hc % 2 if (diff or which == 0) else (hc // 2) % 2
            dst, dstd = (qT, qTd) if which == 0 else (kT, kTd)
            gain = self.wsc(f"{pre}_qg" if which == 0 else f"{pre}_kg")
            c0 = tc * 512
            for kc in range(KC):
                self.mm(ps[:, 4, :], wq[:, which, kc * 128:(kc + 1) * 128], self.hT[:, kc, c0:c0 + 512],
                        kc == 0, kc == KC - 1, [wqd[which]] + allh[tc * 4:tc * 4 + 4], [psd[4]])
            self.act(sq[:], ps[:, 4, :], AF.Square, [psd[4]], [sqd])
            self.act(qg[:], ps[:, 4, :], AF.Identity, [psd[4], self.wsd], [qgd], scale=gain)
            self.mm(ps[:, 5, :], blockmean, sq[:], True, True, [sqd, self.cbd], [psd[5]])
            self.mm(ps[:, 6, :], rotT, qg[:], True, True, [qgd, self.cbd], [psd[6]])
            self.act(t1[:], ps[:, 5, :], AF.Sqrt, [psd[5], self.wsd], [t1d], bias=self.wsc("eps6"))
            self.S.op("dve", lambda e: e.reciprocal(out=t1[:], in_=t1[:]), [t1d], [t1d])
            self.tt("dve", t2[:], qg[:], cs[:, 0, c0:c0 + 512], ALU.mult, [qgd, csd], [t2d])
            self.tt("dve", t3[:], ps[:, 6, :], cs[:, 1, c0:c0 + 512], ALU.mult, [psd[6], csd], [t3d])
            self.tt("pool", t2[:], t2[:], t3[:], ALU.add, [t2d, t3d], [t2d])
            self.tt("dve", dst[:, b, c0:c0 + 512], t2[:], t1[:], ALU.mult, [t2d, t1d], [dstd[b]])

        def proj_v(hc, tg):
            b = hc % 2 if diff else (hc // 2) % 2
            voff = 0 if diff else ((hc // 2) % 2) * 64
            for i in range(4):
                t = tg * 4 + i
                for kc in range(KC):
                    self.mm(ps[:, 7, i * E:(i + 1) * E], self.hT[:, kc, t * 128:(t + 1) * 128],
                            wq[:, 2, kc * 128 + voff:kc * 128 + voff + E], kc == 0 and i == 0, kc == KC - 1,
                            [wqd[2], allh[t]], [psd[7]])
            self.cp("act", V[:, b, tg * 4:tg * 4 + 4, 0:E],
                    ps[:, 7, 0:4 * E].rearrange("p (a e) -> p a e", a=4), [psd[7]], [Vd[b]])

        def proj_units(hc):
            units = []
            kv_new = diff or hc % 2 == 0
            for tc in range(4):
                units.append((lambda tc=tc: proj_qk(hc, 0, tc)))
                if kv_new:
                    units.append((lambda tc=tc: proj_qk(hc, 1, tc)))
                    units.append((lambda tc=tc: proj_v(hc, tc)))
            return units

        pending = []

        def attn_block(hc, qc, c):
            bq = hc % 2
            bk = hc % 2 if diff else (hc // 2) % 2
            pr = slice(c * 64, (c + 1) * 64)
            W = E + 1
            nb = 2 if diff else 1
            def s_exp(kt):
                sb = kt % 2
                self.mm(ps[:, sb, :], kT[pr, bk, kt * 128:(kt + 1) * 128], qT[pr, bq, qc * 512:(qc + 1) * 512],
                        True, True, [kTd[bk], qTd[bq]], [psd[sb]])
                self.act(PT[:, sb, :], ps[:, sb, :], AF.Exp, [psd[sb]], [PTd[sb]], scale=0.125)

            s_exp(0)
            for kt in range(NT):
                sb = kt % 2
                if kt + 1 < NT:
                    s_exp(kt + 1)
                if kt == 3 and pending:
                    pending.pop(0)()
                for qs in range(4):
                    if diff:
                        bank, col = 2 + qs // 2, (qs % 2) * W
                        first = (kt == 0 and qs % 2 == 0)
                    else:
                        bank, col = 2 + c, qs * W
                        first = (kt == 0 and qs == 0)
                    self.mm(ps[:, bank, col:col + W], PT[:, sb, qs * 128:(qs + 1) * 128], V[:, bk, kt, 0:W],
                            first, kt == NT - 1, [PTd[sb], Vd[bk]], [psd[bank]])
            if diff:
                for hb in range(2):
                    bank = 2 + hb
                    pv = ps[:, bank, 0:2 * W].rearrange("p (a w) -> p a w", a=2)
                    rz = sm_[:, hb * 2:hb * 2 + 2]
                    self.S.op("dve", (lambda rz=rz, pv=pv: lambda e: e.reciprocal(out=rz.unsqueeze(2), in_=pv[:, :, E:E + 1]))(),
                              [psd[bank]], [smd])
                    if c == 0:
                        self.tt("dve", oacc[:, hb * 2:hb * 2 + 2, :], pv[:, :, 0:E],
                                rz.unsqueeze(2).to_broadcast([128, 2, E]), ALU.mult, [psd[bank], smd], [oaccd])
                    else:
                        self.ts("dve", rz, rz, neglam, None, ALU.mult, None, [smd], [smd])
                        self.tt("dve", osq[:, hb * 2:hb * 2 + 2, :], pv[:, :, 0:E],
                                rz.unsqueeze(2).to_broadcast([128, 2, E]), ALU.mult, [psd[bank], smd], [osqd])
                        self.tt("pool", oacc[:, hb * 2:hb * 2 + 2, :], oacc[:, hb * 2:hb * 2 + 2, :],
                                osq[:, hb * 2:hb * 2 + 2, :], ALU.add, [osqd, oaccd], [oaccd])
                if c == 1:
                    self.tt("dve", osq[:], oacc[:], oacc[:], ALU.mult, [oaccd], [osqd])
                    self.S.op("dve", lambda e: e.tensor_reduce(out=sm_[:, 4:8], in_=osq[:], axis=AX.X, op=ALU.add),
                              [osqd], [smd])
                    self.act(sm_[:, 4:8], sm_[:, 4:8], AF.Sqrt, [smd, self.wsd], [smd], scale=1.0 / 128,
                             bias=self.wsc("eps5"))
                    self.S.op("dve", lambda e: e.reciprocal(out=sm_[:, 4:8], in_=sm_[:, 4:8]), [smd], [smd])
                    self.tt("dve", osq[:], oacc[:], sm_[:, 4:8].unsqueeze(2).to_broadcast([128, 4, 128]), ALU.mult,
                            [oaccd, smd], [osqd])
                    self.stt(osb[:], osq[:], 1.0 - lam_init,
                             self.wsc("a_subln", 0, 128).unsqueeze(1).to_broadcast([128, 4, 128]),
                             ALU.mult, ALU.mult, [osqd, self.wsd], [osbd])
            else:
                bank = 2 + c
                pv = ps[:, bank, 0:4 * W].rearrange("p (a w) -> p a w", a=4)
                rz = sm_[:, c * 4:c * 4 + 4]
                self.S.op("dve", lambda e: e.reciprocal(out=rz.unsqueeze(2), in_=pv[:, :, E:E + 1]), [psd[bank]], [smd])
                self.tt("dve", osb[:, :, c * 64:(c + 1) * 64], pv[:, :, 0:E],
                        rz.unsqueeze(2).to_broadcast([128, 4, E]), ALU.mult, [psd[bank], smd], [osbd])
            if c == 1:
                def flush(hc=hc, qc=qc):
                    pst = ps[:, 7, :].bitcast(BF16)
                    for qs in range(4):
                        self.tr(pst[:, qs * 128:(qs + 1) * 128], osb[:, qs, :], [osbd, self.cbd], [psd[7]])
                    self.cp("act", OT[:, hc, qc * 512:(qc + 1) * 512], pst[:, 0:512], [psd[7]], [OTd])
                pending.append(flush)

        load_w(0)
        for u in proj_units(0):
            u()
        for hc in range(8):
            units = []
            if hc + 1 < 8:
                load_w(hc + 1)
                units = proj_units(hc + 1)
            ui = 0
            for qc in range(4):
                for c in range(2):
                    attn_block(hc, qc, c)
                    n_take = (len(units) - ui + (7 - (qc * 2 + c))) // (8 - (qc * 2 + c))
                    for _ in range(n_take):
                        units[ui]()
                        ui += 1
        while pending:
            pending.pop(0)()
        wo = al("at_wo", [128, KC, 512], BF16)
        wod = Dep()
        wov = self.wview(f"{pre}_wo")
        for nh in range(2):
            self.load(wo[:], wov[:, :, nh * 512:(nh + 1) * 512].rearrange("k p n -> p k n"), [self.wbf_dep], [wod])
            for t in range(NT):
                bank = 4 + (t % 2)
                for kc in range(KC):
                    self.mm(ps[:, bank, :], OT[:, kc, t * 128:(t + 1) * 128], wo[:, kc, :], kc == 0, kc == KC - 1,
                            [OTd, wod], [psd[bank]])
                self.tt("dve", self.x[:, t, nh * 512:(nh + 1) * 512], self.x[:, t, nh * 512:(nh + 1) * 512],
                        ps[:, bank, :], ALU.add, [psd[bank], self.xd[t]], [self.xd[t]])


Prog._attn_full = _attn_full
Prog.mixer0 = _attn_full
Prog.mixer1 = _attn_full


def _attn_nbr(self, L):
    nc, S = self.nc, self.S
    S.barrier()
    with contextlib.ExitStack() as st0:
        self.rmsnorm_T(f"n{L}_attn", st0)
        st = st0
        al = lambda n, sh, dt: st.enter_context(self.sbt(n, sh, dt))
        wq = al("nb_wq", [128, 3, 1024], BF16)
        qT = al("nb_qT", [128, 2, S_LEN], BF16)
        kT = al("nb_kT", [128, 2, S_LEN], BF16)
        V = al("nb_V", [128, 2, NT, 2, 66], BF16)
        OT = al("nb_OT", [128, KC, S_LEN], BF16)
        Tb = al("nb_T", [128, 2, 2, 14, 64], F32)
        msk = al("nb_msk", [128, 64], F32)
        sb = al("nb_sb", [128, 2, 5, 64], F32)
        PT = al("nb_PT", [128, 2, 5, 64], BF16)
        sq = al("nb_sq", [128, 512], BF16)
        t1 = al("nb_t1", [128, 512], F32)
        osb = al("nb_osb", [128, 2, 128], BF16)
        sm_ = al("nb_sm", [128, 8], F32)
        wo = al("nb_wo", [128, KC, 512], BF16)
        wqd = [Dep(), Dep(), Dep()]
        qTd, kTd, Vd, Td = [Dep(), Dep()], [Dep(), Dep()], [Dep(), Dep()], [Dep(), Dep()]
        sbd, PTd, osbd = [Dep(), Dep()], [Dep(), Dep()], [Dep(), Dep()]
        OTd, mskd, sqd, t1d, smd, wod = (Dep() for _ in range(6))
        allh = list(self.hTd)
        psd, ps = self.psd, self.ps
        cbo = self.cbo
        blockmean = self.cb[:, cbo["blockmean"]:cbo["blockmean"] + 128]
        wqkv = self.wview("c_wqkv")
        o_b, shp = self.wo["c_bias_f32"]
        bias_d = self.wbig[o_b:o_b + 8 * 128 * 1792].rearrange("(a p n) -> a p n", p=128, n=1792)
        self.load(msk[:], self.cf_d[:, self.cfo["nb_mask"]:self.cfo["nb_mask"] + 64], (), [mskd])
        for b in range(2):
            self.memset("pool", V[:, b, :, :, 64:65], 1.0, [Vd[b]])
        self.ts("dve", sm_[:, 0:1], self.wsc("c_qg"), 0.125, None, ALU.mult, None, [self.wsd], [smd])

        def load_w(hc):
            b = hc % 2
            for i, ci in enumerate((hc, 8 + hc, 16 + hc)):
                self.load(wq[:, i, :], wqkv[ci].rearrange("p k c -> p (k c)"), [self.wbf_dep], [wqd[i]])
            self.load(Tb[:, b].rearrange("p c a q -> p (c a q)"), bias_d[hc], (), [Td[b]])
            self.tt("pool", Tb[:, b].rearrange("p c a q -> p (c a) q"), Tb[:, b].rearrange("p c a q -> p (c a) q"),
                    msk[:].unsqueeze(1).to_broadcast([128, 28, 64]), ALU.add, [Td[b], mskd], [Td[b]])

        def proj_qk(hc, which, tc):
            b = hc % 2
            dst, dstd = (qT, qTd) if which == 0 else (kT, kTd)
            gain = sm_[:, 0:1] if which == 0 else self.wsc("c_kg")
            c0 = tc * 512
            for kc in range(KC):
                self.mm(ps[:, 4, :], wq[:, which, kc * 128:(kc + 1) * 128], self.hT[:, kc, c0:c0 + 512],
                        kc == 0, kc == KC - 1, [wqd[which]] + allh[tc * 4:tc * 4 + 4], [psd[4]])
            self.act(sq[:], ps[:, 4, :], AF.Square, [psd[4]], [sqd])
            self.mm(ps[:, 5, :], blockmean, sq[:], True, True, [sqd, self.cbd], [psd[5]])
            self.act(t1[:], ps[:, 5, :], AF.Sqrt, [psd[5], self.wsd], [t1d], bias=self.wsc("eps6"))
            self.S.op("dve", lambda e: e.reciprocal(out=t1[:], in_=t1[:]), [t1d], [t1d])
            self.stt(dst[:, b, c0:c0 + 512], ps[:, 4, :], gain, t1[:], ALU.mult, ALU.mult,
                     [psd[4], t1d, smd, self.wsd], [dstd[b]])

        def proj_v(hc, tg):
            b = hc % 2
            for i in range(4):
                t = tg * 4 + i
                for kc in range(KC):
                    self.mm(ps[:, 7, i * 128:(i + 1) * 128], self.hT[:, kc, t * 128:(t + 1) * 128],
                            wq[:, 2, kc * 128:(kc + 1) * 128], kc == 0 and i == 0, kc == KC - 1,
                            [wqd[2], allh[t]], [psd[7]])
            self.cp("act", V[:, b, tg * 4:tg * 4 + 4, :, 0:64],
                    ps[:, 7, :].rearrange("p (a c e) -> p a c e", a=4, c=2), [psd[7]], [Vd[b]])

        def proj_units(hc):
            units = []
            for tc in range(4):
                units.append((lambda tc=tc: proj_qk(hc, 0, tc)))
                units.append((lambda tc=tc: proj_qk(hc, 1, tc)))
                units.append((lambda tc=tc: proj_v(hc, tc)))
            return units

        pending = []

        def row_block(hc, r):
            b = hc % 2
            rs = min(max(r - 4, 0), 24)
            a0, a1 = rs // 2, (rs + 7) // 2
            ns = a1 - a0 + 1
            ph = (r % 2) * 64
            tb = (r // 2) % 2
            for c in range(2):
                pr = slice(c * 64, (c + 1) * 64)
                sbank = c
                for si in range(ns):
                    a = a0 + si
                    self.mm(ps[:, sbank, si * 64:(si + 1) * 64], kT[pr, b, a * 128:(a + 1) * 128],
                            qT[pr, b, r * 64:(r + 1) * 64], si == 0, True, [kTd[b], qTd[b]], [psd[sbank]])
                base0 = 2 * a0 - r + 7
                self.tt("dve", sb[:, c, 0:ns, :], ps[:, sbank, 0:ns * 64].rearrange("p (s q) -> p s q", s=ns),
                        Tb[:, b, c, base0:base0 + 2 * ns - 1:2, :], ALU.add, [psd[sbank], Td[b]], [sbd[c]])
                if rs % 2 == 0:
                    self.act(PT[:, c, 0:ns, :], sb[:, c, 0:ns, :], AF.Exp, [sbd[c]], [PTd[c]])
                else:
                    self.act(PT[:, c, 0:1, :], sb[:, c, 0:1, :], AF.Exp, [sbd[c], self.wsd], [PTd[c]],
                             bias=self.wsc("hoff", 0, 1))
                    self.act(PT[:, c, 1:ns - 1, :], sb[:, c, 1:ns - 1, :], AF.Exp, [sbd[c]], [PTd[c]])
                    self.act(PT[:, c, ns - 1:ns, :], sb[:, c, ns - 1:ns, :], AF.Exp, [sbd[c], self.wsd], [PTd[c]],
                             bias=self.wsc("hoff", 1, 1))
            if pending:
                pending.pop(0)()
            for c in range(2):
                for si in range(ns):
                    a = a0 + si
                    self.mm(ps[ph:ph + 64, 2, c * 65:(c + 1) * 65], PT[:, c, si, :], V[:, b, a, c, 0:65],
                            si == 0, si == ns - 1, [PTd[c], Vd[b]], [psd[2]])
            pv = ps[ph:ph + 64, 2, 0:130].rearrange("p (c w) -> p c w", c=2)
            rz = sm_[ph:ph + 64, 2:4]
            self.S.op("dve", lambda e: e.reciprocal(out=rz.unsqueeze(2), in_=pv[:, :, 64:65]), [psd[2]], [smd])
            self.tt("dve", osb[ph:ph + 64, tb, :].rearrange("p (c e) -> p c e", c=2), pv[:, :, 0:64],
                    rz.unsqueeze(2).to_broadcast([64, 2, 64]), ALU.mult, [psd[2], smd], [osbd[tb]])
            if r % 2 == 1:
                def flush(hc=hc, r=r, tb=tb):
                    t = r // 2
                    pst = ps[:, 3, :].bitcast(BF16)
                    self.tr(pst[:, 0:128], osb[:, tb, :], [osbd[tb], self.cbd], [psd[3]])
                    self.cp("act", OT[:, hc, t * 128:(t + 1) * 128], pst[:, 0:128], [psd[3]], [OTd])
                pending.append(flush)

        load_w(0)
        for u in proj_units(0):
            u()
        for hc in range(8):
            units = []
            if hc + 1 < 8:
                load_w(hc + 1)
                units = proj_units(hc + 1)
            ui = 0
            for r in range(32):
                row_block(hc, r)
                n_take = (len(units) - ui + (31 - r)) // (32 - r)
                for _ in range(n_take):
                    units[ui]()
                    ui += 1
        while pending:
            pending.pop(0)()
        wov = self.wview("c_wo")
        for nh in range(2):
            self.load(wo[:], wov[:, :, nh * 512:(nh + 1) * 512].rearrange("k p n -> p k n"), [self.wbf_dep], [wod])
            for t in range(NT):
                bank = 4 + (t % 2)
                for kc in range(KC):
                    self.mm(ps[:, bank, :], OT[:, kc, t * 128:(t + 1) * 128], wo[:, kc, :], kc == 0, kc == KC - 1,
                            [OTd, wod], [psd[bank]])
                self.tt("dve", self.x[:, t, nh * 512:(nh + 1) * 512], self.x[:, t, nh * 512:(nh + 1) * 512],
                        ps[:, bank, :], ALU.add, [psd[bank], self.xd[t]], [self.xd[t]])


Prog.mixer2 = _attn_nbr


def _rwkv(self, L):
    nc, S = self.nc, self.S
    ps, psd = self.ps, self.psd
    cbo, cfo = self.cbo, self.cfo
    bank_i = [0]

    def nb():
        b = bank_i[0] % 8
        bank_i[0] += 1
        return b

    def nb2():
        s0 = (bank_i[0] + 1) // 2 * 2
        bank_i[0] = s0 + 2
        return s0 % 8

    def ps2(b):
        return ps[:, b:b + 2, :].rearrange("p a b -> p (a b)")

    def pd2(b):
        return [psd[b], psd[b + 1]]

    def reduce_x(out, in_, r, w):
        self.S.op("dve", lambda e: e.tensor_reduce(out=out, in_=in_, axis=AX.X, op=ALU.add), r, w)

    def recip(ap, r, w):
        self.S.op("dve", lambda e: e.reciprocal(out=ap, in_=ap), r, w)

    S.barrier()
    with contextlib.ExitStack() as st_outer:
        alo = lambda n, sh, dt: st_outer.enter_context(self.sbt(n, sh, dt))
        lT = alo("rw_lT", [128, 3, S_LEN], BF16)
        lTd = Dep()
        allh = list(self.hTd)
        xd = self.xd
        with contextlib.ExitStack() as st:
            self.rmsnorm_T(f"n{L}_attn", st)
        for t in range(NT):
            self.load(self.xsp[t], self.x[:, t, :], [xd[t]], [self.xspd[t]], q="pool")
        S.barrier()
        with contextlib.ExitStack() as st:
            al = lambda n, sh, dt: st.enter_context(self.sbt(n, sh, dt))
            tmp = al("rwa_tmp", [128, S_LEN], F32)
            W = al("rwa_W", [128, KC, 1024], BF16)
            stage = al("rwa_stage", [128, 2, 1024], BF16)
            lw = al("rwa_lw", [128, KC, 128], BF16)
            coef = al("rwa_coef", [128, 2, 48], F32)
            tmpd, Wd, lwd, coefd, shd, mixd = (Dep() for _ in range(6))
            staged = [Dep(), Dep()]
            xb = self.x[:].rearrange("p a b -> p (a b)").bitcast(BF16)
            sh = xb[:, 0:KC * S_LEN].rearrange("p (k t) -> p k t", k=KC)
            mix = xb[:, KC * S_LEN:2 * KC * S_LEN].rearrange("p (k t) -> p k t", k=KC)
            mu = self.wsc("d_mu", 0, 48)
            self.ts("dve", coef[:, 0, :], mu, -1.0, 1.0, ALU.mult, ALU.add, [self.wsd], [coefd])
            self.ts("dve", coef[:, 1, :], mu, 0.5, None, ALU.mult, None, [self.wsd], [coefd])
            for kc in range(KC):
                self.tt("dve", sh[:, kc, 1:S_LEN - 1], self.hT[:, kc, 0:S_LEN - 2],
                        self.hT[:, kc, 2:S_LEN], ALU.add, allh, [shd] + xd)
            self.cp("dve", sh[:, :, 0:1], self.hT[:, :, 1:2], allh, [shd] + xd)
            self.cp("dve", sh[:, :, S_LEN - 1:S_LEN], self.hT[:, :, S_LEN - 2:S_LEN - 1], allh, [shd] + xd)

            def make_mix(j):
                for kc in range(KC):
                    self.act(tmp[:], self.hT[:, kc, :], AF.Identity, allh + [coefd], [tmpd],
                             scale=coef[:, 0, j * 8 + kc:j * 8 + kc + 1])
                    self.stt(mix[:, kc, :], sh[:, kc, :], coef[:, 1, j * 8 + kc:j * 8 + kc + 1], tmp[:],
                             ALU.mult, ALU.add, [shd, tmpd, coefd], [mixd] + xd)

            def proj_tm(wname, idx):
                self.load(W[:], self.wview(wname).rearrange("k p n -> p k n"), [self.wbf_dep], [Wd])
                for t in range(NT):
                    b2 = nb2()
                    for nh in range(2):
                        for kc in range(KC):
                            self.mm(ps[:, b2 + nh, :], mix[:, kc, t * 128:(t + 1) * 128],
                                    W[:, kc, nh * 512:(nh + 1) * 512], kc == 0, kc == KC - 1, [mixd, Wd],
                                    [psd[b2 + nh]])
                    sb_ = t % 2
                    self.cp("act", stage[:, sb_, :], ps2(b2), pd2(b2), [staged[sb_]])
                    self.load(self.rkvsp[idx, t], stage[:, sb_, :], [staged[sb_]], [self.rkvd[idx][t]], q="pool")

            def proj_lora(wname, li, func):
                self.load(lw[:], self.wview(wname).rearrange("k p n -> p k n"), [self.wbf_dep], [lwd])
                for tc in range(4):
                    b = nb()
                    for kc in range(KC):
                        self.mm(ps[:, b, :], lw[:, kc, :], mix[:, kc, tc * 512:(tc + 1) * 512], kc == 0,
                                kc == KC - 1, [mixd, lwd], [psd[b]])
                    self.act(lT[:, li, tc * 512:(tc + 1) * 512], ps[:, b, :], func, [psd[b]], [lTd])

            make_mix(0)
            proj_tm("d_w_r", 0)
            make_mix(1)
            proj_lora("d_w1s", 0, AF.Tanh)
            make_mix(2)
            proj_tm("d_w_k", 1)
            make_mix(3)
            proj_tm("d_w_v", 2)
            make_mix(4)
            proj_lora("d_a1s", 1, AF.Identity)
            make_mix(5)
            proj_lora("d_g1", 2, AF.Sigmoid)
        S.barrier()
        import os
        stop = os.environ.get("RW_STOP", "")
        if stop:
            nbt = int(os.environ.get("RW_NT", "16"))
        if stop == "A":
            for t in range(NT):
                self.load(self.x[:, t, :], self.xsp[t], [self.xspd[t]], [xd[t]])
            return
        hTf = self.hT[:].rearrange("p k t -> p (k t)")
        with contextlib.ExitStack() as st:
            al = lambda n, sh_, dt: st.enter_context(self.sbt(n, sh_, dt))
            tri = al("rwb_tri", [128, 4, 128], F32)
            onesf = al("rwb_onesf", [128, 128], F32)
            brow = al("rwb_brow", [128, 2048], F32)
            kkb = al("rwb_kkb", [128, 2, 1024], F32)
            w2a2 = al("rwb_w2a2", [128, 2, 1024], BF16)
            rkv = al("rwb_rkv", [128, 3, 1024], BF16)
            F = al("rwb_F", [128, 6, 1024], F32)
            Bt = al("rwb_Bt", [128, 4, 1024], BF16)
            ARc = al("rwb_ARc", [128, 8, 2, 128], BF16)
            Bc = al("rwb_Bc", [128, 8, 128], BF16)
            Kc = al("rwb_Kc", [128, 8, 128], BF16)
            IB = al("rwb_IB", [128, 6, 4, 128], F32)
            lvlm = al("rwb_lvlm", [128, 5, 128], F32)
            identf = al("rwb_identf", [128, 128], F32)
            RU = al("rwb_RU", [128, 2, 1024], BF16)
            S32 = al("rwb_S32", [128, 8, 64], F32)
            Sb = al("rwb_Sb", [128, 8, 64], BF16)
            WL = al("rwb_WL", [128, 8], F32)
            small = al("rwb_small", [128, 16], F32)
            cstd, rkvsd, Fd, Btd, cmd, NAd, NTd, RUd, S32d, Sbd, WLd, smd = (
                Dep(), [Dep(), Dep(), Dep()], [Dep() for _ in range(6)], [Dep() for _ in range(4)],
                Dep(), Dep(), Dep(), [Dep(), Dep()], Dep(), Dep(), Dep(), Dep())
            NA3 = hTf[:, 0:6144].rearrange("p (h a t) -> p h a t", h=16, a=3)
            Xb = hTf[:, 6144:8192].rearrange("p (h t) -> p h t", h=16)
            Nf = hTf[:, 8192:12288].bitcast(F32).rearrange("p (h t) -> p h t", h=16)
            NTf = hTf[:, 12288:16384].bitcast(F32).rearrange("p (h t) -> p h t", h=16)
            Nfd, NTfd, Xbd = Dep(), Dep(), Dep()
            ibd = [Dep() for _ in range(6)]
            ibdB = [Dep() for _ in range(6)]
            masks = self.cb[:, cbo["masks"]:cbo["masks"] + 512].rearrange("p (a t) -> p a t", a=4)
            identb = self.ident
            self.load(tri[:].rearrange("p a t -> p (a t)"), self.cf_d[:, cfo["tri"]:cfo["tri"] + 512], (), [cstd])
            self.load(onesf[:], self.cf_d[:, cfo["onesf"]:cfo["onesf"] + 128], (), [cstd])
            self.load(identf[:], self.cf_d[:, cfo["identf"]:cfo["identf"] + 128], (), [cstd])
            self.load(lvlm[:].rearrange("p a t -> p (a t)"), self.cf_d[:, cfo["lvlmask"]:cfo["lvlmask"] + 640], (), [cstd])
            self.load(brow[:], self.cf_d[:, cfo["brow"]:cfo["brow"] + 2048], (), [cstd])
            self.load(kkb[:, 0, :], self.cf_d[:, cfo["d_k_k"]:cfo["d_k_k"] + 1024], (), [cstd])
            self.load(kkb[:, 1, :], self.cf_d[:, cfo["d_k_a"]:cfo["d_k_a"] + 1024], (), [cstd])
            self.load(w2a2[:, 0, :], self.wview("d_w2s"), [self.wbf_dep], [cstd])
            self.load(w2a2[:, 1, :], self.wview("d_a2s"), [self.wbf_dep], [cstd])
            negc2 = self.wsc("negc", 0, 2)

            def lora_sig(d, which, n, dstF, dstd):
                prd = slice(d * 64, (d + 1) * 64)
                rowp = d * 64
                b2 = nb2()
                for nh in range(2):
                    self.mm(ps[:, b2 + nh, :], lT[prd, which, n * 128:(n + 1) * 128],
                            w2a2[prd, which, nh * 512:(nh + 1) * 512], True, False, [lTd, cstd], [psd[b2 + nh]])
                    c0_ = which * 1024 + nh * 512
                    self.mm(ps[:, b2 + nh, :], onesf[rowp:rowp + 1, :], brow[rowp:rowp + 1, c0_:c0_ + 512],
                            False, True, [cstd], [psd[b2 + nh]])
                self.act(dstF, ps2(b2), AF.Sigmoid, pd2(b2), [dstd])

            def scan_tile(d, n):
                for i in range(3):
                    self.load(rkv[:, i, :], self.rkvsp[i, n], [self.rkvd[i][n]], [rkvsd[i]])
                F0, F1, F2, F3, F4, F5 = (F[:, i, :] for i in range(6))
                v3 = lambda ap: ap.rearrange("p (h j) -> p h j", h=16)
                lora_sig(d, 0, n, F0, Fd[0])
                lora_sig(d, 1, n, F1, Fd[1])
                cut = int(os.environ.get("RW_CUT", "99"))
                if cut <= 1:
                    return
                self.tt("dve", F5, rkv[:, 1, :], kkb[:, 0, :], ALU.mult, [rkvsd[1], cstd], [Fd[5]])
                self.tt("pool", F2, F5, F5, ALU.mult, [Fd[5]], [Fd[2]])
                reduce_x(small[:, 0:16], v3(F2), [Fd[2]], [smd])
                self.act(small[:, 0:16], small[:, 0:16], AF.Sqrt, [smd], [smd])
                self.ts("dve", small[:, 0:16], small[:, 0:16], 1e-12, None, ALU.max, None, [smd], [smd])
                recip(small[:, 0:16], [smd], [smd])
                self.tt("dve", v3(F5), v3(F5), small[:, 0:16].unsqueeze(2).to_broadcast([128, 16, 64]), ALU.mult,
                        [Fd[5], smd], [Fd[5]])
                if cut <= 2:
                    return
                bi, be = nb2(), nb2()
                for nh in range(2):
                    self.mm(ps[:, bi + nh, :], tri[:, 2 * d, :], F0[:, nh * 512:(nh + 1) * 512], True, True,
                            [cstd, Fd[0]], [psd[bi + nh]])
                    self.mm(ps[:, be + nh, :], tri[:, 2 * d + 1, :], F0[:, nh * 512:(nh + 1) * 512], True, True,
                            [cstd, Fd[0]], [psd[be + nh]])
                self.act(F2, ps2(bi), AF.Exp, pd2(bi), [Fd[2]])
                self.act(F3, ps2(bi), AF.Exp, pd2(bi), [Fd[3]], scale=-1.0)
                self.act(F4, ps2(be), AF.Exp, pd2(be), [Fd[4]])
                bw = nb()
                for hp in range(8):
                    self.mm(ps[:, bw, hp * 2:hp * 2 + 2], F0[:, hp * 128:(hp + 1) * 128], negc2, hp == 0, True,
                            [Fd[0], self.wsd], [psd[bw]])
                self.act(WL[:], ps[:, bw, 0:16:2], AF.Exp, [psd[bw]], [WLd])
                if cut <= 3:
                    return
                self.stt(Bt[:, 0, :], F5, -1.0, F4, ALU.mult, ALU.mult, [Fd[5], Fd[4]], [Btd[0]])
                self.tt("pool", F4, F5, F1, ALU.mult, [Fd[5], Fd[1]], [Fd[4]])
                self.tt("dve", Bt[:, 1, :], F4, F3, ALU.mult, [Fd[4], Fd[3]], [Btd[1]])
                self.stt(F4, F1, -1.0, kkb[:, 1, :], ALU.add, ALU.mult, [Fd[1], cstd], [Fd[4]])
                self.stt(F5, F4, 1.0, rkv[:, 1, :], ALU.add, ALU.mult, [Fd[4], rkvsd[1]], [Fd[5]])
                self.tt("dve", Bt[:, 2, :], F5, F3, ALU.mult, [Fd[5], Fd[3]], [Btd[2]])
                self.tt("dve", Bt[:, 3, :], rkv[:, 0, :], F2, ALU.mult, [rkvsd[0], Fd[2]], [Btd[3]])
                if cut <= 4:
                    return
                for si, dst, eng in ((0, ARc[:, :, 0, :], "act"), (3, ARc[:, :, 1, :], "dve"), (1, Bc[:], "act"),
                                     (2, Kc[:], "dve")):
                    b = nb()
                    pst = ps[:, b, :].bitcast(BF16)
                    for hp in range(8):
                        self.tr(pst[:, hp * 128:(hp + 1) * 128], Bt[:, si, hp * 128:(hp + 1) * 128],
                                [Btd[si], self.cbd], [psd[b]])
                    self.cp(eng, dst, pst.rearrange("p (h t) -> p h t", h=8), [psd[b]], [cmd])
                if os.environ.get("RW_DUMP"):
                    dd = [Dep() for _ in range(NT)]
                    for i in range(6):
                        self.cp("act", self.x[:, i, :], F[:, i, :], [Fd[i]], [xd[i]])
                    for i in range(4):
                        self.cp("act", self.x[:, 6 + i, :], Bt[:, i, :], [Btd[i]], [xd[6 + i]])
                    for i in range(3):
                        self.cp("act", self.x[:, 10 + i, :], rkv[:, i, :], [rkvsd[i]], [xd[10 + i]])
                    return
                if cut <= 5:
                    return
                m12 = masks[:, 0:2, :] if d == 0 else masks[:, 2:4, :]
                m3 = masks[:, 2, :] if d == 0 else masks[:, 0, :]
                for h in range(16):
                    hp, pr = h // 2, slice((h % 2) * 64, (h % 2) * 64 + 64)
                    b = nb()
                    arv = ARc[pr, hp, :, :].rearrange("p a t -> p (a t)")
                    self.mm(ps[:, b, 0:256], Bc[pr, hp, :], arv, True, True, [cmd], [psd[b]])
                    self.mm(ps[:, b, 256:512], Kc[pr, hp, :], arv, False, True, [cmd], [psd[b]])
                    self.tt("dve", Nf[:, h, :], ps[:, b, 0:128], m12[:, 0, :], ALU.mult, [psd[b], self.cbd],
                            [Nfd] + allh)
                    self.tt("dve", NA3[:, h, 0, :], ps[:, b, 128:256], m12[:, 1, :], ALU.mult, [psd[b], self.cbd],
                            [NAd] + allh)
                    self.tt("dve", NA3[:, h, 1:3, :], ps[:, b, 256:512].rearrange("p (a t) -> p a t", a=2),
                            m12, ALU.mult, [psd[b], self.cbd], [NAd] + allh)
                for g8 in range(2):
                    bb = nb2()
                    for par in range(2):
                        pr = slice(par * 64, par * 64 + 64)
                        for i4 in range(4):
                            hp = g8 * 4 + i4
                            self.mm(ps[:, bb + par, i4 * 128:(i4 + 1) * 128], ARc[pr, hp, 0, :], Bc[pr, hp, :],
                                    i4 == 0, True, [cmd], [psd[bb + par]])
                        self.tt("dve", NTf[:, g8 * 8 + par:g8 * 8 + 8:2, :],
                                ps[:, bb + par, :].rearrange("p (h t) -> p h t", h=4),
                                m3.unsqueeze(1).to_broadcast([128, 4, 128]), ALU.mult, [psd[bb + par], self.cbd],
                                [NTfd] + allh)
                if cut <= 6:
                    return
                bc4 = lambda ap: ap.unsqueeze(1).to_broadcast([128, 4, 128])
                p4 = lambda b_: ps[:, b_, :].rearrange("p (h t) -> p h t", h=4)
                Fv = F[:, 0:3, :].rearrange("p a n -> p (a n)").rearrange("p (i h t) -> p i h t", i=6, h=4)
                bufsets = [([IB[:, i] for i in range(6)], ibd, [], []),
                           ([Fv[:, i] for i in range(6)], ibdB, list(Fd[0:3]), list(Fd[0:3]))]

                def inv_group(g4, bset):
                    bufs_, deps_, xr, xw = bset
                    hs = slice(g4 * 4, g4 * 4 + 4)
                    P1, PT1, Q, P2, PT2, Y1 = bufs_
                    dP1, dPT1, dQ, dP2, dPT2, dY1 = deps_

                    def mm4(bank, lhs, rhs, r):
                        for hh in range(4):
                            self.mm(ps[:, bank, hh * 128:(hh + 1) * 128], lhs[:, hh, :], rhs[:, hh, :], hh == 0, True,
                                    r + xr, [psd[bank]])
                    self.tt("dve", P1, Nf[:, hs, :], bc4(lvlm[:, 0, :]), ALU.mult, [Nfd, cstd] + xr, [dP1] + xw)
                    self.tt("pool", PT1, NTf[:, hs, :], bc4(lvlm[:, 0, :]), ALU.mult, [NTfd, cstd] + xr, [dPT1] + xw)
                    self.tt("pool", Q, P1, bc4(identf[:]), ALU.add, [dP1, cstd] + xr, [dQ] + xw)
                    yield
                    b1, b2_ = nb(), nb()
                    mm4(b1, PT1, P1, [dP1, dPT1])
                    mm4(b2_, P1, PT1, [dP1, dPT1])
                    yield
                    self.cp("act", P2, p4(b1), [psd[b1]] + xr, [dP2])
                    self.cp("act", PT2, p4(b2_), [psd[b2_]] + xr, [dPT2])
                    yield
                    b3, b4 = nb(), nb()
                    mm4(b3, PT2, Q, [dPT2, dQ])
                    mm4(b4, P2, PT2, [dP2, dPT2])
                    yield
                    self.tt("dve", Q, Q, p4(b3), ALU.add, [psd[b3], dQ] + xr, [dQ])
                    self.cp("act", PT1, p4(b4), [psd[b4]] + xr, [dPT1])
                    yield
                    b5 = nb()
                    mm4(b5, PT1, Q, [dPT1, dQ])
                    yield
                    self.tt("dve", Q, Q, p4(b5), ALU.add, [psd[b5], dQ] + xr, [dQ])
                    yield
                    b6 = nb()
                    for hh in range(4):
                        self.tr(ps[:, b6, hh * 128:(hh + 1) * 128], Q[:, hh, :], [dQ, cstd] + xr, [psd[b6]],
                                ident=identf[:])
                    Xt, dXt = P2, dP2
                    X_, dX = Q, dQ
                    NTm, dNTm = PT2, dPT2
                    self.tt("pool", NTm, NTf[:, hs, :], bc4(lvlm[:, 1, :]), ALU.mult, [NTfd, cstd] + xr, [dNTm])
                    yield
                    self.cp("act", Xt, p4(b6), [psd[b6]] + xr, [dXt])
                    for li in range(1, 5):
                        lastl = li == 4
                        by = nb()
                        mm4(by, NTm, X_, [dNTm, dX])
                        yield
                        self.cp("act", Y1, p4(by), [psd[by]] + xr, [dY1])
                        if not lastl:
                            self.tt("pool", NTm, NTf[:, hs, :], bc4(lvlm[:, li + 1, :]), ALU.mult,
                                    [NTfd, cstd] + xr, [dNTm])
                        yield
                        bx = nb()
                        mm4(bx, Xt, Y1, [dXt, dY1])
                        if not lastl:
                            bxt = nb()
                            mm4(bxt, Y1, Xt, [dY1, dXt])
                        yield
                        if not lastl:
                            self.tt("dve", X_, X_, p4(bx), ALU.add, [psd[bx], dX] + xr, [dX])
                            self.tt("dve", Xt, Xt, p4(bxt), ALU.add, [psd[bxt], dXt] + xr, [dXt])
                            yield
                        else:
                            self.tt("dve", Xb[:, hs, :], X_, p4(bx), ALU.add, [psd[bx], dX] + xr, [Xbd] + allh)

                for rnd in range(2):
                    gens = [inv_group(2 * rnd, bufsets[0]), inv_group(2 * rnd + 1, bufsets[1])]
                    alive = list(gens)
                    while alive:
                        nxt = []
                        for g_ in alive:
                            try:
                                next(g_)
                                nxt.append(g_)
                            except StopIteration:
                                pass
                        alive = nxt
                X, Xd = Xb, Xbd
                if cut <= 7:
                    return
                Vt = rkv[:, 2, :]
                bR = nb2()
                for h in range(16):
                    hp, pr = h // 2, slice((h % 2) * 64, (h % 2) * 64 + 64)
                    o = ps[:, bR + h % 2, hp * 64:hp * 64 + 64]
                    self.mm(o, ARc[pr, hp, 0, :], Sb[pr, hp, :], True, False, [cmd, Sbd], [psd[bR + h % 2]])
                    self.mm(o, NA3[:, h, 1, :], Vt[:, h * 64:(h + 1) * 64], False, True, [NAd, rkvsd[2]],
                            [psd[bR + h % 2]])
                self.cp("act", RU[:, 0, :].rearrange("p (h a i) -> p h a i", h=8, a=2),
                        ps2(bR).rearrange("p (a h i) -> p h a i", a=2, h=8), pd2(bR), [RUd[0]])
                bU = nb2()
                for h in range(16):
                    o = ps[:, bU + h // 8, (h % 8) * 64:(h % 8) * 64 + 64]
                    self.mm(o, X[:, h, :], RU[:, 0, h * 64:(h + 1) * 64], True, True, [Xd, RUd[0]], [psd[bU + h // 8]])
                self.cp("dve", RU[:, 1, :], ps2(bU), pd2(bU), [RUd[1]])
                bY = nb2()
                for h in range(16):
                    hp, pr = h // 2, slice((h % 2) * 64, (h % 2) * 64 + 64)
                    o = ps[:, bY + h % 2, hp * 64:hp * 64 + 64]
                    self.mm(o, ARc[pr, hp, 1, :], Sb[pr, hp, :], True, False, [cmd, Sbd], [psd[bY + h % 2]])
                    self.mm(o, NA3[:, h, 0, :], RU[:, 1, h * 64:(h + 1) * 64], False, False, [NAd, RUd[1]],
                            [psd[bY + h % 2]])
                    self.mm(o, NA3[:, h, 2, :], Vt[:, h * 64:(h + 1) * 64], False, True, [NAd, rkvsd[2]],
                            [psd[bY + h % 2]])
                yv = self.x[:, n, :].rearrange("p (h a i) -> p h a i", h=8, a=2)
                pyv = ps2(bY).rearrange("p (a h i) -> p h a i", a=2, h=8)
                if d == 0:
                    self.cp("act", yv, pyv, pd2(bY), [xd[n]])
                else:
                    self.tt("dve", yv, yv, pyv, ALU.add, pd2(bY) + [xd[n]], [xd[n]])
                bS = nb()
                for h in range(16):
                    hp, pr = h // 2, slice((h % 2) * 64, (h % 2) * 64 + 64)
                    o = ps[pr, bS, hp * 64:(hp + 1) * 64]
                    self.mm(o, Bt[:, 1, h * 64:(h + 1) * 64], RU[:, 1, h * 64:(h + 1) * 64], True, False,
                            [Btd[1], RUd[1]], [psd[bS]])
                    self.mm(o, Bt[:, 2, h * 64:(h + 1) * 64], Vt[:, h * 64:(h + 1) * 64], False, True,
                            [Btd[2], rkvsd[2]], [psd[bS]])
                self.tt("dve", S32[:], S32[:], ps[:, bS, :].rearrange("p (h i) -> p h i", h=8), ALU.add,
                        [psd[bS], S32d], [S32d])
                self.tt("dve", S32[:], S32[:], WL[:].unsqueeze(2).to_broadcast([128, 8, 64]), ALU.mult,
                        [S32d, WLd], [S32d])
                self.cp("act", Sb[:], S32[:], [S32d], [Sbd])

            for d in range(2):
                self.memset("pool", S32[:], 0.0, [S32d])
                self.memset("pool", Sb[:], 0.0, [Sbd])
                order = list(range(NT)) if d == 0 else list(range(NT - 1, -1, -1))
                if stop:
                    order = order[:nbt]
                for n in order:
                    scan_tile(d, n)
        S.barrier()
        if stop == "BY":
            return
        if stop == "B":
            for t in range(NT):
                self.load(self.x[:, t, :], self.xsp[t], [self.xspd[t]], [xd[t]])
            return
        with contextlib.ExitStack() as st:
            al = lambda n, sh_, dt: st.enter_context(self.sbt(n, sh_, dt))
            onesf = al("rwc_onesf", [128, 128], F32)
            brow = al("rwc_brow", [128, 2048], F32)
            bc4 = al("rwc_bc4", [128, 4, 1024], F32)
            w2a2 = al("rwc_w2a2", [128, 2, 1024], BF16)
            g2 = al("rwc_g2", [128, 1024], BF16)
            Wo = al("rwc_Wo", [128, KC, 1024], BF16)
            rkv = al("rwc_rkv", [128, 3, 1024], BF16)
            xt = al("rwc_xt", [128, 1024], F32)
            F = al("rwc_F", [128, 5, 1024], F32)
            ob = al("rwc_ob", [128, 1024], BF16)
            oT = al("rwc_oT", [128, KC, 128], BF16)
            small = al("rwc_small", [128, 48], F32)
            cstd, xtd, obd, oTd, smd = (Dep() for _ in range(5))
            rkvsd = [Dep(), Dep(), Dep()]
            Fd = [Dep() for _ in range(5)]
            self.load(onesf[:], self.cf_d[:, cfo["onesf"]:cfo["onesf"] + 128], (), [cstd])
            self.load(brow[:], self.cf_d[:, cfo["brow"]:cfo["brow"] + 2048], (), [cstd])
            for i, nm in enumerate(("d_k_a", "d_r_k", "d_ln_g", "d_ln_b")):
                self.load(bc4[:, i, :], self.cf_d[:, cfo[nm]:cfo[nm] + 1024], (), [cstd])
            self.load(w2a2[:, 0, :], self.wview("d_w2s"), [self.wbf_dep], [cstd])
            self.load(w2a2[:, 1, :], self.wview("d_a2s"), [self.wbf_dep], [cstd])
            self.load(g2[:], self.wview("d_g2"), [self.wbf_dep], [cstd])
            self.load(Wo[:], self.wview("d_w_o").rearrange("k p n -> p k n"), [self.wbf_dep], [cstd])
            v3 = lambda ap: ap.rearrange("p (h j) -> p h j", h=16)
            bc16 = lambda ap: ap.unsqueeze(2).to_broadcast([128, 16, 64])
            for n in range(NT):
                for i in range(3):
                    self.load(rkv[:, i, :], self.rkvsp[i, n], [self.rkvd[i][n]], [rkvsd[i]])
                self.load(xt[:], self.xsp[n], [self.xspd[n]], [xtd])
                F0, F1, F2, F3, F4 = (F[:, i, :] for i in range(5))
                for d in range(2):
                    prd = slice(d * 64, (d + 1) * 64)
                    rowp = d * 64
                    b2 = nb2()
                    for nh in range(2):
                        self.mm(ps[:, b2 + nh, :], lT[prd, 1, n * 128:(n + 1) * 128],
                                w2a2[prd, 1, nh * 512:(nh + 1) * 512], True, False, [lTd, cstd], [psd[b2 + nh]])
                        self.mm(ps[:, b2 + nh, :], onesf[rowp:rowp + 1, :],
                                brow[rowp:rowp + 1, 1024 + nh * 512:1024 + (nh + 1) * 512], False, True, [cstd],
                                [psd[b2 + nh]])
                    self.act(F[:, d, :], ps2(b2), AF.Sigmoid, pd2(b2), [Fd[d]])
                bG = nb2()
                for nh in range(2):
                    self.mm(ps[:, bG + nh, :], lT[:, 2, n * 128:(n + 1) * 128], g2[:, nh * 512:(nh + 1) * 512],
                            True, True, [lTd, cstd], [psd[bG + nh]])
                self.cp("act", F2, ps2(bG), pd2(bG), [Fd[2]])
                y = self.x[:, n, :]
                reduce_x(small[:, 0:16], v3(y), [xd[n]], [smd])
                self.ts("dve", small[:, 0:16], small[:, 0:16], 1.0 / 64, None, ALU.mult, None, [smd], [smd])
                self.tt("dve", v3(F3), v3(y), bc16(small[:, 0:16]), ALU.subtract, [xd[n], smd], [Fd[3]])
                self.tt("pool", F4, F3, F3, ALU.mult, [Fd[3]], [Fd[4]])
                reduce_x(small[:, 16:32], v3(F4), [Fd[4]], [smd])
                self.act(small[:, 16:32], small[:, 16:32], AF.Sqrt, [smd, self.wsd], [smd], scale=1.0 / 64,
                         bias=self.wsc("epsgn"))
                recip(small[:, 16:32], [smd], [smd])
                self.tt("dve", v3(F3), v3(F3), bc16(small[:, 16:32]), ALU.mult, [Fd[3], smd], [Fd[3]])
                self.tt("pool", F3, F3, bc4[:, 2, :], ALU.mult, [Fd[3], cstd], [Fd[3]])
                self.tt("pool", F3, F3, bc4[:, 3, :], ALU.add, [Fd[3], cstd], [Fd[3]])
                self.tt("dve", F0, F0, F1, ALU.add, [Fd[0], Fd[1]], [Fd[0]])
                self.ts("dve", F0, F0, 0.5, -1.0, ALU.mult, ALU.add, [Fd[0]], [Fd[0]])
                self.tt("dve", F0, F0, bc4[:, 0, :], ALU.mult, [Fd[0], cstd], [Fd[0]])
                self.stt(F0, F0, 1.0, rkv[:, 1, :], ALU.add, ALU.mult, [Fd[0], rkvsd[1]], [Fd[0]])
                self.tt("dve", F0, F0, rkv[:, 0, :], ALU.mult, [Fd[0], rkvsd[0]], [Fd[0]])
                self.tt("pool", F0, F0, bc4[:, 1, :], ALU.mult, [Fd[0], cstd], [Fd[0]])
                reduce_x(small[:, 32:48], v3(F0), [Fd[0]], [smd])
                self.tt("dve", v3(F4), v3(rkv[:, 2, :]), bc16(small[:, 32:48]), ALU.mult, [rkvsd[2], smd], [Fd[4]])
                self.tt("pool", F3, F3, F4, ALU.add, [Fd[3], Fd[4]], [Fd[3]])
                self.tt("dve", ob[:], F3, F2, ALU.mult, [Fd[3], Fd[2]], [obd])
                b = nb()
                pst = ps[:, b, :].bitcast(BF16)
                for kc in range(KC):
                    self.tr(pst[:, kc * 128:(kc + 1) * 128], ob[:, kc * 128:(kc + 1) * 128], [obd, self.cbd], [psd[b]])
                self.cp("act", oT[:], pst.rearrange("p (k t) -> p k t", k=KC), [psd[b]], [oTd])
                bO = nb2()
                for nh in range(2):
                    for kc in range(KC):
                        self.mm(ps[:, bO + nh, :], oT[:, kc, :], Wo[:, kc, nh * 512:(nh + 1) * 512], kc == 0,
                                kc == KC - 1, [oTd, cstd], [psd[bO + nh]])
                self.tt("dve", self.x[:, n, :], xt[:], ps2(bO), ALU.add, pd2(bO) + [xtd, xd[n]], [xd[n]])
    S.barrier()


Prog.mixer3 = _rwkv
```
